# Optimizing a Trainium2 kernel written in Bass

```python
import jax
import jax.numpy as jnp
from jax import lax
import numpy as np

D_MODEL = 1024
BATCH = 16
SEQ = 2048
DEPTH = 2

CTX_LEN = 256
GRID_W = 64
HEAD_DIM = 64

NA_HEADS = 6
NA_WIN_ROWS = 8
NA_WIN_COLS = 16

GLA_HEADS = 4
GLA_DK = 32
GLA_DV = 64
GLA_RANK = 16
GLA_GATE_NORM = 16.0
GLA_CHUNK = 64

GQA_Q_HEADS = 6
GQA_KV_HEADS = 2
GQA_REP = GQA_Q_HEADS // GQA_KV_HEADS
Q_BLOCK = 128
ROPE_THETA = 10000.0
ROPE_AXIS_PAIRS = HEAD_DIM // 4

FFN_HIDDEN = ((8 * D_MODEL + 3 * 256 - 1) // (3 * 256)) * 256

NA_W = NA_HEADS * HEAD_DIM
GLA_KW = GLA_HEADS * GLA_DK
GLA_VW = GLA_HEADS * GLA_DV
GQA_QW = GQA_Q_HEADS * HEAD_DIM
GQA_KVW = GQA_KV_HEADS * HEAD_DIM
MIX_W = NA_W + GLA_VW + GQA_QW
IN_WIDTHS = (NA_W, NA_W, NA_W, GLA_KW, GLA_KW, GLA_VW, GLA_VW, 2 * GLA_RANK, GQA_QW, GQA_KVW, GQA_KVW)
IN_COLS = sum(IN_WIDTHS)
IN_SPLITS = [int(s) for s in np.cumsum(IN_WIDTHS)[:-1]]

DEEPNORM_ALPHA = (2.0 * DEPTH) ** 0.25
DEEPNORM_BETA = (8.0 * DEPTH) ** -0.25
LN_EPS = 1e-5
RMS_EPS = 1e-6

kernel_name = 'hybrid_na_gla_gqa_dit_block'


def layer_norm(x, g, b):
    xf = x.astype(jnp.float32)
    mu = jnp.mean(xf, axis=-1, keepdims=True)
    var = jnp.mean(jnp.square(xf - mu), axis=-1, keepdims=True)
    return ((xf - mu) * lax.rsqrt(var + LN_EPS) * g + b).astype(x.dtype)


def rms_norm(x, w):
    xf = x.astype(jnp.float32)
    ms = jnp.mean(jnp.square(xf), axis=-1, keepdims=True)
    return (xf * lax.rsqrt(ms + RMS_EPS) * w).astype(x.dtype)


def modulate(x, shift, scale):
    return x * (1 + scale) + shift


def to_heads(t, n_heads):
    b, n, _ = t.shape
    return t.reshape(b, n, n_heads, -1).transpose(0, 2, 1, 3)


def merge_heads(t):
    b, h, n, d = t.shape
    return t.transpose(0, 2, 1, 3).reshape(b, n, h * d)


def flip_seq(t):
    return jnp.flip(t, axis=2)


def axial_rope_tables(n_tokens):
    t = jnp.arange(n_tokens)
    row = (t // GRID_W).astype(jnp.float32)
    col = (t % GRID_W).astype(jnp.float32)
    inv_freq = ROPE_THETA ** (-jnp.arange(ROPE_AXIS_PAIRS, dtype=jnp.float32) / ROPE_AXIS_PAIRS)
    ang_r = row[:, None] * inv_freq
    ang_c = col[:, None] * inv_freq
    ang = jnp.concatenate([ang_r, ang_r, ang_c, ang_c], axis=-1)
    return jnp.cos(ang), jnp.sin(ang)


def apply_axial_rope(x, cos, sin):
    xf = x.astype(jnp.float32)
    x1, x2, x3, x4 = jnp.split(xf, 4, axis=-1)
    rot = jnp.concatenate([-x2, x1, -x4, x3], axis=-1)
    return (xf * cos + rot * sin).astype(x.dtype)


def context_attention(q, k, v):
    s = jnp.einsum('bgrqd,bgkd->bgrqk', q, k, preferred_element_type=jnp.float32) * (q.shape[-1] ** -0.5)
    p = jax.nn.softmax(s, axis=-1).astype(v.dtype)
    return jnp.einsum('bgrqk,bgkd->bgrqd', p, v)


def neighborhood_attention(q, k, v, k_ctx, v_ctx, rpb):
    B, H, N, d = q.shape
    rows = N // GRID_W
    wr = min(NA_WIN_ROWS, rows)
    scale = d ** -0.5
    qg = q.reshape(B, H, rows, GRID_W, d)
    kg = k.reshape(B, H, rows, GRID_W, d)
    vg = v.reshape(B, H, rows, GRID_W, d)
    cols = jnp.arange(GRID_W)
    col_start = jnp.clip(cols - NA_WIN_COLS // 2, 0, GRID_W - NA_WIN_COLS)
    col_in = (cols[None, :] >= col_start[:, None]) & (cols[None, :] < col_start[:, None] + NA_WIN_COLS)
    dc_idx = jnp.clip(cols[None, :] - cols[:, None], -(NA_WIN_COLS - 1), NA_WIN_COLS - 1) + (NA_WIN_COLS - 1)

    def row_block(r):
        rs = jnp.clip(r - wr // 2, 0, rows - wr)
        q_r = lax.dynamic_index_in_dim(qg, r, axis=2, keepdims=False)
        k_s = lax.dynamic_slice_in_dim(kg, rs, wr, axis=2)
        v_s = lax.dynamic_slice_in_dim(vg, rs, wr, axis=2)
        dr_idx = rs + jnp.arange(wr) - r + (NA_WIN_ROWS - 1)
        bias = rpb[:, dr_idx[None, :, None], dc_idx[:, None, :]]
        s_lat = jnp.einsum('bhqd,bhrkd->bhqrk', q_r, k_s, preferred_element_type=jnp.float32) * scale + bias.astype(jnp.float32)
        s_lat = jnp.where(col_in[:, None, :], s_lat, -jnp.inf).reshape(B, H, GRID_W, wr * GRID_W)
        s_ctx = jnp.einsum('bhqd,bhcd->bhqc', q_r, k_ctx, preferred_element_type=jnp.float32) * scale
        p = jax.nn.softmax(jnp.concatenate([s_lat, s_ctx], axis=-1), axis=-1).astype(v.dtype)
        p_lat = p[..., : wr * GRID_W].reshape(B, H, GRID_W, wr, GRID_W)
        p_ctx = p[..., wr * GRID_W:]
        return jnp.einsum('bhqrk,bhrkd->bhqd', p_lat, v_s) + jnp.einsum('bhqc,bhcd->bhqd', p_ctx, v_ctx)

    o = lax.map(row_block, jnp.arange(rows))
    return o.transpose(1, 2, 0, 3, 4).reshape(B, H, N, d)


def gqa_latent_attention(q, k, v, k_ctx, v_ctx):
    B, G, R, N, d = q.shape
    nb = N // Q_BLOCK
    keys = jnp.concatenate([k, k_ctx], axis=2)
    vals = jnp.concatenate([v, v_ctx], axis=2)
    qb = q.reshape(B, G, R, nb, Q_BLOCK, d).transpose(3, 0, 1, 2, 4, 5)

    def block(q_blk):
        s = jnp.einsum('bgrqd,bgkd->bgrqk', q_blk, keys, preferred_element_type=jnp.float32) * (d ** -0.5)
        p = jax.nn.softmax(s, axis=-1).astype(vals.dtype)
        return jnp.einsum('bgrqk,bgkd->bgrqd', p, vals)

    o = lax.map(block, qb)
    return o.transpose(1, 2, 3, 0, 4, 5).reshape(B, G * R, N, d)


def gla_log_gates(lr, wa2, ba):
    B, N, _ = lr.shape
    z = jnp.einsum('bner,erk->ebnk', lr.reshape(B, N, 2, GLA_RANK), wa2) + ba[:, None, None, :]
    la = jax.nn.log_sigmoid(z.astype(jnp.float32)) / GLA_GATE_NORM
    la = la.reshape(2, B, N, GLA_HEADS, GLA_DK).transpose(0, 1, 3, 2, 4)
    return la[0], la[1]


def gla_chunked(q, k, v, log_a, s0):
    B, H, N, dk = q.shape
    dv = v.shape[-1]
    nc = N // GLA_CHUNK

    def to_chunks(t):
        return t.astype(jnp.float32).reshape(B, H, nc, GLA_CHUNK, t.shape[-1]).transpose(2, 0, 1, 3, 4)

    causal = jnp.tril(jnp.ones((GLA_CHUNK, GLA_CHUNK), dtype=bool))

    def step(S, xs):
        qi, ki, vi, gi = xs
        b = jnp.cumsum(gi, axis=-2)
        b_last = b[..., -1:, :]
        diff = b[..., :, None, :] - b[..., None, :, :]
        decay = jnp.exp(jnp.where(causal[:, :, None], diff, -jnp.inf))
        A = jnp.einsum('bhtd,bhsd,bhtsd->bhts', qi, ki, decay)
        o = jnp.einsum('bhts,bhsv->bhtv', A, vi) + jnp.einsum('bhtd,bhdv->bhtv', qi * jnp.exp(b), S)
        S = jnp.exp(b_last)[..., 0, :, None] * S + jnp.einsum('bhsd,bhsv->bhdv', ki * jnp.exp(b_last - b), vi)
        return S, o

    S, o = lax.scan(step, s0.astype(jnp.float32), (to_chunks(q), to_chunks(k), to_chunks(v), to_chunks(log_a)))
    return o.transpose(1, 2, 0, 3, 4).reshape(B, H, N, dv).astype(v.dtype), S


def gla_final_state(k, v, log_a):
    b = jnp.cumsum(log_a.astype(jnp.float32), axis=2)
    w = jnp.exp(b[:, :, -1:, :] - b)
    return jnp.einsum('bhsd,bhsv->bhdv', k.astype(jnp.float32) * w, v.astype(jnp.float32))


def gla_bidirectional(q, k, v, la_f, la_b, s_f, s_b):
    o_f, s_f_end = gla_chunked(q, k, v, la_f, s_f)
    o_b, s_b_end = gla_chunked(flip_seq(q), flip_seq(k), flip_seq(v), flip_seq(la_b), s_b)
    return o_f + flip_seq(o_b), s_f_end, s_b_end


def gla_output(o, norm_w, g):
    return merge_heads(rms_norm(o, norm_w)) * jax.nn.silu(g)


def token_mixers(u, uc, w_in, rpb, gla_wa2, gla_ba, gla_norm_w, q_norm_w, k_norm_w, rope_cos, rope_sin, ctx_out):
    B, N, _ = u.shape
    (na_q, na_k, na_v, gl_q, gl_k, gl_v, gl_g, gl_lr, ga_q, ga_k, ga_v) = jnp.split(u @ w_in, IN_SPLITS, axis=-1)
    (na_qc, na_kc, na_vc, gl_qc, gl_kc, gl_vc, gl_gc, gl_lrc, ga_qc, ga_kc, ga_vc) = jnp.split(uc @ w_in, IN_SPLITS, axis=-1)

    k_na_c = to_heads(na_kc, NA_HEADS)
    v_na_c = to_heads(na_vc, NA_HEADS)
    o_na = neighborhood_attention(to_heads(na_q, NA_HEADS), to_heads(na_k, NA_HEADS), to_heads(na_v, NA_HEADS), k_na_c, v_na_c, rpb)

    la_f, la_b = gla_log_gates(gl_lr, gla_wa2, gla_ba)
    la_fc, la_bc = gla_log_gates(gl_lrc, gla_wa2, gla_ba)
    k_gl_c = to_heads(gl_kc, GLA_HEADS)
    v_gl_c = to_heads(gl_vc, GLA_HEADS)
    if ctx_out:
        s0 = jnp.zeros((B, GLA_HEADS, GLA_DK, GLA_DV), jnp.float32)
        q_gl_c = to_heads(gl_qc, GLA_HEADS) * GLA_DK ** -0.5
        oc_gl, s_f, s_b = gla_bidirectional(q_gl_c, k_gl_c, v_gl_c, la_fc, la_bc, s0, s0)
    else:
        s_f = gla_final_state(k_gl_c, v_gl_c, la_fc)
        s_b = gla_final_state(flip_seq(k_gl_c), flip_seq(v_gl_c), flip_seq(la_bc))
    o_gl, _, _ = gla_bidirectional(to_heads(gl_q, GLA_HEADS) * GLA_DK ** -0.5, to_heads(gl_k, GLA_HEADS), to_heads(gl_v, GLA_HEADS), la_f, la_b, s_f, s_b)

    q_ga = apply_axial_rope(rms_norm(to_heads(ga_q, GQA_Q_HEADS), q_norm_w), rope_cos, rope_sin)
    k_ga = apply_axial_rope(rms_norm(to_heads(ga_k, GQA_KV_HEADS), k_norm_w), rope_cos, rope_sin)
    k_ga_c = rms_norm(to_heads(ga_kc, GQA_KV_HEADS), k_norm_w)
    v_ga_c = to_heads(ga_vc, GQA_KV_HEADS)
    o_ga = gqa_latent_attention(q_ga.reshape(B, GQA_KV_HEADS, GQA_REP, N, HEAD_DIM), k_ga, to_heads(ga_v, GQA_KV_HEADS), k_ga_c, v_ga_c)

    o = jnp.concatenate([merge_heads(o_na), gla_output(o_gl, gla_norm_w, gl_g), merge_heads(o_ga)], axis=-1)
    if not ctx_out:
        return o, None

    L = uc.shape[1]
    oc_na = context_attention(to_heads(na_qc, NA_HEADS)[:, :, None], k_na_c, v_na_c)[:, :, 0]
    q_ga_c = rms_norm(to_heads(ga_qc, GQA_Q_HEADS), q_norm_w).reshape(B, GQA_KV_HEADS, GQA_REP, L, HEAD_DIM)
    oc_ga = context_attention(q_ga_c, k_ga_c, v_ga_c).reshape(B, GQA_Q_HEADS, L, HEAD_DIM)
    oc = jnp.concatenate([merge_heads(oc_na), gla_output(oc_gl, gla_norm_w, gl_gc), merge_heads(oc_ga)], axis=-1)
    return o, oc


def swiglu(u, w_ffn_in, w_ffn_out):
    a, b = jnp.split(u @ w_ffn_in, 2, axis=-1)
    return (jax.nn.silu(a) * b) @ w_ffn_out


def setup_inputs(seed: int = 0) -> dict:
    key = jax.random.key(seed)
    ks = jax.random.split(key, 20)

    def nrm(k, shape, scale):
        return jax.random.normal(k, shape, jnp.float32) * scale

    L = DEPTH
    return {
        'x': nrm(ks[0], (BATCH, SEQ, D_MODEL), 1.0),
        'c': nrm(ks[1], (BATCH, D_MODEL), 1.0),
        'ctx': nrm(ks[2], (BATCH, CTX_LEN, D_MODEL), 1.0),
        'c_ctx': nrm(ks[3], (D_MODEL,), 1.0),
        'w_ada': nrm(ks[4], (L, D_MODEL, 6 * D_MODEL), 0.5 * D_MODEL ** -0.5),
        'b_ada': nrm(ks[5], (L, 6 * D_MODEL), 0.02),
        'w_in': nrm(ks[6], (L, D_MODEL, IN_COLS), D_MODEL ** -0.5),
        'na_rpb': nrm(ks[7], (L, NA_HEADS, 2 * NA_WIN_ROWS - 1, 2 * NA_WIN_COLS - 1), 0.1),
        'gla_wa2': nrm(ks[8], (L, 2, GLA_RANK, GLA_KW), GLA_RANK ** -0.5),
        'gla_ba': nrm(ks[9], (L, 2, GLA_KW), 0.02),
        'gla_norm_w': 1.0 + nrm(ks[10], (L, GLA_DV), 0.05),
        'gqa_qnorm_w': 1.0 + nrm(ks[11], (L, HEAD_DIM), 0.05),
        'gqa_knorm_w': 1.0 + nrm(ks[12], (L, HEAD_DIM), 0.05),
        'w_out': nrm(ks[13], (L, MIX_W, D_MODEL), DEEPNORM_BETA * MIX_W ** -0.5),
        'ln1_g': 1.0 + nrm(ks[14], (L, D_MODEL), 0.05),
        'ln1_b': nrm(ks[15], (L, D_MODEL), 0.02),
        'w_ffn_in': nrm(ks[16], (L, D_MODEL, 2 * FFN_HIDDEN), D_MODEL ** -0.5),
        'w_ffn_out': nrm(ks[17], (L, FFN_HIDDEN, D_MODEL), DEEPNORM_BETA * FFN_HIDDEN ** -0.5),
        'ln2_g': 1.0 + nrm(ks[18], (L, D_MODEL), 0.05),
        'ln2_b': nrm(ks[19], (L, D_MODEL), 0.02),
    }


def reference(x, c, ctx, c_ctx, w_ada, b_ada, w_in, na_rpb, gla_wa2, gla_ba, gla_norm_w, gqa_qnorm_w, gqa_knorm_w, w_out, ln1_g, ln1_b, w_ffn_in, w_ffn_out, ln2_g, ln2_b):
    rope_cos, rope_sin = axial_rope_tables(x.shape[1])
    silu_c = jax.nn.silu(c)
    silu_cc = jax.nn.silu(c_ctx)
    xc = ctx
    for l in range(DEPTH):
        ctx_out = l < DEPTH - 1
        mod = (silu_c @ w_ada[l] + b_ada[l])[:, None, :]
        mod_c = silu_cc @ w_ada[l] + b_ada[l]
        sh1, sc1, g1, sh2, sc2, g2 = jnp.split(mod, 6, axis=-1)
        sh1c, sc1c, g1c, sh2c, sc2c, g2c = jnp.split(mod_c, 6, axis=-1)

        o, oc = token_mixers(modulate(x, sh1, sc1), modulate(xc, sh1c, sc1c), w_in[l], na_rpb[l], gla_wa2[l], gla_ba[l], gla_norm_w[l], gqa_qnorm_w[l], gqa_knorm_w[l], rope_cos, rope_sin, ctx_out)
        x = layer_norm(DEEPNORM_ALPHA * x + g1 * (o @ w_out[l]), ln1_g[l], ln1_b[l])
        x = layer_norm(DEEPNORM_ALPHA * x + g2 * swiglu(modulate(x, sh2, sc2), w_ffn_in[l], w_ffn_out[l]), ln2_g[l], ln2_b[l])
        if ctx_out:
            xc = layer_norm(DEEPNORM_ALPHA * xc + g1c * (oc @ w_out[l]), ln1_g[l], ln1_b[l])
            xc = layer_norm(DEEPNORM_ALPHA * xc + g2c * swiglu(modulate(xc, sh2c, sc2c), w_ffn_in[l], w_ffn_out[l]), ln2_g[l], ln2_b[l])
    return x
```

```python
import contextlib
import numpy as np
import concourse.bass as bass
import concourse.mybir as mybir
from concourse.bass_utils import run_bass_kernel_spmd

F32 = mybir.dt.float32
BF16 = mybir.dt.bfloat16
AF = mybir.ActivationFunctionType
ALU = mybir.AluOpType
AX = mybir.AxisListType

COMPUTE = ("pe", "act", "dve", "pool")
NDMA_SEM = 6


class Buf:
    __slots__ = ("last_w", "readers", "excl")

    def __init__(self, excl=False):
        self.last_w = None
        self.readers = []
        self.excl = excl


class Sched:
    def __init__(self, nc):
        self.nc = nc
        self.ops = {e: [] for e in ("pe", "act", "dve", "pool", "sp")}
        self.stack = contextlib.ExitStack()
        self.nalloc = 0
        self.bufs = {}
        self.pending = {e: set() for e in self.ops}

    def B(self, *key):
        b = self.bufs.get(key)
        if b is None:
            b = self.bufs[key] = Buf(excl=(key[0] in ("pf", "pb")))
        return b

    def sbuf(self, shape, dtype, name=None):
        self.nalloc += 1
        return self.stack.enter_context(self.nc.sbuf_tensor("sb_" + (name or f"t{self.nalloc}"), list(shape), dtype))

    def psum(self, shape, dtype, name=None):
        self.nalloc += 1
        return self.stack.enter_context(self.nc.psum_tensor("ps_" + (name or f"t{self.nalloc}"), list(shape), dtype))

    def add(self, eng, fn, reads=(), writes=(), dma=False):
        idx = len(self.ops[eng])
        deps = set(self.pending[eng])
        self.pending[eng] = set()
        for b in reads:
            if b.last_w is not None:
                deps.add(b.last_w)
            if b.excl:
                deps.update(r for r in b.readers if r[0] != eng)
        for b in writes:
            if b.last_w is not None:
                deps.add(b.last_w)
            deps.update(b.readers)
        if eng == "pe":
            deps = {d for d in deps if d[0] != "pe"}
        self.ops[eng].append({"fn": fn, "deps": deps, "dma": dma, "marked": False})
        me = (eng, idx)
        for b in reads:
            b.readers.append(me)
        for b in writes:
            b.last_w = me
            b.readers = []
        return me

    def pe(self, fn, reads=(), writes=()):
        return self.add("pe", fn, reads, writes)

    def act(self, fn, reads=(), writes=()):
        return self.add("act", fn, reads, writes)

    def dve(self, fn, reads=(), writes=()):
        return self.add("dve", fn, reads, writes)

    def pool(self, fn, reads=(), writes=()):
        return self.add("pool", fn, reads, writes)

    def dma(self, fn, reads=(), writes=(), q="sp"):
        return self.add(q, fn, reads, writes, dma=True)

    def barrier(self):
        deps = set()
        for e in ("pe", "act", "dve"):
            if self.ops[e]:
                deps.add((e, len(self.ops[e]) - 1))
        for q in ("sp", "pool"):
            nd = 0
            gotc = False
            for i in range(len(self.ops[q]) - 1, -1, -1):
                if self.ops[q][i]["dma"]:
                    if nd < NDMA_SEM:
                        deps.add((q, i))
                        nd += 1
                elif not gotc:
                    deps.add((q, i))
                    gotc = True
                if nd >= NDMA_SEM and (gotc or q == "sp"):
                    break
        for e in self.pending:
            self.pending[e] |= deps

    def emit(self, final_bufs=()):
        nc = self.nc
        ops = self.ops
        dma_info = {}
        for q in ("sp", "pool"):
            n = 0
            for i, op in enumerate(ops[q]):
                if op["dma"]:
                    slot = n % NDMA_SEM
                    val = 16 * (n // NDMA_SEM + 1)
                    dma_info[(q, i)] = (q, slot, val)
                    op["dmainfo"] = (slot, val)
                    n += 1
        for e in ops:
            for op in ops[e]:
                op["deps"] = {d for d in op["deps"]}
                for d in op["deps"]:
                    if d not in dma_info:
                        ops[d[0]][d[1]]["marked"] = True
        final_deps = set()
        for b in final_bufs:
            if b.last_w is not None:
                final_deps.add(b.last_w)
                if b.last_w not in dma_info:
                    ops[b.last_w[0]][b.last_w[1]]["marked"] = True
        for e in COMPUTE:
            c = 0
            for op in ops[e]:
                if op["marked"] and not op["dma"]:
                    c += 1
                    op["semval"] = c
        st = self.stack
        csem = {e: st.enter_context(nc.semaphore(f"s_{e}")) for e in COMPUTE}
        dsem = {q: [st.enter_context(nc.semaphore(f"d_{q}{j}")) for j in range(NDMA_SEM)] for q in ("sp", "pool")}
        block = st.enter_context(nc.Block())
        engobj = {"pe": block.tensor, "act": block.scalar, "dve": block.vector, "pool": block.gpsimd, "sp": block.sync}

        def resolve(dep):
            if dep in dma_info:
                q, slot, val = dma_info[dep]
                return dsem[q][slot], ("d", q, slot), val
            f, k = dep
            return csem[f], ("c", f), ops[f][k]["semval"]

        def make_body(e):
            def body(eng):
                known = {}
                for op in ops[e]:
                    need = {}
                    for dep in op["deps"]:
                        if dep[0] == e and dep not in dma_info and e == "pe":
                            continue
                        sem, key, val = resolve(dep)
                        if need.get(key, (None, 0))[1] < val:
                            need[key] = (sem, val)
                    if op["dma"]:
                        slot, val = op["dmainfo"]
                        if val > 16:
                            key = ("d", e, slot)
                            if need.get(key, (None, 0))[1] < val - 16:
                                need[key] = (dsem[e][slot], val - 16)
                    for key, (sem, val) in need.items():
                        if known.get(key, 0) >= val:
                            continue
                        known[key] = val
                        eng.wait_ge(sem, val)
                    ins = op["fn"](eng)
                    if op["dma"]:
                        ins.then_inc(dsem[e][op["dmainfo"][0]], 16)
                    elif op["marked"]:
                        ins.then_inc(csem[e], 1)
                if e == "sp":
                    need = {}
                    for dep in final_deps:
                        sem, key, val = resolve(dep)
                        if need.get(key, (None, 0))[1] < val:
                            need[key] = (sem, val)
                    for key, (sem, val) in need.items():
                        eng.wait_ge(sem, val)
            return body

        for e in ("sp", "pool", "act", "dve", "pe"):
            engobj[e](make_body(e))

    def close(self):
        self.stack.close()


def I(name, *args, **kw):
    return lambda eng: getattr(eng, name)(*args, **kw)


D = 1024
LC = 256
NL = 2048
T = LC + NL
NTT = T // 128
FH = 2816
NHC = FH // 128
DEPTH = 2
NSEQ = 2
BLOCKS = [(0, 256)] + [(256 + 512 * i, 512) for i in range(4)]
ALPHA2 = 2.0
LN_EPS_S = 1e-5 / ALPHA2
RMS_EPS = 1e-6
NCOMBO = 21
WBYTES = 90112

C_NAQ, C_NAK, C_NAV = 0, 384, 768
C_GLQ, C_GLK, C_GLV, C_GLG, C_GLR = 1152, 1280, 1408, 1664, 1920
C_GAQ, C_GAK, C_GAV = 1952, 2336, 2464


def na_combos():
    combos = []
    idx = {}

    def get(kind, r, d):
        key = (kind, d)
        if key not in idx:
            idx[key] = len(combos)
            combos.append((r, d))
        return idx[key]

    slots = []
    for p in range(16):
        r = 2 * p
        if p == 0:
            s = [(j, get("e0", r, 2 * j - r)) for j in range(0, 4)]
        elif p == 1:
            s = [(j, get("e1", r, 2 * j - r)) for j in range(0, 4)]
        elif p == 14:
            s = [(j, get("e2", r, 2 * j - r)) for j in range(12, 16)]
        elif p == 15:
            s = [(j, get("e3", r, 2 * j - r)) for j in range(12, 16)]
        else:
            s = [(j, get("g", 8, 2 * j - r)) for j in range(p - 2, p + 3)]
        slots.append(s)
    assert len(combos) == NCOMBO
    return combos, slots


def na_index_tables():
    combos, _ = na_combos()
    cols = np.arange(64)
    col_start = np.clip(cols - 8, 0, 48)
    col_in = (cols[None, :] >= col_start[:, None]) & (cols[None, :] < col_start[:, None] + 16)
    valid = np.zeros((NCOMBO, 128, 128), bool)
    dri = np.zeros((NCOMBO, 128, 128), np.int64)
    dci = np.zeros((NCOMBO, 128, 128), np.int64)
    for ci, (r, d) in enumerate(combos):
        for ko in range(2):
            for qo in range(2):
                kr = r + d + ko
                qr = r + qo
                rs = min(max(qr - 4, 0), 24)
                ok_row = (rs <= kr < rs + 8)
                dr = kr - qr
                blk_valid = col_in.T & ok_row
                dc = np.clip(cols[:, None] - cols[None, :], -15, 15) + 15
                valid[ci, 64 * ko:64 * ko + 64, 64 * qo:64 * qo + 64] = blk_valid
                dri[ci, 64 * ko:64 * ko + 64, 64 * qo:64 * qo + 64] = min(max(dr + 7, 0), 14)
                dci[ci, 64 * ko:64 * ko + 64, 64 * qo:64 * qo + 64] = dc
    return valid, dri, dci


def build_nc(stage=99, nseq=NSEQ, dumps=None):
    nc = bass.Bass("TRN2", target_bir_lowering=False)
    S = Sched(nc)
    B = S.B
    dumps = dumps if dumps is not None else {}

    def din(name, shape, dt=F32):
        return nc.dram_tensor(name, list(shape), dt, kind="ExternalInput").ap()

    x_d = din("x", [NSEQ, NL, D])
    ctx_d = din("ctx", [NSEQ, LC, D])
    cT_d = din("cT", [128, 8, 3])
    wada_d = din("w_ada", [DEPTH, D, 6 * D])
    bada_d = din("b_adaT", [DEPTH, 128, 48])
    win_d = din("w_in", [DEPTH, D, 2592])
    wout_d = din("w_out", [DEPTH, D, D])
    wf1_d = din("w_ffn_in", [DEPTH, D, 2 * FH])
    wf2_d = din("w_ffn_out", [DEPTH, FH, D])
    lnT_d = din("lnT", [128, DEPTH, 4, 8])
    nab_d = din("nab", [DEPTH, 128, 6, NCOMBO * 128])
    wa2_d = din("wa2", [DEPTH, 2, 32, 128])
    nba_d = din("nbaT", [128, DEPTH, 2])
    glnw_d = din("glnw", [DEPTH, 1, 256])
    qkw_d = din("qkwT", [128, DEPTH, 2])
    cos_d = din("cosT", [128, NL])
    sin_d = din("sinT", [128, NL])
    cm_d = din("cmats", [128, 5, 128])
    tri_d = din("trim", [128, 2, 256])
    out_d = nc.dram_tensor("out", [NSEQ, NL, D], F32, kind="ExternalOutput").ap()
    xres = [nc.dram_tensor(f"xres{s}", [128, 8, T], F32).ap() for s in range(NSEQ)]
    dump_list = []

    ubuf = S.sbuf([128, 8, T], BF16, "ubuf")
    obuf = S.sbuf([128, 8, T], BF16, "obuf")
    wbuf = [S.sbuf([128, 8 * 1152], BF16, f"wbuf{i}") for i in range(2)]

    def wview(i, dt, k, c):
        if dt == F32:
            return wbuf[i][:].bitcast(F32)[:, 0:k * c].rearrange("p (k c) -> p k c", k=k, c=c)
        return wbuf[i][:, 0:k * c].rearrange("p (k c) -> p k c", k=k, c=c)

    Wr = S.sbuf([128, WBYTES // 4], F32, "Wr")
    identf = S.sbuf([128, 128], F32, "identf")
    onesD = S.sbuf([128, 128], F32, "onesD")
    identb = S.sbuf([128, 128], BF16, "identb")
    bones = S.sbuf([128, 128], BF16, "bones")
    rotm = S.sbuf([128, 128], BF16, "rotm")
    trim = S.sbuf([128, 2, 256], BF16, "trim")
    scT = S.sbuf([128, 8, 3], F32, "scT")
    modT = S.sbuf([128, DEPTH, 48, 3], F32, "modT")
    mder = S.sbuf([128, DEPTH, 6, 8, 3], F32, "mder")
    mder2 = S.sbuf([128, DEPTH, 2, 8, 3], F32, "mder2")
    lnT = S.sbuf([128, DEPTH, 4, 8], F32, "lnT")
    badaT = S.sbuf([128, DEPTH, 48], F32, "badaT")
    nbaT = S.sbuf([128, DEPTH, 2], F32, "nbaT")
    qkwT = S.sbuf([128, DEPTH, 2], F32, "qkwT")
    glnw = S.sbuf([128, DEPTH, 256], F32, "glnw")
    wa2 = S.sbuf([32, DEPTH, 2, 128], F32, "wa2")
    hmask = S.sbuf([128, 4], F32, "hmask")
    epsln = S.sbuf([128, 1], F32, "epsln")
    epsrms = S.sbuf([128, 1], F32, "epsrms")

    pf = [S.psum([128, 512], F32, f"pf{i}") for i in range(7)]
    pb = S.psum([128, 1024], BF16, "pb")
    pctr = [0]

    def nextp():
        i = pctr[0] % 7
        pctr[0] += 1
        return pf[i], B("pf", i)

    class Carver:
        def __init__(self):
            self.off = 0

        def __call__(self, dt, *dims):
            n = int(np.prod(dims))
            nbytes = n * (4 if dt == F32 else 2)
            nbytes = (nbytes + 63) // 64 * 64
            a = self.off // 4
            self.off += nbytes
            assert self.off <= WBYTES, ("scratch overflow", self.off)
            ap = Wr[:, a:a + nbytes // 4]
            if dt == BF16:
                ap = ap.bitcast(BF16)[:, 0:n]
            else:
                ap = ap[:, 0:n]
            if len(dims) == 2:
                ap = ap.rearrange("p (a b) -> p a b", a=dims[0], b=dims[1])
            elif len(dims) == 3:
                ap = ap.rearrange("p (a b c) -> p a b c", a=dims[0], b=dims[1], c=dims[2])
            return ap

    def dump(name, ap, shape, dt, rbuf):
        d = nc.dram_tensor("dbg_" + name, list(shape), dt, kind="ExternalOutput").ap()
        S.dma(I("dma_start", out=d, in_=ap), reads=[rbuf], writes=[B("dump", name)])
        dump_list.append(B("dump", name))
        dumps[name] = (shape, dt)

    def ld(dst, src, bname, q="sp"):
        S.dma(I("dma_start", out=dst, in_=src), writes=[B(bname)], q=q)

    ld(identf[:], cm_d[:, 0, :], "identf")
    ld(onesD[:], cm_d[:, 1, :], "onesD")
    ld(identb[:], cm_d[:, 0, :], "identb", q="pool")
    ld(bones[:], cm_d[:, 2, :], "bones", q="pool")
    ld(rotm[:], cm_d[:, 3, :], "rotm", q="pool")
    ld(trim[:], tri_d, "trim", q="pool")
    ld(hmask[:], cm_d[:, 4, 0:4], "hmask")
    ld(scT[:], cT_d, "scT")
    ld(lnT[:], lnT_d, "lnT")
    ld(badaT[:], bada_d.rearrange("l p j -> p l j"), "badaT")
    ld(nbaT[:], nba_d, "nbaT")
    ld(qkwT[:], qkw_d, "qkwT")
    for l in range(DEPTH):
        ld(glnw[:, l, :], glnw_d[l].broadcast_to([128, 256]), "glnw")
        ld(wa2[:, l, :, :], wa2_d[l].rearrange("e r c -> r e c"), "wa2")
    S.dve(I("tensor_scalar_mul", out=nbaT[:], in0=nbaT[:], scalar1=-1.0), reads=[B("nbaT")], writes=[B("nbaT")])
    S.pool(I("memset", epsln[:], LN_EPS_S), writes=[B("eps")])
    S.pool(I("memset", epsrms[:], RMS_EPS), writes=[B("eps")])
    S.act(I("activation", out=scT[:], in_=scT[:], func=AF.Silu), reads=[B("scT")], writes=[B("scT")])

    for l in range(DEPTH):
        pm, bpm = nextp()
        for js in range(12):
            wv = wview(js % 2, F32, 8, 512)
            S.dma(I("dma_start",
                out=wv, in_=wada_d[l].rearrange("(k p) c -> p k c", p=128)[:, :, 512 * js:512 * js + 512]),
                writes=[B("wbuf", js % 2)])
            for jj in range(4):
                j = 4 * js + jj
                for k in range(8):
                    S.pe(I("matmul",
                        pm[:, 3 * j:3 * j + 3], wv[:, k, 128 * jj:128 * jj + 128], scT[:, k, :],
                        start=(k == 0), stop=(k == 7)),
                        reads=[B("wbuf", js % 2), B("scT")], writes=[bpm])
        S.dve(I("tensor_tensor",
            out=modT[:, l, :, :], in0=pm[:, 0:144].rearrange("p (j b) -> p j b", j=48, b=3),
            in1=badaT[:, l, :].unsqueeze(2).broadcast_to([128, 48, 3]), op=ALU.add),
            reads=[bpm, B("badaT")], writes=[B("modT")])
    inv_a = float(1.0 / np.sqrt(ALPHA2))
    for l in range(DEPTH):
        mT = modT[:, l, :, :]
        rd = [B("modT"), B("lnT")]
        wr = [B("mder")]
        S.dve(I("tensor_scalar_add", out=mder[:, l, 0, :, :], in0=mT[:, 8:16, :], scalar1=1.0), reads=rd, writes=wr)
        S.dve(I("tensor_scalar_mul", out=mder[:, l, 1, :, :], in0=mT[:, 16:24, :], scalar1=inv_a), reads=rd, writes=wr)
        S.dve(I("tensor_scalar_add", out=mder[:, l, 2, :, :], in0=mT[:, 32:40, :], scalar1=1.0), reads=rd, writes=wr)
        S.dve(I("tensor_scalar_mul", out=mder[:, l, 3, :, :], in0=mT[:, 40:48, :], scalar1=inv_a), reads=rd, writes=wr)
        S.dve(I("tensor_tensor", out=mder[:, l, 4, :, :], in0=mder[:, l, 2, :, :],
                                             in1=lnT[:, l, 0, :].unsqueeze(2).broadcast_to([128, 8, 3]), op=ALU.mult), reads=rd + wr, writes=wr)
        S.dve(I("tensor_tensor", out=mder[:, l, 5, :, :], in0=mder[:, l, 2, :, :],
                                             in1=lnT[:, l, 1, :].unsqueeze(2).broadcast_to([128, 8, 3]), op=ALU.mult), reads=rd + wr, writes=wr)
        S.dve(I("tensor_tensor", out=mder[:, l, 5, :, :], in0=mder[:, l, 5, :, :], in1=mT[:, 24:32, :], op=ALU.add), reads=rd + wr, writes=wr)
    for l in range(DEPTH - 1):
        rd = [B("modT"), B("lnT"), B("mder")]
        wr = [B("mder2")]
        S.dve(I("tensor_tensor", out=mder2[:, l, 0, :, :], in0=mder[:, l + 1, 0, :, :],
                                             in1=lnT[:, l, 2, :].unsqueeze(2).broadcast_to([128, 8, 3]), op=ALU.mult), reads=rd, writes=wr)
        S.dve(I("tensor_tensor", out=mder2[:, l, 1, :, :], in0=mder[:, l + 1, 0, :, :],
                                             in1=lnT[:, l, 3, :].unsqueeze(2).broadcast_to([128, 8, 3]), op=ALU.mult), reads=rd + wr, writes=wr)
        S.dve(I("tensor_tensor", out=mder2[:, l, 1, :, :], in0=mder2[:, l, 1, :, :], in1=modT[:, l + 1, 0:8, :], op=ALU.add), reads=rd + wr, writes=wr)
    MD = [B("mder"), B("mder2"), B("modT"), B("lnT")]
    if stage < 99:
        S.pool(I("memset", obuf[:], 0.0), writes=[B("obuf", tt) for tt in range(NTT)])
    if stage == 0:
        dump("modT", modT[:], [128, DEPTH, 48, 3], F32, B("modT"))

    def msel(s, t0):
        return 2 if t0 < LC else s

    gctr = {}

    def bank(group, banks):
        c = gctr.get(group, 0)
        gctr[group] = c + 1
        i = banks[c % len(banks)]
        return pf[i], B("pf", i)

    def tiles_of(t0, n):
        return range(t0 // 128, (t0 + n) // 128)

    def ub(t0, n):
        return [B("ubuf", tt) for tt in tiles_of(t0, n)]

    win_v = [win_d[l].rearrange("(k p) c -> p k c", p=128) for l in range(DEPTH)]
    combos, na_slots = na_combos()
    PBB = B("pb", 0)

    def load_x(s):
        S.barrier()
        cv = Carver()
        xin = [cv(F32, 1024) for _ in range(2)]
        xst = [cv(F32, 8, 128) for _ in range(2)]
        for tt in range(NTT):
            i2 = tt % 2
            src = ctx_d[s, 128 * tt:128 * tt + 128, :] if tt < 2 else x_d[s, 128 * (tt - 2):128 * (tt - 2) + 128, :]
            S.dma(I("dma_start", out=xin[i2], in_=src), writes=[B("xin", i2)])
            mi = msel(s, 128 * tt)
            for half in range(2):
                pt, bpt = bank("ld", [0, 1, 2, 3])
                for kk in range(4):
                    k = 4 * half + kk
                    S.pe(I("transpose", pt[:, 128 * kk:128 * kk + 128], xin[i2][:, 128 * k:128 * k + 128], identf[:]),
                         reads=[B("xin", i2), B("identf")], writes=[bpt])
                S.dve(I("tensor_copy",
                    out=xst[i2][:, 4 * half:4 * half + 4, :], in_=pt[:].rearrange("p (k t) -> p k t", k=4, t=128)),
                    reads=[bpt], writes=[B("xst", i2)])
                for kk in range(4):
                    k = 4 * half + kk
                    S.act(I("activation",
                        out=ubuf[:, k, 128 * tt:128 * tt + 128], in_=pt[:, 128 * kk:128 * kk + 128], func=AF.Identity,
                        scale=mder[:, 0, 0, k, mi:mi + 1], bias=modT[:, 0, k, mi:mi + 1]),
                        reads=[bpt] + MD, writes=[B("ubuf", tt)])
            S.dma(I("dma_start", out=xres[s][:, :, 128 * tt:128 * tt + 128], in_=xst[i2]),
                  reads=[B("xst", i2)], writes=[B("xres", s, tt)])

    def na_phase(s, l):
        S.barrier()
        cv = Carver()
        naq = cv(BF16, 3, T)
        nak = cv(BF16, 3, T)
        nav = cv(BF16, NTT, 6 * 65)
        natab = cv(BF16, 6, NCOMBO * 128)
        naP = [cv(BF16, 7, 128) for _ in range(2)]
        otok = [cv(BF16, 384) for _ in range(2)]
        rden = [cv(F32, 8) for _ in range(2)]
        nav4 = nav.rearrange("p t (h c) -> p t h c", h=6, c=65)
        wv = wview(0, BF16, 8, 1152)
        S.dma(I("dma_start", out=wv, in_=win_v[l][:, :, 0:1152]), writes=[B("wbuf", 0)], q="pool")
        S.dma(I("dma_start", out=natab, in_=nab_d[l]), writes=[B("natab")], q="pool")
        S.pool(I("memset", nav4[:, :, :, 64:65], 1.0), writes=[B("navones")])
        for (t0, n) in BLOCKS:
            for j in range(6):
                if j < 3 and t0 < LC and l == DEPTH - 1:
                    continue
                ps, bps = bank("proj", [0, 1, 2, 3, 4, 5, 6])
                for k in range(8):
                    S.pe(I("matmul", ps[:, 0:n], wv[:, k, 128 * j:128 * j + 128], ubuf[:, k, t0:t0 + n],
                                                                     start=(k == 0), stop=(k == 7)),
                         reads=[B("wbuf", 0)] + ub(t0, n), writes=[bps])
                wr = [B("naqk", tt) for tt in tiles_of(t0, n)]
                if j < 3:
                    S.act(I("activation", out=naq[:, j, t0:t0 + n], in_=ps[:, 0:n], func=AF.Identity, scale=0.125),
                          reads=[bps], writes=wr)
                else:
                    S.dve(I("tensor_copy", out=nak[:, j - 3, t0:t0 + n], in_=ps[:, 0:n]), reads=[bps], writes=wr)
        for tt in range(NTT):
            ps, bps = bank("proj", [0, 1, 2, 3, 4, 5, 6])
            for k in range(8):
                S.pe(I("matmul", ps[:, 0:384], ubuf[:, k, 128 * tt:128 * tt + 128], wv[:, k, 768:1152],
                                                         start=(k == 0), stop=(k == 7)),
                     reads=[B("wbuf", 0), B("ubuf", tt)], writes=[bps])
            S.dve(I("tensor_copy", out=nav4[:, tt, :, 0:64], in_=ps[:, 0:384].rearrange("p (h c) -> p h c", h=6, c=64)),
                  reads=[bps], writes=[B("nav", tt)])
        qtiles = []
        if l < DEPTH - 1:
            qtiles += [(0, [(0, None), (1, None)]), (1, [(0, None), (1, None)])]
        for p in range(16):
            qtiles.append((2 + p, [(2 + j, ci) for (j, ci) in na_slots[p]] + [(0, None), (1, None)]))
        pi = 0
        for qi, (tq, slots) in enumerate(qtiles):
            po, bpo = bank("napo", [4, 5])
            ns = len(slots)
            for h in range(6):
                c = h // 2
                p0 = 64 * (h % 2)
                pa, bpa = bank("napa", [0, 1, 2, 3])
                pa2, bpa2 = bank("napa", [0, 1, 2, 3])
                P = naP[pi % 2]
                bP = B("naP", pi % 2)
                pi += 1
                for si, (tk, ci) in enumerate(slots):
                    bk, bbk = (pa, bpa) if si < 4 else (pa2, bpa2)
                    col = 128 * (si % 4)
                    S.pe(I("matmul",
                        bk[:, col:col + 128], nak[p0:p0 + 64, c, 128 * tk:128 * tk + 128], naq[p0:p0 + 64, c, 128 * tq:128 * tq + 128],
                        start=True, stop=(ci is None)),
                        reads=[B("naqk", tk), B("naqk", tq)], writes=[bbk])
                    if ci is not None:
                        S.pe(I("matmul",
                            bk[:, col:col + 128], identb[:], natab[:, h, 128 * ci:128 * ci + 128], start=False, stop=True),
                            reads=[B("natab"), B("identb")], writes=[bbk])
                n1 = min(ns, 4)
                S.act(I("activation", out=P[:, 0:n1, :], in_=pa[:, 0:128 * n1].rearrange("p (a b) -> p a b", a=n1, b=128), func=AF.Exp),
                      reads=[bpa], writes=[bP])
                if ns > 4:
                    n2 = ns - 4
                    S.act(I("activation", out=P[:, 4:4 + n2, :], in_=pa2[:, 0:128 * n2].rearrange("p (a b) -> p a b", a=n2, b=128), func=AF.Exp),
                          reads=[bpa2], writes=[bP])
                for si, (tk, ci) in enumerate(slots):
                    S.pe(I("matmul",
                        po[:, 65 * h:65 * h + 65], P[:, si, :], nav[:, tk, 65 * h:65 * h + 65], start=(si == 0), stop=(si == ns - 1)),
                        reads=[bP, B("nav", tk), B("navones")], writes=[bpo])
            o2 = qi % 2
            po3 = po[:, 0:390].rearrange("p (h c) -> p h c", h=6, c=65)
            S.dve(I("reciprocal", out=rden[o2][:, 0:6], in_=po3[:, :, 64]), reads=[bpo], writes=[B("rden", o2)])
            S.dve(I("tensor_tensor", out=otok[o2].rearrange("p (h c) -> p h c", h=6, c=64), in0=po3[:, :, 0:64],
                                                          in1=rden[o2][:, 0:6].unsqueeze(2).broadcast_to([128, 6, 64]), op=ALU.mult),
                  reads=[bpo, B("rden", o2)], writes=[B("otok", o2)])
            for c3 in range(3):
                S.pe(I("transpose", pb[:, 128 * c3:128 * c3 + 128], otok[o2][:, 128 * c3:128 * c3 + 128], identb[:]),
                     reads=[B("otok", o2), B("identb")], writes=[PBB])
            S.act(I("copy", out=obuf[:, 0:3, 128 * tq:128 * tq + 128], in_=pb[:, 0:384].rearrange("p (a b) -> p a b", a=3, b=128)),
                  reads=[PBB], writes=[B("obuf", tq)])

    def gqa_phase(s, l):
        S.barrier()
        cv = Carver()
        gaq = cv(BF16, 3, T)
        gak = cv(BF16, 2, T)
        gav = cv(BF16, NTT, 130)
        cosv = cv(F32, NL)
        sinv = cv(F32, NL)
        sq = cv(BF16, 512)
        sd = cv(F32, 512)
        rstd = cv(F32, 512)
        qn = cv(F32, 512)
        qnb = cv(BF16, 512)
        t1 = cv(F32, 512)
        t2 = cv(F32, 512)
        gaP = [cv(BF16, 512) for _ in range(3)]
        otok = cv(BF16, 4, 384)
        rden = [cv(F32, 8) for _ in range(2)]
        gav4 = gav.rearrange("p t (g c) -> p t g c", g=2, c=65)
        wv = wview(1, BF16, 8, 768)
        wb = [B("wbuf", 1)]
        S.dma(I("dma_start", out=wv[:, :, 0:384], in_=win_v[l][:, :, C_GAQ:C_GAQ + 384]), writes=wb, q="pool")
        for g in range(2):
            for d in range(2):
                S.dma(I("dma_start", out=wv[:, :, 384 + 128 * g + 64 * d:384 + 128 * g + 64 * d + 64],
                                                      in_=win_v[l][:, :, C_GAK + 64 * g:C_GAK + 64 * g + 64]), writes=wb, q="pool")
        S.dma(I("dma_start", out=wv[:, :, 640:768], in_=win_v[l][:, :, C_GAV:C_GAV + 128]), writes=wb, q="pool")
        S.dma(I("dma_start", out=cosv, in_=cos_d), writes=[B("cos")])
        S.dma(I("dma_start", out=sinv, in_=sin_d), writes=[B("cos")])
        S.pool(I("memset", gav4[:, :, :, 64:65], 1.0), writes=[B("gavones")])
        TB = lambda n: B("gatmp", n)
        for (t0, n) in BLOCKS:
            for j in range(5):
                if j < 3 and t0 < LC and l == DEPTH - 1:
                    continue
                ps, bps = bank("proj", [0, 1, 2, 3])
                for k in range(8):
                    S.pe(I("matmul", ps[:, 0:n], wv[:, k, 128 * j:128 * j + 128], ubuf[:, k, t0:t0 + n],
                                                                     start=(k == 0), stop=(k == 7)),
                         reads=wb + ub(t0, n), writes=[bps])
                S.act(I("activation", out=sq[:, 0:n], in_=ps[:, 0:n], func=AF.Square), reads=[bps], writes=[TB("sq")])
                pm, bpm = bank("gams", [4, 5])
                S.pe(I("matmul", pm[:, 0:n], bones[:], sq[:, 0:n], start=True, stop=True), reads=[TB("sq"), B("bones")], writes=[bpm])
                S.act(I("activation", out=sd[:, 0:n], in_=pm[:, 0:n], func=AF.Sqrt, bias=epsrms[:, 0:1], scale=1.0),
                      reads=[bpm, B("eps")], writes=[TB("sd")])
                S.dve(I("reciprocal", out=rstd[:, 0:n], in_=sd[:, 0:n]), reads=[TB("sd")], writes=[TB("rstd")])
                wi = 0 if j < 3 else 1
                S.dve(I("scalar_tensor_tensor", out=qn[:, 0:n], in0=ps[:, 0:n], scalar=qkwT[:, l, wi:wi + 1], in1=rstd[:, 0:n],
                                                                      op0=ALU.mult, op1=ALU.mult),
                      reads=[bps, TB("rstd"), B("qkwT")], writes=[TB("qn")])
                dst = gaq[:, j, t0:t0 + n] if j < 3 else gak[:, j - 3, t0:t0 + n]
                wr = [B("gaqk", tt) for tt in tiles_of(t0, n)]
                if t0 >= LC:
                    S.pool(I("tensor_copy", out=qnb[:, 0:n], in_=qn[:, 0:n]), reads=[TB("qn")], writes=[TB("qnb")])
                    pr, bpr = bank("gams", [4, 5])
                    S.pe(I("matmul", pr[:, 0:n], rotm[:], qnb[:, 0:n], start=True, stop=True), reads=[TB("qnb"), B("rotm")], writes=[bpr])
                    S.dve(I("tensor_tensor", out=t1[:, 0:n], in0=qn[:, 0:n], in1=cosv[:, t0 - LC:t0 - LC + n], op=ALU.mult),
                          reads=[TB("qn"), B("cos")], writes=[TB("t1")])
                    S.dve(I("tensor_tensor", out=t2[:, 0:n], in0=pr[:, 0:n], in1=sinv[:, t0 - LC:t0 - LC + n], op=ALU.mult),
                          reads=[bpr, B("cos")], writes=[TB("t2")])
                    S.pool(I("tensor_tensor", out=dst, in0=t1[:, 0:n], in1=t2[:, 0:n], op=ALU.add),
                           reads=[TB("t1"), TB("t2")], writes=wr)
                else:
                    S.pool(I("tensor_copy", out=dst, in_=qn[:, 0:n]), reads=[TB("qn")], writes=wr)
        for tt in range(NTT):
            ps, bps = bank("proj", [0, 1, 2, 3])
            for k in range(8):
                S.pe(I("matmul", ps[:, 0:128], ubuf[:, k, 128 * tt:128 * tt + 128], wv[:, k, 640:768],
                                                         start=(k == 0), stop=(k == 7)),
                     reads=wb + [B("ubuf", tt)], writes=[bps])
            S.dve(I("tensor_copy", out=gav4[:, tt, :, 0:64], in_=ps[:, 0:128].rearrange("p (g c) -> p g c", g=2, c=64)),
                  reads=[bps], writes=[B("gav", tt)])
        qblocks = []
        if l < DEPTH - 1:
            qblocks.append((0, 256, [0, 1]))
        for i in range(4):
            qblocks.append((256 + 512 * i, 512, list(range(NTT))))
        pi = 0
        for (t0, n, ktiles) in qblocks:
            nsub = n // 128
            nk = len(ktiles)
            for h in range(6):
                g = h // 3
                c = h // 2
                p0 = 64 * (h % 2)
                po, bpo = bank("gapo", [5, 6])
                S.dve(I("memset", po[:, 0:65 * nsub], 0.0), writes=[bpo])
                Ps = {}

                def st(ki):
                    nonlocal pi
                    tk = ktiles[ki]
                    pa, bpa = bank("gapa", [0, 1, 2, 3, 4])
                    P = gaP[pi % 3]
                    bP = B("gaP", pi % 3)
                    pi += 1
                    Ps[ki] = (P, bP)
                    S.pe(I("matmul", pa[:, 0:n], gak[p0:p0 + 64, g, 128 * tk:128 * tk + 128], gaq[p0:p0 + 64, c, t0:t0 + n],
                                                        start=True, stop=True),
                         reads=[B("gaqk", tk)] + [B("gaqk", tt) for tt in tiles_of(t0, n)], writes=[bpa])
                    S.act(I("activation", out=P[:, 0:n], in_=pa[:, 0:n], func=AF.Exp, scale=0.125), reads=[bpa], writes=[bP])

                def pv(ki):
                    tk = ktiles[ki]
                    P, bP = Ps.pop(ki)
                    for qs in range(nsub):
                        S.pe(I("matmul", po[:, 65 * qs:65 * qs + 65], P[:, 128 * qs:128 * qs + 128], gav[:, tk, 65 * g:65 * g + 65],
                                                                        start=False, stop=(ki == nk - 1), skip_group_check=True),
                             reads=[bP, B("gav", tk), B("gavones")], writes=[bpo])

                for step in range(nk + 2):
                    if step < nk:
                        st(step)
                    if step >= 2:
                        pv(step - 2)
                o2 = h % 2
                po3 = po[:, 0:65 * nsub].rearrange("p (q c) -> p q c", q=nsub, c=65)
                S.dve(I("reciprocal", out=rden[o2][:, 0:nsub], in_=po3[:, :, 64]), reads=[bpo], writes=[B("rden", o2)])
                S.dve(I("tensor_tensor", out=otok[:, 0:nsub, 64 * h:64 * h + 64], in0=po3[:, :, 0:64],
                                                                   in1=rden[o2][:, 0:nsub].unsqueeze(2).broadcast_to([128, nsub, 64]), op=ALU.mult),
                      reads=[bpo, B("rden", o2)], writes=[B("gaotok")])
            for qs in range(nsub):
                tq = t0 // 128 + qs
                for c3 in range(3):
                    S.pe(I("transpose", pb[:, 128 * c3:128 * c3 + 128], otok[:, qs, 128 * c3:128 * c3 + 128], identb[:]),
                         reads=[B("gaotok"), B("identb")], writes=[PBB])
                S.act(I("copy", out=obuf[:, 5:8, 128 * tq:128 * tq + 128], in_=pb[:, 0:384].rearrange("p (a b) -> p a b", a=3, b=128)),
                      reads=[PBB], writes=[B("obuf", tq)])

    def gla_phase(s, l):
        S.barrier()
        cv = Carver()
        glv = cv(BF16, NTT, 256)
        glgw = cv(BF16, NTT, 256)
        qt = [cv(BF16, T) for _ in range(2)]
        kt = [cv(BF16, T) for _ in range(2)]
        ktok = cv(BF16, NTT, 128)
        Dall = cv(F32, 2, 36)
        S32 = cv(F32, 37, 64)
        Sbf = [cv(BF16, 36, 64) for _ in range(2)]
        qf = cv(F32, 512)
        kf = cv(F32, 512)
        lrf = cv(F32, 512)
        smask = cv(F32, 512)
        tA = cv(F32, 512)
        tB_ = cv(F32, 512)
        tC = cv(F32, 512)
        tD = cv(F32, 512)
        sg = cv(F32, 256)
        Abf = [cv(BF16, 2, 256) for _ in range(2)]
        qmb = [cv(BF16, 2, 4, 128) for _ in range(2)]
        osq = cv(F32, 256)
        on = cv(F32, 256)
        otok = [cv(BF16, 256) for _ in range(2)]
        st4 = [cv(F32, 8) for _ in range(2)]
        TB = lambda n: B("gltmp", n)
        wv = wview(0, BF16, 8, 800)
        wb = [B("wbuf", 0)]
        S.dma(I("dma_start", out=wv, in_=win_v[l][:, :, C_GLQ:C_GLQ + 800]), writes=wb, q="pool")
        S.pool(I("memset", smask, 1.0), writes=[TB("smask")])
        S.pool(I("memset", smask.rearrange("p (c t) -> p c t", c=8, t=64)[:, :, 0:1], 0.0), writes=[TB("smask")])
        for tt in range(NTT):
            ps, bps = bank("proj", [0, 1, 2, 3])
            for k in range(8):
                S.pe(I("matmul", ps[:, 0:512], ubuf[:, k, 128 * tt:128 * tt + 128], wv[:, k, 256:768],
                                                         start=(k == 0), stop=(k == 7)),
                     reads=wb + [B("ubuf", tt)], writes=[bps])
            S.dve(I("tensor_copy", out=glv[:, tt, :], in_=ps[:, 0:256]), reads=[bps], writes=[B("glv", tt)])
            S.act(I("activation", out=sg, in_=ps[:, 256:512], func=AF.Silu), reads=[bps], writes=[TB("sg")])
            S.pool(I("tensor_tensor", out=glgw[:, tt, :], in0=sg, in1=glnw[:, l, :], op=ALU.mult),
                   reads=[TB("sg"), B("glnw")], writes=[B("glgw", tt)])
        import os as _os
        _gs = int(_os.environ.get('GLASTOP', '9'))
        _sub = int(_os.environ.get('GLASUB', '9'))
        if _gs <= 0:
            return
        for (t0, n) in BLOCKS:
            nch = n // 64
            c0 = t0 // 64
            pq, bpq = bank("proj", [0, 1, 2, 3])
            pk, bpk = bank("proj", [0, 1, 2, 3])
            pl, bpl = bank("proj", [0, 1, 2, 3])
            for (pp, col, M) in ((pq, 0, 128), (pk, 128, 128), (pl, 768, 32)):
                bb = {id(pq): bpq, id(pk): bpk, id(pl): bpl}[id(pp)]
                for k in range(8):
                    S.pe(I("matmul", pp[0:M, 0:n], wv[:, k, col:col + M], ubuf[:, k, t0:t0 + n],
                                                                              start=(k == 0), stop=(k == 7)),
                         reads=wb + ub(t0, n), writes=[bb])
            S.act(I("activation", out=qf[:, 0:n], in_=pq[:, 0:n], func=AF.Identity, scale=float(32 ** -0.5)), reads=[bpq], writes=[TB("qf")])
            S.dve(I("tensor_copy", out=kf[:, 0:n], in_=pk[:, 0:n]), reads=[bpk], writes=[TB("kf")])
            S.dve(I("tensor_copy", out=lrf[0:32, 0:n], in_=pl[0:32, 0:n]), reads=[bpl], writes=[TB("lrf")])
            for ed in range(2):
                pz, bpz = bank("glz", [4, 5])
                S.pe(I("matmul", pz[:, 0:n], wa2[:, l, ed, :], lrf[0:32, 0:n], start=True, stop=True),
                     reads=[TB("lrf"), B("wa2")], writes=[bpz])
                S.act(I("activation", out=tA[:, 0:n], in_=pz[:, 0:n], func=AF.Exp, scale=-1.0, bias=nbaT[:, l, ed:ed + 1]),
                      reads=[bpz, B("nbaT")], writes=[TB("tA")])
                S.act(I("activation", out=tB_[:, 0:n], in_=tA[:, 0:n], func=AF.Ln, bias=1.0, scale=1.0), reads=[TB("tA")], writes=[TB("tB")])
                S.dve(I("tensor_tensor_scan", out=tC[:, 0:n], data0=smask[:, 0:n], data1=tB_[:, 0:n], initial=0.0, op0=ALU.mult, op1=ALU.add),
                      reads=[TB("smask"), TB("tB")], writes=[TB("tC")])
                cum = tC
                bcum = TB("tC")
                if ed == 1:
                    tC3 = tC[:, 0:n].rearrange("p (c t) -> p c t", c=nch, t=64)
                    S.dve(I("tensor_tensor", out=tD[:, 0:n].rearrange("p (c t) -> p c t", c=nch, t=64),
                                                                         in0=tC3[:, :, 63:64].broadcast_to([128, nch, 64]), in1=tC3, op=ALU.subtract),
                          reads=[TB("tC")], writes=[TB("tD")])
                    S.dve(I("tensor_tensor", out=tD[:, 0:n], in0=tD[:, 0:n], in1=tB_[:, 0:n], op=ALU.add), reads=[TB("tD"), TB("tB")], writes=[TB("tD")])
                    cum = tD
                    bcum = TB("tD")
                S.act(I("activation", out=tA[:, 0:n], in_=cum[:, 0:n], func=AF.Exp, scale=-1.0 / 16.0), reads=[bcum], writes=[TB("tA")])
                S.act(I("activation", out=tB_[:, 0:n], in_=cum[:, 0:n], func=AF.Exp, scale=1.0 / 16.0), reads=[bcum, TB("tA")], writes=[TB("tB")])
                dcol = 63 if ed == 0 else 0
                S.pool(I("tensor_copy",
                    out=Dall[:, ed, c0:c0 + nch], in_=tA[:, 0:n].rearrange("p (c t) -> p c t", c=nch, t=64)[:, :, dcol]),
                    reads=[TB("tA")], writes=[B("Dall")])
                wr = [B("glqk", ed, tt) for tt in tiles_of(t0, n)]
                S.dve(I("tensor_tensor", out=qt[ed][:, t0:t0 + n], in0=qf[:, 0:n], in1=tA[:, 0:n], op=ALU.mult),
                      reads=[TB("qf"), TB("tA")], writes=wr)
                S.pool(I("tensor_tensor", out=kt[ed][:, t0:t0 + n], in0=kf[:, 0:n], in1=tB_[:, 0:n], op=ALU.mult),
                       reads=[TB("kf"), TB("tB")], writes=wr)
        if _gs <= 1:
            return
        ords = [list(range(36)), [3, 2, 1, 0] + list(range(35, 3, -1))]
        poss = []
        for ed in range(2):
            pos = [0] * 36
            for j, ci in enumerate(ords[ed]):
                pos[ci] = j
            poss.append(pos)
        for ed in range(2):
            pos = poss[ed]
            for tt in range(NTT):
                S.pe(I("transpose", pb[:, 128 * (tt % 4):128 * (tt % 4) + 128], kt[ed][:, 128 * tt:128 * tt + 128], identb[:]),
                     reads=[B("glqk", ed, tt), B("identb")], writes=[PBB])
                if tt % 4 == 3 or tt == NTT - 1:
                    n4 = tt % 4 + 1
                    ta = tt - n4 + 1
                    S.act(I("copy", out=ktok[:, ta:ta + n4, :], in_=pb[:, 0:128 * n4].rearrange("p (a b) -> p a b", a=n4, b=128)),
                          reads=[PBB], writes=[B("ktok")])
            if _sub <= 1:
                continue
            S.pool(I("memset", S32[:, 0, :], 0.0), writes=[B("S32")])
            for half in range(2):
              for grp in range(0, 18, 8):
                pw, bpw = bank("glw", [0, 1, 2, 3])
                cis = [2 * t_ + half for t_ in range(grp, min(grp + 8, 18))]
                for gi, ci in enumerate(cis):
                    tt = ci // 2
                    for h in range(4):
                        S.pe(I("matmul",
                            pw[32 * h:32 * h + 32, 64 * gi:64 * gi + 64], ktok[64 * half:64 * half + 64, tt, 32 * h:32 * h + 32],
                            glv[64 * half:64 * half + 64, tt, 64 * h:64 * h + 64], start=True, stop=True, tile_position=(64 * half, 32 * h)),
                            reads=[B("ktok"), B("glv", tt)], writes=[bpw])
                for gi, ci in enumerate(cis):
                    S.act(I("activation",
                        out=S32[:, 1 + pos[ci], :], in_=pw[:, 64 * gi:64 * gi + 64], func=AF.Identity, scale=Dall[:, ed, ci:ci + 1]),
                        reads=[bpw, B("Dall")], writes=[B("S32")])
            if _sub <= 2:
                continue
            for j in range(36):
                ci = ords[ed][j]
                S.dve(I("scalar_tensor_tensor", out=S32[:, j + 1, :], in0=S32[:, j, :], scalar=Dall[:, ed, ci:ci + 1], in1=S32[:, j + 1, :],
                                                                      op0=ALU.mult, op1=ALU.add),
                      reads=[B("S32"), B("Dall")], writes=[B("S32")])
            if _sub <= 3:
                continue
            S.dve(I("tensor_copy", out=Sbf[ed], in_=S32[:, 0:36, :]), reads=[B("S32")], writes=[B("Sbf", ed)])
        if _gs <= 2:
            return
        tiles_out = range(NTT) if l < DEPTH - 1 else range(2, NTT)
        for oi, tt in enumerate(tiles_out):
            pa, bpa = bank("glpa", [0, 1, 2, 3])
            A = Abf[oi % 2]
            bA = B("Abf", oi % 2)
            qm = qmb[oi % 2]
            bqm = B("qm", oi % 2)
            for ed in range(2):
                for h in range(4):
                    eng = S.pool if (h % 2 == 0) else S.dve
                    eng(I("tensor_scalar_mul", out=qm[:, ed, h, :], in0=qt[ed][:, 128 * tt:128 * tt + 128], scalar1=hmask[:, h:h + 1]),
                        reads=[B("glqk", ed, tt), B("hmask")], writes=[bqm])
            for half in range(2):
                cols = slice(128 * tt + 64 * half, 128 * tt + 64 * half + 64)
                for ed in range(2):
                    for h in range(4):
                        S.pe(I("matmul",
                            pa[64 * half:64 * half + 64, 256 * ed + 64 * h:256 * ed + 64 * h + 64],
                            kt[ed][:, cols], qm[:, ed, h, 64 * half:64 * half + 64], start=True, stop=True,
                            tile_position=(0, 64 * half)),
                            reads=[B("glqk", ed, tt), bqm], writes=[bpa])
            S.dve(I("tensor_tensor", out=A, in0=pa[:, 0:512].rearrange("p (a b) -> p a b", a=2, b=256), in1=trim[:], op=ALU.mult),
                  reads=[bpa, B("trim")], writes=[bA])
            po, bpo = bank("glpo", [4, 5])
            for half in range(2):
                ci = 2 * tt + half
                for h in range(4):
                    for ed in range(2):
                        S.pe(I("matmul",
                            po[64 * half:64 * half + 64, 64 * h:64 * h + 64], A[64 * half:64 * half + 64, ed, 64 * h:64 * h + 64],
                            glv[64 * half:64 * half + 64, tt, 64 * h:64 * h + 64], start=(ed == 0), stop=False,
                            tile_position=(64 * half, 64 * half)),
                            reads=[bA, B("glv", tt)], writes=[bpo])
                    for ed in range(2):
                        pj = poss[ed][ci]
                        S.pe(I("matmul",
                            po[64 * half:64 * half + 64, 64 * h:64 * h + 64], qm[:, ed, h, 64 * half:64 * half + 64],
                            Sbf[ed][:, pj, :], start=False, stop=(ed == 1),
                            tile_position=(0, 64 * half)),
                            reads=[bqm, B("Sbf", ed)], writes=[bpo])
            o2 = oi % 2
            S.act(I("activation", out=osq, in_=po[:, 0:256], func=AF.Square), reads=[bpo], writes=[TB("osq")])
            S.dve(I("reduce_sum", out=st4[o2][:, 0:4], in_=osq.rearrange("p (h c) -> p h c", h=4, c=64), axis=AX.X),
                  reads=[TB("osq")], writes=[B("st4", o2)])
            S.dve(I("tensor_scalar", out=st4[o2][:, 0:4], in0=st4[o2][:, 0:4], scalar1=1.0 / 64.0, scalar2=RMS_EPS, op0=ALU.mult, op1=ALU.add),
                  reads=[B("st4", o2)], writes=[B("st4", o2)])
            S.act(I("activation", out=st4[o2][:, 4:8], in_=st4[o2][:, 0:4], func=AF.Sqrt), reads=[B("st4", o2)], writes=[B("st4", o2)])
            S.dve(I("reciprocal", out=st4[o2][:, 0:4], in_=st4[o2][:, 4:8]), reads=[B("st4", o2)], writes=[B("st4", o2)])
            S.dve(I("tensor_tensor", out=on.rearrange("p (h c) -> p h c", h=4, c=64), in0=po[:, 0:256].rearrange("p (h c) -> p h c", h=4, c=64),
                                                        in1=st4[o2][:, 0:4].unsqueeze(2).broadcast_to([128, 4, 64]), op=ALU.mult),
                  reads=[bpo, B("st4", o2)], writes=[TB("on")])
            S.pool(I("tensor_tensor", out=otok[o2], in0=on, in1=glgw[:, tt, :], op=ALU.mult),
                   reads=[TB("on"), B("glgw", tt)], writes=[B("glotok", o2)])
            for c2 in range(2):
                S.pe(I("transpose", pb[:, 512 + 128 * c2:512 + 128 * c2 + 128], otok[o2][:, 128 * c2:128 * c2 + 128], identb[:]),
                     reads=[B("glotok", o2), B("identb")], writes=[PBB])
            S.act(I("copy", out=obuf[:, 3:5, 128 * tt:128 * tt + 128], in_=pb[:, 512:768].rearrange("p (a b) -> p a b", a=2, b=128)),
                  reads=[PBB], writes=[B("obuf", tt)])

    def layer_norm_block(tbuf, btb, n, mi, gam_beta, outs):
        cvs = self_cv[0]
        pmean, bpmean = bank("lnm", [4, 5])
        pmsq, bpmsq = bank("lnm", [4, 5])
        for oc in range(8):
            S.pe(I("matmul", pmean[:, 0:n], onesD[:], tbuf[:, oc, 0:n], start=(oc == 0), stop=(oc == 7)),
                 reads=[btb, B("onesD")], writes=[bpmean])
        for oc in range(8):
            sqb = lnsq[oc % 2]
            S.act(I("activation", out=sqb[:, 0:n], in_=tbuf[:, oc, 0:n], func=AF.Square), reads=[btb], writes=[B("lnsq", oc % 2)])
            S.pe(I("matmul", pmsq[:, 0:n], onesD[:], sqb[:, 0:n], start=(oc == 0), stop=(oc == 7)),
                 reads=[B("lnsq", oc % 2), B("onesD")], writes=[bpmsq])
        S.act(I("copy", out=lnmean[:, 0:n], in_=pmean[:, 0:n]), reads=[bpmean], writes=[B("lnmean")])
        S.pool(I("tensor_tensor", out=lnm2[:, 0:n], in0=lnmean[:, 0:n], in1=lnmean[:, 0:n], op=ALU.mult), reads=[B("lnmean")], writes=[B("lnm2")])
        S.dve(I("tensor_tensor", out=lnm2[:, 0:n], in0=pmsq[:, 0:n], in1=lnm2[:, 0:n], op=ALU.subtract), reads=[bpmsq, B("lnm2")], writes=[B("lnm2")])
        S.act(I("activation", out=lnm2[:, 0:n], in_=lnm2[:, 0:n], func=AF.Sqrt, bias=epsln[:, 0:1], scale=1.0), reads=[B("lnm2"), B("eps")], writes=[B("lnm2")])
        S.dve(I("reciprocal", out=lnrstd[:, 0:n], in_=lnm2[:, 0:n]), reads=[B("lnm2")], writes=[B("lnrstd")])
        for oc in range(8):
            S.dve(I("tensor_tensor", out=tbuf[:, oc, 0:n], in0=tbuf[:, oc, 0:n], in1=lnmean[:, 0:n], op=ALU.subtract),
                  reads=[btb, B("lnmean")], writes=[btb])
            S.dve(I("tensor_tensor", out=tbuf[:, oc, 0:n], in0=tbuf[:, oc, 0:n], in1=lnrstd[:, 0:n], op=ALU.mult),
                  reads=[btb, B("lnrstd")], writes=[btb])
            for (eng, dstf, scf, bif, wr) in outs:
                if eng == "act":
                    S.act(I("activation", out=dstf(oc), in_=tbuf[:, oc, 0:n], func=AF.Identity, scale=scf(oc), bias=bif(oc)),
                          reads=[btb] + MD, writes=wr)
                else:
                    S.pool(I("tensor_scalar", out=dstf(oc), in0=tbuf[:, oc, 0:n], scalar1=scf(oc), scalar2=bif(oc),
                                                                                      op0=ALU.mult, op1=ALU.add),
                           reads=[btb] + MD, writes=wr)

    self_cv = [None]
    lnsq = [None, None]
    lnmean = lnm2 = lnrstd = None

    def outproj_phase(s, l):
        nonlocal lnsq, lnmean, lnm2, lnrstd
        S.barrier()
        cv = Carver()
        self_cv[0] = cv
        tb = cv(F32, 8, 512)
        xs = cv(F32, 8, 512)
        lnsq = [cv(F32, 512) for _ in range(2)]
        lnmean = cv(F32, 512)
        lnm2 = cv(F32, 512)
        lnrstd = cv(F32, 512)
        wv = wview(1, BF16, 8, 1024)
        wb = [B("wbuf", 1)]
        S.dma(I("dma_start", out=wv, in_=wout_d[l].rearrange("(k p) c -> p k c", p=128)), writes=wb, q="pool")
        for (t0, n) in BLOCKS:
            if t0 < LC and l == DEPTH - 1:
                continue
            mi = msel(s, t0)
            xrb = [B("xres", s, tt) for tt in tiles_of(t0, n)]
            S.dma(I("dma_start", out=xs[:, :, 0:n], in_=xres[s][:, :, t0:t0 + n]), reads=xrb, writes=[B("xs")])
            for oc in range(8):
                py, bpy = bank("proj", [0, 1, 2, 3])
                for k in range(8):
                    S.pe(I("matmul", py[:, 0:n], wv[:, k, 128 * oc:128 * oc + 128], obuf[:, k, t0:t0 + n],
                                                                       start=(k == 0), stop=(k == 7)),
                         reads=wb + [B("obuf", tt) for tt in tiles_of(t0, n)], writes=[bpy])
                S.dve(I("scalar_tensor_tensor", out=tb[:, oc, 0:n], in0=py[:, 0:n], scalar=mder[:, l, 1, oc, mi:mi + 1], in1=xs[:, oc, 0:n],
                                                                             op0=ALU.mult, op1=ALU.add),
                      reads=[bpy, B("xs")] + MD, writes=[B("tb")])
            outs = [
                ("act", lambda oc, n=n: xs[:, oc, 0:n], lambda oc: lnT[:, l, 0, oc:oc + 1], lambda oc: lnT[:, l, 1, oc:oc + 1], [B("xs")]),
                ("pool", lambda oc, t0=t0, n=n: ubuf[:, oc, t0:t0 + n], lambda oc, mi=mi: mder[:, l, 4, oc, mi:mi + 1], lambda oc, mi=mi: mder[:, l, 5, oc, mi:mi + 1], ub(t0, n)),
            ]
            layer_norm_block(tb, B("tb"), n, mi, None, outs)
            S.dma(I("dma_start", out=xres[s][:, :, t0:t0 + n], in_=xs[:, :, 0:n]), reads=[B("xs")], writes=xrb)

    def ffn_phase(s, l):
        nonlocal lnsq, lnmean, lnm2, lnrstd
        S.barrier()
        last = (l == DEPTH - 1)
        cv = Carver()
        self_cv[0] = cv
        hT = cv(BF16, NHC, 1024)
        w2 = [cv(BF16, NHC, 256) for _ in range(2)]
        lnsq = [cv(F32, 512) for _ in range(2)]
        lnmean = cv(F32, 512)
        lnm2 = cv(F32, 512)
        lnrstd = cv(F32, 512)
        sa = [cv(F32, 512) for _ in range(2)]
        ostage = [cv(F32, 1024) for _ in range(2)] if last else None
        tfull = obuf[:].rearrange("p k t -> p (k t)").bitcast(F32)[:, 0:8 * 1024].rearrange("p (k t) -> p k t", k=8, t=1024)
        parts = [[BLOCKS[1], BLOCKS[2]], [BLOCKS[3], BLOCKS[4]]]
        if not last:
            parts = [[BLOCKS[0]]] + parts
        wf1 = wf1_d[l].rearrange("(k p) c -> p k c", p=128)
        wf2 = wf2_d[l].rearrange("(j p) c -> p j c", p=128)
        for part in parts:
            pt0 = part[0][0]
            ptn = sum(n for (_, n) in part)
            mi = msel(s, pt0)
            btf = B("tfull")
            S.dma(I("dma_start", out=tfull[:, :, 0:ptn], in_=xres[s][:, :, pt0:pt0 + ptn]),
                  reads=[B("xres", s, tt) for tt in tiles_of(pt0, ptn)], writes=[btf])
            for js in range(11):
                wi = js % 2
                wv = wview(wi, BF16, 8, 512)
                wb = [B("wbuf", wi)]
                S.dma(I("dma_start", out=wv[:, :, 0:256], in_=wf1[:, :, 256 * js:256 * js + 256]), writes=wb, q="pool")
                S.dma(I("dma_start", out=wv[:, :, 256:512], in_=wf1[:, :, FH + 256 * js:FH + 256 * js + 256]), writes=wb, q="pool")
                for (t0, n) in part:
                    lo = t0 - pt0
                    for jj in range(2):
                        j = 2 * js + jj
                        pa, bpa = bank("ffa", [0, 1, 2])
                        pg, bpg = bank("ffb", [3, 4, 5])
                        for k in range(8):
                            S.pe(I("matmul", pa[:, 0:n], wv[:, k, 128 * jj:128 * jj + 128], ubuf[:, k, t0:t0 + n],
                                                                                      start=(k == 0), stop=(k == 7)),
                                 reads=wb + ub(t0, n), writes=[bpa])
                        for k in range(8):
                            S.pe(I("matmul", pg[:, 0:n], wv[:, k, 256 + 128 * jj:256 + 128 * jj + 128], ubuf[:, k, t0:t0 + n],
                                                                                      start=(k == 0), stop=(k == 7)),
                                 reads=wb + ub(t0, n), writes=[bpg])
                        si = j % 2
                        S.act(I("activation", out=sa[si][:, 0:n], in_=pa[:, 0:n], func=AF.Silu), reads=[bpa], writes=[B("sa", si)])
                        S.dve(I("tensor_tensor", out=hT[:, j, lo:lo + n], in0=pg[:, 0:n], in1=sa[si][:, 0:n], op=ALU.mult),
                              reads=[bpg, B("sa", si)], writes=[B("hT")])
            for oc2 in range(4):
                w2v = w2[oc2 % 2]
                bw2 = [B("w2", oc2 % 2)]
                S.dma(I("dma_start", out=w2v, in_=wf2[:, :, 256 * oc2:256 * oc2 + 256]), writes=bw2, q="pool")
                for (t0, n) in part:
                    lo = t0 - pt0
                    for oo in range(2):
                        oc = 2 * oc2 + oo
                        py, bpy = bank("ffy", [0, 1, 2, 3])
                        for j in range(NHC):
                            S.pe(I("matmul", py[:, 0:n], w2v[:, j, 128 * oo:128 * oo + 128], hT[:, j, lo:lo + n],
                                                                                        start=(j == 0), stop=(j == NHC - 1)),
                                 reads=bw2 + [B("hT")], writes=[bpy])
                        S.dve(I("scalar_tensor_tensor", out=tfull[:, oc, lo:lo + n], in0=py[:, 0:n], scalar=mder[:, l, 3, oc, mi:mi + 1],
                                                                                            in1=tfull[:, oc, lo:lo + n], op0=ALU.mult, op1=ALU.add),
                              reads=[bpy, btf] + MD, writes=[btf])
            for (t0, n) in part:
                lo = t0 - pt0
                tbv = tfull[:, :, lo:lo + n]
                xrb = [B("xres", s, tt) for tt in tiles_of(t0, n)]
                if not last:
                    outs = [
                        ("pool", lambda oc, t0=t0, n=n: ubuf[:, oc, t0:t0 + n], lambda oc, mi=mi: mder2[:, l, 0, oc, mi:mi + 1], lambda oc, mi=mi: mder2[:, l, 1, oc, mi:mi + 1], ub(t0, n)),
                        ("act", lambda oc, tbv=tbv: tbv[:, oc, :], lambda oc: lnT[:, l, 2, oc:oc + 1], lambda oc: lnT[:, l, 3, oc:oc + 1], [btf]),
                    ]
                    layer_norm_block(tbv, btf, n, mi, None, outs)
                    S.dma(I("dma_start", out=xres[s][:, :, t0:t0 + n], in_=tbv), reads=[btf], writes=xrb)
                else:
                    outs = [("act", lambda oc, tbv=tbv: tbv[:, oc, :], lambda oc: lnT[:, l, 2, oc:oc + 1], lambda oc: lnT[:, l, 3, oc:oc + 1], [btf])]
                    layer_norm_block(tbv, btf, n, mi, None, outs)
                    for qs in range(n // 128):
                        tq = (t0 + 128 * qs) // 128
                        og = ostage[tq % 2]
                        for half in range(2):
                            pt, bpt = bank("ffo", [0, 1, 2, 3])
                            for kk in range(4):
                                k = 4 * half + kk
                                S.pe(I("transpose", pt[:, 128 * kk:128 * kk + 128], tbv[:, k, 128 * qs:128 * qs + 128], identf[:]),
                                     reads=[btf, B("identf")], writes=[bpt])
                            S.act(I("copy", out=og[:, 512 * half:512 * half + 512], in_=pt[:, 0:512]),
                                  reads=[bpt], writes=[B("ostage", tq % 2)])
                        S.dma(I("dma_start", out=out_d[s, 128 * (tq - 2):128 * (tq - 2) + 128, :], in_=og),
                              reads=[B("ostage", tq % 2)], writes=[B("out")])

    for s in range(nseq):
        load_x(s)
        if stage == 1 and s == 0:
            dump("u0", ubuf[:], [128, 8, T], BF16, B("ubuf", NTT - 1))
        for l in range(DEPTH):
            if stage >= 2:
                na_phase(s, l)
            if stage >= 3:
                gla_phase(s, l)
            if stage >= 4:
                gqa_phase(s, l)
            if stage in (2, 3, 4) and s == 0 and l == 0:
                S.barrier()
                S.pool(I("memset", epsln[:], LN_EPS_S), reads=[B("obuf", tt) for tt in range(NTT)], writes=[B("eps")])
                dump("o0", obuf[:], [128, 8, T], BF16, B("eps"))
                break
            if stage >= 5:
                outproj_phase(s, l)
            if stage == 5 and s == 0 and l == 0:
                S.barrier()
                S.pool(I("memset", epsln[:], LN_EPS_S), reads=[B("ubuf", tt) for tt in range(NTT)], writes=[B("eps")])
                dump("u2", ubuf[:], [128, 8, T], BF16, B("eps"))
                dump("x1", xres[0], [128, 8, T], F32, B("eps"))
                break
            if stage >= 6:
                ffn_phase(s, l)
            if stage == 6 and s == 0 and l == 0:
                S.barrier()
                S.pool(I("memset", epsln[:], LN_EPS_S), reads=[B("ubuf", tt) for tt in range(NTT)], writes=[B("eps")])
                dump("u1n", ubuf[:], [128, 8, T], BF16, B("eps"))
                dump("x2", xres[0], [128, 8, T], F32, B("eps"))
                break


    S.emit(final_bufs=dump_list + [B("out")])
    S.close()
    return nc


def host_consts():
    eye = np.eye(128, dtype=np.float32)
    onesd = np.full((128, 128), 1.0 / 1024.0, np.float32)
    bon = np.zeros((128, 128), np.float32)
    bon[:64, :64] = 1.0 / 64
    bon[64:, 64:] = 1.0 / 64
    R = np.zeros((64, 64), np.float32)
    for d in range(16):
        R[d, d + 16] = -1.0
        R[d + 16, d] = 1.0
        R[d + 32, d + 48] = -1.0
        R[d + 48, d + 32] = 1.0
    rl = np.zeros((128, 128), np.float32)
    rl[:64, :64] = R.T
    rl[64:, 64:] = R.T
    hm = np.zeros_like(eye)
    for h in range(4):
        hm[32 * h:32 * h + 32, h] = 1.0
    cm = np.stack([eye, onesd, bon, rl, hm], axis=1)
    sidx = np.arange(64)
    mf = (sidx[None, :] >= sidx[:, None]).astype(np.float32)
    mb = (sidx[None, :] <= sidx[:, None]).astype(np.float32)
    tri = np.zeros((128, 2, 256), np.float32)
    for half in range(2):
        tri[64 * half:64 * half + 64, 0, :] = np.tile(mf, (1, 4))
        tri[64 * half:64 * half + 64, 1, :] = np.tile(mb, (1, 4))
    t = np.arange(NL)
    row = (t // 64).astype(np.float32)
    col = (t % 64).astype(np.float32)
    inv_freq = (10000.0 ** (-np.arange(16, dtype=np.float32) / 16)).astype(np.float32)
    ang_r = row[:, None] * inv_freq
    ang_c = col[:, None] * inv_freq
    ang = np.concatenate([ang_r, ang_r, ang_c, ang_c], axis=-1).astype(np.float32)
    cosT = np.tile(np.cos(ang).T, (2, 1)).astype(np.float32)
    sinT = np.tile(np.sin(ang).T, (2, 1)).astype(np.float32)
    return cm, tri, np.ascontiguousarray(cosT), np.ascontiguousarray(sinT)


def prep_shared(inp):
    f = lambda a: np.ascontiguousarray(np.asarray(a, dtype=np.float32))
    cm, tri, cosT, sinT = host_consts()
    sh = {}
    sh["w_ada"] = f(inp["w_ada"])
    sh["b_adaT"] = f(np.asarray(inp["b_ada"]).reshape(DEPTH, 48, 128).transpose(0, 2, 1))
    sh["w_in"] = f(inp["w_in"])
    sh["w_out"] = f(inp["w_out"])
    sh["w_ffn_in"] = f(inp["w_ffn_in"])
    sh["w_ffn_out"] = f(inp["w_ffn_out"])
    ln = np.stack([np.asarray(inp[k]) for k in ("ln1_g", "ln1_b", "ln2_g", "ln2_b")], axis=1)
    sh["lnT"] = f(ln.reshape(DEPTH, 4, 8, 128).transpose(3, 0, 1, 2))
    valid, dri, dci = na_index_tables()
    rpb = np.asarray(inp["na_rpb"], dtype=np.float32)
    g = rpb[:, :, dri, dci]
    g = np.where(valid[None, None], g, np.float32(-30000.0))
    sh["nab"] = f(g.transpose(0, 3, 1, 2, 4).reshape(DEPTH, 128, 6, NCOMBO * 128))
    wa2 = np.zeros((DEPTH, 2, 32, 128), np.float32)
    gw = np.asarray(inp["gla_wa2"], dtype=np.float32)
    wa2[:, 0, 0:16, :] = gw[:, 0]
    wa2[:, 1, 16:32, :] = gw[:, 1]
    sh["wa2"] = wa2
    sh["nbaT"] = f(np.asarray(inp["gla_ba"]).transpose(2, 0, 1))
    sh["glnw"] = f(np.tile(np.asarray(inp["gla_norm_w"]), (1, 4)).reshape(DEPTH, 1, 256))
    qk = np.stack([np.tile(np.asarray(inp["gqa_qnorm_w"]), (1, 2)), np.tile(np.asarray(inp["gqa_knorm_w"]), (1, 2))], axis=2)
    sh["qkwT"] = f(qk.transpose(1, 0, 2))
    sh["cosT"] = cosT
    sh["sinT"] = sinT
    sh["cmats"] = f(cm)
    sh["trim"] = f(tri)
    return sh


def prep_core(inp, core, sh):
    f = lambda a: np.ascontiguousarray(np.asarray(a, dtype=np.float32))
    b0 = core * NSEQ
    m = dict(sh)
    m["x"] = f(inp["x"][b0:b0 + NSEQ])
    m["ctx"] = f(inp["ctx"][b0:b0 + NSEQ])
    cc = np.stack([np.asarray(inp["c"][b0]), np.asarray(inp["c"][b0 + 1]), np.asarray(inp["c_ctx"])], axis=1)
    m["cT"] = f(cc.reshape(8, 128, 3).transpose(1, 0, 2))
    return m


def kernel(**inputs):
    sh = prep_shared(inputs)
    nc = build_nc()
    in_maps = [prep_core(inputs, c, sh) for c in range(8)]
    res = run_bass_kernel_spmd(nc, in_maps, core_ids=list(range(8)))
    out = np.concatenate([np.asarray(r["out"]) for r in res.results], axis=0)
    return out.astype(np.float32)
```

```python
import contextlib
import numpy as np
import concourse.bass as bass
import concourse.mybir as mybir
from concourse.bass_utils import run_bass_kernel_spmd

F32 = mybir.dt.float32
BF16 = mybir.dt.bfloat16
AF = mybir.ActivationFunctionType
ALU = mybir.AluOpType
AX = mybir.AxisListType

COMPUTE = ("pe", "act", "dve", "pool")
NDMA_SEM = 6


class Buf:
    __slots__ = ("last_w", "readers", "excl")

    def __init__(self, excl=False):
        self.last_w = None
        self.readers = []
        self.excl = excl


class Sched:
    def __init__(self, nc):
        self.nc = nc
        self.ops = {e: [] for e in ("pe", "act", "dve", "pool", "sp")}
        self.stack = contextlib.ExitStack()
        self.nalloc = 0
        self.bufs = {}
        self.pending = {e: set() for e in self.ops}
        self.phase = "pro"

    def B(self, *key):
        b = self.bufs.get(key)
        if b is None:
            b = self.bufs[key] = Buf(excl=(key[0] in ("pf", "pb")))
        return b

    def sbuf(self, shape, dtype, name=None):
        self.nalloc += 1
        return self.stack.enter_context(self.nc.sbuf_tensor("sb_" + (name or f"t{self.nalloc}"), list(shape), dtype))

    def psum(self, shape, dtype, name=None):
        self.nalloc += 1
        return self.stack.enter_context(self.nc.psum_tensor("ps_" + (name or f"t{self.nalloc}"), list(shape), dtype))

    def add(self, eng, fn, reads=(), writes=(), dma=False):
        idx = len(self.ops[eng])
        deps = set(self.pending[eng])
        self.pending[eng] = set()
        for b in reads:
            if b.last_w is not None:
                deps.add(b.last_w)
            if b.excl:
                deps.update(r for r in b.readers if r[0] != eng)
        for b in writes:
            if b.last_w is not None:
                deps.add(b.last_w)
            deps.update(b.readers)
        if eng == "pe":
            deps = {d for d in deps if d[0] != "pe"}
        self.ops[eng].append({"fn": fn, "deps": deps, "dma": dma, "marked": False, "ph": self.phase})
        me = (eng, idx)
        for b in reads:
            b.readers.append(me)
        for b in writes:
            b.last_w = me
            b.readers = []
        return me

    def pe(self, fn, reads=(), writes=()):
        return self.add("pe", fn, reads, writes)

    def act(self, fn, reads=(), writes=()):
        return self.add("act", fn, reads, writes)

    def dve(self, fn, reads=(), writes=()):
        return self.add("dve", fn, reads, writes)

    def pool(self, fn, reads=(), writes=()):
        return self.add("pool", fn, reads, writes)

    def dma(self, fn, reads=(), writes=(), q="sp"):
        return self.add(q, fn, reads, writes, dma=True)

    def barrier(self):
        deps = set()
        for e in ("pe", "act", "dve"):
            if self.ops[e]:
                deps.add((e, len(self.ops[e]) - 1))
        for q in ("sp", "pool"):
            nd = 0
            gotc = False
            for i in range(len(self.ops[q]) - 1, -1, -1):
                if self.ops[q][i]["dma"]:
                    if nd < NDMA_SEM:
                        deps.add((q, i))
                        nd += 1
                elif not gotc:
                    deps.add((q, i))
                    gotc = True
                if nd >= NDMA_SEM and (gotc or q == "sp"):
                    break
        for e in self.pending:
            self.pending[e] |= deps

    def emit(self, final_bufs=()):
        nc = self.nc
        ops = self.ops
        dma_info = {}
        for q in ("sp", "pool"):
            n = 0
            for i, op in enumerate(ops[q]):
                if op["dma"]:
                    slot = n % NDMA_SEM
                    val = 16 * (n // NDMA_SEM + 1)
                    dma_info[(q, i)] = (q, slot, val)
                    op["dmainfo"] = (slot, val)
                    n += 1
        for e in ops:
            for op in ops[e]:
                op["deps"] = {d for d in op["deps"]}
                for d in op["deps"]:
                    if d not in dma_info:
                        ops[d[0]][d[1]]["marked"] = True
        final_deps = set()
        for b in final_bufs:
            if b.last_w is not None:
                final_deps.add(b.last_w)
                if b.last_w not in dma_info:
                    ops[b.last_w[0]][b.last_w[1]]["marked"] = True
        for e in COMPUTE:
            c = 0
            for op in ops[e]:
                if op["marked"] and not op["dma"]:
                    c += 1
                    op["semval"] = c
        st = self.stack
        csem = {e: st.enter_context(nc.semaphore(f"s_{e}")) for e in COMPUTE}
        dsem = {q: [st.enter_context(nc.semaphore(f"d_{q}{j}")) for j in range(NDMA_SEM)] for q in ("sp", "pool")}
        block = st.enter_context(nc.Block())
        engobj = {"pe": block.tensor, "act": block.scalar, "dve": block.vector, "pool": block.gpsimd, "sp": block.sync}

        def resolve(dep):
            if dep in dma_info:
                q, slot, val = dma_info[dep]
                return dsem[q][slot], ("d", q, slot), val
            f, k = dep
            return csem[f], ("c", f), ops[f][k]["semval"]

        def make_body(e):
            def body(eng):
                known = {}
                for op in ops[e]:
                    need = {}
                    for dep in op["deps"]:
                        if dep[0] == e and dep not in dma_info and e == "pe":
                            continue
                        sem, key, val = resolve(dep)
                        if need.get(key, (None, 0))[1] < val:
                            need[key] = (sem, val)
                    if op["dma"]:
                        slot, val = op["dmainfo"]
                        if val > 16:
                            key = ("d", e, slot)
                            if need.get(key, (None, 0))[1] < val - 16:
                                need[key] = (dsem[e][slot], val - 16)
                    for key, (sem, val) in need.items():
                        if known.get(key, 0) >= val:
                            continue
                        known[key] = val
                        eng.wait_ge(sem, val)
                    ins = op["fn"](eng)
                    if op["dma"]:
                        ins.then_inc(dsem[e][op["dmainfo"][0]], 16)
                    elif op["marked"]:
                        ins.then_inc(csem[e], 1)
                if e == "sp":
                    need = {}
                    for dep in final_deps:
                        sem, key, val = resolve(dep)
                        if need.get(key, (None, 0))[1] < val:
                            need[key] = (sem, val)
                    for key, (sem, val) in need.items():
                        eng.wait_ge(sem, val)
            return body

        for e in ("sp", "pool", "act", "dve", "pe"):
            engobj[e](make_body(e))

    def close(self):
        self.stack.close()


def I(name, *args, **kw):
    f = lambda eng: getattr(eng, name)(*args, **kw)
    try:
        f.f32 = name in ("matmul", "transpose") and args[1].dtype == F32
    except Exception:
        f.f32 = False
    return f


D = 1024
LC = 256
NL = 2048
T = LC + NL
NTT = T // 128
FH = 2816
NHC = FH // 128
DEPTH = 2
NSEQ = 2
BLOCKS = [(0, 256)] + [(256 + 512 * i, 512) for i in range(4)]
ALPHA2 = 2.0
LN_EPS_S = 1e-5 / ALPHA2
RMS_EPS = 1e-6
NCOMBO = 21
WBYTES = 90112

C_NAQ, C_NAK, C_NAV = 0, 384, 768
C_GLQ, C_GLK, C_GLV, C_GLG, C_GLR = 1152, 1280, 1408, 1664, 1920
C_GAQ, C_GAK, C_GAV = 1952, 2336, 2464


def na_combos():
    combos = []
    idx = {}

    def get(kind, r, d):
        key = (kind, d)
        if key not in idx:
            idx[key] = len(combos)
            combos.append((r, d))
        return idx[key]

    slots = []
    for p in range(16):
        r = 2 * p
        if p == 0:
            s = [(j, get("e0", r, 2 * j - r)) for j in range(0, 4)]
        elif p == 1:
            s = [(j, get("e1", r, 2 * j - r)) for j in range(0, 4)]
        elif p == 14:
            s = [(j, get("e2", r, 2 * j - r)) for j in range(12, 16)]
        elif p == 15:
            s = [(j, get("e3", r, 2 * j - r)) for j in range(12, 16)]
        else:
            s = [(j, get("g", 8, 2 * j - r)) for j in range(p - 2, p + 3)]
        slots.append(s)
    assert len(combos) == NCOMBO
    return combos, slots


def na_index_tables():
    combos, _ = na_combos()
    cols = np.arange(64)
    col_start = np.clip(cols - 8, 0, 48)
    col_in = (cols[None, :] >= col_start[:, None]) & (cols[None, :] < col_start[:, None] + 16)
    valid = np.zeros((NCOMBO, 128, 128), bool)
    dri = np.zeros((NCOMBO, 128, 128), np.int64)
    dci = np.zeros((NCOMBO, 128, 128), np.int64)
    for ci, (r, d) in enumerate(combos):
        for ko in range(2):
            for qo in range(2):
                kr = r + d + ko
                qr = r + qo
                rs = min(max(qr - 4, 0), 24)
                ok_row = (rs <= kr < rs + 8)
                dr = kr - qr
                blk_valid = col_in.T & ok_row
                dc = np.clip(cols[:, None] - cols[None, :], -15, 15) + 15
                valid[ci, 64 * ko:64 * ko + 64, 64 * qo:64 * qo + 64] = blk_valid
                dri[ci, 64 * ko:64 * ko + 64, 64 * qo:64 * qo + 64] = min(max(dr + 7, 0), 14)
                dci[ci, 64 * ko:64 * ko + 64, 64 * qo:64 * qo + 64] = dc
    return valid, dri, dci


def build_nc(stage=99, nseq=NSEQ, dumps=None):
    nc = bass.Bass("TRN2", target_bir_lowering=False)
    S = Sched(nc)
    B = S.B
    dumps = dumps if dumps is not None else {}

    def din(name, shape, dt=F32):
        return nc.dram_tensor(name, list(shape), dt, kind="ExternalInput").ap()

    x_d = din("x", [NSEQ, NL, D])
    ctx_d = din("ctx", [NSEQ, LC, D])
    cT_d = din("cT", [128, 8, 3])
    wada_d = din("w_ada", [DEPTH, D, 6 * D])
    bada_d = din("b_adaT", [DEPTH, 128, 48])
    win_d = din("w_in", [DEPTH, D, 2592])
    wout_d = din("w_out", [DEPTH, D, D])
    wf1_d = din("w_ffn_in", [DEPTH, D, 2 * FH])
    wf2_d = din("w_ffn_out", [DEPTH, FH, D])
    lnT_d = din("lnT", [128, DEPTH, 4, 8])
    nab_d = din("nab", [DEPTH, 128, 6, NCOMBO * 128])
    wa2_d = din("wa2", [DEPTH, 2, 32, 128])
    nba_d = din("nbaT", [128, DEPTH, 2])
    glnw_d = din("glnw", [DEPTH, 1, 256])
    qkw_d = din("qkwT", [128, DEPTH, 2])
    cos_d = din("cosT", [128, NL])
    sin_d = din("sinT", [128, NL])
    cm_d = din("cmats", [128, 5, 128])
    tri_d = din("trim", [128, 2, 256])
    out_d = nc.dram_tensor("out", [NSEQ, NL, D], F32, kind="ExternalOutput").ap()
    xres = [nc.dram_tensor(f"xres{s}", [128, 8, T], F32).ap() for s in range(NSEQ)]
    dump_list = []

    ubuf = S.sbuf([128, 8, T], BF16, "ubuf")
    obuf = S.sbuf([128, 8, T], BF16, "obuf")
    wbuf = [S.sbuf([128, 8 * 1152], BF16, f"wbuf{i}") for i in range(2)]

    def wview(i, dt, k, c):
        if dt == F32:
            return wbuf[i][:].bitcast(F32)[:, 0:k * c].rearrange("p (k c) -> p k c", k=k, c=c)
        return wbuf[i][:, 0:k * c].rearrange("p (k c) -> p k c", k=k, c=c)

    Wr = S.sbuf([128, WBYTES // 4], F32, "Wr")
    identf = S.sbuf([128, 128], F32, "identf")
    onesD = S.sbuf([128, 128], F32, "onesD")
    identb = S.sbuf([128, 128], BF16, "identb")
    bones = S.sbuf([128, 128], BF16, "bones")
    rotm = S.sbuf([128, 128], BF16, "rotm")
    trim = S.sbuf([128, 2, 256], BF16, "trim")
    scT = S.sbuf([128, 8, 3], F32, "scT")
    modT = S.sbuf([128, DEPTH, 48, 3], F32, "modT")
    mder = S.sbuf([128, DEPTH, 6, 8, 3], F32, "mder")
    mder2 = S.sbuf([128, DEPTH, 2, 8, 3], F32, "mder2")
    lnT = S.sbuf([128, DEPTH, 4, 8], F32, "lnT")
    badaT = S.sbuf([128, DEPTH, 48], F32, "badaT")
    nbaT = S.sbuf([128, DEPTH, 2], F32, "nbaT")
    qkwT = S.sbuf([128, DEPTH, 2], F32, "qkwT")
    glnw = S.sbuf([128, DEPTH, 256], F32, "glnw")
    wa2 = S.sbuf([32, DEPTH, 2, 128], F32, "wa2")
    hmask = S.sbuf([128, 4], F32, "hmask")
    epsln = S.sbuf([128, 1], F32, "epsln")
    epsrms = S.sbuf([128, 1], F32, "epsrms")

    pf = [S.psum([128, 512], F32, f"pf{i}") for i in range(7)]
    pb = S.psum([128, 1024], BF16, "pb")
    pctr = [0]

    def nextp():
        i = pctr[0] % 7
        pctr[0] += 1
        return pf[i], B("pf", i)

    class Carver:
        def __init__(self):
            self.off = 0

        def __call__(self, dt, *dims):
            n = int(np.prod(dims))
            nbytes = n * (4 if dt == F32 else 2)
            nbytes = (nbytes + 63) // 64 * 64
            a = self.off // 4
            self.off += nbytes
            assert self.off <= WBYTES, ("scratch overflow", self.off)
            ap = Wr[:, a:a + nbytes // 4]
            if dt == BF16:
                ap = ap.bitcast(BF16)[:, 0:n]
            else:
                ap = ap[:, 0:n]
            if len(dims) == 2:
                ap = ap.rearrange("p (a b) -> p a b", a=dims[0], b=dims[1])
            elif len(dims) == 3:
                ap = ap.rearrange("p (a b c) -> p a b c", a=dims[0], b=dims[1], c=dims[2])
            return ap

    def dump(name, ap, shape, dt, rbuf):
        d = nc.dram_tensor("dbg_" + name, list(shape), dt, kind="ExternalOutput").ap()
        S.dma(I("dma_start", out=d, in_=ap), reads=[rbuf], writes=[B("dump", name)])
        dump_list.append(B("dump", name))
        dumps[name] = (shape, dt)

    def ld(dst, src, bname, q="sp"):
        S.dma(I("dma_start", out=dst, in_=src), writes=[B(bname)], q=q)

    ld(identf[:], cm_d[:, 0, :], "identf")
    ld(onesD[:], cm_d[:, 1, :], "onesD")
    ld(identb[:], cm_d[:, 0, :], "identb", q="pool")
    ld(bones[:], cm_d[:, 2, :], "bones", q="pool")
    ld(rotm[:], cm_d[:, 3, :], "rotm", q="pool")
    ld(trim[:], tri_d, "trim", q="pool")
    ld(hmask[:], cm_d[:, 4, 0:4], "hmask")
    ld(scT[:], cT_d, "scT")
    ld(lnT[:], lnT_d, "lnT")
    ld(badaT[:], bada_d.rearrange("l p j -> p l j"), "badaT")
    ld(nbaT[:], nba_d, "nbaT")
    ld(qkwT[:], qkw_d, "qkwT")
    for l in range(DEPTH):
        ld(glnw[:, l, :], glnw_d[l].broadcast_to([128, 256]), "glnw")
        ld(wa2[:, l, :, :], wa2_d[l].rearrange("e r c -> r e c"), "wa2")
    S.dve(I("tensor_scalar_mul", out=nbaT[:], in0=nbaT[:], scalar1=-1.0), reads=[B("nbaT")], writes=[B("nbaT")])
    S.pool(I("memset", epsln[:], LN_EPS_S), writes=[B("eps")])
    S.pool(I("memset", epsrms[:], RMS_EPS), writes=[B("eps")])
    S.act(I("activation", out=scT[:], in_=scT[:], func=AF.Silu), reads=[B("scT")], writes=[B("scT")])

    for l in range(DEPTH):
        pm, bpm = nextp()
        for js in range(12):
            wv = wview(js % 2, F32, 8, 512)
            S.dma(I("dma_start",
                out=wv, in_=wada_d[l].rearrange("(k p) c -> p k c", p=128)[:, :, 512 * js:512 * js + 512]),
                writes=[B("wbuf", js % 2)])
            for jj in range(4):
                j = 4 * js + jj
                for k in range(8):
                    S.pe(I("matmul",
                        pm[:, 3 * j:3 * j + 3], wv[:, k, 128 * jj:128 * jj + 128], scT[:, k, :],
                        start=(k == 0), stop=(k == 7)),
                        reads=[B("wbuf", js % 2), B("scT")], writes=[bpm])
        S.dve(I("tensor_tensor",
            out=modT[:, l, :, :], in0=pm[:, 0:144].rearrange("p (j b) -> p j b", j=48, b=3),
            in1=badaT[:, l, :].unsqueeze(2).broadcast_to([128, 48, 3]), op=ALU.add),
            reads=[bpm, B("badaT")], writes=[B("modT")])
    inv_a = float(1.0 / np.sqrt(ALPHA2))
    for l in range(DEPTH):
        mT = modT[:, l, :, :]
        rd = [B("modT"), B("lnT")]
        wr = [B("mder")]
        S.dve(I("tensor_scalar_add", out=mder[:, l, 0, :, :], in0=mT[:, 8:16, :], scalar1=1.0), reads=rd, writes=wr)
        S.dve(I("tensor_scalar_mul", out=mder[:, l, 1, :, :], in0=mT[:, 16:24, :], scalar1=inv_a), reads=rd, writes=wr)
        S.dve(I("tensor_scalar_add", out=mder[:, l, 2, :, :], in0=mT[:, 32:40, :], scalar1=1.0), reads=rd, writes=wr)
        S.dve(I("tensor_scalar_mul", out=mder[:, l, 3, :, :], in0=mT[:, 40:48, :], scalar1=inv_a), reads=rd, writes=wr)
        S.dve(I("tensor_tensor", out=mder[:, l, 4, :, :], in0=mder[:, l, 2, :, :],
                                             in1=lnT[:, l, 0, :].unsqueeze(2).broadcast_to([128, 8, 3]), op=ALU.mult), reads=rd + wr, writes=wr)
        S.dve(I("tensor_tensor", out=mder[:, l, 5, :, :], in0=mder[:, l, 2, :, :],
                                             in1=lnT[:, l, 1, :].unsqueeze(2).broadcast_to([128, 8, 3]), op=ALU.mult), reads=rd + wr, writes=wr)
        S.dve(I("tensor_tensor", out=mder[:, l, 5, :, :], in0=mder[:, l, 5, :, :], in1=mT[:, 24:32, :], op=ALU.add), reads=rd + wr, writes=wr)
    for l in range(DEPTH - 1):
        rd = [B("modT"), B("lnT"), B("mder")]
        wr = [B("mder2")]
        S.dve(I("tensor_tensor", out=mder2[:, l, 0, :, :], in0=mder[:, l + 1, 0, :, :],
                                             in1=lnT[:, l, 2, :].unsqueeze(2).broadcast_to([128, 8, 3]), op=ALU.mult), reads=rd, writes=wr)
        S.dve(I("tensor_tensor", out=mder2[:, l, 1, :, :], in0=mder[:, l + 1, 0, :, :],
                                             in1=lnT[:, l, 3, :].unsqueeze(2).broadcast_to([128, 8, 3]), op=ALU.mult), reads=rd + wr, writes=wr)
        S.dve(I("tensor_tensor", out=mder2[:, l, 1, :, :], in0=mder2[:, l, 1, :, :], in1=modT[:, l + 1, 0:8, :], op=ALU.add), reads=rd + wr, writes=wr)
    MD = [B("mder"), B("mder2"), B("modT"), B("lnT")]
    if stage < 99:
        S.pool(I("memset", obuf[:], 0.0), writes=[B("obuf", tt) for tt in range(NTT)])
    if stage == 0:
        dump("modT", modT[:], [128, DEPTH, 48, 3], F32, B("modT"))

    def msel(s, t0):
        return 2 if t0 < LC else s

    gctr = {}

    def bank(group, banks):
        c = gctr.get(group, 0)
        gctr[group] = c + 1
        i = banks[c % len(banks)]
        return pf[i], B("pf", i)

    def tiles_of(t0, n):
        return range(t0 // 128, (t0 + n) // 128)

    def ub(t0, n):
        return [B("ubuf", tt) for tt in tiles_of(t0, n)]

    win_v = [win_d[l].rearrange("(k p) c -> p k c", p=128) for l in range(DEPTH)]
    combos, na_slots = na_combos()
    PBB = B("pb", 0)

    def load_x(s):
        S.phase = "load_x"
        S.barrier()
        cv = Carver()
        xin = [cv(F32, 1024) for _ in range(2)]
        xst = [cv(F32, 8, 128) for _ in range(2)]
        for tt in range(NTT):
            i2 = tt % 2
            src = ctx_d[s, 128 * tt:128 * tt + 128, :] if tt < 2 else x_d[s, 128 * (tt - 2):128 * (tt - 2) + 128, :]
            S.dma(I("dma_start", out=xin[i2], in_=src), writes=[B("xin", i2)])
            mi = msel(s, 128 * tt)
            for half in range(2):
                pt, bpt = bank("ld", [0, 1, 2, 3])
                for kk in range(4):
                    k = 4 * half + kk
                    S.pe(I("transpose", pt[:, 128 * kk:128 * kk + 128], xin[i2][:, 128 * k:128 * k + 128], identf[:]),
                         reads=[B("xin", i2), B("identf")], writes=[bpt])
                S.dve(I("tensor_copy",
                    out=xst[i2][:, 4 * half:4 * half + 4, :], in_=pt[:].rearrange("p (k t) -> p k t", k=4, t=128)),
                    reads=[bpt], writes=[B("xst", i2)])
                for kk in range(4):
                    k = 4 * half + kk
                    S.act(I("activation",
                        out=ubuf[:, k, 128 * tt:128 * tt + 128], in_=pt[:, 128 * kk:128 * kk + 128], func=AF.Identity,
                        scale=mder[:, 0, 0, k, mi:mi + 1], bias=modT[:, 0, k, mi:mi + 1]),
                        reads=[bpt] + MD, writes=[B("ubuf", tt)])
            S.dma(I("dma_start", out=xres[s][:, :, 128 * tt:128 * tt + 128], in_=xst[i2]),
                  reads=[B("xst", i2)], writes=[B("xres", s, tt)])

    def na_phase(s, l):
        S.phase = "na_phase" + str(l)
        S.barrier()
        cv = Carver()
        naq = cv(BF16, 3, T)
        nak = cv(BF16, 3, T)
        nav = cv(BF16, NTT, 6 * 65)
        natab = cv(BF16, 6, NCOMBO * 128)
        naP = [cv(BF16, 7, 128) for _ in range(3)]
        otok = [cv(BF16, 384) for _ in range(2)]
        rden = [cv(F32, 8) for _ in range(2)]
        nav4 = nav.rearrange("p t (h c) -> p t h c", h=6, c=65)
        wv = wview(0, BF16, 8, 1152)
        S.dma(I("dma_start", out=wv, in_=win_v[l][:, :, 0:1152]), writes=[B("wbuf", 0)], q="pool")
        S.dma(I("dma_start", out=natab, in_=nab_d[l]), writes=[B("natab")], q="pool")
        S.pool(I("memset", nav4[:, :, :, 64:65], 1.0), writes=[B("navones")])
        for (t0, n) in BLOCKS:
            for j in range(6):
                if j < 3 and t0 < LC and l == DEPTH - 1:
                    continue
                ps, bps = bank("proj", [0, 1, 2, 3, 4, 5, 6])
                for k in range(8):
                    S.pe(I("matmul", ps[:, 0:n], wv[:, k, 128 * j:128 * j + 128], ubuf[:, k, t0:t0 + n],
                                                                     start=(k == 0), stop=(k == 7)),
                         reads=[B("wbuf", 0)] + ub(t0, n), writes=[bps])
                wr = [B("naqk", tt) for tt in tiles_of(t0, n)]
                if j < 3:
                    S.act(I("activation", out=naq[:, j, t0:t0 + n], in_=ps[:, 0:n], func=AF.Identity, scale=0.125),
                          reads=[bps], writes=wr)
                else:
                    S.dve(I("tensor_copy", out=nak[:, j - 3, t0:t0 + n], in_=ps[:, 0:n]), reads=[bps], writes=wr)
        for tt in range(NTT):
            ps, bps = bank("proj", [0, 1, 2, 3, 4, 5, 6])
            for k in range(8):
                S.pe(I("matmul", ps[:, 0:384], ubuf[:, k, 128 * tt:128 * tt + 128], wv[:, k, 768:1152],
                                                         start=(k == 0), stop=(k == 7)),
                     reads=[B("wbuf", 0), B("ubuf", tt)], writes=[bps])
            S.dve(I("tensor_copy", out=nav4[:, tt, :, 0:64], in_=ps[:, 0:384].rearrange("p (h c) -> p h c", h=6, c=64)),
                  reads=[bps], writes=[B("nav", tt)])
        qtiles = []
        if l < DEPTH - 1:
            qtiles += [(0, [(0, None), (1, None)]), (1, [(0, None), (1, None)])]
        for p in range(16):
            qtiles.append((2 + p, [(2 + j, ci) for (j, ci) in na_slots[p]] + [(0, None), (1, None)]))
        naP3 = naP
        units = [(qi, h) for qi in range(len(qtiles)) for h in range(6)]
        pos_ = {}
        ust = {}

        def na_st(u):
            qi, h = units[u]
            tq, slots = qtiles[qi]
            if h == 0:
                pos_[qi] = bank("napo", [4, 5])
            ns = len(slots)
            c = h // 2
            p0 = 64 * (h % 2)
            pa, bpa = bank("napa", [0, 1, 2, 3])
            pa2, bpa2 = bank("napa", [0, 1, 2, 3])
            P = naP3[u % 3]
            bP = B("naP", u % 3)
            ust[u] = (P, bP)
            for si, (tk, ci) in enumerate(slots):
                bk, bbk = (pa, bpa) if si < 4 else (pa2, bpa2)
                col = 128 * (si % 4)
                S.pe(I("matmul", bk[:, col:col + 128], nak[p0:p0 + 64, c, 128 * tk:128 * tk + 128], naq[p0:p0 + 64, c, 128 * tq:128 * tq + 128],
                       start=True, stop=(ci is None)),
                     reads=[B("naqk", tk), B("naqk", tq)], writes=[bbk])
                if ci is not None:
                    S.pe(I("matmul", bk[:, col:col + 128], identb[:], natab[:, h, 128 * ci:128 * ci + 128], start=False, stop=True),
                         reads=[B("natab"), B("identb")], writes=[bbk])
            n1 = min(ns, 4)
            S.act(I("activation", out=P[:, 0:n1, :], in_=pa[:, 0:128 * n1].rearrange("p (a b) -> p a b", a=n1, b=128), func=AF.Exp),
                  reads=[bpa], writes=[bP])
            if ns > 4:
                n2 = ns - 4
                S.act(I("activation", out=P[:, 4:4 + n2, :], in_=pa2[:, 0:128 * n2].rearrange("p (a b) -> p a b", a=n2, b=128), func=AF.Exp),
                      reads=[bpa2], writes=[bP])

        def na_pv(u):
            qi, h = units[u]
            tq, slots = qtiles[qi]
            ns = len(slots)
            po, bpo = pos_[qi]
            P, bP = ust.pop(u)
            for si, (tk, ci) in enumerate(slots):
                S.pe(I("matmul", po[:, 65 * h:65 * h + 65], P[:, si, :], nav[:, tk, 65 * h:65 * h + 65], start=(si == 0), stop=(si == ns - 1)),
                     reads=[bP, B("nav", tk), B("navones")], writes=[bpo])
            if h < 5:
                return
            o2 = qi % 2
            po3 = po[:, 0:390].rearrange("p (h c) -> p h c", h=6, c=65)
            S.dve(I("reciprocal", out=rden[o2][:, 0:6], in_=po3[:, :, 64]), reads=[bpo], writes=[B("rden", o2)])
            S.dve(I("tensor_tensor", out=otok[o2].rearrange("p (h c) -> p h c", h=6, c=64), in0=po3[:, :, 0:64],
                    in1=rden[o2][:, 0:6].unsqueeze(2).broadcast_to([128, 6, 64]), op=ALU.mult),
                  reads=[bpo, B("rden", o2)], writes=[B("otok", o2)])
            for c3 in range(3):
                S.pe(I("transpose", pb[:, 128 * c3:128 * c3 + 128], otok[o2][:, 128 * c3:128 * c3 + 128], identb[:]),
                     reads=[B("otok", o2), B("identb")], writes=[PBB])
            S.act(I("copy", out=obuf[:, 0:3, 128 * tq:128 * tq + 128], in_=pb[:, 0:384].rearrange("p (a b) -> p a b", a=3, b=128)),
                  reads=[PBB], writes=[B("obuf", tq)])

        LA = 1
        for u in range(len(units) + LA):
            if u < len(units):
                na_st(u)
            if u >= LA:
                na_pv(u - LA)

    def gqa_phase(s, l):
        S.phase = "gqa_phase" + str(l)
        S.barrier()
        cv = Carver()
        gaq = cv(BF16, 3, T)
        gak = cv(BF16, 2, T)
        gav = cv(BF16, NTT, 130)
        cosv = cv(F32, NL)
        sinv = cv(F32, NL)
        sq = cv(BF16, 512)
        sd = cv(F32, 512)
        rstd = cv(F32, 512)
        qn = cv(F32, 512)
        qnb = cv(BF16, 512)
        t1 = cv(F32, 512)
        t2 = cv(F32, 512)
        gaP = [cv(BF16, 512) for _ in range(6)]
        otok = cv(BF16, 4, 384)
        rden = [cv(F32, 8) for _ in range(2)]
        gav4 = gav.rearrange("p t (g c) -> p t g c", g=2, c=65)
        wv = wview(1, BF16, 8, 768)
        wb = [B("wbuf", 1)]
        S.dma(I("dma_start", out=wv[:, :, 0:384], in_=win_v[l][:, :, C_GAQ:C_GAQ + 384]), writes=wb, q="pool")
        for g in range(2):
            for d in range(2):
                S.dma(I("dma_start", out=wv[:, :, 384 + 128 * g + 64 * d:384 + 128 * g + 64 * d + 64],
                                                      in_=win_v[l][:, :, C_GAK + 64 * g:C_GAK + 64 * g + 64]), writes=wb, q="pool")
        S.dma(I("dma_start", out=wv[:, :, 640:768], in_=win_v[l][:, :, C_GAV:C_GAV + 128]), writes=wb, q="pool")
        S.dma(I("dma_start", out=cosv, in_=cos_d), writes=[B("cos")])
        S.dma(I("dma_start", out=sinv, in_=sin_d), writes=[B("cos")])
        S.pool(I("memset", gav4[:, :, :, 64:65], 1.0), writes=[B("gavones")])
        TB = lambda n: B("gatmp", n)
        for (t0, n) in BLOCKS:
            for j in range(5):
                if j < 3 and t0 < LC and l == DEPTH - 1:
                    continue
                ps, bps = bank("proj", [0, 1, 2, 3])
                for k in range(8):
                    S.pe(I("matmul", ps[:, 0:n], wv[:, k, 128 * j:128 * j + 128], ubuf[:, k, t0:t0 + n],
                                                                     start=(k == 0), stop=(k == 7)),
                         reads=wb + ub(t0, n), writes=[bps])
                S.act(I("activation", out=sq[:, 0:n], in_=ps[:, 0:n], func=AF.Square), reads=[bps], writes=[TB("sq")])
                pm, bpm = bank("gams", [4, 5])
                S.pe(I("matmul", pm[:, 0:n], bones[:], sq[:, 0:n], start=True, stop=True), reads=[TB("sq"), B("bones")], writes=[bpm])
                S.act(I("activation", out=sd[:, 0:n], in_=pm[:, 0:n], func=AF.Sqrt, bias=epsrms[:, 0:1], scale=1.0),
                      reads=[bpm, B("eps")], writes=[TB("sd")])
                S.dve(I("reciprocal", out=rstd[:, 0:n], in_=sd[:, 0:n]), reads=[TB("sd")], writes=[TB("rstd")])
                wi = 0 if j < 3 else 1
                S.dve(I("scalar_tensor_tensor", out=qn[:, 0:n], in0=ps[:, 0:n], scalar=qkwT[:, l, wi:wi + 1], in1=rstd[:, 0:n],
                                                                      op0=ALU.mult, op1=ALU.mult),
                      reads=[bps, TB("rstd"), B("qkwT")], writes=[TB("qn")])
                dst = gaq[:, j, t0:t0 + n] if j < 3 else gak[:, j - 3, t0:t0 + n]
                wr = [B("gaqk", tt) for tt in tiles_of(t0, n)]
                if t0 >= LC:
                    S.pool(I("tensor_copy", out=qnb[:, 0:n], in_=qn[:, 0:n]), reads=[TB("qn")], writes=[TB("qnb")])
                    pr, bpr = bank("gams", [4, 5])
                    S.pe(I("matmul", pr[:, 0:n], rotm[:], qnb[:, 0:n], start=True, stop=True), reads=[TB("qnb"), B("rotm")], writes=[bpr])
                    S.dve(I("tensor_tensor", out=t1[:, 0:n], in0=qn[:, 0:n], in1=cosv[:, t0 - LC:t0 - LC + n], op=ALU.mult),
                          reads=[TB("qn"), B("cos")], writes=[TB("t1")])
                    S.dve(I("tensor_tensor", out=t2[:, 0:n], in0=pr[:, 0:n], in1=sinv[:, t0 - LC:t0 - LC + n], op=ALU.mult),
                          reads=[bpr, B("cos")], writes=[TB("t2")])
                    S.pool(I("tensor_tensor", out=dst, in0=t1[:, 0:n], in1=t2[:, 0:n], op=ALU.add),
                           reads=[TB("t1"), TB("t2")], writes=wr)
                else:
                    S.pool(I("tensor_copy", out=dst, in_=qn[:, 0:n]), reads=[TB("qn")], writes=wr)
        for tt in range(NTT):
            ps, bps = bank("proj", [0, 1, 2, 3])
            for k in range(8):
                S.pe(I("matmul", ps[:, 0:128], ubuf[:, k, 128 * tt:128 * tt + 128], wv[:, k, 640:768],
                                                         start=(k == 0), stop=(k == 7)),
                     reads=wb + [B("ubuf", tt)], writes=[bps])
            S.dve(I("tensor_copy", out=gav4[:, tt, :, 0:64], in_=ps[:, 0:128].rearrange("p (g c) -> p g c", g=2, c=64)),
                  reads=[bps], writes=[B("gav", tt)])
        qblocks = []
        if l < DEPTH - 1:
            qblocks.append((0, 256, [0, 1]))
        for i in range(4):
            qblocks.append((256 + 512 * i, 512, list(range(NTT))))
        pi = 0
        for (t0, n, ktiles) in qblocks:
            nsub = n // 128
            nk = len(ktiles)
            for h in range(6):
                g = h // 3
                c = h // 2
                p0 = 64 * (h % 2)
                po, bpo = bank("gapo", [5, 6])
                S.dve(I("memset", po[:, 0:65 * nsub], 0.0), writes=[bpo])
                Ps = {}

                def st(ki):
                    nonlocal pi
                    tk = ktiles[ki]
                    pa, bpa = bank("gapa", [0, 1, 2, 3, 4])
                    P = gaP[pi % 6]
                    bP = B("gaP", pi % 6)
                    pi += 1
                    Ps[ki] = (P, bP)
                    S.pe(I("matmul", pa[:, 0:n], gak[p0:p0 + 64, g, 128 * tk:128 * tk + 128], gaq[p0:p0 + 64, c, t0:t0 + n],
                                                        start=True, stop=True),
                         reads=[B("gaqk", tk)] + [B("gaqk", tt) for tt in tiles_of(t0, n)], writes=[bpa])
                    S.act(I("activation", out=P[:, 0:n], in_=pa[:, 0:n], func=AF.Exp, scale=0.125), reads=[bpa], writes=[bP])

                def pv(ki):
                    tk = ktiles[ki]
                    P, bP = Ps.pop(ki)
                    for qs in range(nsub):
                        S.pe(I("matmul", po[:, 65 * qs:65 * qs + 65], P[:, 128 * qs:128 * qs + 128], gav[:, tk, 65 * g:65 * g + 65],
                                                                        start=False, stop=(ki == nk - 1), skip_group_check=True),
                             reads=[bP, B("gav", tk), B("gavones")], writes=[bpo])

                GLA_ = 3
                for step in range(nk + GLA_):
                    if step < nk:
                        st(step)
                    if step >= GLA_:
                        pv(step - GLA_)
                o2 = h % 2
                po3 = po[:, 0:65 * nsub].rearrange("p (q c) -> p q c", q=nsub, c=65)
                S.dve(I("reciprocal", out=rden[o2][:, 0:nsub], in_=po3[:, :, 64]), reads=[bpo], writes=[B("rden", o2)])
                S.dve(I("tensor_tensor", out=otok[:, 0:nsub, 64 * h:64 * h + 64], in0=po3[:, :, 0:64],
                                                                   in1=rden[o2][:, 0:nsub].unsqueeze(2).broadcast_to([128, nsub, 64]), op=ALU.mult),
                      reads=[bpo, B("rden", o2)], writes=[B("gaotok")])
            for qs in range(nsub):
                tq = t0 // 128 + qs
                for c3 in range(3):
                    S.pe(I("transpose", pb[:, 128 * c3:128 * c3 + 128], otok[:, qs, 128 * c3:128 * c3 + 128], identb[:]),
                         reads=[B("gaotok"), B("identb")], writes=[PBB])
                S.act(I("copy", out=obuf[:, 5:8, 128 * tq:128 * tq + 128], in_=pb[:, 0:384].rearrange("p (a b) -> p a b", a=3, b=128)),
                      reads=[PBB], writes=[B("obuf", tq)])

    def gla_phase(s, l):
        S.phase = "gla_phase" + str(l)
        S.barrier()
        cv = Carver()
        glv = cv(BF16, NTT, 256)
        glgw = cv(BF16, NTT, 256)
        qt = [cv(BF16, T) for _ in range(2)]
        kt = [cv(BF16, T) for _ in range(2)]
        ktok = cv(BF16, NTT, 128)
        Dall = cv(F32, 2, 36)
        S32 = cv(F32, 37, 64)
        Sbf = [cv(BF16, 36, 64) for _ in range(2)]
        qf = cv(F32, 512)
        kf = cv(F32, 512)
        lrf = cv(F32, 512)
        smask = cv(F32, 512)
        tA = cv(F32, 512)
        tB_ = cv(F32, 512)
        tC = cv(F32, 512)
        tD = cv(F32, 512)
        sg = cv(F32, 256)
        Abf = [cv(BF16, 2, 256) for _ in range(2)]
        qmb = [cv(BF16, 2, 4, 128) for _ in range(2)]
        osq = cv(F32, 256)
        on = cv(F32, 256)
        otok = [cv(BF16, 256) for _ in range(2)]
        st4 = [cv(F32, 8) for _ in range(2)]
        TB = lambda n: B("gltmp", n)
        wv = wview(0, BF16, 8, 800)
        wb = [B("wbuf", 0)]
        S.dma(I("dma_start", out=wv, in_=win_v[l][:, :, C_GLQ:C_GLQ + 800]), writes=wb, q="pool")
        S.pool(I("memset", smask, 1.0), writes=[TB("smask")])
        S.pool(I("memset", smask.rearrange("p (c t) -> p c t", c=8, t=64)[:, :, 0:1], 0.0), writes=[TB("smask")])
        for tt in range(NTT):
            ps, bps = bank("proj", [0, 1, 2, 3])
            for k in range(8):
                S.pe(I("matmul", ps[:, 0:512], ubuf[:, k, 128 * tt:128 * tt + 128], wv[:, k, 256:768],
                                                         start=(k == 0), stop=(k == 7)),
                     reads=wb + [B("ubuf", tt)], writes=[bps])
            S.dve(I("tensor_copy", out=glv[:, tt, :], in_=ps[:, 0:256]), reads=[bps], writes=[B("glv", tt)])
            S.act(I("activation", out=sg, in_=ps[:, 256:512], func=AF.Silu), reads=[bps], writes=[TB("sg")])
            S.pool(I("tensor_tensor", out=glgw[:, tt, :], in0=sg, in1=glnw[:, l, :], op=ALU.mult),
                   reads=[TB("sg"), B("glnw")], writes=[B("glgw", tt)])
        import os as _os
        _gs = int(_os.environ.get('GLASTOP', '9'))
        _sub = int(_os.environ.get('GLASUB', '9'))
        if _gs <= 0:
            return
        for (t0, n) in BLOCKS:
            nch = n // 64
            c0 = t0 // 64
            pq, bpq = bank("proj", [0, 1, 2, 3])
            pk, bpk = bank("proj", [0, 1, 2, 3])
            pl, bpl = bank("proj", [0, 1, 2, 3])
            for (pp, col, M) in ((pq, 0, 128), (pk, 128, 128), (pl, 768, 32)):
                bb = {id(pq): bpq, id(pk): bpk, id(pl): bpl}[id(pp)]
                for k in range(8):
                    S.pe(I("matmul", pp[0:M, 0:n], wv[:, k, col:col + M], ubuf[:, k, t0:t0 + n],
                                                                              start=(k == 0), stop=(k == 7)),
                         reads=wb + ub(t0, n), writes=[bb])
            S.act(I("activation", out=qf[:, 0:n], in_=pq[:, 0:n], func=AF.Identity, scale=float(32 ** -0.5)), reads=[bpq], writes=[TB("qf")])
            S.dve(I("tensor_copy", out=kf[:, 0:n], in_=pk[:, 0:n]), reads=[bpk], writes=[TB("kf")])
            S.dve(I("tensor_copy", out=lrf[0:32, 0:n], in_=pl[0:32, 0:n]), reads=[bpl], writes=[TB("lrf")])
            for ed in range(2):
                pz, bpz = bank("glz", [4, 5])
                S.pe(I("matmul", pz[:, 0:n], wa2[:, l, ed, :], lrf[0:32, 0:n], start=True, stop=True),
                     reads=[TB("lrf"), B("wa2")], writes=[bpz])
                S.act(I("activation", out=tA[:, 0:n], in_=pz[:, 0:n], func=AF.Exp, scale=-1.0, bias=nbaT[:, l, ed:ed + 1]),
                      reads=[bpz, B("nbaT")], writes=[TB("tA")])
                S.act(I("activation", out=tB_[:, 0:n], in_=tA[:, 0:n], func=AF.Ln, bias=1.0, scale=1.0), reads=[TB("tA")], writes=[TB("tB")])
                S.dve(I("tensor_tensor_scan", out=tC[:, 0:n], data0=smask[:, 0:n], data1=tB_[:, 0:n], initial=0.0, op0=ALU.mult, op1=ALU.add),
                      reads=[TB("smask"), TB("tB")], writes=[TB("tC")])
                cum = tC
                bcum = TB("tC")
                if ed == 1:
                    tC3 = tC[:, 0:n].rearrange("p (c t) -> p c t", c=nch, t=64)
                    S.dve(I("tensor_tensor", out=tD[:, 0:n].rearrange("p (c t) -> p c t", c=nch, t=64),
                                                                         in0=tC3[:, :, 63:64].broadcast_to([128, nch, 64]), in1=tC3, op=ALU.subtract),
                          reads=[TB("tC")], writes=[TB("tD")])
                    S.dve(I("tensor_tensor", out=tD[:, 0:n], in0=tD[:, 0:n], in1=tB_[:, 0:n], op=ALU.add), reads=[TB("tD"), TB("tB")], writes=[TB("tD")])
                    cum = tD
                    bcum = TB("tD")
                S.act(I("activation", out=tA[:, 0:n], in_=cum[:, 0:n], func=AF.Exp, scale=-1.0 / 16.0), reads=[bcum], writes=[TB("tA")])
                S.act(I("activation", out=tB_[:, 0:n], in_=cum[:, 0:n], func=AF.Exp, scale=1.0 / 16.0), reads=[bcum, TB("tA")], writes=[TB("tB")])
                dcol = 63 if ed == 0 else 0
                S.pool(I("tensor_copy",
                    out=Dall[:, ed, c0:c0 + nch], in_=tA[:, 0:n].rearrange("p (c t) -> p c t", c=nch, t=64)[:, :, dcol]),
                    reads=[TB("tA")], writes=[B("Dall")])
                wr = [B("glqk", ed, tt) for tt in tiles_of(t0, n)]
                S.dve(I("tensor_tensor", out=qt[ed][:, t0:t0 + n], in0=qf[:, 0:n], in1=tA[:, 0:n], op=ALU.mult),
                      reads=[TB("qf"), TB("tA")], writes=wr)
                S.pool(I("tensor_tensor", out=kt[ed][:, t0:t0 + n], in0=kf[:, 0:n], in1=tB_[:, 0:n], op=ALU.mult),
                       reads=[TB("kf"), TB("tB")], writes=wr)
        if _gs <= 1:
            return
        ords = [list(range(36)), [3, 2, 1, 0] + list(range(35, 3, -1))]
        poss = []
        for ed in range(2):
            pos = [0] * 36
            for j, ci in enumerate(ords[ed]):
                pos[ci] = j
            poss.append(pos)
        for ed in range(2):
            pos = poss[ed]
            for tt in range(NTT):
                S.pe(I("transpose", pb[:, 128 * (tt % 4):128 * (tt % 4) + 128], kt[ed][:, 128 * tt:128 * tt + 128], identb[:]),
                     reads=[B("glqk", ed, tt), B("identb")], writes=[PBB])
                if tt % 4 == 3 or tt == NTT - 1:
                    n4 = tt % 4 + 1
                    ta = tt - n4 + 1
                    S.act(I("copy", out=ktok[:, ta:ta + n4, :], in_=pb[:, 0:128 * n4].rearrange("p (a b) -> p a b", a=n4, b=128)),
                          reads=[PBB], writes=[B("ktok")])
            if _sub <= 1:
                continue
            S.pool(I("memset", S32[:, 0, :], 0.0), writes=[B("S32")])
            for half in range(2):
              for grp in range(0, 18, 8):
                pw, bpw = bank("glw", [0, 1, 2, 3])
                cis = [2 * t_ + half for t_ in range(grp, min(grp + 8, 18))]
                for gi, ci in enumerate(cis):
                    tt = ci // 2
                    for h in range(4):
                        S.pe(I("matmul",
                            pw[32 * h:32 * h + 32, 64 * gi:64 * gi + 64], ktok[64 * half:64 * half + 64, tt, 32 * h:32 * h + 32],
                            glv[64 * half:64 * half + 64, tt, 64 * h:64 * h + 64], start=True, stop=True, tile_position=(64 * half, 32 * h)),
                            reads=[B("ktok"), B("glv", tt)], writes=[bpw])
                for gi, ci in enumerate(cis):
                    S.act(I("activation",
                        out=S32[:, 1 + pos[ci], :], in_=pw[:, 64 * gi:64 * gi + 64], func=AF.Identity, scale=Dall[:, ed, ci:ci + 1]),
                        reads=[bpw, B("Dall")], writes=[B("S32")])
            if _sub <= 2:
                continue
            for j in range(36):
                ci = ords[ed][j]
                S.dve(I("scalar_tensor_tensor", out=S32[:, j + 1, :], in0=S32[:, j, :], scalar=Dall[:, ed, ci:ci + 1], in1=S32[:, j + 1, :],
                                                                      op0=ALU.mult, op1=ALU.add),
                      reads=[B("S32"), B("Dall")], writes=[B("S32")])
            if _sub <= 3:
                continue
            S.dve(I("tensor_copy", out=Sbf[ed], in_=S32[:, 0:36, :]), reads=[B("S32")], writes=[B("Sbf", ed)])
        if _gs <= 2:
            return
        tiles_out = range(NTT) if l < DEPTH - 1 else range(2, NTT)
        for oi, tt in enumerate(tiles_out):
            pa, bpa = bank("glpa", [0, 1, 2, 3])
            A = Abf[oi % 2]
            bA = B("Abf", oi % 2)
            qm = qmb[oi % 2]
            bqm = B("qm", oi % 2)
            for ed in range(2):
                for h in range(4):
                    eng = S.pool if (h % 2 == 0) else S.dve
                    eng(I("tensor_scalar_mul", out=qm[:, ed, h, :], in0=qt[ed][:, 128 * tt:128 * tt + 128], scalar1=hmask[:, h:h + 1]),
                        reads=[B("glqk", ed, tt), B("hmask")], writes=[bqm])
            for half in range(2):
                cols = slice(128 * tt + 64 * half, 128 * tt + 64 * half + 64)
                for ed in range(2):
                    for h in range(4):
                        S.pe(I("matmul",
                            pa[64 * half:64 * half + 64, 256 * ed + 64 * h:256 * ed + 64 * h + 64],
                            kt[ed][:, cols], qm[:, ed, h, 64 * half:64 * half + 64], start=True, stop=True,
                            tile_position=(0, 64 * half)),
                            reads=[B("glqk", ed, tt), bqm], writes=[bpa])
            S.dve(I("tensor_tensor", out=A, in0=pa[:, 0:512].rearrange("p (a b) -> p a b", a=2, b=256), in1=trim[:], op=ALU.mult),
                  reads=[bpa, B("trim")], writes=[bA])
            po, bpo = bank("glpo", [4, 5])
            for half in range(2):
                ci = 2 * tt + half
                for h in range(4):
                    for ed in range(2):
                        S.pe(I("matmul",
                            po[64 * half:64 * half + 64, 64 * h:64 * h + 64], A[64 * half:64 * half + 64, ed, 64 * h:64 * h + 64],
                            glv[64 * half:64 * half + 64, tt, 64 * h:64 * h + 64], start=(ed == 0), stop=False,
                            tile_position=(64 * half, 64 * half)),
                            reads=[bA, B("glv", tt)], writes=[bpo])
                    for ed in range(2):
                        pj = poss[ed][ci]
                        S.pe(I("matmul",
                            po[64 * half:64 * half + 64, 64 * h:64 * h + 64], qm[:, ed, h, 64 * half:64 * half + 64],
                            Sbf[ed][:, pj, :], start=False, stop=(ed == 1),
                            tile_position=(0, 64 * half)),
                            reads=[bqm, B("Sbf", ed)], writes=[bpo])
            o2 = oi % 2
            S.act(I("activation", out=osq, in_=po[:, 0:256], func=AF.Square), reads=[bpo], writes=[TB("osq")])
            S.dve(I("reduce_sum", out=st4[o2][:, 0:4], in_=osq.rearrange("p (h c) -> p h c", h=4, c=64), axis=AX.X),
                  reads=[TB("osq")], writes=[B("st4", o2)])
            S.dve(I("tensor_scalar", out=st4[o2][:, 0:4], in0=st4[o2][:, 0:4], scalar1=1.0 / 64.0, scalar2=RMS_EPS, op0=ALU.mult, op1=ALU.add),
                  reads=[B("st4", o2)], writes=[B("st4", o2)])
            S.act(I("activation", out=st4[o2][:, 4:8], in_=st4[o2][:, 0:4], func=AF.Sqrt), reads=[B("st4", o2)], writes=[B("st4", o2)])
            S.dve(I("reciprocal", out=st4[o2][:, 0:4], in_=st4[o2][:, 4:8]), reads=[B("st4", o2)], writes=[B("st4", o2)])
            S.dve(I("tensor_tensor", out=on.rearrange("p (h c) -> p h c", h=4, c=64), in0=po[:, 0:256].rearrange("p (h c) -> p h c", h=4, c=64),
                                                        in1=st4[o2][:, 0:4].unsqueeze(2).broadcast_to([128, 4, 64]), op=ALU.mult),
                  reads=[bpo, B("st4", o2)], writes=[TB("on")])
            S.pool(I("tensor_tensor", out=otok[o2], in0=on, in1=glgw[:, tt, :], op=ALU.mult),
                   reads=[TB("on"), B("glgw", tt)], writes=[B("glotok", o2)])
            for c2 in range(2):
                S.pe(I("transpose", pb[:, 512 + 128 * c2:512 + 128 * c2 + 128], otok[o2][:, 128 * c2:128 * c2 + 128], identb[:]),
                     reads=[B("glotok", o2), B("identb")], writes=[PBB])
            S.act(I("copy", out=obuf[:, 3:5, 128 * tt:128 * tt + 128], in_=pb[:, 512:768].rearrange("p (a b) -> p a b", a=2, b=128)),
                  reads=[PBB], writes=[B("obuf", tt)])

    def layer_norm_block(tbuf, btb, n, mi, gam_beta, outs):
        cvs = self_cv[0]
        pmean, bpmean = bank("lnm", [4, 5])
        pmsq, bpmsq = bank("lnm", [4, 5])
        for oc in range(8):
            S.pe(I("matmul", pmean[:, 0:n], onesD[:], tbuf[:, oc, 0:n], start=(oc == 0), stop=(oc == 7)),
                 reads=[btb, B("onesD")], writes=[bpmean])
        for oc in range(8):
            sqb = lnsq[oc % 2]
            S.act(I("activation", out=sqb[:, 0:n], in_=tbuf[:, oc, 0:n], func=AF.Square), reads=[btb], writes=[B("lnsq", oc % 2)])
            S.pe(I("matmul", pmsq[:, 0:n], onesD[:], sqb[:, 0:n], start=(oc == 0), stop=(oc == 7)),
                 reads=[B("lnsq", oc % 2), B("onesD")], writes=[bpmsq])
        S.act(I("copy", out=lnmean[:, 0:n], in_=pmean[:, 0:n]), reads=[bpmean], writes=[B("lnmean")])
        S.pool(I("tensor_tensor", out=lnm2[:, 0:n], in0=lnmean[:, 0:n], in1=lnmean[:, 0:n], op=ALU.mult), reads=[B("lnmean")], writes=[B("lnm2")])
        S.dve(I("tensor_tensor", out=lnm2[:, 0:n], in0=pmsq[:, 0:n], in1=lnm2[:, 0:n], op=ALU.subtract), reads=[bpmsq, B("lnm2")], writes=[B("lnm2")])
        S.act(I("activation", out=lnm2[:, 0:n], in_=lnm2[:, 0:n], func=AF.Sqrt, bias=epsln[:, 0:1], scale=1.0), reads=[B("lnm2"), B("eps")], writes=[B("lnm2")])
        S.dve(I("reciprocal", out=lnrstd[:, 0:n], in_=lnm2[:, 0:n]), reads=[B("lnm2")], writes=[B("lnrstd")])
        for oc in range(8):
            S.dve(I("tensor_tensor", out=tbuf[:, oc, 0:n], in0=tbuf[:, oc, 0:n], in1=lnmean[:, 0:n], op=ALU.subtract),
                  reads=[btb, B("lnmean")], writes=[btb])
            S.dve(I("tensor_tensor", out=tbuf[:, oc, 0:n], in0=tbuf[:, oc, 0:n], in1=lnrstd[:, 0:n], op=ALU.mult),
                  reads=[btb, B("lnrstd")], writes=[btb])
            for (eng, dstf, scf, bif, wr) in outs:
                if eng == "act":
                    S.act(I("activation", out=dstf(oc), in_=tbuf[:, oc, 0:n], func=AF.Identity, scale=scf(oc), bias=bif(oc)),
                          reads=[btb] + MD, writes=wr)
                else:
                    S.pool(I("tensor_scalar", out=dstf(oc), in0=tbuf[:, oc, 0:n], scalar1=scf(oc), scalar2=bif(oc),
                                                                                      op0=ALU.mult, op1=ALU.add),
                           reads=[btb] + MD, writes=wr)

    self_cv = [None]
    lnsq = [None, None]
    lnmean = lnm2 = lnrstd = None

    def outproj_phase(s, l):
        S.phase = "outproj_phase" + str(l)
        nonlocal lnsq, lnmean, lnm2, lnrstd
        S.barrier()
        cv = Carver()
        self_cv[0] = cv
        tbs = [cv(F32, 8, 512) for _ in range(2)]
        xss = [cv(F32, 8, 512) for _ in range(2)]
        lnsq = [cv(F32, 512) for _ in range(2)]
        lnmean = cv(F32, 512)
        lnm2 = cv(F32, 512)
        lnrstd = cv(F32, 512)
        wv = wview(1, BF16, 8, 1024)
        wb = [B("wbuf", 1)]
        S.dma(I("dma_start", out=wv, in_=wout_d[l].rearrange("(k p) c -> p k c", p=128)), writes=wb, q="pool")
        bi = 0
        for (t0, n) in BLOCKS:
            if t0 < LC and l == DEPTH - 1:
                continue
            tb = tbs[bi % 2]
            xs = xss[bi % 2]
            btb = B("tb", bi % 2)
            bxs = B("xs", bi % 2)
            bi += 1
            mi = msel(s, t0)
            xrb = [B("xres", s, tt) for tt in tiles_of(t0, n)]
            S.dma(I("dma_start", out=xs[:, :, 0:n], in_=xres[s][:, :, t0:t0 + n]), reads=xrb, writes=[bxs])
            for oc in range(8):
                py, bpy = bank("proj", [0, 1, 2, 3])
                for k in range(8):
                    S.pe(I("matmul", py[:, 0:n], wv[:, k, 128 * oc:128 * oc + 128], obuf[:, k, t0:t0 + n],
                                                                       start=(k == 0), stop=(k == 7)),
                         reads=wb + [B("obuf", tt) for tt in tiles_of(t0, n)], writes=[bpy])
                S.dve(I("scalar_tensor_tensor", out=tb[:, oc, 0:n], in0=py[:, 0:n], scalar=mder[:, l, 1, oc, mi:mi + 1], in1=xs[:, oc, 0:n],
                                                                             op0=ALU.mult, op1=ALU.add),
                      reads=[bpy, bxs] + MD, writes=[btb])
            outs = [
                ("act", lambda oc, n=n, xs=xs: xs[:, oc, 0:n], lambda oc: lnT[:, l, 0, oc:oc + 1], lambda oc: lnT[:, l, 1, oc:oc + 1], [bxs]),
                ("pool", lambda oc, t0=t0, n=n: ubuf[:, oc, t0:t0 + n], lambda oc, mi=mi: mder[:, l, 4, oc, mi:mi + 1], lambda oc, mi=mi: mder[:, l, 5, oc, mi:mi + 1], ub(t0, n)),
            ]
            layer_norm_block(tb, btb, n, mi, None, outs)
            S.dma(I("dma_start", out=xres[s][:, :, t0:t0 + n], in_=xs[:, :, 0:n]), reads=[bxs], writes=xrb)

    def ffn_phase(s, l):
        S.phase = "ffn_phase" + str(l)
        nonlocal lnsq, lnmean, lnm2, lnrstd
        S.barrier()
        last = (l == DEPTH - 1)
        cv = Carver()
        self_cv[0] = cv
        hT = cv(BF16, NHC, 1024)
        w2 = [cv(BF16, NHC, 256) for _ in range(2)]
        lnsq = [cv(F32, 512) for _ in range(2)]
        lnmean = cv(F32, 512)
        lnm2 = cv(F32, 512)
        lnrstd = cv(F32, 512)
        sa = [cv(F32, 512) for _ in range(2)]
        ostage = [cv(F32, 1024) for _ in range(2)] if last else None
        tfull = obuf[:].rearrange("p k t -> p (k t)").bitcast(F32)[:, 0:8 * 1024].rearrange("p (k t) -> p k t", k=8, t=1024)
        parts = [[BLOCKS[1], BLOCKS[2]], [BLOCKS[3], BLOCKS[4]]]
        if not last:
            parts = [[BLOCKS[0]]] + parts
        wf1 = wf1_d[l].rearrange("(k p) c -> p k c", p=128)
        wf2 = wf2_d[l].rearrange("(j p) c -> p j c", p=128)
        for part in parts:
            pt0 = part[0][0]
            ptn = sum(n for (_, n) in part)
            mi = msel(s, pt0)
            btf = B("tfull")
            for js in range(11):
                wi = js % 2
                wv = wview(wi, BF16, 8, 512)
                wb = [B("wbuf", wi)]
                S.dma(I("dma_start", out=wv[:, :, 0:256], in_=wf1[:, :, 256 * js:256 * js + 256]), writes=wb, q="pool")
                S.dma(I("dma_start", out=wv[:, :, 256:512], in_=wf1[:, :, FH + 256 * js:FH + 256 * js + 256]), writes=wb, q="pool")
                for (t0, n) in part:
                    lo = t0 - pt0
                    for jj in range(2):
                        j = 2 * js + jj
                        pa, bpa = bank("ffa", [0, 1, 2])
                        pg, bpg = bank("ffb", [3, 4, 5])
                        for k in range(8):
                            S.pe(I("matmul", pa[:, 0:n], wv[:, k, 128 * jj:128 * jj + 128], ubuf[:, k, t0:t0 + n],
                                                                                      start=(k == 0), stop=(k == 7)),
                                 reads=wb + ub(t0, n), writes=[bpa])
                        for k in range(8):
                            S.pe(I("matmul", pg[:, 0:n], wv[:, k, 256 + 128 * jj:256 + 128 * jj + 128], ubuf[:, k, t0:t0 + n],
                                                                                      start=(k == 0), stop=(k == 7)),
                                 reads=wb + ub(t0, n), writes=[bpg])
                        si = j % 2
                        S.act(I("activation", out=sa[si][:, 0:n], in_=pa[:, 0:n], func=AF.Silu), reads=[bpa], writes=[B("sa", si)])
                        S.dve(I("tensor_tensor", out=hT[:, j, lo:lo + n], in0=pg[:, 0:n], in1=sa[si][:, 0:n], op=ALU.mult),
                              reads=[bpg, B("sa", si)], writes=[B("hT")])
            S.dma(I("dma_start", out=tfull[:, :, 0:ptn], in_=xres[s][:, :, pt0:pt0 + ptn]),
                  reads=[B("xres", s, tt) for tt in tiles_of(pt0, ptn)], writes=[btf])
            for oc2 in range(4):
                w2v = w2[oc2 % 2]
                bw2 = [B("w2", oc2 % 2)]
                S.dma(I("dma_start", out=w2v, in_=wf2[:, :, 256 * oc2:256 * oc2 + 256]), writes=bw2, q="pool")
                for (t0, n) in part:
                    lo = t0 - pt0
                    for oo in range(2):
                        oc = 2 * oc2 + oo
                        py, bpy = bank("ffy", [0, 1, 2, 3])
                        for j in range(NHC):
                            S.pe(I("matmul", py[:, 0:n], w2v[:, j, 128 * oo:128 * oo + 128], hT[:, j, lo:lo + n],
                                                                                        start=(j == 0), stop=(j == NHC - 1)),
                                 reads=bw2 + [B("hT")], writes=[bpy])
                        S.dve(I("scalar_tensor_tensor", out=tfull[:, oc, lo:lo + n], in0=py[:, 0:n], scalar=mder[:, l, 3, oc, mi:mi + 1],
                                                                                            in1=tfull[:, oc, lo:lo + n], op0=ALU.mult, op1=ALU.add),
                              reads=[bpy, btf] + MD, writes=[btf])
            for (t0, n) in part:
                lo = t0 - pt0
                tbv = tfull[:, :, lo:lo + n]
                xrb = [B("xres", s, tt) for tt in tiles_of(t0, n)]
                if not last:
                    outs = [
                        ("pool", lambda oc, t0=t0, n=n: ubuf[:, oc, t0:t0 + n], lambda oc, mi=mi: mder2[:, l, 0, oc, mi:mi + 1], lambda oc, mi=mi: mder2[:, l, 1, oc, mi:mi + 1], ub(t0, n)),
                        ("act", lambda oc, tbv=tbv: tbv[:, oc, :], lambda oc: lnT[:, l, 2, oc:oc + 1], lambda oc: lnT[:, l, 3, oc:oc + 1], [btf]),
                    ]
                    layer_norm_block(tbv, btf, n, mi, None, outs)
                    S.dma(I("dma_start", out=xres[s][:, :, t0:t0 + n], in_=tbv), reads=[btf], writes=xrb)
                else:
                    outs = [("act", lambda oc, tbv=tbv: tbv[:, oc, :], lambda oc: lnT[:, l, 2, oc:oc + 1], lambda oc: lnT[:, l, 3, oc:oc + 1], [btf])]
                    layer_norm_block(tbv, btf, n, mi, None, outs)
                    for qs in range(n // 128):
                        tq = (t0 + 128 * qs) // 128
                        og = ostage[tq % 2]
                        for half in range(2):
                            pt, bpt = bank("ffo", [0, 1, 2, 3])
                            for kk in range(4):
                                k = 4 * half + kk
                                S.pe(I("transpose", pt[:, 128 * kk:128 * kk + 128], tbv[:, k, 128 * qs:128 * qs + 128], identf[:]),
                                     reads=[btf, B("identf")], writes=[bpt])
                            S.act(I("copy", out=og[:, 512 * half:512 * half + 512], in_=pt[:, 0:512]),
                                  reads=[bpt], writes=[B("ostage", tq % 2)])
                        S.dma(I("dma_start", out=out_d[s, 128 * (tq - 2):128 * (tq - 2) + 128, :], in_=og),
                              reads=[B("ostage", tq % 2)], writes=[B("out")])

    for s in range(nseq):
        load_x(s)
        if stage == 1 and s == 0:
            dump("u0", ubuf[:], [128, 8, T], BF16, B("ubuf", NTT - 1))
        for l in range(DEPTH):
            if stage >= 2:
                na_phase(s, l)
            if stage >= 3:
                gla_phase(s, l)
            if stage >= 4:
                gqa_phase(s, l)
            if stage in (2, 3, 4) and s == 0 and l == 0:
                S.barrier()
                S.pool(I("memset", epsln[:], LN_EPS_S), reads=[B("obuf", tt) for tt in range(NTT)], writes=[B("eps")])
                dump("o0", obuf[:], [128, 8, T], BF16, B("eps"))
                break
            if stage >= 5:
                outproj_phase(s, l)
            if stage == 5 and s == 0 and l == 0:
                S.barrier()
                S.pool(I("memset", epsln[:], LN_EPS_S), reads=[B("ubuf", tt) for tt in range(NTT)], writes=[B("eps")])
                dump("u2", ubuf[:], [128, 8, T], BF16, B("eps"))
                dump("x1", xres[0], [128, 8, T], F32, B("eps"))
                break
            if stage >= 6:
                ffn_phase(s, l)
            if stage == 6 and s == 0 and l == 0:
                S.barrier()
                S.pool(I("memset", epsln[:], LN_EPS_S), reads=[B("ubuf", tt) for tt in range(NTT)], writes=[B("eps")])
                dump("u1n", ubuf[:], [128, 8, T], BF16, B("eps"))
                dump("x2", xres[0], [128, 8, T], F32, B("eps"))
                break


    S.emit(final_bufs=dump_list + [B("out")])
    S.close()
    global LAST_SCHED
    LAST_SCHED = S
    return nc


def host_consts():
    eye = np.eye(128, dtype=np.float32)
    onesd = np.full((128, 128), 1.0 / 1024.0, np.float32)
    bon = np.zeros((128, 128), np.float32)
    bon[:64, :64] = 1.0 / 64
    bon[64:, 64:] = 1.0 / 64
    R = np.zeros((64, 64), np.float32)
    for d in range(16):
        R[d, d + 16] = -1.0
        R[d + 16, d] = 1.0
        R[d + 32, d + 48] = -1.0
        R[d + 48, d + 32] = 1.0
    rl = np.zeros((128, 128), np.float32)
    rl[:64, :64] = R.T
    rl[64:, 64:] = R.T
    hm = np.zeros_like(eye)
    for h in range(4):
        hm[32 * h:32 * h + 32, h] = 1.0
    cm = np.stack([eye, onesd, bon, rl, hm], axis=1)
    sidx = np.arange(64)
    mf = (sidx[None, :] >= sidx[:, None]).astype(np.float32)
    mb = (sidx[None, :] <= sidx[:, None]).astype(np.float32)
    tri = np.zeros((128, 2, 256), np.float32)
    for half in range(2):
        tri[64 * half:64 * half + 64, 0, :] = np.tile(mf, (1, 4))
        tri[64 * half:64 * half + 64, 1, :] = np.tile(mb, (1, 4))
    t = np.arange(NL)
    row = (t // 64).astype(np.float32)
    col = (t % 64).astype(np.float32)
    inv_freq = (10000.0 ** (-np.arange(16, dtype=np.float32) / 16)).astype(np.float32)
    ang_r = row[:, None] * inv_freq
    ang_c = col[:, None] * inv_freq
    ang = np.concatenate([ang_r, ang_r, ang_c, ang_c], axis=-1).astype(np.float32)
    cosT = np.tile(np.cos(ang).T, (2, 1)).astype(np.float32)
    sinT = np.tile(np.sin(ang).T, (2, 1)).astype(np.float32)
    return cm, tri, np.ascontiguousarray(cosT), np.ascontiguousarray(sinT)


def prep_shared(inp):
    f = lambda a: np.ascontiguousarray(np.asarray(a, dtype=np.float32))
    cm, tri, cosT, sinT = host_consts()
    sh = {}
    sh["w_ada"] = f(inp["w_ada"])
    sh["b_adaT"] = f(np.asarray(inp["b_ada"]).reshape(DEPTH, 48, 128).transpose(0, 2, 1))
    sh["w_in"] = f(inp["w_in"])
    sh["w_out"] = f(inp["w_out"])
    sh["w_ffn_in"] = f(inp["w_ffn_in"])
    sh["w_ffn_out"] = f(inp["w_ffn_out"])
    ln = np.stack([np.asarray(inp[k]) for k in ("ln1_g", "ln1_b", "ln2_g", "ln2_b")], axis=1)
    sh["lnT"] = f(ln.reshape(DEPTH, 4, 8, 128).transpose(3, 0, 1, 2))
    valid, dri, dci = na_index_tables()
    rpb = np.asarray(inp["na_rpb"], dtype=np.float32)
    g = rpb[:, :, dri, dci]
    g = np.where(valid[None, None], g, np.float32(-30000.0))
    sh["nab"] = f(g.transpose(0, 3, 1, 2, 4).reshape(DEPTH, 128, 6, NCOMBO * 128))
    wa2 = np.zeros((DEPTH, 2, 32, 128), np.float32)
    gw = np.asarray(inp["gla_wa2"], dtype=np.float32)
    wa2[:, 0, 0:16, :] = gw[:, 0]
    wa2[:, 1, 16:32, :] = gw[:, 1]
    sh["wa2"] = wa2
    sh["nbaT"] = f(np.asarray(inp["gla_ba"]).transpose(2, 0, 1))
    sh["glnw"] = f(np.tile(np.asarray(inp["gla_norm_w"]), (1, 4)).reshape(DEPTH, 1, 256))
    qk = np.stack([np.tile(np.asarray(inp["gqa_qnorm_w"]), (1, 2)), np.tile(np.asarray(inp["gqa_knorm_w"]), (1, 2))], axis=2)
    sh["qkwT"] = f(qk.transpose(1, 0, 2))
    sh["cosT"] = cosT
    sh["sinT"] = sinT
    sh["cmats"] = f(cm)
    sh["trim"] = f(tri)
    return sh


def prep_core(inp, core, sh):
    f = lambda a: np.ascontiguousarray(np.asarray(a, dtype=np.float32))
    b0 = core * NSEQ
    m = dict(sh)
    m["x"] = f(inp["x"][b0:b0 + NSEQ])
    m["ctx"] = f(inp["ctx"][b0:b0 + NSEQ])
    cc = np.stack([np.asarray(inp["c"][b0]), np.asarray(inp["c"][b0 + 1]), np.asarray(inp["c_ctx"])], axis=1)
    m["cT"] = f(cc.reshape(8, 128, 3).transpose(1, 0, 2))
    return m


def kernel(**inputs):
    sh = prep_shared(inputs)
    nc = build_nc()
    in_maps = [prep_core(inputs, c, sh) for c in range(8)]
    res = run_bass_kernel_spmd(nc, in_maps, core_ids=list(range(8)))
    out = np.concatenate([np.asarray(r["out"]) for r in res.results], axis=0)
    return out.astype(np.float32)
```

```python
import contextlib
import numpy as np
import concourse.bass as bass
import concourse.mybir as mybir
from concourse.bass_utils import run_bass_kernel_spmd

F32 = mybir.dt.float32
BF16 = mybir.dt.bfloat16
AF = mybir.ActivationFunctionType
ALU = mybir.AluOpType
AX = mybir.AxisListType

COMPUTE = ("pe", "act", "dve", "pool")
NDMA_SEM = 6


class Buf:
    __slots__ = ("last_w", "readers", "excl")

    def __init__(self, excl=False):
        self.last_w = None
        self.readers = []
        self.excl = excl


class Sched:
    def __init__(self, nc):
        self.nc = nc
        self.ops = {e: [] for e in ("pe", "act", "dve", "pool", "sp")}
        self.stack = contextlib.ExitStack()
        self.nalloc = 0
        self.bufs = {}
        self.pending = {e: set() for e in self.ops}
        self.phase = "pro"

    def B(self, *key):
        b = self.bufs.get(key)
        if b is None:
            b = self.bufs[key] = Buf(excl=(key[0] in ("pf", "pb")))
        return b

    def sbuf(self, shape, dtype, name=None):
        self.nalloc += 1
        return self.stack.enter_context(self.nc.sbuf_tensor("sb_" + (name or f"t{self.nalloc}"), list(shape), dtype))

    def psum(self, shape, dtype, name=None):
        self.nalloc += 1
        return self.stack.enter_context(self.nc.psum_tensor("ps_" + (name or f"t{self.nalloc}"), list(shape), dtype))

    def add(self, eng, fn, reads=(), writes=(), dma=False):
        idx = len(self.ops[eng])
        deps = set(self.pending[eng])
        self.pending[eng] = set()
        for b in reads:
            if b.last_w is not None:
                deps.add(b.last_w)
            if b.excl:
                deps.update(r for r in b.readers if r[0] != eng)
        for b in writes:
            if b.last_w is not None:
                deps.add(b.last_w)
            deps.update(b.readers)
        if eng == "pe":
            deps = {d for d in deps if d[0] != "pe"}
        self.ops[eng].append({"fn": fn, "deps": deps, "dma": dma, "marked": False, "ph": self.phase})
        me = (eng, idx)
        for b in reads:
            b.readers.append(me)
        for b in writes:
            b.last_w = me
            b.readers = []
        return me

    def pe(self, fn, reads=(), writes=()):
        return self.add("pe", fn, reads, writes)

    def act(self, fn, reads=(), writes=()):
        return self.add("act", fn, reads, writes)

    def dve(self, fn, reads=(), writes=()):
        return self.add("dve", fn, reads, writes)

    def pool(self, fn, reads=(), writes=()):
        return self.add("pool", fn, reads, writes)

    def dma(self, fn, reads=(), writes=(), q="sp"):
        return self.add(q, fn, reads, writes, dma=True)

    def barrier(self):
        deps = set()
        for e in ("pe", "act", "dve"):
            if self.ops[e]:
                deps.add((e, len(self.ops[e]) - 1))
        for q in ("sp", "pool"):
            nd = 0
            gotc = False
            for i in range(len(self.ops[q]) - 1, -1, -1):
                if self.ops[q][i]["dma"]:
                    if nd < NDMA_SEM:
                        deps.add((q, i))
                        nd += 1
                elif not gotc:
                    deps.add((q, i))
                    gotc = True
                if nd >= NDMA_SEM and (gotc or q == "sp"):
                    break
        for e in self.pending:
            self.pending[e] |= deps

    def emit(self, final_bufs=()):
        nc = self.nc
        ops = self.ops
        dma_info = {}
        for q in ("sp", "pool"):
            n = 0
            for i, op in enumerate(ops[q]):
                if op["dma"]:
                    slot = n % NDMA_SEM
                    val = 16 * (n // NDMA_SEM + 1)
                    dma_info[(q, i)] = (q, slot, val)
                    op["dmainfo"] = (slot, val)
                    n += 1
        for e in ops:
            for op in ops[e]:
                op["deps"] = {d for d in op["deps"]}
                for d in op["deps"]:
                    if d not in dma_info:
                        ops[d[0]][d[1]]["marked"] = True
        final_deps = set()
        for b in final_bufs:
            if b.last_w is not None:
                final_deps.add(b.last_w)
                if b.last_w not in dma_info:
                    ops[b.last_w[0]][b.last_w[1]]["marked"] = True
        for e in COMPUTE:
            c = 0
            for op in ops[e]:
                if op["marked"] and not op["dma"]:
                    c += 1
                    op["semval"] = c
        st = self.stack
        csem = {e: st.enter_context(nc.semaphore(f"s_{e}")) for e in COMPUTE}
        dsem = {q: [st.enter_context(nc.semaphore(f"d_{q}{j}")) for j in range(NDMA_SEM)] for q in ("sp", "pool")}
        block = st.enter_context(nc.Block())
        engobj = {"pe": block.tensor, "act": block.scalar, "dve": block.vector, "pool": block.gpsimd, "sp": block.sync}

        def resolve(dep):
            if dep in dma_info:
                q, slot, val = dma_info[dep]
                return dsem[q][slot], ("d", q, slot), val
            f, k = dep
            return csem[f], ("c", f), ops[f][k]["semval"]

        def make_body(e):
            def body(eng):
                known = {}
                for op in ops[e]:
                    need = {}
                    for dep in op["deps"]:
                        if dep[0] == e and dep not in dma_info and e == "pe":
                            continue
                        sem, key, val = resolve(dep)
                        if need.get(key, (None, 0))[1] < val:
                            need[key] = (sem, val)
                    if op["dma"]:
                        slot, val = op["dmainfo"]
                        if val > 16:
                            key = ("d", e, slot)
                            if need.get(key, (None, 0))[1] < val - 16:
                                need[key] = (dsem[e][slot], val - 16)
                    for key, (sem, val) in need.items():
                        if known.get(key, 0) >= val:
                            continue
                        known[key] = val
                        eng.wait_ge(sem, val)
                    ins = op["fn"](eng)
                    if op["dma"]:
                        ins.then_inc(dsem[e][op["dmainfo"][0]], 16)
                    elif op["marked"]:
                        ins.then_inc(csem[e], 1)
                if e == "sp":
                    need = {}
                    for dep in final_deps:
                        sem, key, val = resolve(dep)
                        if need.get(key, (None, 0))[1] < val:
                            need[key] = (sem, val)
                    for key, (sem, val) in need.items():
                        eng.wait_ge(sem, val)
            return body

        for e in ("sp", "pool", "act", "dve", "pe"):
            engobj[e](make_body(e))

    def close(self):
        self.stack.close()


def I(name, *args, **kw):
    f = lambda eng: getattr(eng, name)(*args, **kw)
    try:
        f.f32 = name in ("matmul", "transpose") and args[1].dtype == F32
    except Exception:
        f.f32 = False
    return f


D = 1024
LC = 256
NL = 2048
T = LC + NL
NTT = T // 128
FH = 2816
NHC = FH // 128
DEPTH = 2
NSEQ = 2
BLOCKS = [(0, 256)] + [(256 + 512 * i, 512) for i in range(4)]
ALPHA2 = 2.0
LN_EPS_S = 1e-5 / ALPHA2
RMS_EPS = 1e-6
NCOMBO = 21
WBYTES = 91136

C_NAQ, C_NAK, C_NAV = 0, 384, 768
C_GLQ, C_GLK, C_GLV, C_GLG, C_GLR = 1152, 1280, 1408, 1664, 1920
C_GAQ, C_GAK, C_GAV = 1952, 2336, 2464


def na_combos():
    combos = []
    idx = {}

    def get(kind, r, d):
        key = (kind, d)
        if key not in idx:
            idx[key] = len(combos)
            combos.append((r, d))
        return idx[key]

    slots = []
    for p in range(16):
        r = 2 * p
        if p == 0:
            s = [(j, get("e0", r, 2 * j - r)) for j in range(0, 4)]
        elif p == 1:
            s = [(j, get("e1", r, 2 * j - r)) for j in range(0, 4)]
        elif p == 14:
            s = [(j, get("e2", r, 2 * j - r)) for j in range(12, 16)]
        elif p == 15:
            s = [(j, get("e3", r, 2 * j - r)) for j in range(12, 16)]
        else:
            s = [(j, get("g", 8, 2 * j - r)) for j in range(p - 2, p + 3)]
        slots.append(s)
    assert len(combos) == NCOMBO
    return combos, slots


def na_index_tables():
    combos, _ = na_combos()
    cols = np.arange(64)
    col_start = np.clip(cols - 8, 0, 48)
    col_in = (cols[None, :] >= col_start[:, None]) & (cols[None, :] < col_start[:, None] + 16)
    valid = np.zeros((NCOMBO, 128, 128), bool)
    dri = np.zeros((NCOMBO, 128, 128), np.int64)
    dci = np.zeros((NCOMBO, 128, 128), np.int64)
    for ci, (r, d) in enumerate(combos):
        for ko in range(2):
            for qo in range(2):
                kr = r + d + ko
                qr = r + qo
                rs = min(max(qr - 4, 0), 24)
                ok_row = (rs <= kr < rs + 8)
                dr = kr - qr
                blk_valid = col_in.T & ok_row
                dc = np.clip(cols[:, None] - cols[None, :], -15, 15) + 15
                valid[ci, 64 * ko:64 * ko + 64, 64 * qo:64 * qo + 64] = blk_valid
                dri[ci, 64 * ko:64 * ko + 64, 64 * qo:64 * qo + 64] = min(max(dr + 7, 0), 14)
                dci[ci, 64 * ko:64 * ko + 64, 64 * qo:64 * qo + 64] = dc
    return valid, dri, dci


def build_nc(stage=99, nseq=NSEQ, dumps=None):
    nc = bass.Bass("TRN2", target_bir_lowering=False)
    S = Sched(nc)
    B = S.B
    dumps = dumps if dumps is not None else {}

    def din(name, shape, dt=F32):
        return nc.dram_tensor(name, list(shape), dt, kind="ExternalInput").ap()

    x_d = din("x", [NSEQ, NL, D])
    ctx_d = din("ctx", [NSEQ, LC, D])
    cT_d = din("cT", [128, 8, 3])
    wada_d = din("w_ada", [DEPTH, D, 6 * D])
    bada_d = din("b_adaT", [DEPTH, 128, 48])
    win_d = din("w_in", [DEPTH, D, 2592])
    wout_d = din("w_out", [DEPTH, D, D])
    wf1_d = din("w_ffn_in", [DEPTH, D, 2 * FH])
    wf2_d = din("w_ffn_out", [DEPTH, FH, D])
    lnT_d = din("lnT", [128, DEPTH, 4, 8])
    nab_d = din("nab", [DEPTH, 128, 6, NCOMBO * 128])
    wa2_d = din("wa2", [DEPTH, 2, 32, 128])
    nba_d = din("nbaT", [128, DEPTH, 2])
    glnw_d = din("glnw", [DEPTH, 1, 256])
    qkw_d = din("qkwT", [128, DEPTH, 2])
    cos_d = din("cosT", [128, NL])
    sin_d = din("sinT", [128, NL])
    cm_d = din("cmats", [128, 5, 128])
    tri_d = din("trim", [128, 2, 256])
    out_d = nc.dram_tensor("out", [NSEQ, NL, D], F32, kind="ExternalOutput").ap()
    xres = [nc.dram_tensor(f"xres{s}", [128, 8, T], F32).ap() for s in range(NSEQ)]
    dump_list = []

    ubuf = S.sbuf([128, 8, T], BF16, "ubuf")
    obuf = S.sbuf([128, 8, T], BF16, "obuf")
    wbuf = [S.sbuf([128, 8 * 1152], BF16, f"wbuf{i}") for i in range(2)]

    def wview(i, dt, k, c):
        if dt == F32:
            return wbuf[i][:].bitcast(F32)[:, 0:k * c].rearrange("p (k c) -> p k c", k=k, c=c)
        return wbuf[i][:, 0:k * c].rearrange("p (k c) -> p k c", k=k, c=c)

    Wr = S.sbuf([128, WBYTES // 4], F32, "Wr")
    identf = S.sbuf([128, 128], F32, "identf")
    onesD = S.sbuf([128, 128], F32, "onesD")
    identb = S.sbuf([128, 128], BF16, "identb")
    bones = S.sbuf([128, 128], BF16, "bones")
    rotm = S.sbuf([128, 128], BF16, "rotm")
    trim = S.sbuf([128, 2, 256], BF16, "trim")
    scT = S.sbuf([128, 8, 3], F32, "scT")
    modT = S.sbuf([128, DEPTH, 48, 3], F32, "modT")
    mder = S.sbuf([128, DEPTH, 6, 8, 3], F32, "mder")
    mder2 = S.sbuf([128, DEPTH, 2, 8, 3], F32, "mder2")
    lnT = S.sbuf([128, DEPTH, 4, 8], F32, "lnT")
    badaT = S.sbuf([128, DEPTH, 48], F32, "badaT")
    nbaT = S.sbuf([128, DEPTH, 2], F32, "nbaT")
    qkwT = S.sbuf([128, DEPTH, 2], F32, "qkwT")
    glnw = S.sbuf([128, DEPTH, 256], F32, "glnw")
    wa2 = S.sbuf([32, DEPTH, 2, 128], F32, "wa2")
    hmask = S.sbuf([128, 4], F32, "hmask")
    epsln = S.sbuf([128, 1], F32, "epsln")
    epsrms = S.sbuf([128, 1], F32, "epsrms")

    pf = [S.psum([128, 512], F32, f"pf{i}") for i in range(7)]
    pb = S.psum([128, 1024], BF16, "pb")
    pctr = [0]

    def nextp():
        i = pctr[0] % 7
        pctr[0] += 1
        return pf[i], B("pf", i)

    class Carver:
        def __init__(self):
            self.off = 0

        def __call__(self, dt, *dims):
            n = int(np.prod(dims))
            nbytes = n * (4 if dt == F32 else 2)
            nbytes = (nbytes + 63) // 64 * 64
            a = self.off // 4
            self.off += nbytes
            assert self.off <= WBYTES, ("scratch overflow", self.off)
            ap = Wr[:, a:a + nbytes // 4]
            if dt == BF16:
                ap = ap.bitcast(BF16)[:, 0:n]
            else:
                ap = ap[:, 0:n]
            if len(dims) == 2:
                ap = ap.rearrange("p (a b) -> p a b", a=dims[0], b=dims[1])
            elif len(dims) == 3:
                ap = ap.rearrange("p (a b c) -> p a b c", a=dims[0], b=dims[1], c=dims[2])
            return ap

    def dump(name, ap, shape, dt, rbuf):
        d = nc.dram_tensor("dbg_" + name, list(shape), dt, kind="ExternalOutput").ap()
        S.dma(I("dma_start", out=d, in_=ap), reads=[rbuf], writes=[B("dump", name)])
        dump_list.append(B("dump", name))
        dumps[name] = (shape, dt)

    def ld(dst, src, bname, q="sp"):
        S.dma(I("dma_start", out=dst, in_=src), writes=[B(bname)], q=q)

    ld(identf[:], cm_d[:, 0, :], "identf")
    ld(onesD[:], cm_d[:, 1, :], "onesD")
    ld(identb[:], cm_d[:, 0, :], "identb", q="pool")
    ld(bones[:], cm_d[:, 2, :], "bones", q="pool")
    ld(rotm[:], cm_d[:, 3, :], "rotm", q="pool")
    ld(trim[:], tri_d, "trim", q="pool")
    ld(hmask[:], cm_d[:, 4, 0:4], "hmask")
    ld(scT[:], cT_d, "scT")
    ld(lnT[:], lnT_d, "lnT")
    ld(badaT[:], bada_d.rearrange("l p j -> p l j"), "badaT")
    ld(nbaT[:], nba_d, "nbaT")
    ld(qkwT[:], qkw_d, "qkwT")
    for l in range(DEPTH):
        ld(glnw[:, l, :], glnw_d[l].broadcast_to([128, 256]), "glnw")
        ld(wa2[:, l, :, :], wa2_d[l].rearrange("e r c -> r e c"), "wa2")
    S.dve(I("tensor_scalar_mul", out=nbaT[:], in0=nbaT[:], scalar1=-1.0), reads=[B("nbaT")], writes=[B("nbaT")])
    S.pool(I("memset", epsln[:], LN_EPS_S), writes=[B("eps")])
    S.pool(I("memset", epsrms[:], RMS_EPS), writes=[B("eps")])
    S.act(I("activation", out=scT[:], in_=scT[:], func=AF.Silu), reads=[B("scT")], writes=[B("scT")])

    for l in range(DEPTH):
        pm, bpm = nextp()
        for js in range(12):
            wv = wview(js % 2, F32, 8, 512)
            S.dma(I("dma_start",
                out=wv, in_=wada_d[l].rearrange("(k p) c -> p k c", p=128)[:, :, 512 * js:512 * js + 512]),
                writes=[B("wbuf", js % 2)])
            for jj in range(4):
                j = 4 * js + jj
                for k in range(8):
                    S.pe(I("matmul",
                        pm[:, 3 * j:3 * j + 3], wv[:, k, 128 * jj:128 * jj + 128], scT[:, k, :],
                        start=(k == 0), stop=(k == 7)),
                        reads=[B("wbuf", js % 2), B("scT")], writes=[bpm])
        S.dve(I("tensor_tensor",
            out=modT[:, l, :, :], in0=pm[:, 0:144].rearrange("p (j b) -> p j b", j=48, b=3),
            in1=badaT[:, l, :].unsqueeze(2).broadcast_to([128, 48, 3]), op=ALU.add),
            reads=[bpm, B("badaT")], writes=[B("modT")])
    inv_a = float(1.0 / np.sqrt(ALPHA2))
    for l in range(DEPTH):
        mT = modT[:, l, :, :]
        rd = [B("modT"), B("lnT")]
        wr = [B("mder")]
        S.dve(I("tensor_scalar_add", out=mder[:, l, 0, :, :], in0=mT[:, 8:16, :], scalar1=1.0), reads=rd, writes=wr)
        S.dve(I("tensor_scalar_mul", out=mder[:, l, 1, :, :], in0=mT[:, 16:24, :], scalar1=inv_a), reads=rd, writes=wr)
        S.dve(I("tensor_scalar_add", out=mder[:, l, 2, :, :], in0=mT[:, 32:40, :], scalar1=1.0), reads=rd, writes=wr)
        S.dve(I("tensor_scalar_mul", out=mder[:, l, 3, :, :], in0=mT[:, 40:48, :], scalar1=inv_a), reads=rd, writes=wr)
        S.dve(I("tensor_tensor", out=mder[:, l, 4, :, :], in0=mder[:, l, 2, :, :],
                                             in1=lnT[:, l, 0, :].unsqueeze(2).broadcast_to([128, 8, 3]), op=ALU.mult), reads=rd + wr, writes=wr)
        S.dve(I("tensor_tensor", out=mder[:, l, 5, :, :], in0=mder[:, l, 2, :, :],
                                             in1=lnT[:, l, 1, :].unsqueeze(2).broadcast_to([128, 8, 3]), op=ALU.mult), reads=rd + wr, writes=wr)
        S.dve(I("tensor_tensor", out=mder[:, l, 5, :, :], in0=mder[:, l, 5, :, :], in1=mT[:, 24:32, :], op=ALU.add), reads=rd + wr, writes=wr)
    for l in range(DEPTH - 1):
        rd = [B("modT"), B("lnT"), B("mder")]
        wr = [B("mder2")]
        S.dve(I("tensor_tensor", out=mder2[:, l, 0, :, :], in0=mder[:, l + 1, 0, :, :],
                                             in1=lnT[:, l, 2, :].unsqueeze(2).broadcast_to([128, 8, 3]), op=ALU.mult), reads=rd, writes=wr)
        S.dve(I("tensor_tensor", out=mder2[:, l, 1, :, :], in0=mder[:, l + 1, 0, :, :],
                                             in1=lnT[:, l, 3, :].unsqueeze(2).broadcast_to([128, 8, 3]), op=ALU.mult), reads=rd + wr, writes=wr)
        S.dve(I("tensor_tensor", out=mder2[:, l, 1, :, :], in0=mder2[:, l, 1, :, :], in1=modT[:, l + 1, 0:8, :], op=ALU.add), reads=rd + wr, writes=wr)
    MD = [B("mder"), B("mder2"), B("modT"), B("lnT")]
    if stage < 99:
        S.pool(I("memset", obuf[:], 0.0), writes=[B("obuf", tt) for tt in range(NTT)])
    if stage == 0:
        dump("modT", modT[:], [128, DEPTH, 48, 3], F32, B("modT"))

    def msel(s, t0):
        return 2 if t0 < LC else s

    gctr = {}

    def bank(group, banks):
        c = gctr.get(group, 0)
        gctr[group] = c + 1
        i = banks[c % len(banks)]
        return pf[i], B("pf", i)

    def tiles_of(t0, n):
        return range(t0 // 128, (t0 + n) // 128)

    def ub(t0, n):
        return [B("ubuf", tt) for tt in tiles_of(t0, n)]

    win_v = [win_d[l].rearrange("(k p) c -> p k c", p=128) for l in range(DEPTH)]
    combos, na_slots = na_combos()
    PBB = B("pb", 0)

    def load_x(s):
        S.phase = "load_x"
        S.barrier()
        cv = Carver()
        xin = [cv(F32, 1024) for _ in range(2)]
        xst = [cv(F32, 8, 128) for _ in range(2)]
        for tt in range(NTT):
            i2 = tt % 2
            src = ctx_d[s, 128 * tt:128 * tt + 128, :] if tt < 2 else x_d[s, 128 * (tt - 2):128 * (tt - 2) + 128, :]
            S.dma(I("dma_start", out=xin[i2], in_=src), writes=[B("xin", i2)])
            mi = msel(s, 128 * tt)
            for half in range(2):
                pt, bpt = bank("ld", [0, 1, 2, 3])
                for kk in range(4):
                    k = 4 * half + kk
                    S.pe(I("transpose", pt[:, 128 * kk:128 * kk + 128], xin[i2][:, 128 * k:128 * k + 128], identf[:]),
                         reads=[B("xin", i2), B("identf")], writes=[bpt])
                S.dve(I("tensor_copy",
                    out=xst[i2][:, 4 * half:4 * half + 4, :], in_=pt[:].rearrange("p (k t) -> p k t", k=4, t=128)),
                    reads=[bpt], writes=[B("xst", i2)])
                for kk in range(4):
                    k = 4 * half + kk
                    S.act(I("activation",
                        out=ubuf[:, k, 128 * tt:128 * tt + 128], in_=pt[:, 128 * kk:128 * kk + 128], func=AF.Identity,
                        scale=mder[:, 0, 0, k, mi:mi + 1], bias=modT[:, 0, k, mi:mi + 1]),
                        reads=[bpt] + MD, writes=[B("ubuf", tt)])
            S.dma(I("dma_start", out=xres[s][:, :, 128 * tt:128 * tt + 128], in_=xst[i2]),
                  reads=[B("xst", i2)], writes=[B("xres", s, tt)])

    def na_phase(s, l):
        S.phase = "na_phase" + str(l)
        S.barrier()
        cv = Carver()
        naq = cv(BF16, 3, T)
        nak = cv(BF16, 3, T)
        nav = cv(BF16, NTT, 6 * 65)
        natab = cv(BF16, 6, NCOMBO * 128)
        naP = [cv(BF16, 7, 128) for _ in range(3)]
        otok = [cv(BF16, 384) for _ in range(2)]
        rden = [cv(F32, 8) for _ in range(2)]
        nav4 = nav.rearrange("p t (h c) -> p t h c", h=6, c=65)
        wv = wview(0, BF16, 8, 1152)
        S.dma(I("dma_start", out=wv, in_=win_v[l][:, :, 0:1152]), writes=[B("wbuf", 0)], q="pool")
        S.dma(I("dma_start", out=natab, in_=nab_d[l]), writes=[B("natab")], q="pool")
        S.pool(I("memset", nav4[:, :, :, 64:65], 1.0), writes=[B("navones")])
        for (t0, n) in BLOCKS:
            for j in range(6):
                if j < 3 and t0 < LC and l == DEPTH - 1:
                    continue
                ps, bps = bank("proj", [0, 1, 2, 3, 4, 5, 6])
                for k in range(8):
                    S.pe(I("matmul", ps[:, 0:n], wv[:, k, 128 * j:128 * j + 128], ubuf[:, k, t0:t0 + n],
                                                                     start=(k == 0), stop=(k == 7)),
                         reads=[B("wbuf", 0)] + ub(t0, n), writes=[bps])
                wr = [B("naqk", tt) for tt in tiles_of(t0, n)]
                if j < 3:
                    S.act(I("activation", out=naq[:, j, t0:t0 + n], in_=ps[:, 0:n], func=AF.Identity, scale=0.125),
                          reads=[bps], writes=wr)
                else:
                    S.dve(I("tensor_copy", out=nak[:, j - 3, t0:t0 + n], in_=ps[:, 0:n]), reads=[bps], writes=wr)
        for tt in range(NTT):
            ps, bps = bank("proj", [0, 1, 2, 3, 4, 5, 6])
            for k in range(8):
                S.pe(I("matmul", ps[:, 0:384], ubuf[:, k, 128 * tt:128 * tt + 128], wv[:, k, 768:1152],
                                                         start=(k == 0), stop=(k == 7)),
                     reads=[B("wbuf", 0), B("ubuf", tt)], writes=[bps])
            S.dve(I("tensor_copy", out=nav4[:, tt, :, 0:64], in_=ps[:, 0:384].rearrange("p (h c) -> p h c", h=6, c=64)),
                  reads=[bps], writes=[B("nav", tt)])
        qtiles = []
        if l < DEPTH - 1:
            qtiles += [(0, [(0, None), (1, None)]), (1, [(0, None), (1, None)])]
        for p in range(16):
            qtiles.append((2 + p, [(2 + j, ci) for (j, ci) in na_slots[p]] + [(0, None), (1, None)]))
        naP3 = naP
        units = [(qi, h) for qi in range(len(qtiles)) for h in range(6)]
        pos_ = {}
        ust = {}

        def na_st(u):
            qi, h = units[u]
            tq, slots = qtiles[qi]
            if h == 0:
                pos_[qi] = bank("napo", [4, 5])
            ns = len(slots)
            c = h // 2
            p0 = 64 * (h % 2)
            pa, bpa = bank("napa", [0, 1, 2, 3])
            pa2, bpa2 = bank("napa", [0, 1, 2, 3])
            P = naP3[u % 3]
            bP = B("naP", u % 3)
            ust[u] = (P, bP)
            for si, (tk, ci) in enumerate(slots):
                bk, bbk = (pa, bpa) if si < 4 else (pa2, bpa2)
                col = 128 * (si % 4)
                S.pe(I("matmul", bk[:, col:col + 128], nak[p0:p0 + 64, c, 128 * tk:128 * tk + 128], naq[p0:p0 + 64, c, 128 * tq:128 * tq + 128],
                       start=True, stop=(ci is None)),
                     reads=[B("naqk", tk), B("naqk", tq)], writes=[bbk])
                if ci is not None:
                    S.pe(I("matmul", bk[:, col:col + 128], identb[:], natab[:, h, 128 * ci:128 * ci + 128], start=False, stop=True),
                         reads=[B("natab"), B("identb")], writes=[bbk])
            n1 = min(ns, 4)
            S.act(I("activation", out=P[:, 0:n1, :], in_=pa[:, 0:128 * n1].rearrange("p (a b) -> p a b", a=n1, b=128), func=AF.Exp),
                  reads=[bpa], writes=[bP])
            if ns > 4:
                n2 = ns - 4
                S.act(I("activation", out=P[:, 4:4 + n2, :], in_=pa2[:, 0:128 * n2].rearrange("p (a b) -> p a b", a=n2, b=128), func=AF.Exp),
                      reads=[bpa2], writes=[bP])

        def na_pv(u):
            qi, h = units[u]
            tq, slots = qtiles[qi]
            ns = len(slots)
            po, bpo = pos_[qi]
            P, bP = ust.pop(u)
            for si, (tk, ci) in enumerate(slots):
                S.pe(I("matmul", po[:, 65 * h:65 * h + 65], P[:, si, :], nav[:, tk, 65 * h:65 * h + 65], start=(si == 0), stop=(si == ns - 1)),
                     reads=[bP, B("nav", tk), B("navones")], writes=[bpo])
            if h < 5:
                return
            o2 = qi % 2
            po3 = po[:, 0:390].rearrange("p (h c) -> p h c", h=6, c=65)
            S.dve(I("reciprocal", out=rden[o2][:, 0:6], in_=po3[:, :, 64]), reads=[bpo], writes=[B("rden", o2)])
            S.dve(I("tensor_tensor", out=otok[o2].rearrange("p (h c) -> p h c", h=6, c=64), in0=po3[:, :, 0:64],
                    in1=rden[o2][:, 0:6].unsqueeze(2).broadcast_to([128, 6, 64]), op=ALU.mult),
                  reads=[bpo, B("rden", o2)], writes=[B("otok", o2)])
            for c3 in range(3):
                S.pe(I("transpose", pb[:, 128 * c3:128 * c3 + 128], otok[o2][:, 128 * c3:128 * c3 + 128], identb[:]),
                     reads=[B("otok", o2), B("identb")], writes=[PBB])
            S.act(I("copy", out=obuf[:, 0:3, 128 * tq:128 * tq + 128], in_=pb[:, 0:384].rearrange("p (a b) -> p a b", a=3, b=128)),
                  reads=[PBB], writes=[B("obuf", tq)])

        LA = 1
        for u in range(len(units) + LA):
            if u < len(units):
                na_st(u)
            if u >= LA:
                na_pv(u - LA)

    def gqa_phase(s, l):
        S.phase = "gqa_phase" + str(l)
        S.barrier()
        cv = Carver()
        gaq = cv(BF16, 3, T)
        gak = cv(BF16, 2, T)
        gav = cv(BF16, NTT, 130)
        cosv = cv(F32, NL)
        sinv = cv(F32, NL)
        sq = cv(BF16, 512)
        sd = cv(F32, 512)
        rstd = cv(F32, 512)
        qn = cv(F32, 512)
        qnb = cv(BF16, 512)
        t1 = cv(F32, 512)
        t2 = cv(F32, 512)
        gaP = [cv(BF16, 512) for _ in range(6)]
        otok = cv(BF16, 4, 384)
        rden = [cv(F32, 8) for _ in range(2)]
        gav4 = gav.rearrange("p t (g c) -> p t g c", g=2, c=65)
        wv = wview(1, BF16, 8, 768)
        wb = [B("wbuf", 1)]
        S.dma(I("dma_start", out=wv[:, :, 0:384], in_=win_v[l][:, :, C_GAQ:C_GAQ + 384]), writes=wb, q="pool")
        for g in range(2):
            for d in range(2):
                S.dma(I("dma_start", out=wv[:, :, 384 + 128 * g + 64 * d:384 + 128 * g + 64 * d + 64],
                                                      in_=win_v[l][:, :, C_GAK + 64 * g:C_GAK + 64 * g + 64]), writes=wb, q="pool")
        S.dma(I("dma_start", out=wv[:, :, 640:768], in_=win_v[l][:, :, C_GAV:C_GAV + 128]), writes=wb, q="pool")
        S.dma(I("dma_start", out=cosv, in_=cos_d), writes=[B("cos")])
        S.dma(I("dma_start", out=sinv, in_=sin_d), writes=[B("cos")])
        S.pool(I("memset", gav4[:, :, :, 64:65], 1.0), writes=[B("gavones")])
        TB = lambda n: B("gatmp", n)
        for (t0, n) in BLOCKS:
            for j in range(5):
                if j < 3 and t0 < LC and l == DEPTH - 1:
                    continue
                ps, bps = bank("proj", [0, 1, 2, 3])
                for k in range(8):
                    S.pe(I("matmul", ps[:, 0:n], wv[:, k, 128 * j:128 * j + 128], ubuf[:, k, t0:t0 + n],
                                                                     start=(k == 0), stop=(k == 7)),
                         reads=wb + ub(t0, n), writes=[bps])
                S.act(I("activation", out=sq[:, 0:n], in_=ps[:, 0:n], func=AF.Square), reads=[bps], writes=[TB("sq")])
                pm, bpm = bank("gams", [4, 5])
                S.pe(I("matmul", pm[:, 0:n], bones[:], sq[:, 0:n], start=True, stop=True), reads=[TB("sq"), B("bones")], writes=[bpm])
                S.act(I("activation", out=sd[:, 0:n], in_=pm[:, 0:n], func=AF.Sqrt, bias=epsrms[:, 0:1], scale=1.0),
                      reads=[bpm, B("eps")], writes=[TB("sd")])
                S.dve(I("reciprocal", out=rstd[:, 0:n], in_=sd[:, 0:n]), reads=[TB("sd")], writes=[TB("rstd")])
                wi = 0 if j < 3 else 1
                S.dve(I("scalar_tensor_tensor", out=qn[:, 0:n], in0=ps[:, 0:n], scalar=qkwT[:, l, wi:wi + 1], in1=rstd[:, 0:n],
                                                                      op0=ALU.mult, op1=ALU.mult),
                      reads=[bps, TB("rstd"), B("qkwT")], writes=[TB("qn")])
                dst = gaq[:, j, t0:t0 + n] if j < 3 else gak[:, j - 3, t0:t0 + n]
                wr = [B("gaqk", tt) for tt in tiles_of(t0, n)]
                if t0 >= LC:
                    S.pool(I("tensor_copy", out=qnb[:, 0:n], in_=qn[:, 0:n]), reads=[TB("qn")], writes=[TB("qnb")])
                    pr, bpr = bank("gams", [4, 5])
                    S.pe(I("matmul", pr[:, 0:n], rotm[:], qnb[:, 0:n], start=True, stop=True), reads=[TB("qnb"), B("rotm")], writes=[bpr])
                    S.dve(I("tensor_tensor", out=t1[:, 0:n], in0=qn[:, 0:n], in1=cosv[:, t0 - LC:t0 - LC + n], op=ALU.mult),
                          reads=[TB("qn"), B("cos")], writes=[TB("t1")])
                    S.dve(I("tensor_tensor", out=t2[:, 0:n], in0=pr[:, 0:n], in1=sinv[:, t0 - LC:t0 - LC + n], op=ALU.mult),
                          reads=[bpr, B("cos")], writes=[TB("t2")])
                    S.pool(I("tensor_tensor", out=dst, in0=t1[:, 0:n], in1=t2[:, 0:n], op=ALU.add),
                           reads=[TB("t1"), TB("t2")], writes=wr)
                else:
                    S.pool(I("tensor_copy", out=dst, in_=qn[:, 0:n]), reads=[TB("qn")], writes=wr)
        for tt in range(NTT):
            ps, bps = bank("proj", [0, 1, 2, 3])
            for k in range(8):
                S.pe(I("matmul", ps[:, 0:128], ubuf[:, k, 128 * tt:128 * tt + 128], wv[:, k, 640:768],
                                                         start=(k == 0), stop=(k == 7)),
                     reads=wb + [B("ubuf", tt)], writes=[bps])
            S.dve(I("tensor_copy", out=gav4[:, tt, :, 0:64], in_=ps[:, 0:128].rearrange("p (g c) -> p g c", g=2, c=64)),
                  reads=[bps], writes=[B("gav", tt)])
        qblocks = []
        if l < DEPTH - 1:
            qblocks.append((0, 256, [0, 1]))
        for i in range(4):
            qblocks.append((256 + 512 * i, 512, list(range(NTT))))
        pi = 0
        for (t0, n, ktiles) in qblocks:
            nsub = n // 128
            nk = len(ktiles)
            for h in range(6):
                g = h // 3
                c = h // 2
                p0 = 64 * (h % 2)
                po, bpo = bank("gapo", [5, 6])
                S.dve(I("memset", po[:, 0:65 * nsub], 0.0), writes=[bpo])
                Ps = {}

                def st(ki):
                    nonlocal pi
                    tk = ktiles[ki]
                    pa, bpa = bank("gapa", [0, 1, 2, 3, 4])
                    P = gaP[pi % 6]
                    bP = B("gaP", pi % 6)
                    pi += 1
                    Ps[ki] = (P, bP)
                    S.pe(I("matmul", pa[:, 0:n], gak[p0:p0 + 64, g, 128 * tk:128 * tk + 128], gaq[p0:p0 + 64, c, t0:t0 + n],
                                                        start=True, stop=True),
                         reads=[B("gaqk", tk)] + [B("gaqk", tt) for tt in tiles_of(t0, n)], writes=[bpa])
                    S.act(I("activation", out=P[:, 0:n], in_=pa[:, 0:n], func=AF.Exp, scale=0.125), reads=[bpa], writes=[bP])

                def pv(ki):
                    tk = ktiles[ki]
                    P, bP = Ps.pop(ki)
                    for qs in range(nsub):
                        S.pe(I("matmul", po[:, 65 * qs:65 * qs + 65], P[:, 128 * qs:128 * qs + 128], gav[:, tk, 65 * g:65 * g + 65],
                                                                        start=False, stop=(ki == nk - 1), skip_group_check=True),
                             reads=[bP, B("gav", tk), B("gavones")], writes=[bpo])

                GLA_ = 3
                for step in range(nk + GLA_):
                    if step < nk:
                        st(step)
                    if step >= GLA_:
                        pv(step - GLA_)
                o2 = h % 2
                po3 = po[:, 0:65 * nsub].rearrange("p (q c) -> p q c", q=nsub, c=65)
                S.dve(I("reciprocal", out=rden[o2][:, 0:nsub], in_=po3[:, :, 64]), reads=[bpo], writes=[B("rden", o2)])
                S.dve(I("tensor_tensor", out=otok[:, 0:nsub, 64 * h:64 * h + 64], in0=po3[:, :, 0:64],
                                                                   in1=rden[o2][:, 0:nsub].unsqueeze(2).broadcast_to([128, nsub, 64]), op=ALU.mult),
                      reads=[bpo, B("rden", o2)], writes=[B("gaotok")])
            for qs in range(nsub):
                tq = t0 // 128 + qs
                for c3 in range(3):
                    S.pe(I("transpose", pb[:, 128 * c3:128 * c3 + 128], otok[:, qs, 128 * c3:128 * c3 + 128], identb[:]),
                         reads=[B("gaotok"), B("identb")], writes=[PBB])
                S.act(I("copy", out=obuf[:, 5:8, 128 * tq:128 * tq + 128], in_=pb[:, 0:384].rearrange("p (a b) -> p a b", a=3, b=128)),
                      reads=[PBB], writes=[B("obuf", tq)])

    def gla_phase(s, l):
        S.phase = "gla_phase" + str(l)
        S.barrier()
        cv = Carver()
        glv = cv(BF16, NTT, 256)
        glgw = cv(BF16, NTT, 256)
        qt = [cv(BF16, T) for _ in range(2)]
        kt = [cv(BF16, T) for _ in range(2)]
        ktok = cv(BF16, NTT, 128)
        Dall = cv(F32, 2, 36)
        S32 = cv(F32, 37, 64)
        Sbf = [cv(BF16, 36, 64) for _ in range(2)]
        qf = cv(F32, 512)
        kf = cv(F32, 512)
        lrf = cv(F32, 512)
        smask = cv(F32, 512)
        tAs = [cv(F32, 512) for _ in range(2)]
        tBs = [cv(F32, 512) for _ in range(2)]
        tCs = [cv(F32, 512) for _ in range(2)]
        tD = tCs[0]
        Abf = [cv(BF16, 2, 256) for _ in range(2)]
        qmb = [cv(BF16, 2, 4, 128) for _ in range(2)]
        osq = cv(F32, 256)
        sg = osq
        on = cv(F32, 256)
        otok = [cv(BF16, 256) for _ in range(2)]
        st4 = [cv(F32, 8) for _ in range(2)]
        TB = lambda n: B("gltmp", n)
        TB0 = TB
        wv = wview(0, BF16, 8, 800)
        wb = [B("wbuf", 0)]
        S.dma(I("dma_start", out=wv, in_=win_v[l][:, :, C_GLQ:C_GLQ + 800]), writes=wb, q="pool")
        S.pool(I("memset", smask, 1.0), writes=[TB0("smask")])
        S.pool(I("memset", smask.rearrange("p (c t) -> p c t", c=8, t=64)[:, :, 0:1], 0.0), writes=[TB0("smask")])
        for tt in range(NTT):
            ps, bps = bank("proj", [0, 1, 2, 3])
            for k in range(8):
                S.pe(I("matmul", ps[:, 0:512], ubuf[:, k, 128 * tt:128 * tt + 128], wv[:, k, 256:768],
                                                         start=(k == 0), stop=(k == 7)),
                     reads=wb + [B("ubuf", tt)], writes=[bps])
            S.dve(I("tensor_copy", out=glv[:, tt, :], in_=ps[:, 0:256]), reads=[bps], writes=[B("glv", tt)])
            S.act(I("activation", out=sg, in_=ps[:, 256:512], func=AF.Silu), reads=[bps], writes=[TB0("osq")])
            S.pool(I("tensor_tensor", out=glgw[:, tt, :], in0=sg, in1=glnw[:, l, :], op=ALU.mult),
                   reads=[TB0("osq"), B("glnw")], writes=[B("glgw", tt)])
        import os as _os
        _gs = int(_os.environ.get('GLASTOP', '9'))
        _sub = int(_os.environ.get('GLASUB', '9'))
        if _gs <= 0:
            return
        for (t0, n) in BLOCKS:
            nch = n // 64
            c0 = t0 // 64
            pq, bpq = bank("proj", [0, 1, 2, 3])
            pk, bpk = bank("proj", [0, 1, 2, 3])
            pl, bpl = bank("proj", [0, 1, 2, 3])
            for (pp, col, M) in ((pq, 0, 128), (pk, 128, 128), (pl, 768, 32)):
                bb = {id(pq): bpq, id(pk): bpk, id(pl): bpl}[id(pp)]
                for k in range(8):
                    S.pe(I("matmul", pp[0:M, 0:n], wv[:, k, col:col + M], ubuf[:, k, t0:t0 + n],
                                                                              start=(k == 0), stop=(k == 7)),
                         reads=wb + ub(t0, n), writes=[bb])
            S.act(I("activation", out=qf[:, 0:n], in_=pq[:, 0:n], func=AF.Identity, scale=float(32 ** -0.5)), reads=[bpq], writes=[TB0("qf")])
            S.dve(I("tensor_copy", out=kf[:, 0:n], in_=pk[:, 0:n]), reads=[bpk], writes=[TB0("kf")])
            S.dve(I("tensor_copy", out=lrf[0:32, 0:n], in_=pl[0:32, 0:n]), reads=[bpl], writes=[TB0("lrf")])
            for ed in range(2):
                tA, tB_, tC = tAs[ed], tBs[ed], tCs[ed]
                TBd = lambda n_, ed=ed: B("gltmp", n_, ed)
                pz, bpz = bank("glz", [4, 5])
                S.pe(I("matmul", pz[:, 0:n], wa2[:, l, ed, :], lrf[0:32, 0:n], start=True, stop=True),
                     reads=[TB0("lrf"), B("wa2")], writes=[bpz])
                S.act(I("activation", out=tA[:, 0:n], in_=pz[:, 0:n], func=AF.Exp, scale=-1.0, bias=nbaT[:, l, ed:ed + 1]),
                      reads=[bpz, B("nbaT")], writes=[TBd("tA")])
                S.act(I("activation", out=tB_[:, 0:n], in_=tA[:, 0:n], func=AF.Ln, bias=1.0, scale=1.0), reads=[TBd("tA")], writes=[TBd("tB")])
                S.dve(I("tensor_tensor_scan", out=tC[:, 0:n], data0=smask[:, 0:n], data1=tB_[:, 0:n], initial=0.0, op0=ALU.mult, op1=ALU.add),
                      reads=[TB0("smask"), TBd("tB")], writes=[TBd("tC")])
                cum = tC
                bcum = TBd("tC")
                if ed == 1:
                    tC3 = tC[:, 0:n].rearrange("p (c t) -> p c t", c=nch, t=64)
                    S.dve(I("tensor_tensor", out=tD[:, 0:n].rearrange("p (c t) -> p c t", c=nch, t=64),
                                                                         in0=tC3[:, :, 63:64].broadcast_to([128, nch, 64]), in1=tC3, op=ALU.subtract),
                          reads=[TBd("tC")], writes=[B("gltmp", "tC", 0)])
                    S.dve(I("tensor_tensor", out=tD[:, 0:n], in0=tD[:, 0:n], in1=tB_[:, 0:n], op=ALU.add), reads=[B("gltmp", "tC", 0), TBd("tB")], writes=[B("gltmp", "tC", 0)])
                    cum = tD
                    bcum = B("gltmp", "tC", 0)
                S.act(I("activation", out=tA[:, 0:n], in_=cum[:, 0:n], func=AF.Exp, scale=-1.0 / 16.0), reads=[bcum], writes=[TBd("tA")])
                S.act(I("activation", out=tB_[:, 0:n], in_=cum[:, 0:n], func=AF.Exp, scale=1.0 / 16.0), reads=[bcum, TBd("tA")], writes=[TBd("tB")])
                dcol = 63 if ed == 0 else 0
                S.pool(I("tensor_copy",
                    out=Dall[:, ed, c0:c0 + nch], in_=tA[:, 0:n].rearrange("p (c t) -> p c t", c=nch, t=64)[:, :, dcol]),
                    reads=[TBd("tA")], writes=[B("Dall")])
                wr = [B("glqk", ed, tt) for tt in tiles_of(t0, n)]
                S.dve(I("tensor_tensor", out=qt[ed][:, t0:t0 + n], in0=qf[:, 0:n], in1=tA[:, 0:n], op=ALU.mult),
                      reads=[TB0("qf"), TBd("tA")], writes=wr)
                S.pool(I("tensor_tensor", out=kt[ed][:, t0:t0 + n], in0=kf[:, 0:n], in1=tB_[:, 0:n], op=ALU.mult),
                       reads=[TB0("kf"), TBd("tB")], writes=wr)
        if _gs <= 1:
            return
        ords = [list(range(36)), [3, 2, 1, 0] + list(range(35, 3, -1))]
        poss = []
        for ed in range(2):
            pos = [0] * 36
            for j, ci in enumerate(ords[ed]):
                pos[ci] = j
            poss.append(pos)
        for ed in range(2):
            pos = poss[ed]
            for tt in range(NTT):
                S.pe(I("transpose", pb[:, 128 * (tt % 4):128 * (tt % 4) + 128], kt[ed][:, 128 * tt:128 * tt + 128], identb[:]),
                     reads=[B("glqk", ed, tt), B("identb")], writes=[PBB])
                if tt % 4 == 3 or tt == NTT - 1:
                    n4 = tt % 4 + 1
                    ta = tt - n4 + 1
                    S.act(I("copy", out=ktok[:, ta:ta + n4, :], in_=pb[:, 0:128 * n4].rearrange("p (a b) -> p a b", a=n4, b=128)),
                          reads=[PBB], writes=[B("ktok")])
            if _sub <= 1:
                continue
            S.pool(I("memset", S32[:, 0, :], 0.0), writes=[B("S32")])
            for half in range(2):
              for grp in range(0, 18, 8):
                pw, bpw = bank("glw", [0, 1, 2, 3])
                cis = [2 * t_ + half for t_ in range(grp, min(grp + 8, 18))]
                for gi, ci in enumerate(cis):
                    tt = ci // 2
                    for h in range(4):
                        S.pe(I("matmul",
                            pw[32 * h:32 * h + 32, 64 * gi:64 * gi + 64], ktok[64 * half:64 * half + 64, tt, 32 * h:32 * h + 32],
                            glv[64 * half:64 * half + 64, tt, 64 * h:64 * h + 64], start=True, stop=True, tile_position=(64 * half, 32 * h)),
                            reads=[B("ktok"), B("glv", tt)], writes=[bpw])
                for gi, ci in enumerate(cis):
                    S.act(I("activation",
                        out=S32[:, 1 + pos[ci], :], in_=pw[:, 64 * gi:64 * gi + 64], func=AF.Identity, scale=Dall[:, ed, ci:ci + 1]),
                        reads=[bpw, B("Dall")], writes=[B("S32")])
            if _sub <= 2:
                continue
            for j in range(36):
                ci = ords[ed][j]
                S.dve(I("scalar_tensor_tensor", out=S32[:, j + 1, :], in0=S32[:, j, :], scalar=Dall[:, ed, ci:ci + 1], in1=S32[:, j + 1, :],
                                                                      op0=ALU.mult, op1=ALU.add),
                      reads=[B("S32"), B("Dall")], writes=[B("S32")])
            if _sub <= 3:
                continue
            S.dve(I("tensor_copy", out=Sbf[ed], in_=S32[:, 0:36, :]), reads=[B("S32")], writes=[B("Sbf", ed)])
        if _gs <= 2:
            return
        tiles_out = range(NTT) if l < DEPTH - 1 else range(2, NTT)
        tiles_out = list(tiles_out)

        def g_st1(oi):
            tt = tiles_out[oi]
            pa, bpa = bank("glpa", [0, 1, 2, 3])
            A = Abf[oi % 2]
            bA = B("Abf", oi % 2)
            qm = qmb[oi % 2]
            bqm = B("qm", oi % 2)
            for ed in range(2):
                for h in range(4):
                    eng = S.pool if (h % 2 == 0) else S.dve
                    eng(I("tensor_scalar_mul", out=qm[:, ed, h, :], in0=qt[ed][:, 128 * tt:128 * tt + 128], scalar1=hmask[:, h:h + 1]),
                        reads=[B("glqk", ed, tt), B("hmask")], writes=[bqm])
            for half in range(2):
                cols = slice(128 * tt + 64 * half, 128 * tt + 64 * half + 64)
                for ed in range(2):
                    for h in range(4):
                        S.pe(I("matmul",
                            pa[64 * half:64 * half + 64, 256 * ed + 64 * h:256 * ed + 64 * h + 64],
                            kt[ed][:, cols], qm[:, ed, h, 64 * half:64 * half + 64], start=True, stop=True,
                            tile_position=(0, 64 * half)),
                            reads=[B("glqk", ed, tt), bqm], writes=[bpa])
            S.dve(I("tensor_tensor", out=A, in0=pa[:, 0:512].rearrange("p (a b) -> p a b", a=2, b=256), in1=trim[:], op=ALU.mult),
                  reads=[bpa, B("trim")], writes=[bA])

        def g_st2(oi):
            tt = tiles_out[oi]
            A = Abf[oi % 2]
            bA = B("Abf", oi % 2)
            qm = qmb[oi % 2]
            bqm = B("qm", oi % 2)
            po, bpo = bank("glpo", [4, 5])
            for half in range(2):
                ci = 2 * tt + half
                for h in range(4):
                    for ed in range(2):
                        S.pe(I("matmul",
                            po[64 * half:64 * half + 64, 64 * h:64 * h + 64], A[64 * half:64 * half + 64, ed, 64 * h:64 * h + 64],
                            glv[64 * half:64 * half + 64, tt, 64 * h:64 * h + 64], start=(ed == 0), stop=False,
                            tile_position=(64 * half, 64 * half)),
                            reads=[bA, B("glv", tt)], writes=[bpo])
                    for ed in range(2):
                        pj = poss[ed][ci]
                        S.pe(I("matmul",
                            po[64 * half:64 * half + 64, 64 * h:64 * h + 64], qm[:, ed, h, 64 * half:64 * half + 64],
                            Sbf[ed][:, pj, :], start=False, stop=(ed == 1),
                            tile_position=(0, 64 * half)),
                            reads=[bqm, B("Sbf", ed)], writes=[bpo])
            o2 = oi % 2
            S.act(I("activation", out=osq, in_=po[:, 0:256], func=AF.Square), reads=[bpo], writes=[TB0("osq")])
            S.dve(I("reduce_sum", out=st4[o2][:, 0:4], in_=osq.rearrange("p (h c) -> p h c", h=4, c=64), axis=AX.X),
                  reads=[TB0("osq")], writes=[B("st4", o2)])
            S.dve(I("tensor_scalar", out=st4[o2][:, 0:4], in0=st4[o2][:, 0:4], scalar1=1.0 / 64.0, scalar2=RMS_EPS, op0=ALU.mult, op1=ALU.add),
                  reads=[B("st4", o2)], writes=[B("st4", o2)])
            S.act(I("activation", out=st4[o2][:, 4:8], in_=st4[o2][:, 0:4], func=AF.Sqrt), reads=[B("st4", o2)], writes=[B("st4", o2)])
            S.dve(I("reciprocal", out=st4[o2][:, 0:4], in_=st4[o2][:, 4:8]), reads=[B("st4", o2)], writes=[B("st4", o2)])
            S.dve(I("tensor_tensor", out=on.rearrange("p (h c) -> p h c", h=4, c=64), in0=po[:, 0:256].rearrange("p (h c) -> p h c", h=4, c=64),
                                                        in1=st4[o2][:, 0:4].unsqueeze(2).broadcast_to([128, 4, 64]), op=ALU.mult),
                  reads=[bpo, B("st4", o2)], writes=[TB0("on")])
            S.pool(I("tensor_tensor", out=otok[o2], in0=on, in1=glgw[:, tt, :], op=ALU.mult),
                   reads=[TB0("on"), B("glgw", tt)], writes=[B("glotok", o2)])

        def g_st3(oi):
            tt = tiles_out[oi]
            o2 = oi % 2
            for c2 in range(2):
                S.pe(I("transpose", pb[:, 512 + 128 * c2:512 + 128 * c2 + 128], otok[o2][:, 128 * c2:128 * c2 + 128], identb[:]),
                     reads=[B("glotok", o2), B("identb")], writes=[PBB])
            S.act(I("copy", out=obuf[:, 3:5, 128 * tt:128 * tt + 128], in_=pb[:, 512:768].rearrange("p (a b) -> p a b", a=2, b=128)),
                  reads=[PBB], writes=[B("obuf", tt)])

        for oi in range(len(tiles_out) + 2):
            if oi < len(tiles_out):
                g_st1(oi)
            if 1 <= oi <= len(tiles_out):
                g_st2(oi - 1)
            if oi >= 2:
                g_st3(oi - 2)

    def layer_norm_block(tbuf, bk, n, outs):
        pmean, bpmean = bank("lnm", [4, 5])
        pmsq, bpmsq = bank("lnm", [4, 5])
        for oc in range(8):
            S.pe(I("matmul", pmean[:, 0:n], onesD[:], tbuf[:, oc, 0:n], start=(oc == 0), stop=(oc == 7)),
                 reads=[bk(oc), B("onesD")], writes=[bpmean])
        for oc in range(8):
            sqb = lnsq[oc % 2]
            S.act(I("activation", out=sqb[:, 0:n], in_=tbuf[:, oc, 0:n], func=AF.Square), reads=[bk(oc)], writes=[B("lnsq", oc % 2)])
            S.pe(I("matmul", pmsq[:, 0:n], onesD[:], sqb[:, 0:n], start=(oc == 0), stop=(oc == 7)),
                 reads=[B("lnsq", oc % 2), B("onesD")], writes=[bpmsq])
        S.act(I("copy", out=lnmean[:, 0:n], in_=pmean[:, 0:n]), reads=[bpmean], writes=[B("lnmean")])
        S.pool(I("tensor_tensor", out=lnm2[:, 0:n], in0=lnmean[:, 0:n], in1=lnmean[:, 0:n], op=ALU.mult), reads=[B("lnmean")], writes=[B("lnm2")])
        S.dve(I("tensor_tensor", out=lnm2[:, 0:n], in0=pmsq[:, 0:n], in1=lnm2[:, 0:n], op=ALU.subtract), reads=[bpmsq, B("lnm2")], writes=[B("lnm2")])
        S.act(I("activation", out=lnm2[:, 0:n], in_=lnm2[:, 0:n], func=AF.Ln, bias=epsln[:, 0:1], scale=1.0), reads=[B("lnm2"), B("eps")], writes=[B("lnm2")])
        S.act(I("activation", out=lnrstd[:, 0:n], in_=lnm2[:, 0:n], func=AF.Exp, scale=-0.5), reads=[B("lnm2")], writes=[B("lnrstd")])
        for oc in range(8):
            S.dve(I("tensor_tensor", out=tbuf[:, oc, 0:n], in0=tbuf[:, oc, 0:n], in1=lnmean[:, 0:n], op=ALU.subtract),
                  reads=[bk(oc), B("lnmean")], writes=[bk(oc)])
            S.pool(I("tensor_tensor", out=tbuf[:, oc, 0:n], in0=tbuf[:, oc, 0:n], in1=lnrstd[:, 0:n], op=ALU.mult),
                   reads=[bk(oc), B("lnrstd")], writes=[bk(oc)])
            for (eng, dstf, scf, bif, wrf) in outs:
                if eng == "act":
                    S.act(I("activation", out=dstf(oc), in_=tbuf[:, oc, 0:n], func=AF.Identity, scale=scf(oc), bias=bif(oc)),
                          reads=[bk(oc)] + MD, writes=wrf(oc))
                else:
                    S.dve(I("tensor_scalar", out=dstf(oc), in0=tbuf[:, oc, 0:n], scalar1=scf(oc), scalar2=bif(oc),
                            op0=ALU.mult, op1=ALU.add),
                          reads=[bk(oc)] + MD, writes=wrf(oc))

    self_cv = [None]
    lnsq = [None, None]
    lnmean = lnm2 = lnrstd = None

    def outproj_phase(s, l):
        S.phase = "outproj_phase" + str(l)
        nonlocal lnsq, lnmean, lnm2, lnrstd
        S.barrier()
        cv = Carver()
        self_cv[0] = cv
        tbs = [cv(F32, 8, 512) for _ in range(2)]
        xss = [cv(F32, 8, 512) for _ in range(2)]
        lnsq = [cv(F32, 512) for _ in range(2)]
        lnmean = cv(F32, 512)
        lnm2 = cv(F32, 512)
        lnrstd = cv(F32, 512)
        wv = wview(1, BF16, 8, 1024)
        wb = [B("wbuf", 1)]
        S.dma(I("dma_start", out=wv, in_=wout_d[l].rearrange("(k p) c -> p k c", p=128)), writes=wb, q="pool")
        bi = 0
        for (t0, n) in BLOCKS:
            if t0 < LC and l == DEPTH - 1:
                continue
            tb = tbs[bi % 2]
            xs = xss[bi % 2]
            par = bi % 2
            btb = lambda oc, par=par: B("tb", par, oc)
            bxs = lambda oc, par=par: B("xs", par, oc)
            bxs_all = [bxs(oc) for oc in range(8)]
            bi += 1
            mi = msel(s, t0)
            xrb = [B("xres", s, tt) for tt in tiles_of(t0, n)]
            S.dma(I("dma_start", out=xs[:, :, 0:n], in_=xres[s][:, :, t0:t0 + n]), reads=xrb, writes=bxs_all)
            for oc in range(8):
                py, bpy = bank("proj", [0, 1, 2, 3])
                for k in range(8):
                    S.pe(I("matmul", py[:, 0:n], wv[:, k, 128 * oc:128 * oc + 128], obuf[:, k, t0:t0 + n],
                                                                       start=(k == 0), stop=(k == 7)),
                         reads=wb + [B("obuf", tt) for tt in tiles_of(t0, n)], writes=[bpy])
                S.dve(I("scalar_tensor_tensor", out=tb[:, oc, 0:n], in0=py[:, 0:n], scalar=mder[:, l, 1, oc, mi:mi + 1], in1=xs[:, oc, 0:n],
                                                                             op0=ALU.mult, op1=ALU.add),
                      reads=[bpy, bxs(oc)] + MD, writes=[btb(oc)])
            outs = [
                ("act", lambda oc, n=n, xs=xs: xs[:, oc, 0:n], lambda oc: lnT[:, l, 0, oc:oc + 1], lambda oc: lnT[:, l, 1, oc:oc + 1], lambda oc, bxs=bxs: [bxs(oc)]),
                ("dve", lambda oc, t0=t0, n=n: ubuf[:, oc, t0:t0 + n], lambda oc, mi=mi: mder[:, l, 4, oc, mi:mi + 1], lambda oc, mi=mi: mder[:, l, 5, oc, mi:mi + 1],
                 lambda oc, t0=t0, n=n: ub(t0, n)),
            ]
            layer_norm_block(tb, btb, n, outs)
            S.dma(I("dma_start", out=xres[s][:, :, t0:t0 + n], in_=xs[:, :, 0:n]), reads=bxs_all, writes=xrb)

    def ffn_phase(s, l):
        S.phase = "ffn_phase" + str(l)
        nonlocal lnsq, lnmean, lnm2, lnrstd
        S.barrier()
        last = (l == DEPTH - 1)
        cv = Carver()
        self_cv[0] = cv
        hT = cv(BF16, NHC, 1024)
        w2 = [cv(BF16, NHC, 256) for _ in range(2)]
        lnsq = [cv(F32, 512) for _ in range(2)]
        lnmean = cv(F32, 512)
        lnm2 = cv(F32, 512)
        lnrstd = cv(F32, 512)
        sa = [cv(F32, 512) for _ in range(2)]
        ostage = [cv(F32, 1024) for _ in range(2)] if last else None
        tfull = obuf[:].rearrange("p k t -> p (k t)").bitcast(F32)[:, 0:8 * 1024].rearrange("p (k t) -> p k t", k=8, t=1024)
        parts = [[BLOCKS[1], BLOCKS[2]], [BLOCKS[3], BLOCKS[4]]]
        if not last:
            parts = [[BLOCKS[0]]] + parts
        wf1 = wf1_d[l].rearrange("(k p) c -> p k c", p=128)
        wf2 = wf2_d[l].rearrange("(j p) c -> p j c", p=128)
        for part in parts:
            pt0 = part[0][0]
            ptn = sum(n for (_, n) in part)
            mi = msel(s, pt0)
            nblk = len(part)
            btf = lambda bi_, oc: B("tfull", bi_, oc)
            btf_all = [btf(bi_, oc) for bi_ in range(nblk) for oc in range(8)]
            for js in range(11):
                wi = js % 2
                wv = wview(wi, BF16, 8, 512)
                wb = [B("wbuf", wi)]
                S.dma(I("dma_start", out=wv[:, :, 0:256], in_=wf1[:, :, 256 * js:256 * js + 256]), writes=wb, q="pool")
                S.dma(I("dma_start", out=wv[:, :, 256:512], in_=wf1[:, :, FH + 256 * js:FH + 256 * js + 256]), writes=wb, q="pool")
                for (t0, n) in part:
                    lo = t0 - pt0
                    for jj in range(2):
                        j = 2 * js + jj
                        pa, bpa = bank("ffa", [0, 1, 2])
                        pg, bpg = bank("ffb", [3, 4, 5])
                        for k in range(8):
                            S.pe(I("matmul", pa[:, 0:n], wv[:, k, 128 * jj:128 * jj + 128], ubuf[:, k, t0:t0 + n],
                                                                                      start=(k == 0), stop=(k == 7)),
                                 reads=wb + ub(t0, n), writes=[bpa])
                        for k in range(8):
                            S.pe(I("matmul", pg[:, 0:n], wv[:, k, 256 + 128 * jj:256 + 128 * jj + 128], ubuf[:, k, t0:t0 + n],
                                                                                      start=(k == 0), stop=(k == 7)),
                                 reads=wb + ub(t0, n), writes=[bpg])
                        si = j % 2
                        S.act(I("activation", out=sa[si][:, 0:n], in_=pa[:, 0:n], func=AF.Silu), reads=[bpa], writes=[B("sa", si)])
                        S.dve(I("tensor_tensor", out=hT[:, j, lo:lo + n], in0=pg[:, 0:n], in1=sa[si][:, 0:n], op=ALU.mult),
                              reads=[bpg, B("sa", si)], writes=[B("hT")])
            S.dma(I("dma_start", out=tfull[:, :, 0:ptn], in_=xres[s][:, :, pt0:pt0 + ptn]),
                  reads=[B("xres", s, tt) for tt in tiles_of(pt0, ptn)], writes=btf_all)
            for oc2 in range(4):
                w2v = w2[oc2 % 2]
                bw2 = [B("w2", oc2 % 2)]
                S.dma(I("dma_start", out=w2v, in_=wf2[:, :, 256 * oc2:256 * oc2 + 256]), writes=bw2, q="pool")
                for bi_, (t0, n) in enumerate(part):
                    lo = t0 - pt0
                    for oo in range(2):
                        oc = 2 * oc2 + oo
                        py, bpy = bank("ffy", [0, 1, 2, 3])
                        for j in range(NHC):
                            S.pe(I("matmul", py[:, 0:n], w2v[:, j, 128 * oo:128 * oo + 128], hT[:, j, lo:lo + n],
                                                                                        start=(j == 0), stop=(j == NHC - 1)),
                                 reads=bw2 + [B("hT")], writes=[bpy])
                        S.dve(I("scalar_tensor_tensor", out=tfull[:, oc, lo:lo + n], in0=py[:, 0:n], scalar=mder[:, l, 3, oc, mi:mi + 1],
                                                                                            in1=tfull[:, oc, lo:lo + n], op0=ALU.mult, op1=ALU.add),
                              reads=[bpy, btf(bi_, oc)] + MD, writes=[btf(bi_, oc)])
            for bi_, (t0, n) in enumerate(part):
                lo = t0 - pt0
                tbv = tfull[:, :, lo:lo + n]
                bkf = lambda oc, bi_=bi_: btf(bi_, oc)
                bkf_all = [bkf(oc) for oc in range(8)]
                xrb = [B("xres", s, tt) for tt in tiles_of(t0, n)]
                if not last:
                    outs = [
                        ("dve", lambda oc, t0=t0, n=n: ubuf[:, oc, t0:t0 + n], lambda oc, mi=mi: mder2[:, l, 0, oc, mi:mi + 1], lambda oc, mi=mi: mder2[:, l, 1, oc, mi:mi + 1],
                         lambda oc, t0=t0, n=n: ub(t0, n)),
                        ("act", lambda oc, tbv=tbv: tbv[:, oc, :], lambda oc: lnT[:, l, 2, oc:oc + 1], lambda oc: lnT[:, l, 3, oc:oc + 1], lambda oc, bkf=bkf: [bkf(oc)]),
                    ]
                    layer_norm_block(tbv, bkf, n, outs)
                    S.dma(I("dma_start", out=xres[s][:, :, t0:t0 + n], in_=tbv), reads=bkf_all, writes=xrb)
                else:
                    outs = [("act", lambda oc, tbv=tbv: tbv[:, oc, :], lambda oc: lnT[:, l, 2, oc:oc + 1], lambda oc: lnT[:, l, 3, oc:oc + 1], lambda oc, bkf=bkf: [bkf(oc)])]
                    layer_norm_block(tbv, bkf, n, outs)
                    for qs in range(n // 128):
                        tq = (t0 + 128 * qs) // 128
                        og = ostage[tq % 2]
                        for half in range(2):
                            pt, bpt = bank("ffo", [0, 1, 2, 3])
                            for kk in range(4):
                                k = 4 * half + kk
                                S.pe(I("transpose", pt[:, 128 * kk:128 * kk + 128], tbv[:, k, 128 * qs:128 * qs + 128], identf[:]),
                                     reads=[bkf(k), B("identf")], writes=[bpt])
                            S.act(I("copy", out=og[:, 512 * half:512 * half + 512], in_=pt[:, 0:512]),
                                  reads=[bpt], writes=[B("ostage", tq % 2)])
                        S.dma(I("dma_start", out=out_d[s, 128 * (tq - 2):128 * (tq - 2) + 128, :], in_=og),
                              reads=[B("ostage", tq % 2)], writes=[B("out")])

    for s in range(nseq):
        load_x(s)
        if stage == 1 and s == 0:
            dump("u0", ubuf[:], [128, 8, T], BF16, B("ubuf", NTT - 1))
        for l in range(DEPTH):
            if stage >= 2:
                na_phase(s, l)
            if stage >= 3:
                gla_phase(s, l)
            if stage >= 4:
                gqa_phase(s, l)
            if stage in (2, 3, 4) and s == 0 and l == 0:
                S.barrier()
                S.pool(I("memset", epsln[:], LN_EPS_S), reads=[B("obuf", tt) for tt in range(NTT)], writes=[B("eps")])
                dump("o0", obuf[:], [128, 8, T], BF16, B("eps"))
                break
            if stage >= 5:
                outproj_phase(s, l)
            if stage == 5 and s == 0 and l == 0:
                S.barrier()
                S.pool(I("memset", epsln[:], LN_EPS_S), reads=[B("ubuf", tt) for tt in range(NTT)], writes=[B("eps")])
                dump("u2", ubuf[:], [128, 8, T], BF16, B("eps"))
                dump("x1", xres[0], [128, 8, T], F32, B("eps"))
                break
            if stage >= 6:
                ffn_phase(s, l)
            if stage == 6 and s == 0 and l == 0:
                S.barrier()
                S.pool(I("memset", epsln[:], LN_EPS_S), reads=[B("ubuf", tt) for tt in range(NTT)], writes=[B("eps")])
                dump("u1n", ubuf[:], [128, 8, T], BF16, B("eps"))
                dump("x2", xres[0], [128, 8, T], F32, B("eps"))
                break


    S.emit(final_bufs=dump_list + [B("out")])
    S.close()
    global LAST_SCHED
    LAST_SCHED = S
    return nc


def host_consts():
    eye = np.eye(128, dtype=np.float32)
    onesd = np.full((128, 128), 1.0 / 1024.0, np.float32)
    bon = np.zeros((128, 128), np.float32)
    bon[:64, :64] = 1.0 / 64
    bon[64:, 64:] = 1.0 / 64
    R = np.zeros((64, 64), np.float32)
    for d in range(16):
        R[d, d + 16] = -1.0
        R[d + 16, d] = 1.0
        R[d + 32, d + 48] = -1.0
        R[d + 48, d + 32] = 1.0
    rl = np.zeros((128, 128), np.float32)
    rl[:64, :64] = R.T
    rl[64:, 64:] = R.T
    hm = np.zeros_like(eye)
    for h in range(4):
        hm[32 * h:32 * h + 32, h] = 1.0
    cm = np.stack([eye, onesd, bon, rl, hm], axis=1)
    sidx = np.arange(64)
    mf = (sidx[None, :] >= sidx[:, None]).astype(np.float32)
    mb = (sidx[None, :] <= sidx[:, None]).astype(np.float32)
    tri = np.zeros((128, 2, 256), np.float32)
    for half in range(2):
        tri[64 * half:64 * half + 64, 0, :] = np.tile(mf, (1, 4))
        tri[64 * half:64 * half + 64, 1, :] = np.tile(mb, (1, 4))
    t = np.arange(NL)
    row = (t // 64).astype(np.float32)
    col = (t % 64).astype(np.float32)
    inv_freq = (10000.0 ** (-np.arange(16, dtype=np.float32) / 16)).astype(np.float32)
    ang_r = row[:, None] * inv_freq
    ang_c = col[:, None] * inv_freq
    ang = np.concatenate([ang_r, ang_r, ang_c, ang_c], axis=-1).astype(np.float32)
    cosT = np.tile(np.cos(ang).T, (2, 1)).astype(np.float32)
    sinT = np.tile(np.sin(ang).T, (2, 1)).astype(np.float32)
    return cm, tri, np.ascontiguousarray(cosT), np.ascontiguousarray(sinT)


def prep_shared(inp):
    f = lambda a: np.ascontiguousarray(np.asarray(a, dtype=np.float32))
    cm, tri, cosT, sinT = host_consts()
    sh = {}
    sh["w_ada"] = f(inp["w_ada"])
    sh["b_adaT"] = f(np.asarray(inp["b_ada"]).reshape(DEPTH, 48, 128).transpose(0, 2, 1))
    sh["w_in"] = f(inp["w_in"])
    sh["w_out"] = f(inp["w_out"])
    sh["w_ffn_in"] = f(inp["w_ffn_in"])
    sh["w_ffn_out"] = f(inp["w_ffn_out"])
    ln = np.stack([np.asarray(inp[k]) for k in ("ln1_g", "ln1_b", "ln2_g", "ln2_b")], axis=1)
    sh["lnT"] = f(ln.reshape(DEPTH, 4, 8, 128).transpose(3, 0, 1, 2))
    valid, dri, dci = na_index_tables()
    rpb = np.asarray(inp["na_rpb"], dtype=np.float32)
    g = rpb[:, :, dri, dci]
    g = np.where(valid[None, None], g, np.float32(-30000.0))
    sh["nab"] = f(g.transpose(0, 3, 1, 2, 4).reshape(DEPTH, 128, 6, NCOMBO * 128))
    wa2 = np.zeros((DEPTH, 2, 32, 128), np.float32)
    gw = np.asarray(inp["gla_wa2"], dtype=np.float32)
    wa2[:, 0, 0:16, :] = gw[:, 0]
    wa2[:, 1, 16:32, :] = gw[:, 1]
    sh["wa2"] = wa2
    sh["nbaT"] = f(np.asarray(inp["gla_ba"]).transpose(2, 0, 1))
    sh["glnw"] = f(np.tile(np.asarray(inp["gla_norm_w"]), (1, 4)).reshape(DEPTH, 1, 256))
    qk = np.stack([np.tile(np.asarray(inp["gqa_qnorm_w"]), (1, 2)), np.tile(np.asarray(inp["gqa_knorm_w"]), (1, 2))], axis=2)
    sh["qkwT"] = f(qk.transpose(1, 0, 2))
    sh["cosT"] = cosT
    sh["sinT"] = sinT
    sh["cmats"] = f(cm)
    sh["trim"] = f(tri)
    return sh


def prep_core(inp, core, sh):
    f = lambda a: np.ascontiguousarray(np.asarray(a, dtype=np.float32))
    b0 = core * NSEQ
    m = dict(sh)
    m["x"] = f(inp["x"][b0:b0 + NSEQ])
    m["ctx"] = f(inp["ctx"][b0:b0 + NSEQ])
    cc = np.stack([np.asarray(inp["c"][b0]), np.asarray(inp["c"][b0 + 1]), np.asarray(inp["c_ctx"])], axis=1)
    m["cT"] = f(cc.reshape(8, 128, 3).transpose(1, 0, 2))
    return m


def kernel(**inputs):
    sh = prep_shared(inputs)
    nc = build_nc()
    in_maps = [prep_core(inputs, c, sh) for c in range(8)]
    res = run_bass_kernel_spmd(nc, in_maps, core_ids=list(range(8)))
    out = np.concatenate([np.asarray(r["out"]) for r in res.results], axis=0)
    return out.astype(np.float32)
```

```python
import contextlib
import numpy as np
import concourse.bass as bass
import concourse.mybir as mybir
from concourse.bass_utils import run_bass_kernel_spmd

F32 = mybir.dt.float32
BF16 = mybir.dt.bfloat16
AF = mybir.ActivationFunctionType
ALU = mybir.AluOpType
AX = mybir.AxisListType

COMPUTE = ("pe", "act", "dve", "pool")
NDMA_SEM = 6


class Buf:
    __slots__ = ("last_w", "readers", "excl")

    def __init__(self, excl=False):
        self.last_w = None
        self.readers = []
        self.excl = excl


class Sched:
    def __init__(self, nc):
        self.nc = nc
        self.ops = {e: [] for e in ("pe", "act", "dve", "pool", "sp")}
        self.stack = contextlib.ExitStack()
        self.nalloc = 0
        self.bufs = {}
        self.pending = {e: set() for e in self.ops}
        self.phase = "pro"

    def B(self, *key):
        b = self.bufs.get(key)
        if b is None:
            b = self.bufs[key] = Buf(excl=(key[0] in ("pf", "pb")))
        return b

    def sbuf(self, shape, dtype, name=None):
        self.nalloc += 1
        return self.stack.enter_context(self.nc.sbuf_tensor("sb_" + (name or f"t{self.nalloc}"), list(shape), dtype))

    def psum(self, shape, dtype, name=None):
        self.nalloc += 1
        return self.stack.enter_context(self.nc.psum_tensor("ps_" + (name or f"t{self.nalloc}"), list(shape), dtype))

    def add(self, eng, fn, reads=(), writes=(), dma=False):
        idx = len(self.ops[eng])
        deps = set(self.pending[eng])
        self.pending[eng] = set()
        for b in reads:
            if b.last_w is not None:
                deps.add(b.last_w)
            if b.excl:
                deps.update(r for r in b.readers if r[0] != eng)
        for b in writes:
            if b.last_w is not None:
                deps.add(b.last_w)
            deps.update(b.readers)
        if eng == "pe":
            deps = {d for d in deps if d[0] != "pe"}
        self.ops[eng].append({"fn": fn, "deps": deps, "dma": dma, "marked": False, "ph": self.phase})
        me = (eng, idx)
        for b in reads:
            b.readers.append(me)
        for b in writes:
            b.last_w = me
            b.readers = []
        return me

    def pe(self, fn, reads=(), writes=()):
        return self.add("pe", fn, reads, writes)

    def act(self, fn, reads=(), writes=()):
        return self.add("act", fn, reads, writes)

    def dve(self, fn, reads=(), writes=()):
        return self.add("dve", fn, reads, writes)

    def pool(self, fn, reads=(), writes=()):
        return self.add("pool", fn, reads, writes)

    def dma(self, fn, reads=(), writes=(), q="sp"):
        return self.add(q, fn, reads, writes, dma=True)

    def barrier(self):
        deps = set()
        for e in ("pe", "act", "dve"):
            if self.ops[e]:
                deps.add((e, len(self.ops[e]) - 1))
        for q in ("sp", "pool"):
            nd = 0
            gotc = False
            for i in range(len(self.ops[q]) - 1, -1, -1):
                if self.ops[q][i]["dma"]:
                    if nd < NDMA_SEM:
                        deps.add((q, i))
                        nd += 1
                elif not gotc:
                    deps.add((q, i))
                    gotc = True
                if nd >= NDMA_SEM and (gotc or q == "sp"):
                    break
        for e in self.pending:
            self.pending[e] |= deps

    def emit(self, final_bufs=()):
        nc = self.nc
        ops = self.ops
        dma_info = {}
        for q in ("sp", "pool"):
            n = 0
            for i, op in enumerate(ops[q]):
                if op["dma"]:
                    slot = n % NDMA_SEM
                    val = 16 * (n // NDMA_SEM + 1)
                    dma_info[(q, i)] = (q, slot, val)
                    op["dmainfo"] = (slot, val)
                    n += 1
        for e in ops:
            for op in ops[e]:
                op["deps"] = {d for d in op["deps"]}
                for d in op["deps"]:
                    if d not in dma_info:
                        ops[d[0]][d[1]]["marked"] = True
        final_deps = set()
        for b in final_bufs:
            if b.last_w is not None:
                final_deps.add(b.last_w)
                if b.last_w not in dma_info:
                    ops[b.last_w[0]][b.last_w[1]]["marked"] = True
        for e in COMPUTE:
            c = 0
            for op in ops[e]:
                if op["marked"] and not op["dma"]:
                    c += 1
                    op["semval"] = c
        st = self.stack
        csem = {e: st.enter_context(nc.semaphore(f"s_{e}")) for e in COMPUTE}
        dsem = {q: [st.enter_context(nc.semaphore(f"d_{q}{j}")) for j in range(NDMA_SEM)] for q in ("sp", "pool")}
        block = st.enter_context(nc.Block())
        engobj = {"pe": block.tensor, "act": block.scalar, "dve": block.vector, "pool": block.gpsimd, "sp": block.sync}

        def resolve(dep):
            if dep in dma_info:
                q, slot, val = dma_info[dep]
                return dsem[q][slot], ("d", q, slot), val
            f, k = dep
            return csem[f], ("c", f), ops[f][k]["semval"]

        def make_body(e):
            def body(eng):
                known = {}
                for op in ops[e]:
                    need = {}
                    for dep in op["deps"]:
                        if dep[0] == e and dep not in dma_info and e == "pe":
                            continue
                        sem, key, val = resolve(dep)
                        if need.get(key, (None, 0))[1] < val:
                            need[key] = (sem, val)
                    if op["dma"]:
                        slot, val = op["dmainfo"]
                        if val > 16:
                            key = ("d", e, slot)
                            if need.get(key, (None, 0))[1] < val - 16:
                                need[key] = (dsem[e][slot], val - 16)
                    for key, (sem, val) in need.items():
                        if known.get(key, 0) >= val:
                            continue
                        known[key] = val
                        eng.wait_ge(sem, val)
                    ins = op["fn"](eng)
                    if op["dma"]:
                        ins.then_inc(dsem[e][op["dmainfo"][0]], 16)
                    elif op["marked"]:
                        ins.then_inc(csem[e], 1)
                if e == "sp":
                    need = {}
                    for dep in final_deps:
                        sem, key, val = resolve(dep)
                        if need.get(key, (None, 0))[1] < val:
                            need[key] = (sem, val)
                    for key, (sem, val) in need.items():
                        eng.wait_ge(sem, val)
            return body

        for e in ("sp", "pool", "act", "dve", "pe"):
            engobj[e](make_body(e))

    def close(self):
        self.stack.close()


def I(name, *args, **kw):
    f = lambda eng: getattr(eng, name)(*args, **kw)
    try:
        f.f32 = name in ("matmul", "transpose") and args[1].dtype == F32
    except Exception:
        f.f32 = False
    return f


D = 1024
LC = 256
NL = 2048
T = LC + NL
NTT = T // 128
FH = 2816
NHC = FH // 128
DEPTH = 2
NSEQ = 2
BLOCKS = [(0, 256)] + [(256 + 512 * i, 512) for i in range(4)]
ALPHA2 = 2.0
LN_EPS_S = 1e-5 / ALPHA2
RMS_EPS = 1e-6
NCOMBO = 21
WBYTES = 91136

C_NAQ, C_NAK, C_NAV = 0, 384, 768
C_GLQ, C_GLK, C_GLV, C_GLG, C_GLR = 1152, 1280, 1408, 1664, 1920
C_GAQ, C_GAK, C_GAV = 1952, 2336, 2464


def na_combos():
    combos = []
    idx = {}

    def get(kind, r, d):
        key = (kind, d)
        if key not in idx:
            idx[key] = len(combos)
            combos.append((r, d))
        return idx[key]

    slots = []
    for p in range(16):
        r = 2 * p
        if p == 0:
            s = [(j, get("e0", r, 2 * j - r)) for j in range(0, 4)]
        elif p == 1:
            s = [(j, get("e1", r, 2 * j - r)) for j in range(0, 4)]
        elif p == 14:
            s = [(j, get("e2", r, 2 * j - r)) for j in range(12, 16)]
        elif p == 15:
            s = [(j, get("e3", r, 2 * j - r)) for j in range(12, 16)]
        else:
            s = [(j, get("g", 8, 2 * j - r)) for j in range(p - 2, p + 3)]
        slots.append(s)
    assert len(combos) == NCOMBO
    return combos, slots


def na_index_tables():
    combos, _ = na_combos()
    cols = np.arange(64)
    col_start = np.clip(cols - 8, 0, 48)
    col_in = (cols[None, :] >= col_start[:, None]) & (cols[None, :] < col_start[:, None] + 16)
    valid = np.zeros((NCOMBO, 128, 128), bool)
    dri = np.zeros((NCOMBO, 128, 128), np.int64)
    dci = np.zeros((NCOMBO, 128, 128), np.int64)
    for ci, (r, d) in enumerate(combos):
        for ko in range(2):
            for qo in range(2):
                kr = r + d + ko
                qr = r + qo
                rs = min(max(qr - 4, 0), 24)
                ok_row = (rs <= kr < rs + 8)
                dr = kr - qr
                blk_valid = col_in.T & ok_row
                dc = np.clip(cols[:, None] - cols[None, :], -15, 15) + 15
                valid[ci, 64 * ko:64 * ko + 64, 64 * qo:64 * qo + 64] = blk_valid
                dri[ci, 64 * ko:64 * ko + 64, 64 * qo:64 * qo + 64] = min(max(dr + 7, 0), 14)
                dci[ci, 64 * ko:64 * ko + 64, 64 * qo:64 * qo + 64] = dc
    return valid, dri, dci


def build_nc(stage=99, nseq=NSEQ, dumps=None):
    nc = bass.Bass("TRN2", target_bir_lowering=False)
    S = Sched(nc)
    B = S.B
    dumps = dumps if dumps is not None else {}

    def din(name, shape, dt=F32):
        return nc.dram_tensor(name, list(shape), dt, kind="ExternalInput").ap()

    x_d = din("x", [NSEQ, NL, D])
    ctx_d = din("ctx", [NSEQ, LC, D])
    cT_d = din("cT", [128, 8, 3])
    wada_d = din("w_ada", [DEPTH, D, 6 * D])
    bada_d = din("b_adaT", [DEPTH, 128, 48])
    win_d = din("w_in", [DEPTH, D, 2592])
    wout_d = din("w_out", [DEPTH, D, D])
    wf1_d = din("w_ffn_in", [DEPTH, D, 2 * FH])
    wf2_d = din("w_ffn_out", [DEPTH, FH, D])
    lnT_d = din("lnT", [128, DEPTH, 4, 8])
    nab_d = din("nab", [DEPTH, 128, 6, NCOMBO * 128])
    wa2_d = din("wa2", [DEPTH, 2, 32, 128])
    nba_d = din("nbaT", [128, DEPTH, 2])
    glnw_d = din("glnw", [DEPTH, 1, 256])
    qkw_d = din("qkwT", [128, DEPTH, 2])
    cos_d = din("cosT", [128, NL])
    sin_d = din("sinT", [128, NL])
    cm_d = din("cmats", [128, 5, 128])
    tri_d = din("trim", [128, 2, 256])
    out_d = nc.dram_tensor("out", [NSEQ, NL, D], F32, kind="ExternalOutput").ap()
    xres = [nc.dram_tensor(f"xres{s}", [128, 8, T], F32).ap() for s in range(NSEQ)]
    dump_list = []

    ubuf = S.sbuf([128, 8, T], BF16, "ubuf")
    obuf = S.sbuf([128, 8, T], BF16, "obuf")
    wbuf = [S.sbuf([128, 8 * 1152], BF16, f"wbuf{i}") for i in range(2)]

    def wview(i, dt, k, c):
        if dt == F32:
            return wbuf[i][:].bitcast(F32)[:, 0:k * c].rearrange("p (k c) -> p k c", k=k, c=c)
        return wbuf[i][:, 0:k * c].rearrange("p (k c) -> p k c", k=k, c=c)

    Wr = S.sbuf([128, WBYTES // 4], F32, "Wr")
    identf = S.sbuf([128, 128], F32, "identf")
    onesD = S.sbuf([128, 128], F32, "onesD")
    identb = S.sbuf([128, 128], BF16, "identb")
    bones = S.sbuf([128, 128], BF16, "bones")
    rotm = S.sbuf([128, 128], BF16, "rotm")
    trim = S.sbuf([128, 2, 256], BF16, "trim")
    scT = S.sbuf([128, 8, 3], F32, "scT")
    modT = S.sbuf([128, DEPTH, 48, 3], F32, "modT")
    mder = S.sbuf([128, DEPTH, 6, 8, 3], F32, "mder")
    mder2 = S.sbuf([128, DEPTH, 2, 8, 3], F32, "mder2")
    lnT = S.sbuf([128, DEPTH, 4, 8], F32, "lnT")
    badaT = S.sbuf([128, DEPTH, 48], F32, "badaT")
    nbaT = S.sbuf([128, DEPTH, 2], F32, "nbaT")
    qkwT = S.sbuf([128, DEPTH, 2], F32, "qkwT")
    glnw = S.sbuf([128, DEPTH, 256], F32, "glnw")
    wa2 = S.sbuf([32, DEPTH, 2, 128], F32, "wa2")
    hmask = S.sbuf([128, 4], F32, "hmask")
    epsln = S.sbuf([128, 1], F32, "epsln")
    epsrms = S.sbuf([128, 1], F32, "epsrms")

    pf = [S.psum([128, 512], F32, f"pf{i}") for i in range(7)]
    pb = S.psum([128, 1024], BF16, "pb")
    pctr = [0]

    def nextp():
        i = pctr[0] % 7
        pctr[0] += 1
        return pf[i], B("pf", i)

    class Carver:
        def __init__(self):
            self.off = 0

        def __call__(self, dt, *dims):
            n = int(np.prod(dims))
            nbytes = n * (4 if dt == F32 else 2)
            nbytes = (nbytes + 63) // 64 * 64
            a = self.off // 4
            self.off += nbytes
            assert self.off <= WBYTES, ("scratch overflow", self.off)
            ap = Wr[:, a:a + nbytes // 4]
            if dt == BF16:
                ap = ap.bitcast(BF16)[:, 0:n]
            else:
                ap = ap[:, 0:n]
            if len(dims) == 2:
                ap = ap.rearrange("p (a b) -> p a b", a=dims[0], b=dims[1])
            elif len(dims) == 3:
                ap = ap.rearrange("p (a b c) -> p a b c", a=dims[0], b=dims[1], c=dims[2])
            return ap

    def dump(name, ap, shape, dt, rbuf):
        d = nc.dram_tensor("dbg_" + name, list(shape), dt, kind="ExternalOutput").ap()
        S.dma(I("dma_start", out=d, in_=ap), reads=[rbuf], writes=[B("dump", name)])
        dump_list.append(B("dump", name))
        dumps[name] = (shape, dt)

    def ld(dst, src, bname, q="sp"):
        S.dma(I("dma_start", out=dst, in_=src), writes=[B(bname)], q=q)

    ld(identf[:], cm_d[:, 0, :], "identf")
    ld(onesD[:], cm_d[:, 1, :], "onesD")
    ld(identb[:], cm_d[:, 0, :], "identb", q="pool")
    ld(bones[:], cm_d[:, 2, :], "bones", q="pool")
    ld(rotm[:], cm_d[:, 3, :], "rotm", q="pool")
    ld(trim[:], tri_d, "trim", q="pool")
    ld(hmask[:], cm_d[:, 4, 0:4], "hmask")
    ld(scT[:], cT_d, "scT")
    ld(lnT[:], lnT_d, "lnT")
    ld(badaT[:], bada_d.rearrange("l p j -> p l j"), "badaT")
    ld(nbaT[:], nba_d, "nbaT")
    ld(qkwT[:], qkw_d, "qkwT")
    for l in range(DEPTH):
        ld(glnw[:, l, :], glnw_d[l].broadcast_to([128, 256]), "glnw")
        ld(wa2[:, l, :, :], wa2_d[l].rearrange("e r c -> r e c"), "wa2")
    S.dve(I("tensor_scalar_mul", out=nbaT[:], in0=nbaT[:], scalar1=-1.0), reads=[B("nbaT")], writes=[B("nbaT")])
    S.pool(I("memset", epsln[:], LN_EPS_S), writes=[B("eps")])
    S.pool(I("memset", epsrms[:], RMS_EPS), writes=[B("eps")])
    S.act(I("activation", out=scT[:], in_=scT[:], func=AF.Silu), reads=[B("scT")], writes=[B("scT")])

    for l in range(DEPTH):
        pm, bpm = nextp()
        for js in range(12):
            wv = wview(js % 2, F32, 8, 512)
            S.dma(I("dma_start",
                out=wv, in_=wada_d[l].rearrange("(k p) c -> p k c", p=128)[:, :, 512 * js:512 * js + 512]),
                writes=[B("wbuf", js % 2)])
            for jj in range(4):
                j = 4 * js + jj
                for k in range(8):
                    S.pe(I("matmul",
                        pm[:, 3 * j:3 * j + 3], wv[:, k, 128 * jj:128 * jj + 128], scT[:, k, :],
                        start=(k == 0), stop=(k == 7)),
                        reads=[B("wbuf", js % 2), B("scT")], writes=[bpm])
        S.dve(I("tensor_tensor",
            out=modT[:, l, :, :], in0=pm[:, 0:144].rearrange("p (j b) -> p j b", j=48, b=3),
            in1=badaT[:, l, :].unsqueeze(2).broadcast_to([128, 48, 3]), op=ALU.add),
            reads=[bpm, B("badaT")], writes=[B("modT")])
    inv_a = float(1.0 / np.sqrt(ALPHA2))
    for l in range(DEPTH):
        mT = modT[:, l, :, :]
        rd = [B("modT"), B("lnT")]
        wr = [B("mder")]
        S.dve(I("tensor_scalar_add", out=mder[:, l, 0, :, :], in0=mT[:, 8:16, :], scalar1=1.0), reads=rd, writes=wr)
        S.dve(I("tensor_scalar_mul", out=mder[:, l, 1, :, :], in0=mT[:, 16:24, :], scalar1=inv_a), reads=rd, writes=wr)
        S.dve(I("tensor_scalar_add", out=mder[:, l, 2, :, :], in0=mT[:, 32:40, :], scalar1=1.0), reads=rd, writes=wr)
        S.dve(I("tensor_scalar_mul", out=mder[:, l, 3, :, :], in0=mT[:, 40:48, :], scalar1=inv_a), reads=rd, writes=wr)
        S.dve(I("tensor_tensor", out=mder[:, l, 4, :, :], in0=mder[:, l, 2, :, :],
                                             in1=lnT[:, l, 0, :].unsqueeze(2).broadcast_to([128, 8, 3]), op=ALU.mult), reads=rd + wr, writes=wr)
        S.dve(I("tensor_tensor", out=mder[:, l, 5, :, :], in0=mder[:, l, 2, :, :],
                                             in1=lnT[:, l, 1, :].unsqueeze(2).broadcast_to([128, 8, 3]), op=ALU.mult), reads=rd + wr, writes=wr)
        S.dve(I("tensor_tensor", out=mder[:, l, 5, :, :], in0=mder[:, l, 5, :, :], in1=mT[:, 24:32, :], op=ALU.add), reads=rd + wr, writes=wr)
    for l in range(DEPTH - 1):
        rd = [B("modT"), B("lnT"), B("mder")]
        wr = [B("mder2")]
        S.dve(I("tensor_tensor", out=mder2[:, l, 0, :, :], in0=mder[:, l + 1, 0, :, :],
                                             in1=lnT[:, l, 2, :].unsqueeze(2).broadcast_to([128, 8, 3]), op=ALU.mult), reads=rd, writes=wr)
        S.dve(I("tensor_tensor", out=mder2[:, l, 1, :, :], in0=mder[:, l + 1, 0, :, :],
                                             in1=lnT[:, l, 3, :].unsqueeze(2).broadcast_to([128, 8, 3]), op=ALU.mult), reads=rd + wr, writes=wr)
        S.dve(I("tensor_tensor", out=mder2[:, l, 1, :, :], in0=mder2[:, l, 1, :, :], in1=modT[:, l + 1, 0:8, :], op=ALU.add), reads=rd + wr, writes=wr)
    MD = [B("mder"), B("mder2"), B("modT"), B("lnT")]
    if stage < 99:
        S.pool(I("memset", obuf[:], 0.0), writes=[B("obuf", tt) for tt in range(NTT)])
    if stage == 0:
        dump("modT", modT[:], [128, DEPTH, 48, 3], F32, B("modT"))

    def msel(s, t0):
        return 2 if t0 < LC else s

    gctr = {}

    def bank(group, banks):
        c = gctr.get(group, 0)
        gctr[group] = c + 1
        i = banks[c % len(banks)]
        return pf[i], B("pf", i)

    def tiles_of(t0, n):
        return range(t0 // 128, (t0 + n) // 128)

    def ub(t0, n):
        return [B("ubuf", tt) for tt in tiles_of(t0, n)]

    win_v = [win_d[l].rearrange("(k p) c -> p k c", p=128) for l in range(DEPTH)]
    combos, na_slots = na_combos()
    PBB = B("pb", 0)

    def load_x(s):
        S.phase = "load_x"
        S.barrier()
        cv = Carver()
        xin = [cv(F32, 1024) for _ in range(2)]
        xst = [cv(F32, 8, 128) for _ in range(2)]
        for tt in range(NTT):
            i2 = tt % 2
            src = ctx_d[s, 128 * tt:128 * tt + 128, :] if tt < 2 else x_d[s, 128 * (tt - 2):128 * (tt - 2) + 128, :]
            S.dma(I("dma_start", out=xin[i2], in_=src), writes=[B("xin", i2)])
            mi = msel(s, 128 * tt)
            for half in range(2):
                pt, bpt = bank("ld", [0, 1, 2, 3])
                for kk in range(4):
                    k = 4 * half + kk
                    S.pe(I("transpose", pt[:, 128 * kk:128 * kk + 128], xin[i2][:, 128 * k:128 * k + 128], identf[:]),
                         reads=[B("xin", i2), B("identf")], writes=[bpt])
                S.dve(I("tensor_copy",
                    out=xst[i2][:, 4 * half:4 * half + 4, :], in_=pt[:].rearrange("p (k t) -> p k t", k=4, t=128)),
                    reads=[bpt], writes=[B("xst", i2)])
                for kk in range(4):
                    k = 4 * half + kk
                    S.act(I("activation",
                        out=ubuf[:, k, 128 * tt:128 * tt + 128], in_=pt[:, 128 * kk:128 * kk + 128], func=AF.Identity,
                        scale=mder[:, 0, 0, k, mi:mi + 1], bias=modT[:, 0, k, mi:mi + 1]),
                        reads=[bpt] + MD, writes=[B("ubuf", tt)])
            S.dma(I("dma_start", out=xres[s][:, :, 128 * tt:128 * tt + 128], in_=xst[i2]),
                  reads=[B("xst", i2)], writes=[B("xres", s, tt)])

    def na_phase(s, l):
        S.phase = "na_phase" + str(l)
        S.barrier()
        cv = Carver()
        naq = cv(BF16, 3, T)
        nak = cv(BF16, 3, T)
        nav = cv(BF16, NTT, 6 * 65)
        natab = cv(BF16, 6, NCOMBO * 128)
        naP = [cv(BF16, 7, 128) for _ in range(3)]
        otok = [cv(BF16, 384) for _ in range(2)]
        rden = [cv(F32, 8) for _ in range(2)]
        nav4 = nav.rearrange("p t (h c) -> p t h c", h=6, c=65)
        wv = wview(0, BF16, 8, 1152)
        S.dma(I("dma_start", out=wv, in_=win_v[l][:, :, 0:1152]), writes=[B("wbuf", 0)], q="pool")
        S.dma(I("dma_start", out=natab, in_=nab_d[l]), writes=[B("natab")], q="pool")
        S.pool(I("memset", nav4[:, :, :, 64:65], 1.0), writes=[B("navones")])
        for (t0, n) in BLOCKS:
            for j in range(6):
                if j < 3 and t0 < LC and l == DEPTH - 1:
                    continue
                ps, bps = bank("proj", [0, 1, 2, 3, 4, 5, 6])
                for k in range(8):
                    S.pe(I("matmul", ps[:, 0:n], wv[:, k, 128 * j:128 * j + 128], ubuf[:, k, t0:t0 + n],
                                                                     start=(k == 0), stop=(k == 7)),
                         reads=[B("wbuf", 0)] + ub(t0, n), writes=[bps])
                wr = [B("naqk", tt) for tt in tiles_of(t0, n)]
                if j < 3:
                    S.act(I("activation", out=naq[:, j, t0:t0 + n], in_=ps[:, 0:n], func=AF.Identity, scale=0.125),
                          reads=[bps], writes=wr)
                else:
                    S.dve(I("tensor_copy", out=nak[:, j - 3, t0:t0 + n], in_=ps[:, 0:n]), reads=[bps], writes=wr)
        for tt in range(NTT):
            ps, bps = bank("proj", [0, 1, 2, 3, 4, 5, 6])
            for k in range(8):
                S.pe(I("matmul", ps[:, 0:384], ubuf[:, k, 128 * tt:128 * tt + 128], wv[:, k, 768:1152],
                                                         start=(k == 0), stop=(k == 7)),
                     reads=[B("wbuf", 0), B("ubuf", tt)], writes=[bps])
            S.dve(I("tensor_copy", out=nav4[:, tt, :, 0:64], in_=ps[:, 0:384].rearrange("p (h c) -> p h c", h=6, c=64)),
                  reads=[bps], writes=[B("nav", tt)])
        qtiles = []
        if l < DEPTH - 1:
            qtiles += [(0, [(0, None), (1, None)]), (1, [(0, None), (1, None)])]
        for p in range(16):
            qtiles.append((2 + p, [(2 + j, ci) for (j, ci) in na_slots[p]] + [(0, None), (1, None)]))
        naP3 = naP
        units = [(qi, h) for qi in range(len(qtiles)) for h in range(6)]
        pos_ = {}
        ust = {}

        def na_st(u):
            qi, h = units[u]
            tq, slots = qtiles[qi]
            if h == 0:
                pos_[qi] = bank("napo", [4, 5])
            ns = len(slots)
            c = h // 2
            p0 = 64 * (h % 2)
            pa, bpa = bank("napa", [0, 1, 2, 3])
            pa2, bpa2 = bank("napa", [0, 1, 2, 3])
            P = naP3[u % 3]
            bP = B("naP", u % 3)
            ust[u] = (P, bP)
            nlat = sum(1 for (_, ci_) in slots if ci_ is not None)
            ci0 = slots[0][1]
            has_b = [False, False]
            if nlat >= 4:
                S.pe(I("matmul", pa[:, 0:512], identb[:], natab[:, h, 128 * ci0:128 * ci0 + 512], start=True, stop=False),
                     reads=[B("natab"), B("identb")], writes=[bpa])
                has_b[0] = True
            if nlat == 5:
                S.pe(I("matmul", pa2[:, 0:128], identb[:], natab[:, h, 128 * (ci0 + 4):128 * (ci0 + 4) + 128], start=True, stop=False),
                     reads=[B("natab"), B("identb")], writes=[bpa2])
                has_b[1] = True
            for si, (tk, ci) in enumerate(slots):
                bi_ = 0 if si < 4 else 1
                bk, bbk = (pa, bpa) if si < 4 else (pa2, bpa2)
                col = 128 * (si % 4)
                last_in_bank = (si == min(ns, 4) - 1) if bi_ == 0 else (si == ns - 1)
                if has_b[bi_]:
                    st_, sp_ = False, last_in_bank
                else:
                    st_, sp_ = True, True
                S.pe(I("matmul", bk[:, col:col + 128], nak[p0:p0 + 64, c, 128 * tk:128 * tk + 128], naq[p0:p0 + 64, c, 128 * tq:128 * tq + 128],
                       start=st_, stop=sp_),
                     reads=[B("naqk", tk), B("naqk", tq)], writes=[bbk])
            n1 = min(ns, 4)
            S.act(I("activation", out=P[:, 0:n1, :], in_=pa[:, 0:128 * n1].rearrange("p (a b) -> p a b", a=n1, b=128), func=AF.Exp),
                  reads=[bpa], writes=[bP])
            if ns > 4:
                n2 = ns - 4
                S.act(I("activation", out=P[:, 4:4 + n2, :], in_=pa2[:, 0:128 * n2].rearrange("p (a b) -> p a b", a=n2, b=128), func=AF.Exp),
                      reads=[bpa2], writes=[bP])

        def na_pv(u):
            qi, h = units[u]
            tq, slots = qtiles[qi]
            ns = len(slots)
            po, bpo = pos_[qi]
            P, bP = ust.pop(u)
            for si, (tk, ci) in enumerate(slots):
                S.pe(I("matmul", po[:, 65 * h:65 * h + 65], P[:, si, :], nav[:, tk, 65 * h:65 * h + 65], start=(si == 0), stop=(si == ns - 1)),
                     reads=[bP, B("nav", tk), B("navones")], writes=[bpo])
            if h < 5:
                return
            o2 = qi % 2
            po3 = po[:, 0:390].rearrange("p (h c) -> p h c", h=6, c=65)
            S.dve(I("reciprocal", out=rden[o2][:, 0:6], in_=po3[:, :, 64]), reads=[bpo], writes=[B("rden", o2)])
            S.dve(I("tensor_tensor", out=otok[o2].rearrange("p (h c) -> p h c", h=6, c=64), in0=po3[:, :, 0:64],
                    in1=rden[o2][:, 0:6].unsqueeze(2).broadcast_to([128, 6, 64]), op=ALU.mult),
                  reads=[bpo, B("rden", o2)], writes=[B("otok", o2)])
            for c3 in range(3):
                S.pe(I("transpose", pb[:, 128 * c3:128 * c3 + 128], otok[o2][:, 128 * c3:128 * c3 + 128], identb[:]),
                     reads=[B("otok", o2), B("identb")], writes=[PBB])
            S.act(I("copy", out=obuf[:, 0:3, 128 * tq:128 * tq + 128], in_=pb[:, 0:384].rearrange("p (a b) -> p a b", a=3, b=128)),
                  reads=[PBB], writes=[B("obuf", tq)])

        LA = 1
        for u in range(len(units) + LA):
            if u < len(units):
                na_st(u)
            if u >= LA:
                na_pv(u - LA)

    def gqa_phase(s, l):
        S.phase = "gqa_phase" + str(l)
        S.barrier()
        cv = Carver()
        gaq = cv(BF16, 3, T)
        gak = cv(BF16, 2, T)
        gav = cv(BF16, NTT, 130)
        cosv = cv(F32, NL)
        sinv = cv(F32, NL)
        sqs = [cv(BF16, 512) for _ in range(2)]
        rstds = [cv(F32, 512) for _ in range(2)]
        qns = [cv(F32, 512) for _ in range(2)]
        qnbs = [cv(BF16, 512) for _ in range(2)]
        t1s = [cv(F32, 512) for _ in range(2)]
        t2s = [cv(F32, 512) for _ in range(2)]
        gaP = [cv(BF16, 512) for _ in range(6)]
        otok = cv(BF16, 4, 384)
        rden = [cv(F32, 8) for _ in range(2)]
        gav4 = gav.rearrange("p t (g c) -> p t g c", g=2, c=65)
        wv = wview(1, BF16, 8, 768)
        wb = [B("wbuf", 1)]
        S.dma(I("dma_start", out=wv[:, :, 0:384], in_=win_v[l][:, :, C_GAQ:C_GAQ + 384]), writes=wb, q="pool")
        for g in range(2):
            for d in range(2):
                S.dma(I("dma_start", out=wv[:, :, 384 + 128 * g + 64 * d:384 + 128 * g + 64 * d + 64],
                                                      in_=win_v[l][:, :, C_GAK + 64 * g:C_GAK + 64 * g + 64]), writes=wb, q="pool")
        S.dma(I("dma_start", out=wv[:, :, 640:768], in_=win_v[l][:, :, C_GAV:C_GAV + 128]), writes=wb, q="pool")
        S.dma(I("dma_start", out=cosv, in_=cos_d), writes=[B("cos")])
        S.dma(I("dma_start", out=sinv, in_=sin_d), writes=[B("cos")])
        S.pool(I("memset", gav4[:, :, :, 64:65], 1.0), writes=[B("gavones")])
        cb = 0
        for (t0, n) in BLOCKS:
            for j in range(5):
                if j < 3 and t0 < LC and l == DEPTH - 1:
                    continue
                p2 = cb % 2
                cb += 1
                sq, rstd, qn, qnb, t1, t2 = sqs[p2], rstds[p2], qns[p2], qnbs[p2], t1s[p2], t2s[p2]
                TB = lambda nm, p2=p2: B("gatmp", nm, p2)
                ps, bps = bank("proj", [0, 1, 2, 3])
                for k in range(8):
                    S.pe(I("matmul", ps[:, 0:n], wv[:, k, 128 * j:128 * j + 128], ubuf[:, k, t0:t0 + n],
                                                                     start=(k == 0), stop=(k == 7)),
                         reads=wb + ub(t0, n), writes=[bps])
                S.act(I("activation", out=sq[:, 0:n], in_=ps[:, 0:n], func=AF.Square), reads=[bps], writes=[TB("sq")])
                pm, bpm = bank("gams", [4, 5])
                S.pe(I("matmul", pm[:, 0:n], bones[:], sq[:, 0:n], start=True, stop=True), reads=[TB("sq"), B("bones")], writes=[bpm])
                S.act(I("activation", out=rstd[:, 0:n], in_=pm[:, 0:n], func=AF.Ln, bias=epsrms[:, 0:1], scale=1.0),
                      reads=[bpm, B("eps")], writes=[TB("rstd")])
                S.act(I("activation", out=rstd[:, 0:n], in_=rstd[:, 0:n], func=AF.Exp, scale=-0.5), reads=[TB("rstd")], writes=[TB("rstd")])
                wi = 0 if j < 3 else 1
                S.dve(I("scalar_tensor_tensor", out=qn[:, 0:n], in0=ps[:, 0:n], scalar=qkwT[:, l, wi:wi + 1], in1=rstd[:, 0:n],
                                                                      op0=ALU.mult, op1=ALU.mult),
                      reads=[bps, TB("rstd"), B("qkwT")], writes=[TB("qn")])
                dst = gaq[:, j, t0:t0 + n] if j < 3 else gak[:, j - 3, t0:t0 + n]
                wr = [B("gaqk", tt) for tt in tiles_of(t0, n)]
                if t0 >= LC:
                    S.act(I("copy", out=qnb[:, 0:n], in_=qn[:, 0:n]), reads=[TB("qn")], writes=[TB("qnb")])
                    pr, bpr = bank("gams", [4, 5])
                    S.pe(I("matmul", pr[:, 0:n], rotm[:], qnb[:, 0:n], start=True, stop=True), reads=[TB("qnb"), B("rotm")], writes=[bpr])
                    S.dve(I("tensor_tensor", out=t1[:, 0:n], in0=qn[:, 0:n], in1=cosv[:, t0 - LC:t0 - LC + n], op=ALU.mult),
                          reads=[TB("qn"), B("cos")], writes=[TB("t1")])
                    S.dve(I("tensor_tensor", out=t2[:, 0:n], in0=pr[:, 0:n], in1=sinv[:, t0 - LC:t0 - LC + n], op=ALU.mult),
                          reads=[bpr, B("cos")], writes=[TB("t2")])
                    S.dve(I("tensor_tensor", out=dst, in0=t1[:, 0:n], in1=t2[:, 0:n], op=ALU.add),
                          reads=[TB("t1"), TB("t2")], writes=wr)
                else:
                    S.act(I("copy", out=dst, in_=qn[:, 0:n]), reads=[TB("qn")], writes=wr)
        for tt in range(NTT):
            ps, bps = bank("proj", [0, 1, 2, 3])
            for k in range(8):
                S.pe(I("matmul", ps[:, 0:128], ubuf[:, k, 128 * tt:128 * tt + 128], wv[:, k, 640:768],
                                                         start=(k == 0), stop=(k == 7)),
                     reads=wb + [B("ubuf", tt)], writes=[bps])
            S.dve(I("tensor_copy", out=gav4[:, tt, :, 0:64], in_=ps[:, 0:128].rearrange("p (g c) -> p g c", g=2, c=64)),
                  reads=[bps], writes=[B("gav", tt)])
        qblocks = []
        if l < DEPTH - 1:
            qblocks.append((0, 256, [0, 1]))
        for i in range(4):
            qblocks.append((256 + 512 * i, 512, list(range(NTT))))
        pi = 0
        for (t0, n, ktiles) in qblocks:
            nsub = n // 128
            nk = len(ktiles)
            for h in range(6):
                g = h // 3
                c = h // 2
                p0 = 64 * (h % 2)
                po, bpo = bank("gapo", [5, 6])
                S.dve(I("memset", po[:, 0:65 * nsub], 0.0), writes=[bpo])
                Ps = {}

                def st(ki):
                    nonlocal pi
                    tk = ktiles[ki]
                    pa, bpa = bank("gapa", [0, 1, 2, 3, 4])
                    P = gaP[pi % 6]
                    bP = B("gaP", pi % 6)
                    pi += 1
                    Ps[ki] = (P, bP)
                    S.pe(I("matmul", pa[:, 0:n], gak[p0:p0 + 64, g, 128 * tk:128 * tk + 128], gaq[p0:p0 + 64, c, t0:t0 + n],
                                                        start=True, stop=True),
                         reads=[B("gaqk", tk)] + [B("gaqk", tt) for tt in tiles_of(t0, n)], writes=[bpa])
                    S.act(I("activation", out=P[:, 0:n], in_=pa[:, 0:n], func=AF.Exp, scale=0.125), reads=[bpa], writes=[bP])

                def pv(ki):
                    tk = ktiles[ki]
                    P, bP = Ps.pop(ki)
                    for qs in range(nsub):
                        S.pe(I("matmul", po[:, 65 * qs:65 * qs + 65], P[:, 128 * qs:128 * qs + 128], gav[:, tk, 65 * g:65 * g + 65],
                                                                        start=False, stop=(ki == nk - 1), skip_group_check=True),
                             reads=[bP, B("gav", tk), B("gavones")], writes=[bpo])

                GLA_ = 3
                for step in range(nk + GLA_):
                    if step < nk:
                        st(step)
                    if step >= GLA_:
                        pv(step - GLA_)
                o2 = h % 2
                po3 = po[:, 0:65 * nsub].rearrange("p (q c) -> p q c", q=nsub, c=65)
                S.dve(I("reciprocal", out=rden[o2][:, 0:nsub], in_=po3[:, :, 64]), reads=[bpo], writes=[B("rden", o2)])
                S.dve(I("tensor_tensor", out=otok[:, 0:nsub, 64 * h:64 * h + 64], in0=po3[:, :, 0:64],
                                                                   in1=rden[o2][:, 0:nsub].unsqueeze(2).broadcast_to([128, nsub, 64]), op=ALU.mult),
                      reads=[bpo, B("rden", o2)], writes=[B("gaotok")])
            for qs in range(nsub):
                tq = t0 // 128 + qs
                for c3 in range(3):
                    S.pe(I("transpose", pb[:, 128 * c3:128 * c3 + 128], otok[:, qs, 128 * c3:128 * c3 + 128], identb[:]),
                         reads=[B("gaotok"), B("identb")], writes=[PBB])
                S.act(I("copy", out=obuf[:, 5:8, 128 * tq:128 * tq + 128], in_=pb[:, 0:384].rearrange("p (a b) -> p a b", a=3, b=128)),
                      reads=[PBB], writes=[B("obuf", tq)])

    def gla_phase(s, l):
        S.phase = "gla_phase" + str(l)
        S.barrier()
        cv = Carver()
        glv = cv(BF16, NTT, 256)
        glgw = cv(BF16, NTT, 256)
        qt = [cv(BF16, T) for _ in range(2)]
        kt = [cv(BF16, T) for _ in range(2)]
        ktok = cv(BF16, NTT, 128)
        Dall = cv(F32, 2, 36)
        S32 = cv(F32, 37, 64)
        Sbf = [cv(BF16, 36, 64) for _ in range(2)]
        qf = cv(F32, 512)
        kf = cv(F32, 512)
        lrf = cv(F32, 512)
        smask = cv(F32, 512)
        tAs = [cv(F32, 512) for _ in range(2)]
        tBs = [cv(F32, 512) for _ in range(2)]
        tCs = [cv(F32, 512) for _ in range(2)]
        tD = tCs[0]
        Abf = [cv(BF16, 2, 256) for _ in range(2)]
        qmb = [cv(BF16, 2, 4, 128) for _ in range(2)]
        osq = cv(F32, 256)
        sg = osq
        on = cv(F32, 256)
        otok = [cv(BF16, 256) for _ in range(2)]
        st4 = [cv(F32, 8) for _ in range(2)]
        TB = lambda n: B("gltmp", n)
        TB0 = TB
        wv = wview(0, BF16, 8, 800)
        wb = [B("wbuf", 0)]
        S.dma(I("dma_start", out=wv, in_=win_v[l][:, :, C_GLQ:C_GLQ + 800]), writes=wb, q="pool")
        S.pool(I("memset", smask, 1.0), writes=[TB0("smask")])
        S.pool(I("memset", smask.rearrange("p (c t) -> p c t", c=8, t=64)[:, :, 0:1], 0.0), writes=[TB0("smask")])
        for tt in range(NTT):
            ps, bps = bank("proj", [0, 1, 2, 3])
            for k in range(8):
                S.pe(I("matmul", ps[:, 0:512], ubuf[:, k, 128 * tt:128 * tt + 128], wv[:, k, 256:768],
                                                         start=(k == 0), stop=(k == 7)),
                     reads=wb + [B("ubuf", tt)], writes=[bps])
            S.dve(I("tensor_copy", out=glv[:, tt, :], in_=ps[:, 0:256]), reads=[bps], writes=[B("glv", tt)])
            S.act(I("activation", out=sg, in_=ps[:, 256:512], func=AF.Silu), reads=[bps], writes=[TB0("osq")])
            S.pool(I("tensor_tensor", out=glgw[:, tt, :], in0=sg, in1=glnw[:, l, :], op=ALU.mult),
                   reads=[TB0("osq"), B("glnw")], writes=[B("glgw", tt)])
        import os as _os
        _gs = int(_os.environ.get('GLASTOP', '9'))
        _sub = int(_os.environ.get('GLASUB', '9'))
        if _gs <= 0:
            return
        for (t0, n) in BLOCKS:
            nch = n // 64
            c0 = t0 // 64
            pq, bpq = bank("proj", [0, 1, 2, 3])
            pk, bpk = bank("proj", [0, 1, 2, 3])
            pl, bpl = bank("proj", [0, 1, 2, 3])
            for (pp, col, M) in ((pq, 0, 128), (pk, 128, 128), (pl, 768, 32)):
                bb = {id(pq): bpq, id(pk): bpk, id(pl): bpl}[id(pp)]
                for k in range(8):
                    S.pe(I("matmul", pp[0:M, 0:n], wv[:, k, col:col + M], ubuf[:, k, t0:t0 + n],
                                                                              start=(k == 0), stop=(k == 7)),
                         reads=wb + ub(t0, n), writes=[bb])
            S.act(I("activation", out=qf[:, 0:n], in_=pq[:, 0:n], func=AF.Identity, scale=float(32 ** -0.5)), reads=[bpq], writes=[TB0("qf")])
            S.dve(I("tensor_copy", out=kf[:, 0:n], in_=pk[:, 0:n]), reads=[bpk], writes=[TB0("kf")])
            S.dve(I("tensor_copy", out=lrf[0:32, 0:n], in_=pl[0:32, 0:n]), reads=[bpl], writes=[TB0("lrf")])
            for ed in range(2):
                tA, tB_, tC = tAs[ed], tBs[ed], tCs[ed]
                TBd = lambda n_, ed=ed: B("gltmp", n_, ed)
                pz, bpz = bank("glz", [4, 5])
                S.pe(I("matmul", pz[:, 0:n], wa2[:, l, ed, :], lrf[0:32, 0:n], start=True, stop=True),
                     reads=[TB0("lrf"), B("wa2")], writes=[bpz])
                S.act(I("activation", out=tA[:, 0:n], in_=pz[:, 0:n], func=AF.Exp, scale=-1.0, bias=nbaT[:, l, ed:ed + 1]),
                      reads=[bpz, B("nbaT")], writes=[TBd("tA")])
                S.act(I("activation", out=tB_[:, 0:n], in_=tA[:, 0:n], func=AF.Ln, bias=1.0, scale=1.0), reads=[TBd("tA")], writes=[TBd("tB")])
                S.dve(I("tensor_tensor_scan", out=tC[:, 0:n], data0=smask[:, 0:n], data1=tB_[:, 0:n], initial=0.0, op0=ALU.mult, op1=ALU.add),
                      reads=[TB0("smask"), TBd("tB")], writes=[TBd("tC")])
                cum = tC
                bcum = TBd("tC")
                if ed == 1:
                    tC3 = tC[:, 0:n].rearrange("p (c t) -> p c t", c=nch, t=64)
                    S.dve(I("tensor_tensor", out=tD[:, 0:n].rearrange("p (c t) -> p c t", c=nch, t=64),
                                                                         in0=tC3[:, :, 63:64].broadcast_to([128, nch, 64]), in1=tC3, op=ALU.subtract),
                          reads=[TBd("tC")], writes=[B("gltmp", "tC", 0)])
                    S.dve(I("tensor_tensor", out=tD[:, 0:n], in0=tD[:, 0:n], in1=tB_[:, 0:n], op=ALU.add), reads=[B("gltmp", "tC", 0), TBd("tB")], writes=[B("gltmp", "tC", 0)])
                    cum = tD
                    bcum = B("gltmp", "tC", 0)
                S.act(I("activation", out=tA[:, 0:n], in_=cum[:, 0:n], func=AF.Exp, scale=-1.0 / 16.0), reads=[bcum], writes=[TBd("tA")])
                S.act(I("activation", out=tB_[:, 0:n], in_=cum[:, 0:n], func=AF.Exp, scale=1.0 / 16.0), reads=[bcum, TBd("tA")], writes=[TBd("tB")])
                dcol = 63 if ed == 0 else 0
                S.dve(I("tensor_copy",
                    out=Dall[:, ed, c0:c0 + nch], in_=tA[:, 0:n].rearrange("p (c t) -> p c t", c=nch, t=64)[:, :, dcol]),
                    reads=[TBd("tA")], writes=[B("Dall")])
                wr = [B("glqk", ed, tt) for tt in tiles_of(t0, n)]
                S.dve(I("tensor_tensor", out=qt[ed][:, t0:t0 + n], in0=qf[:, 0:n], in1=tA[:, 0:n], op=ALU.mult),
                      reads=[TB0("qf"), TBd("tA")], writes=wr)
                S.dve(I("tensor_tensor", out=kt[ed][:, t0:t0 + n], in0=kf[:, 0:n], in1=tB_[:, 0:n], op=ALU.mult),
                       reads=[TB0("kf"), TBd("tB")], writes=wr)
        if _gs <= 1:
            return
        ords = [list(range(36)), [3, 2, 1, 0] + list(range(35, 3, -1))]
        poss = []
        for ed in range(2):
            pos = [0] * 36
            for j, ci in enumerate(ords[ed]):
                pos[ci] = j
            poss.append(pos)
        for ed in range(2):
            pos = poss[ed]
            for tt in range(NTT):
                S.pe(I("transpose", pb[:, 128 * (tt % 4):128 * (tt % 4) + 128], kt[ed][:, 128 * tt:128 * tt + 128], identb[:]),
                     reads=[B("glqk", ed, tt), B("identb")], writes=[PBB])
                if tt % 4 == 3 or tt == NTT - 1:
                    n4 = tt % 4 + 1
                    ta = tt - n4 + 1
                    S.act(I("copy", out=ktok[:, ta:ta + n4, :], in_=pb[:, 0:128 * n4].rearrange("p (a b) -> p a b", a=n4, b=128)),
                          reads=[PBB], writes=[B("ktok")])
            if _sub <= 1:
                continue
            S.pool(I("memset", S32[:, 0, :], 0.0), writes=[B("S32")])
            for half in range(2):
              for grp in range(0, 18, 8):
                pw, bpw = bank("glw", [0, 1, 2, 3])
                cis = [2 * t_ + half for t_ in range(grp, min(grp + 8, 18))]
                for gi, ci in enumerate(cis):
                    tt = ci // 2
                    for h in range(4):
                        S.pe(I("matmul",
                            pw[32 * h:32 * h + 32, 64 * gi:64 * gi + 64], ktok[64 * half:64 * half + 64, tt, 32 * h:32 * h + 32],
                            glv[64 * half:64 * half + 64, tt, 64 * h:64 * h + 64], start=True, stop=True, tile_position=(64 * half, 32 * h)),
                            reads=[B("ktok"), B("glv", tt)], writes=[bpw])
                for gi, ci in enumerate(cis):
                    S.act(I("activation",
                        out=S32[:, 1 + pos[ci], :], in_=pw[:, 64 * gi:64 * gi + 64], func=AF.Identity, scale=Dall[:, ed, ci:ci + 1]),
                        reads=[bpw, B("Dall")], writes=[B("S32")])
            if _sub <= 2:
                continue
            for j in range(36):
                ci = ords[ed][j]
                S.dve(I("scalar_tensor_tensor", out=S32[:, j + 1, :], in0=S32[:, j, :], scalar=Dall[:, ed, ci:ci + 1], in1=S32[:, j + 1, :],
                                                                      op0=ALU.mult, op1=ALU.add),
                      reads=[B("S32"), B("Dall")], writes=[B("S32")])
            if _sub <= 3:
                continue
            S.dve(I("tensor_copy", out=Sbf[ed], in_=S32[:, 0:36, :]), reads=[B("S32")], writes=[B("Sbf", ed)])
        if _gs <= 2:
            return
        tiles_out = range(NTT) if l < DEPTH - 1 else range(2, NTT)
        tiles_out = list(tiles_out)

        def g_st1(oi):
            tt = tiles_out[oi]
            pa, bpa = bank("glpa", [0, 1, 2, 3])
            A = Abf[oi % 2]
            bA = B("Abf", oi % 2)
            qm = qmb[oi % 2]
            bqm = B("qm", oi % 2)
            for ed in range(2):
                for h in range(4):
                    eng = S.dve
                    eng(I("tensor_scalar_mul", out=qm[:, ed, h, :], in0=qt[ed][:, 128 * tt:128 * tt + 128], scalar1=hmask[:, h:h + 1]),
                        reads=[B("glqk", ed, tt), B("hmask")], writes=[bqm])
            for half in range(2):
                cols = slice(128 * tt + 64 * half, 128 * tt + 64 * half + 64)
                for ed in range(2):
                    for h in range(4):
                        S.pe(I("matmul",
                            pa[64 * half:64 * half + 64, 256 * ed + 64 * h:256 * ed + 64 * h + 64],
                            kt[ed][:, cols], qm[:, ed, h, 64 * half:64 * half + 64], start=True, stop=True,
                            tile_position=(0, 64 * half)),
                            reads=[B("glqk", ed, tt), bqm], writes=[bpa])
            S.dve(I("tensor_tensor", out=A, in0=pa[:, 0:512].rearrange("p (a b) -> p a b", a=2, b=256), in1=trim[:], op=ALU.mult),
                  reads=[bpa, B("trim")], writes=[bA])

        def g_st2(oi):
            tt = tiles_out[oi]
            A = Abf[oi % 2]
            bA = B("Abf", oi % 2)
            qm = qmb[oi % 2]
            bqm = B("qm", oi % 2)
            po, bpo = bank("glpo", [4, 5])
            for half in range(2):
                ci = 2 * tt + half
                for h in range(4):
                    for ed in range(2):
                        S.pe(I("matmul",
                            po[64 * half:64 * half + 64, 64 * h:64 * h + 64], A[64 * half:64 * half + 64, ed, 64 * h:64 * h + 64],
                            glv[64 * half:64 * half + 64, tt, 64 * h:64 * h + 64], start=(ed == 0), stop=False,
                            tile_position=(64 * half, 64 * half)),
                            reads=[bA, B("glv", tt)], writes=[bpo])
                    for ed in range(2):
                        pj = poss[ed][ci]
                        S.pe(I("matmul",
                            po[64 * half:64 * half + 64, 64 * h:64 * h + 64], qm[:, ed, h, 64 * half:64 * half + 64],
                            Sbf[ed][:, pj, :], start=False, stop=(ed == 1),
                            tile_position=(0, 64 * half)),
                            reads=[bqm, B("Sbf", ed)], writes=[bpo])
            o2 = oi % 2
            S.act(I("activation", out=osq, in_=po[:, 0:256], func=AF.Square), reads=[bpo], writes=[TB0("osq")])
            S.dve(I("reduce_sum", out=st4[o2][:, 0:4], in_=osq.rearrange("p (h c) -> p h c", h=4, c=64), axis=AX.X),
                  reads=[TB0("osq")], writes=[B("st4", o2)])
            S.dve(I("tensor_scalar", out=st4[o2][:, 0:4], in0=st4[o2][:, 0:4], scalar1=1.0 / 64.0, scalar2=RMS_EPS, op0=ALU.mult, op1=ALU.add),
                  reads=[B("st4", o2)], writes=[B("st4", o2)])
            S.act(I("activation", out=st4[o2][:, 4:8], in_=st4[o2][:, 0:4], func=AF.Sqrt), reads=[B("st4", o2)], writes=[B("st4", o2)])
            S.dve(I("reciprocal", out=st4[o2][:, 0:4], in_=st4[o2][:, 4:8]), reads=[B("st4", o2)], writes=[B("st4", o2)])
            S.dve(I("tensor_tensor", out=on.rearrange("p (h c) -> p h c", h=4, c=64), in0=po[:, 0:256].rearrange("p (h c) -> p h c", h=4, c=64),
                                                        in1=st4[o2][:, 0:4].unsqueeze(2).broadcast_to([128, 4, 64]), op=ALU.mult),
                  reads=[bpo, B("st4", o2)], writes=[TB0("on")])
            S.pool(I("tensor_tensor", out=otok[o2], in0=on, in1=glgw[:, tt, :], op=ALU.mult),
                   reads=[TB0("on"), B("glgw", tt)], writes=[B("glotok", o2)])

        def g_st3(oi):
            tt = tiles_out[oi]
            o2 = oi % 2
            for c2 in range(2):
                S.pe(I("transpose", pb[:, 512 + 128 * c2:512 + 128 * c2 + 128], otok[o2][:, 128 * c2:128 * c2 + 128], identb[:]),
                     reads=[B("glotok", o2), B("identb")], writes=[PBB])
            S.act(I("copy", out=obuf[:, 3:5, 128 * tt:128 * tt + 128], in_=pb[:, 512:768].rearrange("p (a b) -> p a b", a=2, b=128)),
                  reads=[PBB], writes=[B("obuf", tt)])

        for oi in range(len(tiles_out) + 2):
            if oi < len(tiles_out):
                g_st1(oi)
            if 1 <= oi <= len(tiles_out):
                g_st2(oi - 1)
            if oi >= 2:
                g_st3(oi - 2)

    def layer_norm_block(tbuf, bk, n, outs):
        pmean, bpmean = bank("lnm", [4, 5])
        pmsq, bpmsq = bank("lnm", [4, 5])
        for oc in range(8):
            S.pe(I("matmul", pmean[:, 0:n], onesD[:], tbuf[:, oc, 0:n], start=(oc == 0), stop=(oc == 7)),
                 reads=[bk(oc), B("onesD")], writes=[bpmean])
        for oc in range(8):
            sqb = lnsq[oc % 2]
            S.act(I("activation", out=sqb[:, 0:n], in_=tbuf[:, oc, 0:n], func=AF.Square), reads=[bk(oc)], writes=[B("lnsq", oc % 2)])
            S.pe(I("matmul", pmsq[:, 0:n], onesD[:], sqb[:, 0:n], start=(oc == 0), stop=(oc == 7)),
                 reads=[B("lnsq", oc % 2), B("onesD")], writes=[bpmsq])
        S.act(I("copy", out=lnmean[:, 0:n], in_=pmean[:, 0:n]), reads=[bpmean], writes=[B("lnmean")])
        S.dve(I("tensor_tensor", out=lnm2[:, 0:n], in0=lnmean[:, 0:n], in1=lnmean[:, 0:n], op=ALU.mult), reads=[B("lnmean")], writes=[B("lnm2")])
        S.dve(I("tensor_tensor", out=lnm2[:, 0:n], in0=pmsq[:, 0:n], in1=lnm2[:, 0:n], op=ALU.subtract), reads=[bpmsq, B("lnm2")], writes=[B("lnm2")])
        S.act(I("activation", out=lnm2[:, 0:n], in_=lnm2[:, 0:n], func=AF.Ln, bias=epsln[:, 0:1], scale=1.0), reads=[B("lnm2"), B("eps")], writes=[B("lnm2")])
        S.act(I("activation", out=lnrstd[:, 0:n], in_=lnm2[:, 0:n], func=AF.Exp, scale=-0.5), reads=[B("lnm2")], writes=[B("lnrstd")])
        for oc in range(8):
            S.dve(I("tensor_tensor", out=tbuf[:, oc, 0:n], in0=tbuf[:, oc, 0:n], in1=lnmean[:, 0:n], op=ALU.subtract),
                  reads=[bk(oc), B("lnmean")], writes=[bk(oc)])
            S.dve(I("tensor_tensor", out=tbuf[:, oc, 0:n], in0=tbuf[:, oc, 0:n], in1=lnrstd[:, 0:n], op=ALU.mult),
                  reads=[bk(oc), B("lnrstd")], writes=[bk(oc)])
            for (eng, dstf, scf, bif, wrf) in outs:
                if eng == "act":
                    S.act(I("activation", out=dstf(oc), in_=tbuf[:, oc, 0:n], func=AF.Identity, scale=scf(oc), bias=bif(oc)),
                          reads=[bk(oc)] + MD, writes=wrf(oc))
                else:
                    S.dve(I("tensor_scalar", out=dstf(oc), in0=tbuf[:, oc, 0:n], scalar1=scf(oc), scalar2=bif(oc),
                            op0=ALU.mult, op1=ALU.add),
                          reads=[bk(oc)] + MD, writes=wrf(oc))

    self_cv = [None]
    lnsq = [None, None]
    lnmean = lnm2 = lnrstd = None

    def outproj_phase(s, l):
        S.phase = "outproj_phase" + str(l)
        nonlocal lnsq, lnmean, lnm2, lnrstd
        S.barrier()
        cv = Carver()
        self_cv[0] = cv
        tbs = [cv(F32, 8, 512) for _ in range(2)]
        xss = [cv(F32, 8, 512) for _ in range(2)]
        lnsq = [cv(F32, 512) for _ in range(2)]
        lnmean = cv(F32, 512)
        lnm2 = cv(F32, 512)
        lnrstd = cv(F32, 512)
        wv = wview(1, BF16, 8, 1024)
        wb = [B("wbuf", 1)]
        S.dma(I("dma_start", out=wv, in_=wout_d[l].rearrange("(k p) c -> p k c", p=128)), writes=wb, q="pool")
        bi = 0
        for (t0, n) in BLOCKS:
            if t0 < LC and l == DEPTH - 1:
                continue
            tb = tbs[bi % 2]
            xs = xss[bi % 2]
            par = bi % 2
            btb = lambda oc, par=par: B("tb", par, oc)
            bxs = lambda oc, par=par: B("xs", par, oc)
            bxs_all = [bxs(oc) for oc in range(8)]
            bi += 1
            mi = msel(s, t0)
            xrb = [B("xres", s, tt) for tt in tiles_of(t0, n)]
            S.dma(I("dma_start", out=xs[:, :, 0:n], in_=xres[s][:, :, t0:t0 + n]), reads=xrb, writes=bxs_all)
            for oc in range(8):
                py, bpy = bank("proj", [0, 1, 2, 3])
                for k in range(8):
                    S.pe(I("matmul", py[:, 0:n], wv[:, k, 128 * oc:128 * oc + 128], obuf[:, k, t0:t0 + n],
                                                                       start=(k == 0), stop=(k == 7)),
                         reads=wb + [B("obuf", tt) for tt in tiles_of(t0, n)], writes=[bpy])
                S.dve(I("scalar_tensor_tensor", out=tb[:, oc, 0:n], in0=py[:, 0:n], scalar=mder[:, l, 1, oc, mi:mi + 1], in1=xs[:, oc, 0:n],
                                                                             op0=ALU.mult, op1=ALU.add),
                      reads=[bpy, bxs(oc)] + MD, writes=[btb(oc)])
            outs = [
                ("act", lambda oc, n=n, xs=xs: xs[:, oc, 0:n], lambda oc: lnT[:, l, 0, oc:oc + 1], lambda oc: lnT[:, l, 1, oc:oc + 1], lambda oc, bxs=bxs: [bxs(oc)]),
                ("dve", lambda oc, t0=t0, n=n: ubuf[:, oc, t0:t0 + n], lambda oc, mi=mi: mder[:, l, 4, oc, mi:mi + 1], lambda oc, mi=mi: mder[:, l, 5, oc, mi:mi + 1],
                 lambda oc, t0=t0, n=n: ub(t0, n)),
            ]
            layer_norm_block(tb, btb, n, outs)
            S.dma(I("dma_start", out=xres[s][:, :, t0:t0 + n], in_=xs[:, :, 0:n]), reads=bxs_all, writes=xrb)

    def ffn_phase(s, l):
        S.phase = "ffn_phase" + str(l)
        nonlocal lnsq, lnmean, lnm2, lnrstd
        S.barrier()
        last = (l == DEPTH - 1)
        cv = Carver()
        self_cv[0] = cv
        hT = cv(BF16, NHC, 1024)
        w2 = [cv(BF16, NHC, 256) for _ in range(2)]
        lnsq = [cv(F32, 512) for _ in range(2)]
        lnmean = cv(F32, 512)
        lnm2 = cv(F32, 512)
        lnrstd = cv(F32, 512)
        sa = [cv(F32, 512) for _ in range(2)]
        ostage = [cv(F32, 1024) for _ in range(2)] if last else None
        tfull = obuf[:].rearrange("p k t -> p (k t)").bitcast(F32)[:, 0:8 * 1024].rearrange("p (k t) -> p k t", k=8, t=1024)
        parts = [[BLOCKS[1], BLOCKS[2]], [BLOCKS[3], BLOCKS[4]]]
        if not last:
            parts = [[BLOCKS[0]]] + parts
        wf1 = wf1_d[l].rearrange("(k p) c -> p k c", p=128)
        wf2 = wf2_d[l].rearrange("(j p) c -> p j c", p=128)
        for part in parts:
            pt0 = part[0][0]
            ptn = sum(n for (_, n) in part)
            mi = msel(s, pt0)
            nblk = len(part)
            btf = lambda bi_, oc: B("tfull", bi_, oc)
            btf_all = [btf(bi_, oc) for bi_ in range(nblk) for oc in range(8)]
            for js in range(11):
                wi = js % 2
                wv = wview(wi, BF16, 8, 512)
                wb = [B("wbuf", wi)]
                S.dma(I("dma_start", out=wv[:, :, 0:256], in_=wf1[:, :, 256 * js:256 * js + 256]), writes=wb, q="pool")
                S.dma(I("dma_start", out=wv[:, :, 256:512], in_=wf1[:, :, FH + 256 * js:FH + 256 * js + 256]), writes=wb, q="pool")
                for (t0, n) in part:
                    lo = t0 - pt0
                    for jj in range(2):
                        j = 2 * js + jj
                        pa, bpa = bank("ffa", [0, 1, 2])
                        pg, bpg = bank("ffb", [3, 4, 5])
                        for k in range(8):
                            S.pe(I("matmul", pa[:, 0:n], wv[:, k, 128 * jj:128 * jj + 128], ubuf[:, k, t0:t0 + n],
                                                                                      start=(k == 0), stop=(k == 7)),
                                 reads=wb + ub(t0, n), writes=[bpa])
                        for k in range(8):
                            S.pe(I("matmul", pg[:, 0:n], wv[:, k, 256 + 128 * jj:256 + 128 * jj + 128], ubuf[:, k, t0:t0 + n],
                                                                                      start=(k == 0), stop=(k == 7)),
                                 reads=wb + ub(t0, n), writes=[bpg])
                        si = j % 2
                        S.act(I("activation", out=sa[si][:, 0:n], in_=pa[:, 0:n], func=AF.Silu), reads=[bpa], writes=[B("sa", si)])
                        S.dve(I("tensor_tensor", out=hT[:, j, lo:lo + n], in0=pg[:, 0:n], in1=sa[si][:, 0:n], op=ALU.mult),
                              reads=[bpg, B("sa", si)], writes=[B("hT")])
            S.dma(I("dma_start", out=tfull[:, :, 0:ptn], in_=xres[s][:, :, pt0:pt0 + ptn]),
                  reads=[B("xres", s, tt) for tt in tiles_of(pt0, ptn)], writes=btf_all)
            for oc2 in range(4):
                w2v = w2[oc2 % 2]
                bw2 = [B("w2", oc2 % 2)]
                S.dma(I("dma_start", out=w2v, in_=wf2[:, :, 256 * oc2:256 * oc2 + 256]), writes=bw2, q="pool")
                for bi_, (t0, n) in enumerate(part):
                    lo = t0 - pt0
                    for oo in range(2):
                        oc = 2 * oc2 + oo
                        py, bpy = bank("ffy", [0, 1, 2, 3])
                        for j in range(NHC):
                            S.pe(I("matmul", py[:, 0:n], w2v[:, j, 128 * oo:128 * oo + 128], hT[:, j, lo:lo + n],
                                                                                        start=(j == 0), stop=(j == NHC - 1)),
                                 reads=bw2 + [B("hT")], writes=[bpy])
                        S.dve(I("scalar_tensor_tensor", out=tfull[:, oc, lo:lo + n], in0=py[:, 0:n], scalar=mder[:, l, 3, oc, mi:mi + 1],
                                                                                            in1=tfull[:, oc, lo:lo + n], op0=ALU.mult, op1=ALU.add),
                              reads=[bpy, btf(bi_, oc)] + MD, writes=[btf(bi_, oc)])
            for bi_, (t0, n) in enumerate(part):
                lo = t0 - pt0
                tbv = tfull[:, :, lo:lo + n]
                bkf = lambda oc, bi_=bi_: btf(bi_, oc)
                bkf_all = [bkf(oc) for oc in range(8)]
                xrb = [B("xres", s, tt) for tt in tiles_of(t0, n)]
                if not last:
                    outs = [
                        ("dve", lambda oc, t0=t0, n=n: ubuf[:, oc, t0:t0 + n], lambda oc, mi=mi: mder2[:, l, 0, oc, mi:mi + 1], lambda oc, mi=mi: mder2[:, l, 1, oc, mi:mi + 1],
                         lambda oc, t0=t0, n=n: ub(t0, n)),
                        ("act", lambda oc, tbv=tbv: tbv[:, oc, :], lambda oc: lnT[:, l, 2, oc:oc + 1], lambda oc: lnT[:, l, 3, oc:oc + 1], lambda oc, bkf=bkf: [bkf(oc)]),
                    ]
                    layer_norm_block(tbv, bkf, n, outs)
                    S.dma(I("dma_start", out=xres[s][:, :, t0:t0 + n], in_=tbv), reads=bkf_all, writes=xrb)
                else:
                    outs = [("act", lambda oc, tbv=tbv: tbv[:, oc, :], lambda oc: lnT[:, l, 2, oc:oc + 1], lambda oc: lnT[:, l, 3, oc:oc + 1], lambda oc, bkf=bkf: [bkf(oc)])]
                    layer_norm_block(tbv, bkf, n, outs)
                    for qs in range(n // 128):
                        tq = (t0 + 128 * qs) // 128
                        og = ostage[tq % 2]
                        for half in range(2):
                            pt, bpt = bank("ffo", [0, 1, 2, 3])
                            for kk in range(4):
                                k = 4 * half + kk
                                S.pe(I("transpose", pt[:, 128 * kk:128 * kk + 128], tbv[:, k, 128 * qs:128 * qs + 128], identf[:]),
                                     reads=[bkf(k), B("identf")], writes=[bpt])
                            S.act(I("copy", out=og[:, 512 * half:512 * half + 512], in_=pt[:, 0:512]),
                                  reads=[bpt], writes=[B("ostage", tq % 2)])
                        S.dma(I("dma_start", out=out_d[s, 128 * (tq - 2):128 * (tq - 2) + 128, :], in_=og),
                              reads=[B("ostage", tq % 2)], writes=[B("out")])

    for s in range(nseq):
        load_x(s)
        if stage == 1 and s == 0:
            dump("u0", ubuf[:], [128, 8, T], BF16, B("ubuf", NTT - 1))
        for l in range(DEPTH):
            if stage >= 2:
                na_phase(s, l)
            if stage >= 3:
                gla_phase(s, l)
            if stage >= 4:
                gqa_phase(s, l)
            if stage in (2, 3, 4) and s == 0 and l == 0:
                S.barrier()
                S.pool(I("memset", epsln[:], LN_EPS_S), reads=[B("obuf", tt) for tt in range(NTT)], writes=[B("eps")])
                dump("o0", obuf[:], [128, 8, T], BF16, B("eps"))
                break
            if stage >= 5:
                outproj_phase(s, l)
            if stage == 5 and s == 0 and l == 0:
                S.barrier()
                S.pool(I("memset", epsln[:], LN_EPS_S), reads=[B("ubuf", tt) for tt in range(NTT)], writes=[B("eps")])
                dump("u2", ubuf[:], [128, 8, T], BF16, B("eps"))
                dump("x1", xres[0], [128, 8, T], F32, B("eps"))
                break
            if stage >= 6:
                ffn_phase(s, l)
            if stage == 6 and s == 0 and l == 0:
                S.barrier()
                S.pool(I("memset", epsln[:], LN_EPS_S), reads=[B("ubuf", tt) for tt in range(NTT)], writes=[B("eps")])
                dump("u1n", ubuf[:], [128, 8, T], BF16, B("eps"))
                dump("x2", xres[0], [128, 8, T], F32, B("eps"))
                break


    S.emit(final_bufs=dump_list + [B("out")])
    S.close()
    global LAST_SCHED
    LAST_SCHED = S
    return nc


def host_consts():
    eye = np.eye(128, dtype=np.float32)
    onesd = np.full((128, 128), 1.0 / 1024.0, np.float32)
    bon = np.zeros((128, 128), np.float32)
    bon[:64, :64] = 1.0 / 64
    bon[64:, 64:] = 1.0 / 64
    R = np.zeros((64, 64), np.float32)
    for d in range(16):
        R[d, d + 16] = -1.0
        R[d + 16, d] = 1.0
        R[d + 32, d + 48] = -1.0
        R[d + 48, d + 32] = 1.0
    rl = np.zeros((128, 128), np.float32)
    rl[:64, :64] = R.T
    rl[64:, 64:] = R.T
    hm = np.zeros_like(eye)
    for h in range(4):
        hm[32 * h:32 * h + 32, h] = 1.0
    cm = np.stack([eye, onesd, bon, rl, hm], axis=1)
    sidx = np.arange(64)
    mf = (sidx[None, :] >= sidx[:, None]).astype(np.float32)
    mb = (sidx[None, :] <= sidx[:, None]).astype(np.float32)
    tri = np.zeros((128, 2, 256), np.float32)
    for half in range(2):
        tri[64 * half:64 * half + 64, 0, :] = np.tile(mf, (1, 4))
        tri[64 * half:64 * half + 64, 1, :] = np.tile(mb, (1, 4))
    t = np.arange(NL)
    row = (t // 64).astype(np.float32)
    col = (t % 64).astype(np.float32)
    inv_freq = (10000.0 ** (-np.arange(16, dtype=np.float32) / 16)).astype(np.float32)
    ang_r = row[:, None] * inv_freq
    ang_c = col[:, None] * inv_freq
    ang = np.concatenate([ang_r, ang_r, ang_c, ang_c], axis=-1).astype(np.float32)
    cosT = np.tile(np.cos(ang).T, (2, 1)).astype(np.float32)
    sinT = np.tile(np.sin(ang).T, (2, 1)).astype(np.float32)
    return cm, tri, np.ascontiguousarray(cosT), np.ascontiguousarray(sinT)


def prep_shared(inp):
    f = lambda a: np.ascontiguousarray(np.asarray(a, dtype=np.float32))
    cm, tri, cosT, sinT = host_consts()
    sh = {}
    sh["w_ada"] = f(inp["w_ada"])
    sh["b_adaT"] = f(np.asarray(inp["b_ada"]).reshape(DEPTH, 48, 128).transpose(0, 2, 1))
    sh["w_in"] = f(inp["w_in"])
    sh["w_out"] = f(inp["w_out"])
    sh["w_ffn_in"] = f(inp["w_ffn_in"])
    sh["w_ffn_out"] = f(inp["w_ffn_out"])
    ln = np.stack([np.asarray(inp[k]) for k in ("ln1_g", "ln1_b", "ln2_g", "ln2_b")], axis=1)
    sh["lnT"] = f(ln.reshape(DEPTH, 4, 8, 128).transpose(3, 0, 1, 2))
    valid, dri, dci = na_index_tables()
    rpb = np.asarray(inp["na_rpb"], dtype=np.float32)
    g = rpb[:, :, dri, dci]
    g = np.where(valid[None, None], g, np.float32(-30000.0))
    sh["nab"] = f(g.transpose(0, 3, 1, 2, 4).reshape(DEPTH, 128, 6, NCOMBO * 128))
    wa2 = np.zeros((DEPTH, 2, 32, 128), np.float32)
    gw = np.asarray(inp["gla_wa2"], dtype=np.float32)
    wa2[:, 0, 0:16, :] = gw[:, 0]
    wa2[:, 1, 16:32, :] = gw[:, 1]
    sh["wa2"] = wa2
    sh["nbaT"] = f(np.asarray(inp["gla_ba"]).transpose(2, 0, 1))
    sh["glnw"] = f(np.tile(np.asarray(inp["gla_norm_w"]), (1, 4)).reshape(DEPTH, 1, 256))
    qk = np.stack([np.tile(np.asarray(inp["gqa_qnorm_w"]), (1, 2)), np.tile(np.asarray(inp["gqa_knorm_w"]), (1, 2))], axis=2)
    sh["qkwT"] = f(qk.transpose(1, 0, 2))
    sh["cosT"] = cosT
    sh["sinT"] = sinT
    sh["cmats"] = f(cm)
    sh["trim"] = f(tri)
    return sh


def prep_core(inp, core, sh):
    f = lambda a: np.ascontiguousarray(np.asarray(a, dtype=np.float32))
    b0 = core * NSEQ
    m = dict(sh)
    m["x"] = f(inp["x"][b0:b0 + NSEQ])
    m["ctx"] = f(inp["ctx"][b0:b0 + NSEQ])
    cc = np.stack([np.asarray(inp["c"][b0]), np.asarray(inp["c"][b0 + 1]), np.asarray(inp["c_ctx"])], axis=1)
    m["cT"] = f(cc.reshape(8, 128, 3).transpose(1, 0, 2))
    return m


def kernel(**inputs):
    sh = prep_shared(inputs)
    nc = build_nc()
    in_maps = [prep_core(inputs, c, sh) for c in range(8)]
    res = run_bass_kernel_spmd(nc, in_maps, core_ids=list(range(8)))
    out = np.concatenate([np.asarray(r["out"]) for r in res.results], axis=0)
    return out.astype(np.float32)
```

```python
import contextlib
import numpy as np
import concourse.bass as bass
import concourse.mybir as mybir
from concourse.bass_utils import run_bass_kernel_spmd

F32 = mybir.dt.float32
BF16 = mybir.dt.bfloat16
AF = mybir.ActivationFunctionType
ALU = mybir.AluOpType
AX = mybir.AxisListType

COMPUTE = ("pe", "act", "dve", "pool")
NDMA_SEM = 6


class Buf:
    __slots__ = ("last_w", "readers", "excl")

    def __init__(self, excl=False):
        self.last_w = None
        self.readers = []
        self.excl = excl


class Sched:
    def __init__(self, nc):
        self.nc = nc
        self.ops = {e: [] for e in ("pe", "act", "dve", "pool", "sp")}
        self.stack = contextlib.ExitStack()
        self.nalloc = 0
        self.bufs = {}
        self.pending = {e: set() for e in self.ops}
        self.phase = "pro"

    def B(self, *key):
        b = self.bufs.get(key)
        if b is None:
            b = self.bufs[key] = Buf(excl=(key[0] in ("pf", "pb")))
        return b

    def sbuf(self, shape, dtype, name=None):
        self.nalloc += 1
        return self.stack.enter_context(self.nc.sbuf_tensor("sb_" + (name or f"t{self.nalloc}"), list(shape), dtype))

    def psum(self, shape, dtype, name=None):
        self.nalloc += 1
        return self.stack.enter_context(self.nc.psum_tensor("ps_" + (name or f"t{self.nalloc}"), list(shape), dtype))

    def add(self, eng, fn, reads=(), writes=(), dma=False):
        idx = len(self.ops[eng])
        deps = set(self.pending[eng])
        self.pending[eng] = set()
        for b in reads:
            if b.last_w is not None:
                deps.add(b.last_w)
            if b.excl:
                deps.update(r for r in b.readers if r[0] != eng)
        for b in writes:
            if b.last_w is not None:
                deps.add(b.last_w)
            deps.update(b.readers)
        if eng == "pe":
            deps = {d for d in deps if d[0] != "pe"}
        self.ops[eng].append({"fn": fn, "deps": deps, "dma": dma, "marked": False, "ph": self.phase})
        me = (eng, idx)
        for b in reads:
            b.readers.append(me)
        for b in writes:
            b.last_w = me
            b.readers = []
        return me

    def pe(self, fn, reads=(), writes=()):
        return self.add("pe", fn, reads, writes)

    def act(self, fn, reads=(), writes=()):
        return self.add("act", fn, reads, writes)

    def dve(self, fn, reads=(), writes=()):
        return self.add("dve", fn, reads, writes)

    def pool(self, fn, reads=(), writes=()):
        return self.add("pool", fn, reads, writes)

    def dma(self, fn, reads=(), writes=(), q="sp"):
        return self.add(q, fn, reads, writes, dma=True)

    def barrier(self):
        deps = set()
        for e in ("pe", "act", "dve"):
            if self.ops[e]:
                deps.add((e, len(self.ops[e]) - 1))
        for q in ("sp", "pool"):
            nd = 0
            gotc = False
            for i in range(len(self.ops[q]) - 1, -1, -1):
                if self.ops[q][i]["dma"]:
                    if nd < NDMA_SEM:
                        deps.add((q, i))
                        nd += 1
                elif not gotc:
                    deps.add((q, i))
                    gotc = True
                if nd >= NDMA_SEM and (gotc or q == "sp"):
                    break
        for e in self.pending:
            self.pending[e] |= deps

    def emit(self, final_bufs=()):
        nc = self.nc
        ops = self.ops
        dma_info = {}
        for q in ("sp", "pool"):
            n = 0
            for i, op in enumerate(ops[q]):
                if op["dma"]:
                    slot = n % NDMA_SEM
                    val = 16 * (n // NDMA_SEM + 1)
                    dma_info[(q, i)] = (q, slot, val)
                    op["dmainfo"] = (slot, val)
                    n += 1
        for e in ops:
            for op in ops[e]:
                op["deps"] = {d for d in op["deps"]}
                for d in op["deps"]:
                    if d not in dma_info:
                        ops[d[0]][d[1]]["marked"] = True
        final_deps = set()
        for b in final_bufs:
            if b.last_w is not None:
                final_deps.add(b.last_w)
                if b.last_w not in dma_info:
                    ops[b.last_w[0]][b.last_w[1]]["marked"] = True
        for e in COMPUTE:
            c = 0
            for op in ops[e]:
                if op["marked"] and not op["dma"]:
                    c += 1
                    op["semval"] = c
        st = self.stack
        csem = {e: st.enter_context(nc.semaphore(f"s_{e}")) for e in COMPUTE}
        dsem = {q: [st.enter_context(nc.semaphore(f"d_{q}{j}")) for j in range(NDMA_SEM)] for q in ("sp", "pool")}
        block = st.enter_context(nc.Block())
        engobj = {"pe": block.tensor, "act": block.scalar, "dve": block.vector, "pool": block.gpsimd, "sp": block.sync}

        def resolve(dep):
            if dep in dma_info:
                q, slot, val = dma_info[dep]
                return dsem[q][slot], ("d", q, slot), val
            f, k = dep
            return csem[f], ("c", f), ops[f][k]["semval"]

        def make_body(e):
            def body(eng):
                known = {}
                for op in ops[e]:
                    need = {}
                    for dep in op["deps"]:
                        if dep[0] == e and dep not in dma_info and e == "pe":
                            continue
                        sem, key, val = resolve(dep)
                        if need.get(key, (None, 0))[1] < val:
                            need[key] = (sem, val)
                    if op["dma"]:
                        slot, val = op["dmainfo"]
                        if val > 16:
                            key = ("d", e, slot)
                            if need.get(key, (None, 0))[1] < val - 16:
                                need[key] = (dsem[e][slot], val - 16)
                    for key, (sem, val) in need.items():
                        if known.get(key, 0) >= val:
                            continue
                        known[key] = val
                        eng.wait_ge(sem, val)
                    ins = op["fn"](eng)
                    if op["dma"]:
                        ins.then_inc(dsem[e][op["dmainfo"][0]], 16)
                    elif op["marked"]:
                        ins.then_inc(csem[e], 1)
                if e == "sp":
                    need = {}
                    for dep in final_deps:
                        sem, key, val = resolve(dep)
                        if need.get(key, (None, 0))[1] < val:
                            need[key] = (sem, val)
                    for key, (sem, val) in need.items():
                        eng.wait_ge(sem, val)
            return body

        for e in ("sp", "pool", "act", "dve", "pe"):
            engobj[e](make_body(e))

    def close(self):
        self.stack.close()


def I(name, *args, **kw):
    f = lambda eng: getattr(eng, name)(*args, **kw)
    try:
        f.f32 = name in ("matmul", "transpose") and args[1].dtype == F32
    except Exception:
        f.f32 = False
    return f


D = 1024
LC = 256
NL = 2048
T = LC + NL
NTT = T // 128
FH = 2816
NHC = FH // 128
DEPTH = 2
NSEQ = 2
BLOCKS = [(0, 256)] + [(256 + 512 * i, 512) for i in range(4)]
ALPHA2 = 2.0
LN_EPS_S = 1e-5 / ALPHA2
RMS_EPS = 1e-6
NCOMBO = 21
WBYTES = 91136

C_NAQ, C_NAK, C_NAV = 0, 384, 768
C_GLQ, C_GLK, C_GLV, C_GLG, C_GLR = 1152, 1280, 1408, 1664, 1920
C_GAQ, C_GAK, C_GAV = 1952, 2336, 2464


def na_combos():
    combos = []
    idx = {}

    def get(kind, r, d):
        key = (kind, d)
        if key not in idx:
            idx[key] = len(combos)
            combos.append((r, d))
        return idx[key]

    slots = []
    for p in range(16):
        r = 2 * p
        if p == 0:
            s = [(j, get("e0", r, 2 * j - r)) for j in range(0, 4)]
        elif p == 1:
            s = [(j, get("e1", r, 2 * j - r)) for j in range(0, 4)]
        elif p == 14:
            s = [(j, get("e2", r, 2 * j - r)) for j in range(12, 16)]
        elif p == 15:
            s = [(j, get("e3", r, 2 * j - r)) for j in range(12, 16)]
        else:
            s = [(j, get("g", 8, 2 * j - r)) for j in range(p - 2, p + 3)]
        slots.append(s)
    assert len(combos) == NCOMBO
    return combos, slots


def na_index_tables():
    combos, _ = na_combos()
    cols = np.arange(64)
    col_start = np.clip(cols - 8, 0, 48)
    col_in = (cols[None, :] >= col_start[:, None]) & (cols[None, :] < col_start[:, None] + 16)
    valid = np.zeros((NCOMBO, 128, 128), bool)
    dri = np.zeros((NCOMBO, 128, 128), np.int64)
    dci = np.zeros((NCOMBO, 128, 128), np.int64)
    for ci, (r, d) in enumerate(combos):
        for ko in range(2):
            for qo in range(2):
                kr = r + d + ko
                qr = r + qo
                rs = min(max(qr - 4, 0), 24)
                ok_row = (rs <= kr < rs + 8)
                dr = kr - qr
                blk_valid = col_in.T & ok_row
                dc = np.clip(cols[:, None] - cols[None, :], -15, 15) + 15
                valid[ci, 64 * ko:64 * ko + 64, 64 * qo:64 * qo + 64] = blk_valid
                dri[ci, 64 * ko:64 * ko + 64, 64 * qo:64 * qo + 64] = min(max(dr + 7, 0), 14)
                dci[ci, 64 * ko:64 * ko + 64, 64 * qo:64 * qo + 64] = dc
    return valid, dri, dci


def build_nc(stage=99, nseq=NSEQ, dumps=None):
    nc = bass.Bass("TRN2", target_bir_lowering=False)
    S = Sched(nc)
    B = S.B
    dumps = dumps if dumps is not None else {}

    def din(name, shape, dt=F32):
        return nc.dram_tensor(name, list(shape), dt, kind="ExternalInput").ap()

    x_d = din("x", [NSEQ, NL, D])
    ctx_d = din("ctx", [NSEQ, LC, D])
    cT_d = din("cT", [128, 8, 3])
    wada_d = din("w_ada", [DEPTH, D, 6 * D])
    bada_d = din("b_adaT", [DEPTH, 128, 48])
    win_d = din("w_in", [DEPTH, D, 2592])
    wout_d = din("w_out", [DEPTH, D, D])
    wf1_d = din("w_ffn_in", [DEPTH, D, 2 * FH])
    wf2_d = din("w_ffn_out", [DEPTH, FH, D])
    lnT_d = din("lnT", [128, DEPTH, 4, 8])
    nab_d = din("nab", [DEPTH, 128, 6, NCOMBO * 128])
    wa2_d = din("wa2", [DEPTH, 2, 32, 128])
    nba_d = din("nbaT", [128, DEPTH, 2])
    glnw_d = din("glnw", [DEPTH, 1, 256])
    qkw_d = din("qkwT", [128, DEPTH, 2])
    cos_d = din("cosT", [128, NL])
    sin_d = din("sinT", [128, NL])
    cm_d = din("cmats", [128, 5, 128])
    tri_d = din("trim", [128, 2, 256])
    out_d = nc.dram_tensor("out", [NSEQ, NL, D], F32, kind="ExternalOutput").ap()
    xres = [nc.dram_tensor(f"xres{s}", [128, 8, T], F32).ap() for s in range(NSEQ)]
    dump_list = []

    ubuf = S.sbuf([128, 8, T], BF16, "ubuf")
    obuf = S.sbuf([128, 8, T], BF16, "obuf")
    wbuf = [S.sbuf([128, 8 * 1152], BF16, f"wbuf{i}") for i in range(2)]

    def wview(i, dt, k, c):
        if dt == F32:
            return wbuf[i][:].bitcast(F32)[:, 0:k * c].rearrange("p (k c) -> p k c", k=k, c=c)
        return wbuf[i][:, 0:k * c].rearrange("p (k c) -> p k c", k=k, c=c)

    Wr = S.sbuf([128, WBYTES // 4], F32, "Wr")
    identf = S.sbuf([128, 128], F32, "identf")
    onesD = S.sbuf([128, 128], F32, "onesD")
    identb = S.sbuf([128, 128], BF16, "identb")
    bones = S.sbuf([128, 128], BF16, "bones")
    rotm = S.sbuf([128, 128], BF16, "rotm")
    trim = S.sbuf([128, 2, 256], BF16, "trim")
    scT = S.sbuf([128, 8, 3], F32, "scT")
    modT = S.sbuf([128, DEPTH, 48, 3], F32, "modT")
    mder = S.sbuf([128, DEPTH, 6, 8, 3], F32, "mder")
    mder2 = S.sbuf([128, DEPTH, 2, 8, 3], F32, "mder2")
    lnT = S.sbuf([128, DEPTH, 4, 8], F32, "lnT")
    badaT = S.sbuf([128, DEPTH, 48], F32, "badaT")
    nbaT = S.sbuf([128, DEPTH, 2], F32, "nbaT")
    qkwT = S.sbuf([128, DEPTH, 2], F32, "qkwT")
    glnw = S.sbuf([128, DEPTH, 256], F32, "glnw")
    wa2 = S.sbuf([32, DEPTH, 2, 128], F32, "wa2")
    hmask = S.sbuf([128, 4], F32, "hmask")
    epsln = S.sbuf([128, 1], F32, "epsln")
    epsrms = S.sbuf([128, 1], F32, "epsrms")

    pf = [S.psum([128, 512], F32, f"pf{i}") for i in range(7)]
    pb = S.psum([128, 1024], BF16, "pb")
    pctr = [0]

    def nextp():
        i = pctr[0] % 7
        pctr[0] += 1
        return pf[i], B("pf", i)

    class Carver:
        def __init__(self):
            self.off = 0

        def __call__(self, dt, *dims):
            n = int(np.prod(dims))
            nbytes = n * (4 if dt == F32 else 2)
            nbytes = (nbytes + 63) // 64 * 64
            a = self.off // 4
            self.off += nbytes
            assert self.off <= WBYTES, ("scratch overflow", self.off)
            ap = Wr[:, a:a + nbytes // 4]
            if dt == BF16:
                ap = ap.bitcast(BF16)[:, 0:n]
            else:
                ap = ap[:, 0:n]
            if len(dims) == 2:
                ap = ap.rearrange("p (a b) -> p a b", a=dims[0], b=dims[1])
            elif len(dims) == 3:
                ap = ap.rearrange("p (a b c) -> p a b c", a=dims[0], b=dims[1], c=dims[2])
            return ap

    def dump(name, ap, shape, dt, rbuf):
        d = nc.dram_tensor("dbg_" + name, list(shape), dt, kind="ExternalOutput").ap()
        S.dma(I("dma_start", out=d, in_=ap), reads=[rbuf], writes=[B("dump", name)])
        dump_list.append(B("dump", name))
        dumps[name] = (shape, dt)

    def ld(dst, src, bname, q="sp"):
        S.dma(I("dma_start", out=dst, in_=src), writes=[B(bname)], q=q)

    ld(identf[:], cm_d[:, 0, :], "identf")
    ld(onesD[:], cm_d[:, 1, :], "onesD")
    ld(identb[:], cm_d[:, 0, :], "identb", q="pool")
    ld(bones[:], cm_d[:, 2, :], "bones", q="pool")
    ld(rotm[:], cm_d[:, 3, :], "rotm", q="pool")
    ld(trim[:], tri_d, "trim", q="pool")
    ld(hmask[:], cm_d[:, 4, 0:4], "hmask")
    ld(scT[:], cT_d, "scT")
    ld(lnT[:], lnT_d, "lnT")
    ld(badaT[:], bada_d.rearrange("l p j -> p l j"), "badaT")
    ld(nbaT[:], nba_d, "nbaT")
    ld(qkwT[:], qkw_d, "qkwT")
    for l in range(DEPTH):
        ld(glnw[:, l, :], glnw_d[l].broadcast_to([128, 256]), "glnw")
        ld(wa2[:, l, :, :], wa2_d[l].rearrange("e r c -> r e c"), "wa2")
    S.dve(I("tensor_scalar_mul", out=nbaT[:], in0=nbaT[:], scalar1=-1.0), reads=[B("nbaT")], writes=[B("nbaT")])
    S.pool(I("memset", epsln[:], LN_EPS_S), writes=[B("eps")])
    S.pool(I("memset", epsrms[:], RMS_EPS), writes=[B("eps")])
    S.act(I("activation", out=scT[:], in_=scT[:], func=AF.Silu), reads=[B("scT")], writes=[B("scT")])

    for l in range(DEPTH):
        pm, bpm = nextp()
        for js in range(12):
            wv = wview(js % 2, F32, 8, 512)
            S.dma(I("dma_start",
                out=wv, in_=wada_d[l].rearrange("(k p) c -> p k c", p=128)[:, :, 512 * js:512 * js + 512]),
                writes=[B("wbuf", js % 2)])
            for jj in range(4):
                j = 4 * js + jj
                for k in range(8):
                    S.pe(I("matmul",
                        pm[:, 3 * j:3 * j + 3], wv[:, k, 128 * jj:128 * jj + 128], scT[:, k, :],
                        start=(k == 0), stop=(k == 7)),
                        reads=[B("wbuf", js % 2), B("scT")], writes=[bpm])
        S.dve(I("tensor_tensor",
            out=modT[:, l, :, :], in0=pm[:, 0:144].rearrange("p (j b) -> p j b", j=48, b=3),
            in1=badaT[:, l, :].unsqueeze(2).broadcast_to([128, 48, 3]), op=ALU.add),
            reads=[bpm, B("badaT")], writes=[B("modT")])
    inv_a = float(1.0 / np.sqrt(ALPHA2))
    for l in range(DEPTH):
        mT = modT[:, l, :, :]
        rd = [B("modT"), B("lnT")]
        wr = [B("mder")]
        S.dve(I("tensor_scalar_add", out=mder[:, l, 0, :, :], in0=mT[:, 8:16, :], scalar1=1.0), reads=rd, writes=wr)
        S.dve(I("tensor_scalar_mul", out=mder[:, l, 1, :, :], in0=mT[:, 16:24, :], scalar1=inv_a), reads=rd, writes=wr)
        S.dve(I("tensor_scalar_add", out=mder[:, l, 2, :, :], in0=mT[:, 32:40, :], scalar1=1.0), reads=rd, writes=wr)
        S.dve(I("tensor_scalar_mul", out=mder[:, l, 3, :, :], in0=mT[:, 40:48, :], scalar1=inv_a), reads=rd, writes=wr)
        S.dve(I("tensor_tensor", out=mder[:, l, 4, :, :], in0=mder[:, l, 2, :, :],
                                             in1=lnT[:, l, 0, :].unsqueeze(2).broadcast_to([128, 8, 3]), op=ALU.mult), reads=rd + wr, writes=wr)
        S.dve(I("tensor_tensor", out=mder[:, l, 5, :, :], in0=mder[:, l, 2, :, :],
                                             in1=lnT[:, l, 1, :].unsqueeze(2).broadcast_to([128, 8, 3]), op=ALU.mult), reads=rd + wr, writes=wr)
        S.dve(I("tensor_tensor", out=mder[:, l, 5, :, :], in0=mder[:, l, 5, :, :], in1=mT[:, 24:32, :], op=ALU.add), reads=rd + wr, writes=wr)
    for l in range(DEPTH - 1):
        rd = [B("modT"), B("lnT"), B("mder")]
        wr = [B("mder2")]
        S.dve(I("tensor_tensor", out=mder2[:, l, 0, :, :], in0=mder[:, l + 1, 0, :, :],
                                             in1=lnT[:, l, 2, :].unsqueeze(2).broadcast_to([128, 8, 3]), op=ALU.mult), reads=rd, writes=wr)
        S.dve(I("tensor_tensor", out=mder2[:, l, 1, :, :], in0=mder[:, l + 1, 0, :, :],
                                             in1=lnT[:, l, 3, :].unsqueeze(2).broadcast_to([128, 8, 3]), op=ALU.mult), reads=rd + wr, writes=wr)
        S.dve(I("tensor_tensor", out=mder2[:, l, 1, :, :], in0=mder2[:, l, 1, :, :], in1=modT[:, l + 1, 0:8, :], op=ALU.add), reads=rd + wr, writes=wr)
    MD = [B("mder"), B("mder2"), B("modT"), B("lnT")]
    if stage < 99:
        S.pool(I("memset", obuf[:], 0.0), writes=[B("obuf", tt) for tt in range(NTT)])
    if stage == 0:
        dump("modT", modT[:], [128, DEPTH, 48, 3], F32, B("modT"))

    def msel(s, t0):
        return 2 if t0 < LC else s

    gctr = {}

    def bank(group, banks):
        c = gctr.get(group, 0)
        gctr[group] = c + 1
        i = banks[c % len(banks)]
        return pf[i], B("pf", i)

    def tiles_of(t0, n):
        return range(t0 // 128, (t0 + n) // 128)

    def ub(t0, n):
        return [B("ubuf", tt) for tt in tiles_of(t0, n)]

    win_v = [win_d[l].rearrange("(k p) c -> p k c", p=128) for l in range(DEPTH)]
    combos, na_slots = na_combos()
    PBB = B("pb", 0)

    def load_x(s):
        S.phase = "load_x"
        S.barrier()
        cv = Carver()
        xin = [cv(F32, 1024) for _ in range(2)]
        xst = [cv(F32, 8, 128) for _ in range(2)]
        for tt in range(NTT):
            i2 = tt % 2
            src = ctx_d[s, 128 * tt:128 * tt + 128, :] if tt < 2 else x_d[s, 128 * (tt - 2):128 * (tt - 2) + 128, :]
            S.dma(I("dma_start", out=xin[i2], in_=src), writes=[B("xin", i2)])
            mi = msel(s, 128 * tt)
            for half in range(2):
                pt, bpt = bank("ld", [0, 1, 2, 3])
                for kk in range(4):
                    k = 4 * half + kk
                    S.pe(I("transpose", pt[:, 128 * kk:128 * kk + 128], xin[i2][:, 128 * k:128 * k + 128], identf[:]),
                         reads=[B("xin", i2), B("identf")], writes=[bpt])
                S.dve(I("tensor_copy",
                    out=xst[i2][:, 4 * half:4 * half + 4, :], in_=pt[:].rearrange("p (k t) -> p k t", k=4, t=128)),
                    reads=[bpt], writes=[B("xst", i2)])
                for kk in range(4):
                    k = 4 * half + kk
                    S.act(I("activation",
                        out=ubuf[:, k, 128 * tt:128 * tt + 128], in_=pt[:, 128 * kk:128 * kk + 128], func=AF.Identity,
                        scale=mder[:, 0, 0, k, mi:mi + 1], bias=modT[:, 0, k, mi:mi + 1]),
                        reads=[bpt] + MD, writes=[B("ubuf", tt)])
            S.dma(I("dma_start", out=xres[s][:, :, 128 * tt:128 * tt + 128], in_=xst[i2]),
                  reads=[B("xst", i2)], writes=[B("xres", s, tt)])

    def na_phase(s, l):
        S.phase = "na_phase" + str(l)
        S.barrier()
        cv = Carver()
        naq = cv(BF16, 3, T)
        nak = cv(BF16, 3, T)
        nav = cv(BF16, NTT, 6 * 65)
        natab = cv(BF16, 6, NCOMBO * 128)
        naP = [cv(BF16, 7, 128) for _ in range(3)]
        otok = [cv(BF16, 384) for _ in range(2)]
        rden = [cv(F32, 8) for _ in range(2)]
        nav4 = nav.rearrange("p t (h c) -> p t h c", h=6, c=65)
        wv = wview(0, BF16, 8, 1152)
        S.dma(I("dma_start", out=wv, in_=win_v[l][:, :, 0:1152]), writes=[B("wbuf", 0)], q="pool")
        S.dma(I("dma_start", out=natab, in_=nab_d[l]), writes=[B("natab")], q="pool")
        S.pool(I("memset", nav4[:, :, :, 64:65], 1.0), writes=[B("navones")])
        for (t0, n) in BLOCKS:
            for j in range(6):
                if j < 3 and t0 < LC and l == DEPTH - 1:
                    continue
                ps, bps = bank("proj", [0, 1, 2, 3, 4, 5, 6])
                for k in range(8):
                    S.pe(I("matmul", ps[:, 0:n], wv[:, k, 128 * j:128 * j + 128], ubuf[:, k, t0:t0 + n],
                                                                     start=(k == 0), stop=(k == 7)),
                         reads=[B("wbuf", 0)] + ub(t0, n), writes=[bps])
                wr = [B("naqk", tt) for tt in tiles_of(t0, n)]
                if j < 3:
                    S.act(I("activation", out=naq[:, j, t0:t0 + n], in_=ps[:, 0:n], func=AF.Identity, scale=0.125),
                          reads=[bps], writes=wr)
                else:
                    S.dve(I("tensor_copy", out=nak[:, j - 3, t0:t0 + n], in_=ps[:, 0:n]), reads=[bps], writes=wr)
        for tt in range(NTT):
            ps, bps = bank("proj", [0, 1, 2, 3, 4, 5, 6])
            for k in range(8):
                S.pe(I("matmul", ps[:, 0:384], ubuf[:, k, 128 * tt:128 * tt + 128], wv[:, k, 768:1152],
                                                         start=(k == 0), stop=(k == 7)),
                     reads=[B("wbuf", 0), B("ubuf", tt)], writes=[bps])
            S.dve(I("tensor_copy", out=nav4[:, tt, :, 0:64], in_=ps[:, 0:384].rearrange("p (h c) -> p h c", h=6, c=64)),
                  reads=[bps], writes=[B("nav", tt)])
        qtiles = []
        if l < DEPTH - 1:
            qtiles += [(0, [(0, None), (1, None)]), (1, [(0, None), (1, None)])]
        for p in range(16):
            qtiles.append((2 + p, [(2 + j, ci) for (j, ci) in na_slots[p]] + [(0, None), (1, None)]))
        naP3 = naP
        units = [(qi, h) for qi in range(len(qtiles)) for h in range(6)]
        pos_ = {}
        ust = {}

        def na_st(u):
            qi, h = units[u]
            tq, slots = qtiles[qi]
            if h == 0:
                pos_[qi] = bank("napo", [4, 5])
            ns = len(slots)
            c = h // 2
            p0 = 64 * (h % 2)
            pa, bpa = bank("napa", [0, 1, 2, 3])
            pa2, bpa2 = bank("napa", [0, 1, 2, 3])
            P = naP3[u % 3]
            bP = B("naP", u % 3)
            ust[u] = (P, bP)
            nlat = sum(1 for (_, ci_) in slots if ci_ is not None)
            ci0 = slots[0][1]
            has_b = [False, False]
            if nlat >= 4:
                S.pe(I("matmul", pa[:, 0:512], identb[:], natab[:, h, 128 * ci0:128 * ci0 + 512], start=True, stop=False),
                     reads=[B("natab"), B("identb")], writes=[bpa])
                has_b[0] = True
            if nlat == 5:
                S.pe(I("matmul", pa2[:, 0:128], identb[:], natab[:, h, 128 * (ci0 + 4):128 * (ci0 + 4) + 128], start=True, stop=False),
                     reads=[B("natab"), B("identb")], writes=[bpa2])
                has_b[1] = True
            for si, (tk, ci) in enumerate(slots):
                bi_ = 0 if si < 4 else 1
                bk, bbk = (pa, bpa) if si < 4 else (pa2, bpa2)
                col = 128 * (si % 4)
                last_in_bank = (si == min(ns, 4) - 1) if bi_ == 0 else (si == ns - 1)
                if has_b[bi_]:
                    st_, sp_ = False, last_in_bank
                else:
                    st_, sp_ = True, True
                S.pe(I("matmul", bk[:, col:col + 128], nak[p0:p0 + 64, c, 128 * tk:128 * tk + 128], naq[p0:p0 + 64, c, 128 * tq:128 * tq + 128],
                       start=st_, stop=sp_),
                     reads=[B("naqk", tk), B("naqk", tq)], writes=[bbk])
            n1 = min(ns, 4)
            S.act(I("activation", out=P[:, 0:n1, :], in_=pa[:, 0:128 * n1].rearrange("p (a b) -> p a b", a=n1, b=128), func=AF.Exp),
                  reads=[bpa], writes=[bP])
            if ns > 4:
                n2 = ns - 4
                S.act(I("activation", out=P[:, 4:4 + n2, :], in_=pa2[:, 0:128 * n2].rearrange("p (a b) -> p a b", a=n2, b=128), func=AF.Exp),
                      reads=[bpa2], writes=[bP])

        def na_pv(u):
            qi, h = units[u]
            tq, slots = qtiles[qi]
            ns = len(slots)
            po, bpo = pos_[qi]
            P, bP = ust.pop(u)
            for si, (tk, ci) in enumerate(slots):
                S.pe(I("matmul", po[:, 65 * h:65 * h + 65], P[:, si, :], nav[:, tk, 65 * h:65 * h + 65], start=(si == 0), stop=(si == ns - 1)),
                     reads=[bP, B("nav", tk), B("navones")], writes=[bpo])
            if h < 5:
                return
            o2 = qi % 2
            po3 = po[:, 0:390].rearrange("p (h c) -> p h c", h=6, c=65)
            S.dve(I("reciprocal", out=rden[o2][:, 0:6], in_=po3[:, :, 64]), reads=[bpo], writes=[B("rden", o2)])
            S.dve(I("tensor_tensor", out=otok[o2].rearrange("p (h c) -> p h c", h=6, c=64), in0=po3[:, :, 0:64],
                    in1=rden[o2][:, 0:6].unsqueeze(2).broadcast_to([128, 6, 64]), op=ALU.mult),
                  reads=[bpo, B("rden", o2)], writes=[B("otok", o2)])
            for c3 in range(3):
                S.pe(I("transpose", pb[:, 128 * c3:128 * c3 + 128], otok[o2][:, 128 * c3:128 * c3 + 128], identb[:]),
                     reads=[B("otok", o2), B("identb")], writes=[PBB])
            S.act(I("copy", out=obuf[:, 0:3, 128 * tq:128 * tq + 128], in_=pb[:, 0:384].rearrange("p (a b) -> p a b", a=3, b=128)),
                  reads=[PBB], writes=[B("obuf", tq)])

        LA = 1
        for u in range(len(units) + LA):
            if u < len(units):
                na_st(u)
            if u >= LA:
                na_pv(u - LA)

    def gqa_phase(s, l):
        S.phase = "gqa_phase" + str(l)
        S.barrier()
        cv = Carver()
        gaq = cv(BF16, 3, T)
        gak = cv(BF16, 2, T)
        gav = cv(BF16, NTT, 130)
        cosv = cv(F32, NL)
        sinv = cv(F32, NL)
        sqs = [cv(BF16, 512) for _ in range(2)]
        rstds = [cv(F32, 512) for _ in range(2)]
        qns = [cv(F32, 512) for _ in range(2)]
        qnbs = [cv(BF16, 512) for _ in range(2)]
        t1s = [cv(F32, 512) for _ in range(2)]
        t2s = [cv(F32, 512) for _ in range(2)]
        gaP = [cv(BF16, 512) for _ in range(6)]
        otok = cv(BF16, 4, 384)
        rden = [cv(F32, 8) for _ in range(2)]
        gav4 = gav.rearrange("p t (g c) -> p t g c", g=2, c=65)
        wv = wview(1, BF16, 8, 768)
        wb = [B("wbuf", 1)]
        S.dma(I("dma_start", out=wv[:, :, 0:384], in_=win_v[l][:, :, C_GAQ:C_GAQ + 384]), writes=wb, q="pool")
        for g in range(2):
            for d in range(2):
                S.dma(I("dma_start", out=wv[:, :, 384 + 128 * g + 64 * d:384 + 128 * g + 64 * d + 64],
                                                      in_=win_v[l][:, :, C_GAK + 64 * g:C_GAK + 64 * g + 64]), writes=wb, q="pool")
        S.dma(I("dma_start", out=wv[:, :, 640:768], in_=win_v[l][:, :, C_GAV:C_GAV + 128]), writes=wb, q="pool")
        S.dma(I("dma_start", out=cosv, in_=cos_d), writes=[B("cos")])
        S.dma(I("dma_start", out=sinv, in_=sin_d), writes=[B("cos")])
        S.pool(I("memset", gav4[:, :, :, 64:65], 1.0), writes=[B("gavones")])
        cb = 0
        for (t0, n) in BLOCKS:
            for j in range(5):
                if j < 3 and t0 < LC and l == DEPTH - 1:
                    continue
                p2 = cb % 2
                cb += 1
                sq, rstd, qn, qnb, t1, t2 = sqs[p2], rstds[p2], qns[p2], qnbs[p2], t1s[p2], t2s[p2]
                TB = lambda nm, p2=p2: B("gatmp", nm, p2)
                ps, bps = bank("proj", [0, 1, 2, 3])
                for k in range(8):
                    S.pe(I("matmul", ps[:, 0:n], wv[:, k, 128 * j:128 * j + 128], ubuf[:, k, t0:t0 + n],
                                                                     start=(k == 0), stop=(k == 7)),
                         reads=wb + ub(t0, n), writes=[bps])
                S.act(I("activation", out=sq[:, 0:n], in_=ps[:, 0:n], func=AF.Square), reads=[bps], writes=[TB("sq")])
                pm, bpm = bank("gams", [4, 5])
                S.pe(I("matmul", pm[:, 0:n], bones[:], sq[:, 0:n], start=True, stop=True), reads=[TB("sq"), B("bones")], writes=[bpm])
                S.act(I("activation", out=rstd[:, 0:n], in_=pm[:, 0:n], func=AF.Ln, bias=epsrms[:, 0:1], scale=1.0),
                      reads=[bpm, B("eps")], writes=[TB("rstd")])
                S.act(I("activation", out=rstd[:, 0:n], in_=rstd[:, 0:n], func=AF.Exp, scale=-0.5), reads=[TB("rstd")], writes=[TB("rstd")])
                wi = 0 if j < 3 else 1
                S.dve(I("scalar_tensor_tensor", out=qn[:, 0:n], in0=ps[:, 0:n], scalar=qkwT[:, l, wi:wi + 1], in1=rstd[:, 0:n],
                                                                      op0=ALU.mult, op1=ALU.mult),
                      reads=[bps, TB("rstd"), B("qkwT")], writes=[TB("qn")])
                dst = gaq[:, j, t0:t0 + n] if j < 3 else gak[:, j - 3, t0:t0 + n]
                wr = [B("gaqk", tt) for tt in tiles_of(t0, n)]
                if t0 >= LC:
                    S.act(I("copy", out=qnb[:, 0:n], in_=qn[:, 0:n]), reads=[TB("qn")], writes=[TB("qnb")])
                    pr, bpr = bank("gams", [4, 5])
                    S.pe(I("matmul", pr[:, 0:n], rotm[:], qnb[:, 0:n], start=True, stop=True), reads=[TB("qnb"), B("rotm")], writes=[bpr])
                    S.dve(I("tensor_tensor", out=t1[:, 0:n], in0=qn[:, 0:n], in1=cosv[:, t0 - LC:t0 - LC + n], op=ALU.mult),
                          reads=[TB("qn"), B("cos")], writes=[TB("t1")])
                    S.dve(I("tensor_tensor", out=t2[:, 0:n], in0=pr[:, 0:n], in1=sinv[:, t0 - LC:t0 - LC + n], op=ALU.mult),
                          reads=[bpr, B("cos")], writes=[TB("t2")])
                    S.dve(I("tensor_tensor", out=dst, in0=t1[:, 0:n], in1=t2[:, 0:n], op=ALU.add),
                          reads=[TB("t1"), TB("t2")], writes=wr)
                else:
                    S.act(I("copy", out=dst, in_=qn[:, 0:n]), reads=[TB("qn")], writes=wr)
        for tt in range(NTT):
            ps, bps = bank("proj", [0, 1, 2, 3])
            for k in range(8):
                S.pe(I("matmul", ps[:, 0:128], ubuf[:, k, 128 * tt:128 * tt + 128], wv[:, k, 640:768],
                                                         start=(k == 0), stop=(k == 7)),
                     reads=wb + [B("ubuf", tt)], writes=[bps])
            S.dve(I("tensor_copy", out=gav4[:, tt, :, 0:64], in_=ps[:, 0:128].rearrange("p (g c) -> p g c", g=2, c=64)),
                  reads=[bps], writes=[B("gav", tt)])
        qblocks = []
        if l < DEPTH - 1:
            qblocks.append((0, 256, [0, 1]))
        for i in range(4):
            qblocks.append((256 + 512 * i, 512, list(range(NTT))))
        pi = 0
        for (t0, n, ktiles) in qblocks:
            nsub = n // 128
            nk = len(ktiles)
            for h in range(6):
                g = h // 3
                c = h // 2
                p0 = 64 * (h % 2)
                po, bpo = bank("gapo", [5, 6])
                S.dve(I("memset", po[:, 0:65 * nsub], 0.0), writes=[bpo])
                Ps = {}

                def st(ki):
                    nonlocal pi
                    tk = ktiles[ki]
                    pa, bpa = bank("gapa", [0, 1, 2, 3, 4])
                    P = gaP[pi % 6]
                    bP = B("gaP", pi % 6)
                    pi += 1
                    Ps[ki] = (P, bP)
                    S.pe(I("matmul", pa[:, 0:n], gak[p0:p0 + 64, g, 128 * tk:128 * tk + 128], gaq[p0:p0 + 64, c, t0:t0 + n],
                                                        start=True, stop=True),
                         reads=[B("gaqk", tk)] + [B("gaqk", tt) for tt in tiles_of(t0, n)], writes=[bpa])
                    S.act(I("activation", out=P[:, 0:n], in_=pa[:, 0:n], func=AF.Exp, scale=0.125), reads=[bpa], writes=[bP])

                def pv(ki):
                    tk = ktiles[ki]
                    P, bP = Ps.pop(ki)
                    for qs in range(nsub):
                        S.pe(I("matmul", po[:, 65 * qs:65 * qs + 65], P[:, 128 * qs:128 * qs + 128], gav[:, tk, 65 * g:65 * g + 65],
                                                                        start=False, stop=(ki == nk - 1), skip_group_check=True),
                             reads=[bP, B("gav", tk), B("gavones")], writes=[bpo])

                GLA_ = 3
                for step in range(nk + GLA_):
                    if step < nk:
                        st(step)
                    if step >= GLA_:
                        pv(step - GLA_)
                o2 = h % 2
                po3 = po[:, 0:65 * nsub].rearrange("p (q c) -> p q c", q=nsub, c=65)
                S.dve(I("reciprocal", out=rden[o2][:, 0:nsub], in_=po3[:, :, 64]), reads=[bpo], writes=[B("rden", o2)])
                S.dve(I("tensor_tensor", out=otok[:, 0:nsub, 64 * h:64 * h + 64], in0=po3[:, :, 0:64],
                                                                   in1=rden[o2][:, 0:nsub].unsqueeze(2).broadcast_to([128, nsub, 64]), op=ALU.mult),
                      reads=[bpo, B("rden", o2)], writes=[B("gaotok")])
            for qs in range(nsub):
                tq = t0 // 128 + qs
                for c3 in range(3):
                    S.pe(I("transpose", pb[:, 128 * c3:128 * c3 + 128], otok[:, qs, 128 * c3:128 * c3 + 128], identb[:]),
                         reads=[B("gaotok"), B("identb")], writes=[PBB])
                S.act(I("copy", out=obuf[:, 5:8, 128 * tq:128 * tq + 128], in_=pb[:, 0:384].rearrange("p (a b) -> p a b", a=3, b=128)),
                      reads=[PBB], writes=[B("obuf", tq)])

    def gla_phase(s, l):
        S.phase = "gla_phase" + str(l)
        S.barrier()
        cv = Carver()
        glv = cv(BF16, NTT, 256)
        glgw = cv(BF16, NTT, 256)
        qt = [cv(BF16, T) for _ in range(2)]
        kt = [cv(BF16, T) for _ in range(2)]
        ktok = cv(BF16, NTT, 128)
        Dall = cv(F32, 2, 36)
        S32 = cv(F32, 37, 64)
        Sbf = [cv(BF16, 36, 64) for _ in range(2)]
        off_gate_tmp = cv.off
        qf = cv(F32, 512)
        kf = cv(F32, 512)
        lrf = cv(F32, 512)
        smask = cv(F32, 512)
        tAs = [cv(F32, 512) for _ in range(2)]
        tBs = [cv(F32, 512) for _ in range(2)]
        tCs = [cv(F32, 512) for _ in range(2)]
        tD = tCs[0]
        Abf = [cv(BF16, 2, 256) for _ in range(2)]
        qmb = [cv(BF16, 2, 4, 128) for _ in range(2)]
        osq = cv(F32, 256)
        sg = osq
        on = cv(F32, 256)
        otok = [cv(BF16, 256) for _ in range(2)]
        st4 = [cv(F32, 8) for _ in range(2)]
        TB = lambda n: B("gltmp", n)
        TB0 = TB
        wv = wview(0, BF16, 8, 800)
        wb = [B("wbuf", 0)]
        S.dma(I("dma_start", out=wv, in_=win_v[l][:, :, C_GLQ:C_GLQ + 800]), writes=wb, q="pool")
        S.pool(I("memset", smask, 1.0), writes=[TB0("smask")])
        S.pool(I("memset", smask.rearrange("p (c t) -> p c t", c=8, t=64)[:, :, 0:1], 0.0), writes=[TB0("smask")])
        for tt in range(NTT):
            ps, bps = bank("proj", [0, 1, 2, 3])
            for k in range(8):
                S.pe(I("matmul", ps[:, 0:512], ubuf[:, k, 128 * tt:128 * tt + 128], wv[:, k, 256:768],
                                                         start=(k == 0), stop=(k == 7)),
                     reads=wb + [B("ubuf", tt)], writes=[bps])
            S.dve(I("tensor_copy", out=glv[:, tt, :], in_=ps[:, 0:256]), reads=[bps], writes=[B("glv", tt)])
            S.act(I("activation", out=sg, in_=ps[:, 256:512], func=AF.Silu), reads=[bps], writes=[TB0("osq")])
            S.pool(I("tensor_tensor", out=glgw[:, tt, :], in0=sg, in1=glnw[:, l, :], op=ALU.mult),
                   reads=[TB0("osq"), B("glnw")], writes=[B("glgw", tt)])
        import os as _os
        _gs = int(_os.environ.get('GLASTOP', '9'))
        _sub = int(_os.environ.get('GLASUB', '9'))
        if _gs <= 0:
            return
        for (t0, n) in BLOCKS:
            nch = n // 64
            c0 = t0 // 64
            pq, bpq = bank("proj", [0, 1, 2, 3])
            pk, bpk = bank("proj", [0, 1, 2, 3])
            pl, bpl = bank("proj", [0, 1, 2, 3])
            for (pp, col, M) in ((pq, 0, 128), (pk, 128, 128), (pl, 768, 32)):
                bb = {id(pq): bpq, id(pk): bpk, id(pl): bpl}[id(pp)]
                for k in range(8):
                    S.pe(I("matmul", pp[0:M, 0:n], wv[:, k, col:col + M], ubuf[:, k, t0:t0 + n],
                                                                              start=(k == 0), stop=(k == 7)),
                         reads=wb + ub(t0, n), writes=[bb])
            S.act(I("activation", out=qf[:, 0:n], in_=pq[:, 0:n], func=AF.Identity, scale=float(32 ** -0.5)), reads=[bpq], writes=[TB0("qf")])
            S.dve(I("tensor_copy", out=kf[:, 0:n], in_=pk[:, 0:n]), reads=[bpk], writes=[TB0("kf")])
            S.dve(I("tensor_copy", out=lrf[0:32, 0:n], in_=pl[0:32, 0:n]), reads=[bpl], writes=[TB0("lrf")])
            for ed in range(2):
                tA, tB_, tC = tAs[ed], tBs[ed], tCs[ed]
                TBd = lambda n_, ed=ed: B("gltmp", n_, ed)
                pz, bpz = bank("glz", [4, 5])
                S.pe(I("matmul", pz[:, 0:n], wa2[:, l, ed, :], lrf[0:32, 0:n], start=True, stop=True),
                     reads=[TB0("lrf"), B("wa2")], writes=[bpz])
                S.act(I("activation", out=tA[:, 0:n], in_=pz[:, 0:n], func=AF.Exp, scale=-1.0, bias=nbaT[:, l, ed:ed + 1]),
                      reads=[bpz, B("nbaT")], writes=[TBd("tA")])
                S.act(I("activation", out=tB_[:, 0:n], in_=tA[:, 0:n], func=AF.Ln, bias=1.0, scale=1.0), reads=[TBd("tA")], writes=[TBd("tB")])
                S.dve(I("tensor_tensor_scan", out=tC[:, 0:n], data0=smask[:, 0:n], data1=tB_[:, 0:n], initial=0.0, op0=ALU.mult, op1=ALU.add),
                      reads=[TB0("smask"), TBd("tB")], writes=[TBd("tC")])
                cum = tC
                bcum = TBd("tC")
                if ed == 1:
                    tC3 = tC[:, 0:n].rearrange("p (c t) -> p c t", c=nch, t=64)
                    S.dve(I("tensor_tensor", out=tD[:, 0:n].rearrange("p (c t) -> p c t", c=nch, t=64),
                                                                         in0=tC3[:, :, 63:64].broadcast_to([128, nch, 64]), in1=tC3, op=ALU.subtract),
                          reads=[TBd("tC")], writes=[B("gltmp", "tC", 0)])
                    S.dve(I("tensor_tensor", out=tD[:, 0:n], in0=tD[:, 0:n], in1=tB_[:, 0:n], op=ALU.add), reads=[B("gltmp", "tC", 0), TBd("tB")], writes=[B("gltmp", "tC", 0)])
                    cum = tD
                    bcum = B("gltmp", "tC", 0)
                S.act(I("activation", out=tA[:, 0:n], in_=cum[:, 0:n], func=AF.Exp, scale=-1.0 / 16.0), reads=[bcum], writes=[TBd("tA")])
                S.act(I("activation", out=tB_[:, 0:n], in_=cum[:, 0:n], func=AF.Exp, scale=1.0 / 16.0), reads=[bcum, TBd("tA")], writes=[TBd("tB")])
                dcol = 63 if ed == 0 else 0
                S.dve(I("tensor_copy",
                    out=Dall[:, ed, c0:c0 + nch], in_=tA[:, 0:n].rearrange("p (c t) -> p c t", c=nch, t=64)[:, :, dcol]),
                    reads=[TBd("tA")], writes=[B("Dall")])
                wr = [B("glqk", ed, tt) for tt in tiles_of(t0, n)]
                S.dve(I("tensor_tensor", out=qt[ed][:, t0:t0 + n], in0=qf[:, 0:n], in1=tA[:, 0:n], op=ALU.mult),
                      reads=[TB0("qf"), TBd("tA")], writes=wr)
                S.dve(I("tensor_tensor", out=kt[ed][:, t0:t0 + n], in0=kf[:, 0:n], in1=tB_[:, 0:n], op=ALU.mult),
                       reads=[TB0("kf"), TBd("tB")], writes=wr)
        if _gs <= 1:
            return
        ords = [list(range(36)), [3, 2, 1, 0] + list(range(35, 3, -1))]
        poss = []
        for ed in range(2):
            pos = [0] * 36
            for j, ci in enumerate(ords[ed]):
                pos[ci] = j
            poss.append(pos)
        S.barrier()
        cv2 = Carver()
        cv2.off = off_gate_tmp
        Drep = cv2(F32, 64, 36)
        Sall = cv2(F32, 64 * 36)
        Dord = cv2(F32, 64)
        assert cv2.off <= off_gate_tmp + 20480
        Wd3 = S32.rearrange("p a b -> p (a b)")[:, 0:64 * 36].rearrange("p (v j) -> p v j", v=64, j=36)
        for ed in range(2):
            pos = poss[ed]
            for tt in range(NTT):
                S.pe(I("transpose", pb[:, 128 * (tt % 4):128 * (tt % 4) + 128], kt[ed][:, 128 * tt:128 * tt + 128], identb[:]),
                     reads=[B("glqk", ed, tt), B("identb")], writes=[PBB])
                if tt % 4 == 3 or tt == NTT - 1:
                    n4 = tt % 4 + 1
                    ta = tt - n4 + 1
                    S.act(I("copy", out=ktok[:, ta:ta + n4, :], in_=pb[:, 0:128 * n4].rearrange("p (a b) -> p a b", a=n4, b=128)),
                          reads=[PBB], writes=[B("ktok")])
            if ed == 0:
                S.dve(I("tensor_copy", out=Dord[:, 0:36], in_=Dall[:, 0, 0:36]), reads=[B("Dall")], writes=[B("Dord")])
            else:
                S.dve(I("tensor_copy", out=Dord[:, 0:4], in_=Dall[:, 1, 3::-1]), reads=[B("Dall")], writes=[B("Dord")])
                S.dve(I("tensor_copy", out=Dord[:, 4:36], in_=Dall[:, 1, 35:3:-1]), reads=[B("Dall")], writes=[B("Dord")])
            S.dve(I("memset", Dord[:, 0:1], 0.0), writes=[B("Dord")])
            S.dve(I("tensor_copy", out=Drep, in_=Dord[:, 0:36].unsqueeze(1).broadcast_to([128, 64, 36])), reads=[B("Dord")], writes=[B("Drep")])
            for half in range(2):
              for grp in range(0, 18, 8):
                pw, bpw = bank("glw", [0, 1, 2, 3])
                cis = [2 * t_ + half for t_ in range(grp, min(grp + 8, 18))]
                for gi, ci in enumerate(cis):
                    tt = ci // 2
                    for h in range(4):
                        S.pe(I("matmul",
                            pw[32 * h:32 * h + 32, 64 * gi:64 * gi + 64], ktok[64 * half:64 * half + 64, tt, 32 * h:32 * h + 32],
                            glv[64 * half:64 * half + 64, tt, 64 * h:64 * h + 64], start=True, stop=True, tile_position=(64 * half, 32 * h)),
                            reads=[B("ktok"), B("glv", tt)], writes=[bpw])
                for gi, ci in enumerate(cis):
                    S.act(I("activation",
                        out=Wd3[:, :, pos[ci]], in_=pw[:, 64 * gi:64 * gi + 64], func=AF.Identity, scale=Dall[:, ed, ci:ci + 1]),
                        reads=[bpw, B("Dall")], writes=[B("S32")])
            S.dve(I("tensor_tensor_scan", out=Sall, data0=Drep.rearrange("p v j -> p (v j)"), data1=Wd3.rearrange("p v j -> p (v j)"),
                    initial=0.0, op0=ALU.mult, op1=ALU.add),
                  reads=[B("Drep"), B("S32")], writes=[B("Sall")])
            S.dve(I("memset", Sbf[ed][:, 0, :], 0.0), writes=[B("Sbf", ed)])
            S.dve(I("tensor_copy", out=Sbf[ed][:, 1:36, :], in_=Sall.rearrange("p (v j) -> p j v", v=64, j=36)[:, 0:35, :]),
                  reads=[B("Sall")], writes=[B("Sbf", ed)])
        if _gs <= 2:
            return
        tiles_out = range(NTT) if l < DEPTH - 1 else range(2, NTT)
        tiles_out = list(tiles_out)

        def g_st1(oi):
            tt = tiles_out[oi]
            pa, bpa = bank("glpa", [0, 1, 2, 3])
            A = Abf[oi % 2]
            bA = B("Abf", oi % 2)
            qm = qmb[oi % 2]
            bqm = B("qm", oi % 2)
            for ed in range(2):
                for h in range(4):
                    eng = S.dve
                    eng(I("tensor_scalar_mul", out=qm[:, ed, h, :], in0=qt[ed][:, 128 * tt:128 * tt + 128], scalar1=hmask[:, h:h + 1]),
                        reads=[B("glqk", ed, tt), B("hmask")], writes=[bqm])
            for half in range(2):
                cols = slice(128 * tt + 64 * half, 128 * tt + 64 * half + 64)
                for ed in range(2):
                    for h in range(4):
                        S.pe(I("matmul",
                            pa[64 * half:64 * half + 64, 256 * ed + 64 * h:256 * ed + 64 * h + 64],
                            kt[ed][:, cols], qm[:, ed, h, 64 * half:64 * half + 64], start=True, stop=True,
                            tile_position=(0, 64 * half)),
                            reads=[B("glqk", ed, tt), bqm], writes=[bpa])
            S.dve(I("tensor_tensor", out=A, in0=pa[:, 0:512].rearrange("p (a b) -> p a b", a=2, b=256), in1=trim[:], op=ALU.mult),
                  reads=[bpa, B("trim")], writes=[bA])

        def g_st2(oi):
            tt = tiles_out[oi]
            A = Abf[oi % 2]
            bA = B("Abf", oi % 2)
            qm = qmb[oi % 2]
            bqm = B("qm", oi % 2)
            po, bpo = bank("glpo", [4, 5])
            for half in range(2):
                ci = 2 * tt + half
                for h in range(4):
                    for ed in range(2):
                        S.pe(I("matmul",
                            po[64 * half:64 * half + 64, 64 * h:64 * h + 64], A[64 * half:64 * half + 64, ed, 64 * h:64 * h + 64],
                            glv[64 * half:64 * half + 64, tt, 64 * h:64 * h + 64], start=(ed == 0), stop=False,
                            tile_position=(64 * half, 64 * half)),
                            reads=[bA, B("glv", tt)], writes=[bpo])
                    for ed in range(2):
                        pj = poss[ed][ci]
                        S.pe(I("matmul",
                            po[64 * half:64 * half + 64, 64 * h:64 * h + 64], qm[:, ed, h, 64 * half:64 * half + 64],
                            Sbf[ed][:, pj, :], start=False, stop=(ed == 1),
                            tile_position=(0, 64 * half)),
                            reads=[bqm, B("Sbf", ed)], writes=[bpo])
            o2 = oi % 2
            S.act(I("activation", out=osq, in_=po[:, 0:256], func=AF.Square), reads=[bpo], writes=[TB0("osq")])
            S.dve(I("reduce_sum", out=st4[o2][:, 0:4], in_=osq.rearrange("p (h c) -> p h c", h=4, c=64), axis=AX.X),
                  reads=[TB0("osq")], writes=[B("st4", o2)])
            S.dve(I("tensor_scalar", out=st4[o2][:, 0:4], in0=st4[o2][:, 0:4], scalar1=1.0 / 64.0, scalar2=RMS_EPS, op0=ALU.mult, op1=ALU.add),
                  reads=[B("st4", o2)], writes=[B("st4", o2)])
            S.act(I("activation", out=st4[o2][:, 4:8], in_=st4[o2][:, 0:4], func=AF.Sqrt), reads=[B("st4", o2)], writes=[B("st4", o2)])
            S.dve(I("reciprocal", out=st4[o2][:, 0:4], in_=st4[o2][:, 4:8]), reads=[B("st4", o2)], writes=[B("st4", o2)])
            S.dve(I("tensor_tensor", out=on.rearrange("p (h c) -> p h c", h=4, c=64), in0=po[:, 0:256].rearrange("p (h c) -> p h c", h=4, c=64),
                                                        in1=st4[o2][:, 0:4].unsqueeze(2).broadcast_to([128, 4, 64]), op=ALU.mult),
                  reads=[bpo, B("st4", o2)], writes=[TB0("on")])
            S.pool(I("tensor_tensor", out=otok[o2], in0=on, in1=glgw[:, tt, :], op=ALU.mult),
                   reads=[TB0("on"), B("glgw", tt)], writes=[B("glotok", o2)])

        def g_st3(oi):
            tt = tiles_out[oi]
            o2 = oi % 2
            for c2 in range(2):
                S.pe(I("transpose", pb[:, 512 + 128 * c2:512 + 128 * c2 + 128], otok[o2][:, 128 * c2:128 * c2 + 128], identb[:]),
                     reads=[B("glotok", o2), B("identb")], writes=[PBB])
            S.act(I("copy", out=obuf[:, 3:5, 128 * tt:128 * tt + 128], in_=pb[:, 512:768].rearrange("p (a b) -> p a b", a=2, b=128)),
                  reads=[PBB], writes=[B("obuf", tt)])

        for oi in range(len(tiles_out) + 2):
            if oi < len(tiles_out):
                g_st1(oi)
            if 1 <= oi <= len(tiles_out):
                g_st2(oi - 1)
            if oi >= 2:
                g_st3(oi - 2)

    def layer_norm_block(tbuf, bk, n, outs):
        pmean, bpmean = bank("lnm", [4, 5])
        pmsq, bpmsq = bank("lnm", [4, 5])
        for oc in range(8):
            S.pe(I("matmul", pmean[:, 0:n], onesD[:], tbuf[:, oc, 0:n], start=(oc == 0), stop=(oc == 7)),
                 reads=[bk(oc), B("onesD")], writes=[bpmean])
        for oc in range(8):
            sqb = lnsq[oc % 2]
            S.act(I("activation", out=sqb[:, 0:n], in_=tbuf[:, oc, 0:n], func=AF.Square), reads=[bk(oc)], writes=[B("lnsq", oc % 2)])
            S.pe(I("matmul", pmsq[:, 0:n], onesD[:], sqb[:, 0:n], start=(oc == 0), stop=(oc == 7)),
                 reads=[B("lnsq", oc % 2), B("onesD")], writes=[bpmsq])
        S.act(I("copy", out=lnmean[:, 0:n], in_=pmean[:, 0:n]), reads=[bpmean], writes=[B("lnmean")])
        S.dve(I("tensor_tensor", out=lnm2[:, 0:n], in0=lnmean[:, 0:n], in1=lnmean[:, 0:n], op=ALU.mult), reads=[B("lnmean")], writes=[B("lnm2")])
        S.dve(I("tensor_tensor", out=lnm2[:, 0:n], in0=pmsq[:, 0:n], in1=lnm2[:, 0:n], op=ALU.subtract), reads=[bpmsq, B("lnm2")], writes=[B("lnm2")])
        S.act(I("activation", out=lnm2[:, 0:n], in_=lnm2[:, 0:n], func=AF.Ln, bias=epsln[:, 0:1], scale=1.0), reads=[B("lnm2"), B("eps")], writes=[B("lnm2")])
        S.act(I("activation", out=lnrstd[:, 0:n], in_=lnm2[:, 0:n], func=AF.Exp, scale=-0.5), reads=[B("lnm2")], writes=[B("lnrstd")])
        for oc in range(8):
            S.dve(I("tensor_tensor", out=tbuf[:, oc, 0:n], in0=tbuf[:, oc, 0:n], in1=lnmean[:, 0:n], op=ALU.subtract),
                  reads=[bk(oc), B("lnmean")], writes=[bk(oc)])
            S.dve(I("tensor_tensor", out=tbuf[:, oc, 0:n], in0=tbuf[:, oc, 0:n], in1=lnrstd[:, 0:n], op=ALU.mult),
                  reads=[bk(oc), B("lnrstd")], writes=[bk(oc)])
            for (eng, dstf, scf, bif, wrf) in outs:
                if eng == "act":
                    S.act(I("activation", out=dstf(oc), in_=tbuf[:, oc, 0:n], func=AF.Identity, scale=scf(oc), bias=bif(oc)),
                          reads=[bk(oc)] + MD, writes=wrf(oc))
                else:
                    S.dve(I("tensor_scalar", out=dstf(oc), in0=tbuf[:, oc, 0:n], scalar1=scf(oc), scalar2=bif(oc),
                            op0=ALU.mult, op1=ALU.add),
                          reads=[bk(oc)] + MD, writes=wrf(oc))

    self_cv = [None]
    lnsq = [None, None]
    lnmean = lnm2 = lnrstd = None

    def outproj_phase(s, l):
        S.phase = "outproj_phase" + str(l)
        nonlocal lnsq, lnmean, lnm2, lnrstd
        S.barrier()
        cv = Carver()
        self_cv[0] = cv
        tbs = [cv(F32, 8, 512) for _ in range(2)]
        xss = [cv(F32, 8, 512) for _ in range(2)]
        lnsq = [cv(F32, 512) for _ in range(2)]
        lnmean = cv(F32, 512)
        lnm2 = cv(F32, 512)
        lnrstd = cv(F32, 512)
        wv = wview(1, BF16, 8, 1024)
        wb = [B("wbuf", 1)]
        S.dma(I("dma_start", out=wv, in_=wout_d[l].rearrange("(k p) c -> p k c", p=128)), writes=wb, q="pool")
        bi = 0
        for (t0, n) in BLOCKS:
            if t0 < LC and l == DEPTH - 1:
                continue
            tb = tbs[bi % 2]
            xs = xss[bi % 2]
            par = bi % 2
            btb = lambda oc, par=par: B("tb", par, oc)
            bxs = lambda oc, par=par: B("xs", par, oc)
            bxs_all = [bxs(oc) for oc in range(8)]
            bi += 1
            mi = msel(s, t0)
            xrb = [B("xres", s, tt) for tt in tiles_of(t0, n)]
            S.dma(I("dma_start", out=xs[:, :, 0:n], in_=xres[s][:, :, t0:t0 + n]), reads=xrb, writes=bxs_all)
            for oc in range(8):
                py, bpy = bank("proj", [0, 1, 2, 3])
                for k in range(8):
                    S.pe(I("matmul", py[:, 0:n], wv[:, k, 128 * oc:128 * oc + 128], obuf[:, k, t0:t0 + n],
                                                                       start=(k == 0), stop=(k == 7)),
                         reads=wb + [B("obuf", tt) for tt in tiles_of(t0, n)], writes=[bpy])
                S.dve(I("scalar_tensor_tensor", out=tb[:, oc, 0:n], in0=py[:, 0:n], scalar=mder[:, l, 1, oc, mi:mi + 1], in1=xs[:, oc, 0:n],
                                                                             op0=ALU.mult, op1=ALU.add),
                      reads=[bpy, bxs(oc)] + MD, writes=[btb(oc)])
            outs = [
                ("act", lambda oc, n=n, xs=xs: xs[:, oc, 0:n], lambda oc: lnT[:, l, 0, oc:oc + 1], lambda oc: lnT[:, l, 1, oc:oc + 1], lambda oc, bxs=bxs: [bxs(oc)]),
                ("dve", lambda oc, t0=t0, n=n: ubuf[:, oc, t0:t0 + n], lambda oc, mi=mi: mder[:, l, 4, oc, mi:mi + 1], lambda oc, mi=mi: mder[:, l, 5, oc, mi:mi + 1],
                 lambda oc, t0=t0, n=n: ub(t0, n)),
            ]
            layer_norm_block(tb, btb, n, outs)
            S.dma(I("dma_start", out=xres[s][:, :, t0:t0 + n], in_=xs[:, :, 0:n]), reads=bxs_all, writes=xrb)

    def ffn_phase(s, l):
        S.phase = "ffn_phase" + str(l)
        nonlocal lnsq, lnmean, lnm2, lnrstd
        S.barrier()
        last = (l == DEPTH - 1)
        cv = Carver()
        self_cv[0] = cv
        PTOK = 1024 if last else 1152
        hT = cv(BF16, NHC, PTOK)
        w2 = [cv(BF16, NHC, 256) for _ in range(2)]
        lnsq = [cv(F32, 512) for _ in range(2)]
        lnmean = cv(F32, 512)
        lnm2 = cv(F32, 512)
        lnrstd = cv(F32, 512)
        sa = [cv(F32, 512) for _ in range(2)]
        ostage = [cv(F32, 1024) for _ in range(2)] if last else None
        tfull = obuf[:].rearrange("p k t -> p (k t)").bitcast(F32)[:, 0:8 * PTOK].rearrange("p (k t) -> p k t", k=8, t=PTOK)
        if last:
            parts = [[BLOCKS[1], BLOCKS[2]], [BLOCKS[3], BLOCKS[4]]]
        else:
            parts = [[(0, 256), (256, 512), (768, 384)], [(1152, 512), (1664, 512), (2176, 128)]]
        wf1 = wf1_d[l].rearrange("(k p) c -> p k c", p=128)
        wf2 = wf2_d[l].rearrange("(j p) c -> p j c", p=128)
        for part in parts:
            pt0 = part[0][0]
            ptn = sum(n for (_, n) in part)
            mi = msel(s, pt0)
            nblk = len(part)
            btf = lambda bi_, oc: B("tfull", bi_, oc)
            btf_all = [btf(bi_, oc) for bi_ in range(nblk) for oc in range(8)]
            for js in range(11):
                wi = js % 2
                wv = wview(wi, BF16, 8, 512)
                wb = [B("wbuf", wi)]
                S.dma(I("dma_start", out=wv[:, :, 0:256], in_=wf1[:, :, 256 * js:256 * js + 256]), writes=wb, q="pool")
                S.dma(I("dma_start", out=wv[:, :, 256:512], in_=wf1[:, :, FH + 256 * js:FH + 256 * js + 256]), writes=wb, q="pool")
                for (t0, n) in part:
                    lo = t0 - pt0
                    for jj in range(2):
                        j = 2 * js + jj
                        pa, bpa = bank("ffa", [0, 1, 2])
                        pg, bpg = bank("ffb", [3, 4, 5])
                        for k in range(8):
                            S.pe(I("matmul", pa[:, 0:n], wv[:, k, 128 * jj:128 * jj + 128], ubuf[:, k, t0:t0 + n],
                                                                                      start=(k == 0), stop=(k == 7)),
                                 reads=wb + ub(t0, n), writes=[bpa])
                        for k in range(8):
                            S.pe(I("matmul", pg[:, 0:n], wv[:, k, 256 + 128 * jj:256 + 128 * jj + 128], ubuf[:, k, t0:t0 + n],
                                                                                      start=(k == 0), stop=(k == 7)),
                                 reads=wb + ub(t0, n), writes=[bpg])
                        si = j % 2
                        S.act(I("activation", out=sa[si][:, 0:n], in_=pa[:, 0:n], func=AF.Silu), reads=[bpa], writes=[B("sa", si)])
                        S.dve(I("tensor_tensor", out=hT[:, j, lo:lo + n], in0=pg[:, 0:n], in1=sa[si][:, 0:n], op=ALU.mult),
                              reads=[bpg, B("sa", si)], writes=[B("hT")])
            S.dma(I("dma_start", out=tfull[:, :, 0:ptn], in_=xres[s][:, :, pt0:pt0 + ptn]),
                  reads=[B("xres", s, tt) for tt in tiles_of(pt0, ptn)], writes=btf_all)
            for oc2 in range(4):
                w2v = w2[oc2 % 2]
                bw2 = [B("w2", oc2 % 2)]
                S.dma(I("dma_start", out=w2v, in_=wf2[:, :, 256 * oc2:256 * oc2 + 256]), writes=bw2, q="pool")
                for bi_, (t0, n) in enumerate(part):
                    lo = t0 - pt0
                    mi = msel(s, t0)
                    for oo in range(2):
                        oc = 2 * oc2 + oo
                        py, bpy = bank("ffy", [0, 1, 2, 3])
                        for j in range(NHC):
                            S.pe(I("matmul", py[:, 0:n], w2v[:, j, 128 * oo:128 * oo + 128], hT[:, j, lo:lo + n],
                                                                                        start=(j == 0), stop=(j == NHC - 1)),
                                 reads=bw2 + [B("hT")], writes=[bpy])
                        S.dve(I("scalar_tensor_tensor", out=tfull[:, oc, lo:lo + n], in0=py[:, 0:n], scalar=mder[:, l, 3, oc, mi:mi + 1],
                                                                                            in1=tfull[:, oc, lo:lo + n], op0=ALU.mult, op1=ALU.add),
                              reads=[bpy, btf(bi_, oc)] + MD, writes=[btf(bi_, oc)])
            for bi_, (t0, n) in enumerate(part):
                lo = t0 - pt0
                mi = msel(s, t0)
                tbv = tfull[:, :, lo:lo + n]
                bkf = lambda oc, bi_=bi_: btf(bi_, oc)
                bkf_all = [bkf(oc) for oc in range(8)]
                xrb = [B("xres", s, tt) for tt in tiles_of(t0, n)]
                if not last:
                    outs = [
                        ("dve", lambda oc, t0=t0, n=n: ubuf[:, oc, t0:t0 + n], lambda oc, mi=mi: mder2[:, l, 0, oc, mi:mi + 1], lambda oc, mi=mi: mder2[:, l, 1, oc, mi:mi + 1],
                         lambda oc, t0=t0, n=n: ub(t0, n)),
                        ("act", lambda oc, tbv=tbv: tbv[:, oc, :], lambda oc: lnT[:, l, 2, oc:oc + 1], lambda oc: lnT[:, l, 3, oc:oc + 1], lambda oc, bkf=bkf: [bkf(oc)]),
                    ]
                    layer_norm_block(tbv, bkf, n, outs)
                    S.dma(I("dma_start", out=xres[s][:, :, t0:t0 + n], in_=tbv), reads=bkf_all, writes=xrb)
                else:
                    outs = [("act", lambda oc, tbv=tbv: tbv[:, oc, :], lambda oc: lnT[:, l, 2, oc:oc + 1], lambda oc: lnT[:, l, 3, oc:oc + 1], lambda oc, bkf=bkf: [bkf(oc)])]
                    layer_norm_block(tbv, bkf, n, outs)
                    for qs in range(n // 128):
                        tq = (t0 + 128 * qs) // 128
                        og = ostage[tq % 2]
                        for half in range(2):
                            pt, bpt = bank("ffo", [0, 1, 2, 3])
                            for kk in range(4):
                                k = 4 * half + kk
                                S.pe(I("transpose", pt[:, 128 * kk:128 * kk + 128], tbv[:, k, 128 * qs:128 * qs + 128], identf[:]),
                                     reads=[bkf(k), B("identf")], writes=[bpt])
                            S.act(I("copy", out=og[:, 512 * half:512 * half + 512], in_=pt[:, 0:512]),
                                  reads=[bpt], writes=[B("ostage", tq % 2)])
                        S.dma(I("dma_start", out=out_d[s, 128 * (tq - 2):128 * (tq - 2) + 128, :], in_=og),
                              reads=[B("ostage", tq % 2)], writes=[B("out")])

    for s in range(nseq):
        load_x(s)
        if stage == 1 and s == 0:
            dump("u0", ubuf[:], [128, 8, T], BF16, B("ubuf", NTT - 1))
        for l in range(DEPTH):
            if stage >= 2:
                na_phase(s, l)
            if stage >= 3:
                gla_phase(s, l)
            if stage >= 4:
                gqa_phase(s, l)
            if stage in (2, 3, 4) and s == 0 and l == 0:
                S.barrier()
                S.pool(I("memset", epsln[:], LN_EPS_S), reads=[B("obuf", tt) for tt in range(NTT)], writes=[B("eps")])
                dump("o0", obuf[:], [128, 8, T], BF16, B("eps"))
                break
            if stage >= 5:
                outproj_phase(s, l)
            if stage == 5 and s == 0 and l == 0:
                S.barrier()
                S.pool(I("memset", epsln[:], LN_EPS_S), reads=[B("ubuf", tt) for tt in range(NTT)], writes=[B("eps")])
                dump("u2", ubuf[:], [128, 8, T], BF16, B("eps"))
                dump("x1", xres[0], [128, 8, T], F32, B("eps"))
                break
            if stage >= 6:
                ffn_phase(s, l)
            if stage == 6 and s == 0 and l == 0:
                S.barrier()
                S.pool(I("memset", epsln[:], LN_EPS_S), reads=[B("ubuf", tt) for tt in range(NTT)], writes=[B("eps")])
                dump("u1n", ubuf[:], [128, 8, T], BF16, B("eps"))
                dump("x2", xres[0], [128, 8, T], F32, B("eps"))
                break


    S.emit(final_bufs=dump_list + [B("out")])
    S.close()
    global LAST_SCHED
    LAST_SCHED = S
    return nc


def host_consts():
    eye = np.eye(128, dtype=np.float32)
    onesd = np.full((128, 128), 1.0 / 1024.0, np.float32)
    bon = np.zeros((128, 128), np.float32)
    bon[:64, :64] = 1.0 / 64
    bon[64:, 64:] = 1.0 / 64
    R = np.zeros((64, 64), np.float32)
    for d in range(16):
        R[d, d + 16] = -1.0
        R[d + 16, d] = 1.0
        R[d + 32, d + 48] = -1.0
        R[d + 48, d + 32] = 1.0
    rl = np.zeros((128, 128), np.float32)
    rl[:64, :64] = R.T
    rl[64:, 64:] = R.T
    hm = np.zeros_like(eye)
    for h in range(4):
        hm[32 * h:32 * h + 32, h] = 1.0
    cm = np.stack([eye, onesd, bon, rl, hm], axis=1)
    sidx = np.arange(64)
    mf = (sidx[None, :] >= sidx[:, None]).astype(np.float32)
    mb = (sidx[None, :] <= sidx[:, None]).astype(np.float32)
    tri = np.zeros((128, 2, 256), np.float32)
    for half in range(2):
        tri[64 * half:64 * half + 64, 0, :] = np.tile(mf, (1, 4))
        tri[64 * half:64 * half + 64, 1, :] = np.tile(mb, (1, 4))
    t = np.arange(NL)
    row = (t // 64).astype(np.float32)
    col = (t % 64).astype(np.float32)
    inv_freq = (10000.0 ** (-np.arange(16, dtype=np.float32) / 16)).astype(np.float32)
    ang_r = row[:, None] * inv_freq
    ang_c = col[:, None] * inv_freq
    ang = np.concatenate([ang_r, ang_r, ang_c, ang_c], axis=-1).astype(np.float32)
    cosT = np.tile(np.cos(ang).T, (2, 1)).astype(np.float32)
    sinT = np.tile(np.sin(ang).T, (2, 1)).astype(np.float32)
    return cm, tri, np.ascontiguousarray(cosT), np.ascontiguousarray(sinT)


def prep_shared(inp):
    f = lambda a: np.ascontiguousarray(np.asarray(a, dtype=np.float32))
    cm, tri, cosT, sinT = host_consts()
    sh = {}
    sh["w_ada"] = f(inp["w_ada"])
    sh["b_adaT"] = f(np.asarray(inp["b_ada"]).reshape(DEPTH, 48, 128).transpose(0, 2, 1))
    sh["w_in"] = f(inp["w_in"])
    sh["w_out"] = f(inp["w_out"])
    sh["w_ffn_in"] = f(inp["w_ffn_in"])
    sh["w_ffn_out"] = f(inp["w_ffn_out"])
    ln = np.stack([np.asarray(inp[k]) for k in ("ln1_g", "ln1_b", "ln2_g", "ln2_b")], axis=1)
    sh["lnT"] = f(ln.reshape(DEPTH, 4, 8, 128).transpose(3, 0, 1, 2))
    valid, dri, dci = na_index_tables()
    rpb = np.asarray(inp["na_rpb"], dtype=np.float32)
    g = rpb[:, :, dri, dci]
    g = np.where(valid[None, None], g, np.float32(-30000.0))
    sh["nab"] = f(g.transpose(0, 3, 1, 2, 4).reshape(DEPTH, 128, 6, NCOMBO * 128))
    wa2 = np.zeros((DEPTH, 2, 32, 128), np.float32)
    gw = np.asarray(inp["gla_wa2"], dtype=np.float32)
    wa2[:, 0, 0:16, :] = gw[:, 0]
    wa2[:, 1, 16:32, :] = gw[:, 1]
    sh["wa2"] = wa2
    sh["nbaT"] = f(np.asarray(inp["gla_ba"]).transpose(2, 0, 1))
    sh["glnw"] = f(np.tile(np.asarray(inp["gla_norm_w"]), (1, 4)).reshape(DEPTH, 1, 256))
    qk = np.stack([np.tile(np.asarray(inp["gqa_qnorm_w"]), (1, 2)), np.tile(np.asarray(inp["gqa_knorm_w"]), (1, 2))], axis=2)
    sh["qkwT"] = f(qk.transpose(1, 0, 2))
    sh["cosT"] = cosT
    sh["sinT"] = sinT
    sh["cmats"] = f(cm)
    sh["trim"] = f(tri)
    return sh


def prep_core(inp, core, sh):
    f = lambda a: np.ascontiguousarray(np.asarray(a, dtype=np.float32))
    b0 = core * NSEQ
    m = dict(sh)
    m["x"] = f(inp["x"][b0:b0 + NSEQ])
    m["ctx"] = f(inp["ctx"][b0:b0 + NSEQ])
    cc = np.stack([np.asarray(inp["c"][b0]), np.asarray(inp["c"][b0 + 1]), np.asarray(inp["c_ctx"])], axis=1)
    m["cT"] = f(cc.reshape(8, 128, 3).transpose(1, 0, 2))
    return m


def kernel(**inputs):
    sh = prep_shared(inputs)
    nc = build_nc()
    in_maps = [prep_core(inputs, c, sh) for c in range(8)]
    res = run_bass_kernel_spmd(nc, in_maps, core_ids=list(range(8)))
    out = np.concatenate([np.asarray(r["out"]) for r in res.results], axis=0)
    return out.astype(np.float32)
```

```python
import contextlib
import numpy as np
import concourse.bass as bass
import concourse.mybir as mybir
from concourse.bass_utils import run_bass_kernel_spmd

F32 = mybir.dt.float32
BF16 = mybir.dt.bfloat16
AF = mybir.ActivationFunctionType
ALU = mybir.AluOpType
AX = mybir.AxisListType

COMPUTE = ("pe", "act", "dve", "pool")
NDMA_SEM = 6


class Buf:
    __slots__ = ("last_w", "readers", "excl")

    def __init__(self, excl=False):
        self.last_w = None
        self.readers = []
        self.excl = excl


class Sched:
    def __init__(self, nc):
        self.nc = nc
        self.ops = {e: [] for e in ("pe", "act", "dve", "pool", "sp")}
        self.stack = contextlib.ExitStack()
        self.nalloc = 0
        self.bufs = {}
        self.pending = {e: set() for e in self.ops}
        self.phase = "pro"

    def B(self, *key):
        b = self.bufs.get(key)
        if b is None:
            b = self.bufs[key] = Buf(excl=(key[0] in ("pf", "pb")))
        return b

    def sbuf(self, shape, dtype, name=None):
        self.nalloc += 1
        return self.stack.enter_context(self.nc.sbuf_tensor("sb_" + (name or f"t{self.nalloc}"), list(shape), dtype))

    def psum(self, shape, dtype, name=None):
        self.nalloc += 1
        return self.stack.enter_context(self.nc.psum_tensor("ps_" + (name or f"t{self.nalloc}"), list(shape), dtype))

    def add(self, eng, fn, reads=(), writes=(), dma=False):
        idx = len(self.ops[eng])
        deps = set(self.pending[eng])
        self.pending[eng] = set()
        for b in reads:
            if b.last_w is not None:
                deps.add(b.last_w)
            if b.excl:
                deps.update(r for r in b.readers if r[0] != eng)
        for b in writes:
            if b.last_w is not None:
                deps.add(b.last_w)
            deps.update(b.readers)
        if eng == "pe":
            deps = {d for d in deps if d[0] != "pe"}
        self.ops[eng].append({"fn": fn, "deps": deps, "dma": dma, "marked": False, "ph": self.phase})
        me = (eng, idx)
        for b in reads:
            b.readers.append(me)
        for b in writes:
            b.last_w = me
            b.readers = []
        return me

    def pe(self, fn, reads=(), writes=()):
        return self.add("pe", fn, reads, writes)

    def act(self, fn, reads=(), writes=()):
        return self.add("act", fn, reads, writes)

    def dve(self, fn, reads=(), writes=()):
        return self.add("dve", fn, reads, writes)

    def pool(self, fn, reads=(), writes=()):
        return self.add("pool", fn, reads, writes)

    def dma(self, fn, reads=(), writes=(), q="sp"):
        return self.add(q, fn, reads, writes, dma=True)

    def barrier(self):
        deps = set()
        for e in ("pe", "act", "dve"):
            if self.ops[e]:
                deps.add((e, len(self.ops[e]) - 1))
        for q in ("sp", "pool"):
            nd = 0
            gotc = False
            for i in range(len(self.ops[q]) - 1, -1, -1):
                if self.ops[q][i]["dma"]:
                    if nd < NDMA_SEM:
                        deps.add((q, i))
                        nd += 1
                elif not gotc:
                    deps.add((q, i))
                    gotc = True
                if nd >= NDMA_SEM and (gotc or q == "sp"):
                    break
        for e in self.pending:
            self.pending[e] |= deps

    def emit(self, final_bufs=()):
        nc = self.nc
        ops = self.ops
        dma_info = {}
        for q in ("sp", "pool"):
            n = 0
            for i, op in enumerate(ops[q]):
                if op["dma"]:
                    slot = n % NDMA_SEM
                    val = 16 * (n // NDMA_SEM + 1)
                    dma_info[(q, i)] = (q, slot, val)
                    op["dmainfo"] = (slot, val)
                    n += 1
        for e in ops:
            for op in ops[e]:
                op["deps"] = {d for d in op["deps"]}
                for d in op["deps"]:
                    if d not in dma_info:
                        ops[d[0]][d[1]]["marked"] = True
        final_deps = set()
        for b in final_bufs:
            if b.last_w is not None:
                final_deps.add(b.last_w)
                if b.last_w not in dma_info:
                    ops[b.last_w[0]][b.last_w[1]]["marked"] = True
        for e in COMPUTE:
            c = 0
            for op in ops[e]:
                if op["marked"] and not op["dma"]:
                    c += 1
                    op["semval"] = c
        st = self.stack
        csem = {e: st.enter_context(nc.semaphore(f"s_{e}")) for e in COMPUTE}
        dsem = {q: [st.enter_context(nc.semaphore(f"d_{q}{j}")) for j in range(NDMA_SEM)] for q in ("sp", "pool")}
        block = st.enter_context(nc.Block())
        engobj = {"pe": block.tensor, "act": block.scalar, "dve": block.vector, "pool": block.gpsimd, "sp": block.sync}

        def resolve(dep):
            if dep in dma_info:
                q, slot, val = dma_info[dep]
                return dsem[q][slot], ("d", q, slot), val
            f, k = dep
            return csem[f], ("c", f), ops[f][k]["semval"]

        def make_body(e):
            def body(eng):
                known = {}
                for op in ops[e]:
                    need = {}
                    for dep in op["deps"]:
                        if dep[0] == e and dep not in dma_info and e == "pe":
                            continue
                        sem, key, val = resolve(dep)
                        if need.get(key, (None, 0))[1] < val:
                            need[key] = (sem, val)
                    if op["dma"]:
                        slot, val = op["dmainfo"]
                        if val > 16:
                            key = ("d", e, slot)
                            if need.get(key, (None, 0))[1] < val - 16:
                                need[key] = (dsem[e][slot], val - 16)
                    for key, (sem, val) in need.items():
                        if known.get(key, 0) >= val:
                            continue
                        known[key] = val
                        eng.wait_ge(sem, val)
                    ins = op["fn"](eng)
                    if op["dma"]:
                        ins.then_inc(dsem[e][op["dmainfo"][0]], 16)
                    elif op["marked"]:
                        ins.then_inc(csem[e], 1)
                if e == "sp":
                    need = {}
                    for dep in final_deps:
                        sem, key, val = resolve(dep)
                        if need.get(key, (None, 0))[1] < val:
                            need[key] = (sem, val)
                    for key, (sem, val) in need.items():
                        eng.wait_ge(sem, val)
            return body

        for e in ("sp", "pool", "act", "dve", "pe"):
            engobj[e](make_body(e))

    def close(self):
        self.stack.close()


def I(name, *args, **kw):
    f = lambda eng: getattr(eng, name)(*args, **kw)
    try:
        f.f32 = name in ("matmul", "transpose") and args[1].dtype == F32
    except Exception:
        f.f32 = False
    return f


D = 1024
LC = 256
NL = 2048
T = LC + NL
NTT = T // 128
FH = 2816
NHC = FH // 128
DEPTH = 2
NSEQ = 2
BLOCKS = [(0, 256)] + [(256 + 512 * i, 512) for i in range(4)]
ALPHA2 = 2.0
LN_EPS_S = 1e-5 / ALPHA2
RMS_EPS = 1e-6
NCOMBO = 21
WBYTES = 91136

C_NAQ, C_NAK, C_NAV = 0, 384, 768
C_GLQ, C_GLK, C_GLV, C_GLG, C_GLR = 1152, 1280, 1408, 1664, 1920
C_GAQ, C_GAK, C_GAV = 1952, 2336, 2464


def na_combos():
    combos = []
    idx = {}

    def get(kind, r, d):
        key = (kind, d)
        if key not in idx:
            idx[key] = len(combos)
            combos.append((r, d))
        return idx[key]

    slots = []
    for p in range(16):
        r = 2 * p
        if p == 0:
            s = [(j, get("e0", r, 2 * j - r)) for j in range(0, 4)]
        elif p == 1:
            s = [(j, get("e1", r, 2 * j - r)) for j in range(0, 4)]
        elif p == 14:
            s = [(j, get("e2", r, 2 * j - r)) for j in range(12, 16)]
        elif p == 15:
            s = [(j, get("e3", r, 2 * j - r)) for j in range(12, 16)]
        else:
            s = [(j, get("g", 8, 2 * j - r)) for j in range(p - 2, p + 3)]
        slots.append(s)
    assert len(combos) == NCOMBO
    return combos, slots


def na_index_tables():
    combos, _ = na_combos()
    cols = np.arange(64)
    col_start = np.clip(cols - 8, 0, 48)
    col_in = (cols[None, :] >= col_start[:, None]) & (cols[None, :] < col_start[:, None] + 16)
    valid = np.zeros((NCOMBO, 128, 128), bool)
    dri = np.zeros((NCOMBO, 128, 128), np.int64)
    dci = np.zeros((NCOMBO, 128, 128), np.int64)
    for ci, (r, d) in enumerate(combos):
        for ko in range(2):
            for qo in range(2):
                kr = r + d + ko
                qr = r + qo
                rs = min(max(qr - 4, 0), 24)
                ok_row = (rs <= kr < rs + 8)
                dr = kr - qr
                blk_valid = col_in.T & ok_row
                dc = np.clip(cols[:, None] - cols[None, :], -15, 15) + 15
                valid[ci, 64 * ko:64 * ko + 64, 64 * qo:64 * qo + 64] = blk_valid
                dri[ci, 64 * ko:64 * ko + 64, 64 * qo:64 * qo + 64] = min(max(dr + 7, 0), 14)
                dci[ci, 64 * ko:64 * ko + 64, 64 * qo:64 * qo + 64] = dc
    return valid, dri, dci


def build_nc(stage=99, nseq=NSEQ, dumps=None):
    nc = bass.Bass("TRN2", target_bir_lowering=False)
    S = Sched(nc)
    B = S.B
    dumps = dumps if dumps is not None else {}

    def din(name, shape, dt=F32):
        return nc.dram_tensor(name, list(shape), dt, kind="ExternalInput").ap()

    x_d = din("x", [NSEQ, NL, D])
    ctx_d = din("ctx", [NSEQ, LC, D])
    cT_d = din("cT", [128, 8, 3])
    wada_d = din("w_ada", [DEPTH, D, 6 * D])
    bada_d = din("b_adaT", [DEPTH, 128, 48])
    win_d = din("w_in", [DEPTH, D, 2592])
    wout_d = din("w_out", [DEPTH, D, D])
    wf1_d = din("w_ffn_in", [DEPTH, D, 2 * FH])
    wf2_d = din("w_ffn_out", [DEPTH, FH, D])
    lnT_d = din("lnT", [128, DEPTH, 4, 8])
    nab_d = din("nab", [DEPTH, 128, 6, NCOMBO * 128])
    wa2_d = din("wa2", [DEPTH, 2, 32, 128])
    nba_d = din("nbaT", [128, DEPTH, 2])
    glnw_d = din("glnw", [DEPTH, 1, 256])
    qkw_d = din("qkwT", [128, DEPTH, 2])
    cos_d = din("cosT", [128, NL])
    sin_d = din("sinT", [128, NL])
    cm_d = din("cmats", [128, 5, 128])
    tri_d = din("trim", [128, 2, 256])
    out_d = nc.dram_tensor("out", [NSEQ, NL, D], F32, kind="ExternalOutput").ap()
    xres = [nc.dram_tensor(f"xres{s}", [128, 8, T], F32).ap() for s in range(NSEQ)]
    dump_list = []

    ubuf = S.sbuf([128, 8, T], BF16, "ubuf")
    obuf = S.sbuf([128, 8, T], BF16, "obuf")
    wbuf = [S.sbuf([128, 8 * 1152], BF16, f"wbuf{i}") for i in range(2)]

    def wview(i, dt, k, c):
        if dt == F32:
            return wbuf[i][:].bitcast(F32)[:, 0:k * c].rearrange("p (k c) -> p k c", k=k, c=c)
        return wbuf[i][:, 0:k * c].rearrange("p (k c) -> p k c", k=k, c=c)

    Wr = S.sbuf([128, WBYTES // 4], F32, "Wr")
    identf = S.sbuf([128, 128], F32, "identf")
    onesD = S.sbuf([128, 128], F32, "onesD")
    identb = S.sbuf([128, 128], BF16, "identb")
    bones = S.sbuf([128, 128], BF16, "bones")
    rotm = S.sbuf([128, 128], BF16, "rotm")
    trim = S.sbuf([128, 2, 256], BF16, "trim")
    scT = S.sbuf([128, 8, 3], F32, "scT")
    modT = S.sbuf([128, DEPTH, 48, 3], F32, "modT")
    mder = S.sbuf([128, DEPTH, 6, 8, 3], F32, "mder")
    mder2 = S.sbuf([128, DEPTH, 2, 8, 3], F32, "mder2")
    lnT = S.sbuf([128, DEPTH, 4, 8], F32, "lnT")
    badaT = S.sbuf([128, DEPTH, 48], F32, "badaT")
    nbaT = S.sbuf([128, DEPTH, 2], F32, "nbaT")
    qkwT = S.sbuf([128, DEPTH, 2], F32, "qkwT")
    glnw = S.sbuf([128, DEPTH, 256], F32, "glnw")
    wa2 = S.sbuf([32, DEPTH, 2, 128], F32, "wa2")
    hmask = S.sbuf([128, 4], F32, "hmask")
    epsln = S.sbuf([128, 1], F32, "epsln")
    epsrms = S.sbuf([128, 1], F32, "epsrms")

    pf = [S.psum([128, 512], F32, f"pf{i}") for i in range(7)]
    pb = S.psum([128, 1024], BF16, "pb")
    pctr = [0]

    def nextp():
        i = pctr[0] % 7
        pctr[0] += 1
        return pf[i], B("pf", i)

    class Carver:
        def __init__(self):
            self.off = 0

        def __call__(self, dt, *dims):
            n = int(np.prod(dims))
            nbytes = n * (4 if dt == F32 else 2)
            nbytes = (nbytes + 63) // 64 * 64
            a = self.off // 4
            self.off += nbytes
            assert self.off <= WBYTES, ("scratch overflow", self.off)
            ap = Wr[:, a:a + nbytes // 4]
            if dt == BF16:
                ap = ap.bitcast(BF16)[:, 0:n]
            else:
                ap = ap[:, 0:n]
            if len(dims) == 2:
                ap = ap.rearrange("p (a b) -> p a b", a=dims[0], b=dims[1])
            elif len(dims) == 3:
                ap = ap.rearrange("p (a b c) -> p a b c", a=dims[0], b=dims[1], c=dims[2])
            return ap

    def dump(name, ap, shape, dt, rbuf):
        d = nc.dram_tensor("dbg_" + name, list(shape), dt, kind="ExternalOutput").ap()
        S.dma(I("dma_start", out=d, in_=ap), reads=[rbuf], writes=[B("dump", name)])
        dump_list.append(B("dump", name))
        dumps[name] = (shape, dt)

    def ld(dst, src, bname, q="sp"):
        S.dma(I("dma_start", out=dst, in_=src), writes=[B(bname)], q=q)

    ld(identf[:], cm_d[:, 0, :], "identf")
    ld(onesD[:], cm_d[:, 1, :], "onesD")
    ld(identb[:], cm_d[:, 0, :], "identb", q="pool")
    ld(bones[:], cm_d[:, 2, :], "bones", q="pool")
    ld(rotm[:], cm_d[:, 3, :], "rotm", q="pool")
    ld(trim[:], tri_d, "trim", q="pool")
    ld(hmask[:], cm_d[:, 4, 0:4], "hmask")
    ld(scT[:], cT_d, "scT")
    ld(lnT[:], lnT_d, "lnT")
    ld(badaT[:], bada_d.rearrange("l p j -> p l j"), "badaT")
    ld(nbaT[:], nba_d, "nbaT")
    ld(qkwT[:], qkw_d, "qkwT")
    for l in range(DEPTH):
        ld(glnw[:, l, :], glnw_d[l].broadcast_to([128, 256]), "glnw")
        ld(wa2[:, l, :, :], wa2_d[l].rearrange("e r c -> r e c"), "wa2")
    S.dve(I("tensor_scalar_mul", out=nbaT[:], in0=nbaT[:], scalar1=-1.0), reads=[B("nbaT")], writes=[B("nbaT")])
    S.pool(I("memset", epsln[:], LN_EPS_S), writes=[B("eps")])
    S.pool(I("memset", epsrms[:], RMS_EPS), writes=[B("eps")])
    S.act(I("activation", out=scT[:], in_=scT[:], func=AF.Silu), reads=[B("scT")], writes=[B("scT")])

    for l in range(DEPTH):
        pm, bpm = nextp()
        for js in range(12):
            wv = wview(js % 2, F32, 8, 512)
            S.dma(I("dma_start",
                out=wv, in_=wada_d[l].rearrange("(k p) c -> p k c", p=128)[:, :, 512 * js:512 * js + 512]),
                writes=[B("wbuf", js % 2)])
            for jj in range(4):
                j = 4 * js + jj
                for k in range(8):
                    S.pe(I("matmul",
                        pm[:, 3 * j:3 * j + 3], wv[:, k, 128 * jj:128 * jj + 128], scT[:, k, :],
                        start=(k == 0), stop=(k == 7)),
                        reads=[B("wbuf", js % 2), B("scT")], writes=[bpm])
        S.dve(I("tensor_tensor",
            out=modT[:, l, :, :], in0=pm[:, 0:144].rearrange("p (j b) -> p j b", j=48, b=3),
            in1=badaT[:, l, :].unsqueeze(2).broadcast_to([128, 48, 3]), op=ALU.add),
            reads=[bpm, B("badaT")], writes=[B("modT")])
    inv_a = float(1.0 / np.sqrt(ALPHA2))
    for l in range(DEPTH):
        mT = modT[:, l, :, :]
        rd = [B("modT"), B("lnT")]
        wr = [B("mder")]
        S.dve(I("tensor_scalar_add", out=mder[:, l, 0, :, :], in0=mT[:, 8:16, :], scalar1=1.0), reads=rd, writes=wr)
        S.dve(I("tensor_scalar_mul", out=mder[:, l, 1, :, :], in0=mT[:, 16:24, :], scalar1=inv_a), reads=rd, writes=wr)
        S.dve(I("tensor_scalar_add", out=mder[:, l, 2, :, :], in0=mT[:, 32:40, :], scalar1=1.0), reads=rd, writes=wr)
        S.dve(I("tensor_scalar_mul", out=mder[:, l, 3, :, :], in0=mT[:, 40:48, :], scalar1=inv_a), reads=rd, writes=wr)
        S.dve(I("tensor_tensor", out=mder[:, l, 4, :, :], in0=mder[:, l, 2, :, :],
                                             in1=lnT[:, l, 0, :].unsqueeze(2).broadcast_to([128, 8, 3]), op=ALU.mult), reads=rd + wr, writes=wr)
        S.dve(I("tensor_tensor", out=mder[:, l, 5, :, :], in0=mder[:, l, 2, :, :],
                                             in1=lnT[:, l, 1, :].unsqueeze(2).broadcast_to([128, 8, 3]), op=ALU.mult), reads=rd + wr, writes=wr)
        S.dve(I("tensor_tensor", out=mder[:, l, 5, :, :], in0=mder[:, l, 5, :, :], in1=mT[:, 24:32, :], op=ALU.add), reads=rd + wr, writes=wr)
    for l in range(DEPTH - 1):
        rd = [B("modT"), B("lnT"), B("mder")]
        wr = [B("mder2")]
        S.dve(I("tensor_tensor", out=mder2[:, l, 0, :, :], in0=mder[:, l + 1, 0, :, :],
                                             in1=lnT[:, l, 2, :].unsqueeze(2).broadcast_to([128, 8, 3]), op=ALU.mult), reads=rd, writes=wr)
        S.dve(I("tensor_tensor", out=mder2[:, l, 1, :, :], in0=mder[:, l + 1, 0, :, :],
                                             in1=lnT[:, l, 3, :].unsqueeze(2).broadcast_to([128, 8, 3]), op=ALU.mult), reads=rd + wr, writes=wr)
        S.dve(I("tensor_tensor", out=mder2[:, l, 1, :, :], in0=mder2[:, l, 1, :, :], in1=modT[:, l + 1, 0:8, :], op=ALU.add), reads=rd + wr, writes=wr)
    MD = [B("mder"), B("mder2"), B("modT"), B("lnT")]
    if stage < 99:
        S.pool(I("memset", obuf[:], 0.0), writes=[B("obuf", tt) for tt in range(NTT)])
    if stage == 0:
        dump("modT", modT[:], [128, DEPTH, 48, 3], F32, B("modT"))

    def msel(s, t0):
        return 2 if t0 < LC else s

    gctr = {}

    def bank(group, banks):
        c = gctr.get(group, 0)
        gctr[group] = c + 1
        i = banks[c % len(banks)]
        return pf[i], B("pf", i)

    def tiles_of(t0, n):
        return range(t0 // 128, (t0 + n) // 128)

    def ub(t0, n):
        return [B("ubuf", tt) for tt in tiles_of(t0, n)]

    win_v = [win_d[l].rearrange("(k p) c -> p k c", p=128) for l in range(DEPTH)]
    issued = set()

    def ensure(kind, s_, l_):
        key = (kind, s_, l_)
        if key in issued or s_ >= nseq or l_ >= DEPTH:
            return
        issued.add(key)
        if kind == "na":
            S.dma(I("dma_start", out=wview(0, BF16, 8, 1152), in_=win_v[l_][:, :, 0:1152]), writes=[B("wbuf", 0)], q="pool")
        elif kind == "gla":
            S.dma(I("dma_start", out=wview(0, BF16, 8, 800), in_=win_v[l_][:, :, C_GLQ:C_GLQ + 800]), writes=[B("wbuf", 0)], q="pool")
        elif kind == "gqa":
            wv_ = wview(1, BF16, 8, 768)
            wb_ = [B("wbuf", 1)]
            S.dma(I("dma_start", out=wv_[:, :, 0:384], in_=win_v[l_][:, :, C_GAQ:C_GAQ + 384]), writes=wb_, q="pool")
            for g in range(2):
                for d in range(2):
                    S.dma(I("dma_start", out=wv_[:, :, 384 + 128 * g + 64 * d:384 + 128 * g + 64 * d + 64],
                            in_=win_v[l_][:, :, C_GAK + 64 * g:C_GAK + 64 * g + 64]), writes=wb_, q="pool")
            S.dma(I("dma_start", out=wv_[:, :, 640:768], in_=win_v[l_][:, :, C_GAV:C_GAV + 128]), writes=wb_, q="pool")
        elif kind == "out":
            S.dma(I("dma_start", out=wview(1, BF16, 8, 1024), in_=wout_d[l_].rearrange("(k p) c -> p k c", p=128)), writes=[B("wbuf", 1)], q="pool")
        elif kind == "ffn0":
            wf1_ = wf1_d[l_].rearrange("(k p) c -> p k c", p=128)
            wv_ = wview(0, BF16, 8, 512)
            S.dma(I("dma_start", out=wv_[:, :, 0:256], in_=wf1_[:, :, 0:256]), writes=[B("wbuf", 0)], q="pool")
            S.dma(I("dma_start", out=wv_[:, :, 256:512], in_=wf1_[:, :, FH:FH + 256]), writes=[B("wbuf", 0)], q="pool")

    def next_sl(s_, l_):
        return (s_, l_ + 1) if l_ + 1 < DEPTH else (s_ + 1, 0)
    combos, na_slots = na_combos()
    PBB = B("pb", 0)

    def load_x(s):
        S.phase = "load_x"
        S.barrier()
        cv = Carver()
        xin = [cv(F32, 1024) for _ in range(2)]
        xst = [cv(F32, 8, 128) for _ in range(2)]
        for tt in range(NTT):
            i2 = tt % 2
            src = ctx_d[s, 128 * tt:128 * tt + 128, :] if tt < 2 else x_d[s, 128 * (tt - 2):128 * (tt - 2) + 128, :]
            S.dma(I("dma_start", out=xin[i2], in_=src), writes=[B("xin", i2)])
            mi = msel(s, 128 * tt)
            for half in range(2):
                pt, bpt = bank("ld", [0, 1, 2, 3])
                for kk in range(4):
                    k = 4 * half + kk
                    S.pe(I("transpose", pt[:, 128 * kk:128 * kk + 128], xin[i2][:, 128 * k:128 * k + 128], identf[:]),
                         reads=[B("xin", i2), B("identf")], writes=[bpt])
                S.dve(I("tensor_copy",
                    out=xst[i2][:, 4 * half:4 * half + 4, :], in_=pt[:].rearrange("p (k t) -> p k t", k=4, t=128)),
                    reads=[bpt], writes=[B("xst", i2)])
                for kk in range(4):
                    k = 4 * half + kk
                    S.act(I("activation",
                        out=ubuf[:, k, 128 * tt:128 * tt + 128], in_=pt[:, 128 * kk:128 * kk + 128], func=AF.Identity,
                        scale=mder[:, 0, 0, k, mi:mi + 1], bias=modT[:, 0, k, mi:mi + 1]),
                        reads=[bpt] + MD, writes=[B("ubuf", tt)])
            S.dma(I("dma_start", out=xres[s][:, :, 128 * tt:128 * tt + 128], in_=xst[i2]),
                  reads=[B("xst", i2)], writes=[B("xres", s, tt)])

    def na_phase(s, l):
        S.phase = "na_phase" + str(l)
        S.barrier()
        cv = Carver()
        naq = cv(BF16, 3, T)
        nak = cv(BF16, 3, T)
        nav = cv(BF16, NTT, 6 * 65)
        natab = cv(BF16, 6, NCOMBO * 128)
        naP = [cv(BF16, 7, 128) for _ in range(3)]
        otok = [cv(BF16, 384) for _ in range(2)]
        rden = [cv(F32, 8) for _ in range(2)]
        nav4 = nav.rearrange("p t (h c) -> p t h c", h=6, c=65)
        wv = wview(0, BF16, 8, 1152)
        ensure("na", s, l)
        S.dma(I("dma_start", out=natab, in_=nab_d[l]), writes=[B("natab")], q="pool")
        ensure("gqa", s, l)
        S.pool(I("memset", nav4[:, :, :, 64:65], 1.0), writes=[B("navones")])
        for (t0, n) in BLOCKS:
            for j in range(6):
                if j < 3 and t0 < LC and l == DEPTH - 1:
                    continue
                ps, bps = bank("proj", [0, 1, 2, 3, 4, 5, 6])
                for k in range(8):
                    S.pe(I("matmul", ps[:, 0:n], wv[:, k, 128 * j:128 * j + 128], ubuf[:, k, t0:t0 + n],
                                                                     start=(k == 0), stop=(k == 7)),
                         reads=[B("wbuf", 0)] + ub(t0, n), writes=[bps])
                wr = [B("naqk", tt) for tt in tiles_of(t0, n)]
                if j < 3:
                    S.act(I("activation", out=naq[:, j, t0:t0 + n], in_=ps[:, 0:n], func=AF.Identity, scale=0.125),
                          reads=[bps], writes=wr)
                else:
                    S.dve(I("tensor_copy", out=nak[:, j - 3, t0:t0 + n], in_=ps[:, 0:n]), reads=[bps], writes=wr)
        for tt in range(NTT):
            ps, bps = bank("proj", [0, 1, 2, 3, 4, 5, 6])
            for k in range(8):
                S.pe(I("matmul", ps[:, 0:384], ubuf[:, k, 128 * tt:128 * tt + 128], wv[:, k, 768:1152],
                                                         start=(k == 0), stop=(k == 7)),
                     reads=[B("wbuf", 0), B("ubuf", tt)], writes=[bps])
            S.dve(I("tensor_copy", out=nav4[:, tt, :, 0:64], in_=ps[:, 0:384].rearrange("p (h c) -> p h c", h=6, c=64)),
                  reads=[bps], writes=[B("nav", tt)])
        ensure("gla", s, l)
        qtiles = []
        if l < DEPTH - 1:
            qtiles += [(0, [(0, None), (1, None)]), (1, [(0, None), (1, None)])]
        for p in range(16):
            qtiles.append((2 + p, [(2 + j, ci) for (j, ci) in na_slots[p]] + [(0, None), (1, None)]))
        naP3 = naP
        units = [(qi, h) for qi in range(len(qtiles)) for h in range(6)]
        pos_ = {}
        ust = {}

        def na_st(u):
            qi, h = units[u]
            tq, slots = qtiles[qi]
            if h == 0:
                pos_[qi] = bank("napo", [4, 5])
            ns = len(slots)
            c = h // 2
            p0 = 64 * (h % 2)
            pa, bpa = bank("napa", [0, 1, 2, 3])
            pa2, bpa2 = bank("napa", [0, 1, 2, 3])
            P = naP3[u % 3]
            bP = B("naP", u % 3)
            ust[u] = (P, bP)
            nlat = sum(1 for (_, ci_) in slots if ci_ is not None)
            ci0 = slots[0][1]
            has_b = [False, False]
            if nlat >= 4:
                S.pe(I("matmul", pa[:, 0:512], identb[:], natab[:, h, 128 * ci0:128 * ci0 + 512], start=True, stop=False),
                     reads=[B("natab"), B("identb")], writes=[bpa])
                has_b[0] = True
            if nlat == 5:
                S.pe(I("matmul", pa2[:, 0:128], identb[:], natab[:, h, 128 * (ci0 + 4):128 * (ci0 + 4) + 128], start=True, stop=False),
                     reads=[B("natab"), B("identb")], writes=[bpa2])
                has_b[1] = True
            for si, (tk, ci) in enumerate(slots):
                bi_ = 0 if si < 4 else 1
                bk, bbk = (pa, bpa) if si < 4 else (pa2, bpa2)
                col = 128 * (si % 4)
                last_in_bank = (si == min(ns, 4) - 1) if bi_ == 0 else (si == ns - 1)
                if has_b[bi_]:
                    st_, sp_ = False, last_in_bank
                else:
                    st_, sp_ = True, True
                S.pe(I("matmul", bk[:, col:col + 128], nak[p0:p0 + 64, c, 128 * tk:128 * tk + 128], naq[p0:p0 + 64, c, 128 * tq:128 * tq + 128],
                       start=st_, stop=sp_),
                     reads=[B("naqk", tk), B("naqk", tq)], writes=[bbk])
            n1 = min(ns, 4)
            S.act(I("activation", out=P[:, 0:n1, :], in_=pa[:, 0:128 * n1].rearrange("p (a b) -> p a b", a=n1, b=128), func=AF.Exp),
                  reads=[bpa], writes=[bP])
            if ns > 4:
                n2 = ns - 4
                S.act(I("activation", out=P[:, 4:4 + n2, :], in_=pa2[:, 0:128 * n2].rearrange("p (a b) -> p a b", a=n2, b=128), func=AF.Exp),
                      reads=[bpa2], writes=[bP])

        def na_pv(u):
            qi, h = units[u]
            tq, slots = qtiles[qi]
            ns = len(slots)
            po, bpo = pos_[qi]
            P, bP = ust.pop(u)
            for si, (tk, ci) in enumerate(slots):
                S.pe(I("matmul", po[:, 65 * h:65 * h + 65], P[:, si, :], nav[:, tk, 65 * h:65 * h + 65], start=(si == 0), stop=(si == ns - 1)),
                     reads=[bP, B("nav", tk), B("navones")], writes=[bpo])
            if h < 5:
                return
            o2 = qi % 2
            po3 = po[:, 0:390].rearrange("p (h c) -> p h c", h=6, c=65)
            S.dve(I("reciprocal", out=rden[o2][:, 0:6], in_=po3[:, :, 64]), reads=[bpo], writes=[B("rden", o2)])
            S.dve(I("tensor_tensor", out=otok[o2].rearrange("p (h c) -> p h c", h=6, c=64), in0=po3[:, :, 0:64],
                    in1=rden[o2][:, 0:6].unsqueeze(2).broadcast_to([128, 6, 64]), op=ALU.mult),
                  reads=[bpo, B("rden", o2)], writes=[B("otok", o2)])
            for c3 in range(3):
                S.pe(I("transpose", pb[:, 128 * c3:128 * c3 + 128], otok[o2][:, 128 * c3:128 * c3 + 128], identb[:]),
                     reads=[B("otok", o2), B("identb")], writes=[PBB])
            S.act(I("copy", out=obuf[:, 0:3, 128 * tq:128 * tq + 128], in_=pb[:, 0:384].rearrange("p (a b) -> p a b", a=3, b=128)),
                  reads=[PBB], writes=[B("obuf", tq)])

        LA = 1
        for u in range(len(units) + LA):
            if u < len(units):
                na_st(u)
            if u >= LA:
                na_pv(u - LA)

    def gqa_phase(s, l):
        S.phase = "gqa_phase" + str(l)
        S.barrier()
        cv = Carver()
        gaq = cv(BF16, 3, T)
        gak = cv(BF16, 2, T)
        gav = cv(BF16, NTT, 130)
        cosv = cv(F32, NL)
        sinv = cv(F32, NL)
        sqs = [cv(BF16, 512) for _ in range(2)]
        rstds = [cv(F32, 512) for _ in range(2)]
        qns = [cv(F32, 512) for _ in range(2)]
        qnbs = [cv(BF16, 512) for _ in range(2)]
        t1s = [cv(F32, 512) for _ in range(2)]
        t2s = [cv(F32, 512) for _ in range(2)]
        gaP = [cv(BF16, 512) for _ in range(6)]
        otok = cv(BF16, 4, 384)
        rden = [cv(F32, 8) for _ in range(2)]
        gav4 = gav.rearrange("p t (g c) -> p t g c", g=2, c=65)
        wv = wview(1, BF16, 8, 768)
        wb = [B("wbuf", 1)]
        ensure("gqa", s, l)
        S.dma(I("dma_start", out=cosv, in_=cos_d), writes=[B("cos")])
        S.dma(I("dma_start", out=sinv, in_=sin_d), writes=[B("cos")])
        S.pool(I("memset", gav4[:, :, :, 64:65], 1.0), writes=[B("gavones")])
        cb = 0
        for (t0, n) in BLOCKS:
            for j in range(5):
                if j < 3 and t0 < LC and l == DEPTH - 1:
                    continue
                p2 = cb % 2
                cb += 1
                sq, rstd, qn, qnb, t1, t2 = sqs[p2], rstds[p2], qns[p2], qnbs[p2], t1s[p2], t2s[p2]
                TB = lambda nm, p2=p2: B("gatmp", nm, p2)
                ps, bps = bank("proj", [0, 1, 2, 3])
                for k in range(8):
                    S.pe(I("matmul", ps[:, 0:n], wv[:, k, 128 * j:128 * j + 128], ubuf[:, k, t0:t0 + n],
                                                                     start=(k == 0), stop=(k == 7)),
                         reads=wb + ub(t0, n), writes=[bps])
                S.act(I("activation", out=sq[:, 0:n], in_=ps[:, 0:n], func=AF.Square), reads=[bps], writes=[TB("sq")])
                pm, bpm = bank("gams", [4, 5])
                S.pe(I("matmul", pm[:, 0:n], bones[:], sq[:, 0:n], start=True, stop=True), reads=[TB("sq"), B("bones")], writes=[bpm])
                S.act(I("activation", out=rstd[:, 0:n], in_=pm[:, 0:n], func=AF.Ln, bias=epsrms[:, 0:1], scale=1.0),
                      reads=[bpm, B("eps")], writes=[TB("rstd")])
                S.act(I("activation", out=rstd[:, 0:n], in_=rstd[:, 0:n], func=AF.Exp, scale=-0.5), reads=[TB("rstd")], writes=[TB("rstd")])
                wi = 0 if j < 3 else 1
                S.dve(I("scalar_tensor_tensor", out=qn[:, 0:n], in0=ps[:, 0:n], scalar=qkwT[:, l, wi:wi + 1], in1=rstd[:, 0:n],
                                                                      op0=ALU.mult, op1=ALU.mult),
                      reads=[bps, TB("rstd"), B("qkwT")], writes=[TB("qn")])
                dst = gaq[:, j, t0:t0 + n] if j < 3 else gak[:, j - 3, t0:t0 + n]
                wr = [B("gaqk", tt) for tt in tiles_of(t0, n)]
                if t0 >= LC:
                    S.act(I("copy", out=qnb[:, 0:n], in_=qn[:, 0:n]), reads=[TB("qn")], writes=[TB("qnb")])
                    pr, bpr = bank("gams", [4, 5])
                    S.pe(I("matmul", pr[:, 0:n], rotm[:], qnb[:, 0:n], start=True, stop=True), reads=[TB("qnb"), B("rotm")], writes=[bpr])
                    S.dve(I("tensor_tensor", out=t1[:, 0:n], in0=qn[:, 0:n], in1=cosv[:, t0 - LC:t0 - LC + n], op=ALU.mult),
                          reads=[TB("qn"), B("cos")], writes=[TB("t1")])
                    S.dve(I("tensor_tensor", out=t2[:, 0:n], in0=pr[:, 0:n], in1=sinv[:, t0 - LC:t0 - LC + n], op=ALU.mult),
                          reads=[bpr, B("cos")], writes=[TB("t2")])
                    S.dve(I("tensor_tensor", out=dst, in0=t1[:, 0:n], in1=t2[:, 0:n], op=ALU.add),
                          reads=[TB("t1"), TB("t2")], writes=wr)
                else:
                    S.act(I("copy", out=dst, in_=qn[:, 0:n]), reads=[TB("qn")], writes=wr)
        for tt in range(NTT):
            ps, bps = bank("proj", [0, 1, 2, 3])
            for k in range(8):
                S.pe(I("matmul", ps[:, 0:128], ubuf[:, k, 128 * tt:128 * tt + 128], wv[:, k, 640:768],
                                                         start=(k == 0), stop=(k == 7)),
                     reads=wb + [B("ubuf", tt)], writes=[bps])
            S.dve(I("tensor_copy", out=gav4[:, tt, :, 0:64], in_=ps[:, 0:128].rearrange("p (g c) -> p g c", g=2, c=64)),
                  reads=[bps], writes=[B("gav", tt)])
        ensure("out", s, l)
        qblocks = []
        if l < DEPTH - 1:
            qblocks.append((0, 256, [0, 1]))
        for i in range(4):
            qblocks.append((256 + 512 * i, 512, list(range(NTT))))
        pi = 0
        for (t0, n, ktiles) in qblocks:
            nsub = n // 128
            nk = len(ktiles)
            for h in range(6):
                g = h // 3
                c = h // 2
                p0 = 64 * (h % 2)
                po, bpo = bank("gapo", [5, 6])
                S.dve(I("memset", po[:, 0:65 * nsub], 0.0), writes=[bpo])
                Ps = {}

                def st(ki):
                    nonlocal pi
                    tk = ktiles[ki]
                    pa, bpa = bank("gapa", [0, 1, 2, 3, 4])
                    P = gaP[pi % 6]
                    bP = B("gaP", pi % 6)
                    pi += 1
                    Ps[ki] = (P, bP)
                    S.pe(I("matmul", pa[:, 0:n], gak[p0:p0 + 64, g, 128 * tk:128 * tk + 128], gaq[p0:p0 + 64, c, t0:t0 + n],
                                                        start=True, stop=True),
                         reads=[B("gaqk", tk)] + [B("gaqk", tt) for tt in tiles_of(t0, n)], writes=[bpa])
                    S.act(I("activation", out=P[:, 0:n], in_=pa[:, 0:n], func=AF.Exp, scale=0.125), reads=[bpa], writes=[bP])

                def pv(ki):
                    tk = ktiles[ki]
                    P, bP = Ps.pop(ki)
                    for qs in range(nsub):
                        S.pe(I("matmul", po[:, 65 * qs:65 * qs + 65], P[:, 128 * qs:128 * qs + 128], gav[:, tk, 65 * g:65 * g + 65],
                                                                        start=False, stop=(ki == nk - 1), skip_group_check=True),
                             reads=[bP, B("gav", tk), B("gavones")], writes=[bpo])

                GLA_ = 3
                for step in range(nk + GLA_):
                    if step < nk:
                        st(step)
                    if step >= GLA_:
                        pv(step - GLA_)
                o2 = h % 2
                po3 = po[:, 0:65 * nsub].rearrange("p (q c) -> p q c", q=nsub, c=65)
                S.dve(I("reciprocal", out=rden[o2][:, 0:nsub], in_=po3[:, :, 64]), reads=[bpo], writes=[B("rden", o2)])
                S.dve(I("tensor_tensor", out=otok[:, 0:nsub, 64 * h:64 * h + 64], in0=po3[:, :, 0:64],
                                                                   in1=rden[o2][:, 0:nsub].unsqueeze(2).broadcast_to([128, nsub, 64]), op=ALU.mult),
                      reads=[bpo, B("rden", o2)], writes=[B("gaotok")])
            for qs in range(nsub):
                tq = t0 // 128 + qs
                for c3 in range(3):
                    S.pe(I("transpose", pb[:, 128 * c3:128 * c3 + 128], otok[:, qs, 128 * c3:128 * c3 + 128], identb[:]),
                         reads=[B("gaotok"), B("identb")], writes=[PBB])
                S.act(I("copy", out=obuf[:, 5:8, 128 * tq:128 * tq + 128], in_=pb[:, 0:384].rearrange("p (a b) -> p a b", a=3, b=128)),
                      reads=[PBB], writes=[B("obuf", tq)])

    def gla_phase(s, l):
        S.phase = "gla_phase" + str(l)
        S.barrier()
        cv = Carver()
        glv = cv(BF16, NTT, 256)
        glgw = cv(BF16, NTT, 256)
        qt = [cv(BF16, T) for _ in range(2)]
        kt = [cv(BF16, T) for _ in range(2)]
        ktok = cv(BF16, NTT, 128)
        Dall = cv(F32, 2, 36)
        S32 = cv(F32, 37, 64)
        Sbf = [cv(BF16, 36, 64) for _ in range(2)]
        off_gate_tmp = cv.off
        qf = cv(F32, 512)
        kf = cv(F32, 512)
        lrf = cv(F32, 512)
        smask = cv(F32, 512)
        tAs = [cv(F32, 512) for _ in range(2)]
        tBs = [cv(F32, 512) for _ in range(2)]
        tCs = [cv(F32, 512) for _ in range(2)]
        tD = tCs[0]
        Abf = [cv(BF16, 2, 256) for _ in range(2)]
        qmb = [cv(BF16, 2, 4, 128) for _ in range(2)]
        osq = cv(F32, 256)
        sg = osq
        on = cv(F32, 256)
        otok = [cv(BF16, 256) for _ in range(2)]
        st4 = [cv(F32, 8) for _ in range(2)]
        TB = lambda n: B("gltmp", n)
        TB0 = TB
        wv = wview(0, BF16, 8, 800)
        wb = [B("wbuf", 0)]
        ensure("gla", s, l)
        S.pool(I("memset", smask, 1.0), writes=[TB0("smask")])
        S.pool(I("memset", smask.rearrange("p (c t) -> p c t", c=8, t=64)[:, :, 0:1], 0.0), writes=[TB0("smask")])
        for tt in range(NTT):
            ps, bps = bank("proj", [0, 1, 2, 3])
            for k in range(8):
                S.pe(I("matmul", ps[:, 0:512], ubuf[:, k, 128 * tt:128 * tt + 128], wv[:, k, 256:768],
                                                         start=(k == 0), stop=(k == 7)),
                     reads=wb + [B("ubuf", tt)], writes=[bps])
            S.dve(I("tensor_copy", out=glv[:, tt, :], in_=ps[:, 0:256]), reads=[bps], writes=[B("glv", tt)])
            S.act(I("activation", out=sg, in_=ps[:, 256:512], func=AF.Silu), reads=[bps], writes=[TB0("osq")])
            S.pool(I("tensor_tensor", out=glgw[:, tt, :], in0=sg, in1=glnw[:, l, :], op=ALU.mult),
                   reads=[TB0("osq"), B("glnw")], writes=[B("glgw", tt)])
        import os as _os
        _gs = int(_os.environ.get('GLASTOP', '9'))
        _sub = int(_os.environ.get('GLASUB', '9'))
        if _gs <= 0:
            return
        for (t0, n) in BLOCKS:
            nch = n // 64
            c0 = t0 // 64
            pq, bpq = bank("proj", [0, 1, 2, 3])
            pk, bpk = bank("proj", [0, 1, 2, 3])
            pl, bpl = bank("proj", [0, 1, 2, 3])
            for (pp, col, M) in ((pq, 0, 128), (pk, 128, 128), (pl, 768, 32)):
                bb = {id(pq): bpq, id(pk): bpk, id(pl): bpl}[id(pp)]
                for k in range(8):
                    S.pe(I("matmul", pp[0:M, 0:n], wv[:, k, col:col + M], ubuf[:, k, t0:t0 + n],
                                                                              start=(k == 0), stop=(k == 7)),
                         reads=wb + ub(t0, n), writes=[bb])
            S.act(I("activation", out=qf[:, 0:n], in_=pq[:, 0:n], func=AF.Identity, scale=float(32 ** -0.5)), reads=[bpq], writes=[TB0("qf")])
            S.dve(I("tensor_copy", out=kf[:, 0:n], in_=pk[:, 0:n]), reads=[bpk], writes=[TB0("kf")])
            S.dve(I("tensor_copy", out=lrf[0:32, 0:n], in_=pl[0:32, 0:n]), reads=[bpl], writes=[TB0("lrf")])
            for ed in range(2):
                tA, tB_, tC = tAs[ed], tBs[ed], tCs[ed]
                TBd = lambda n_, ed=ed: B("gltmp", n_, ed)
                pz, bpz = bank("glz", [4, 5])
                S.pe(I("matmul", pz[:, 0:n], wa2[:, l, ed, :], lrf[0:32, 0:n], start=True, stop=True),
                     reads=[TB0("lrf"), B("wa2")], writes=[bpz])
                S.act(I("activation", out=tA[:, 0:n], in_=pz[:, 0:n], func=AF.Exp, scale=-1.0, bias=nbaT[:, l, ed:ed + 1]),
                      reads=[bpz, B("nbaT")], writes=[TBd("tA")])
                S.act(I("activation", out=tB_[:, 0:n], in_=tA[:, 0:n], func=AF.Ln, bias=1.0, scale=1.0), reads=[TBd("tA")], writes=[TBd("tB")])
                S.dve(I("tensor_tensor_scan", out=tC[:, 0:n], data0=smask[:, 0:n], data1=tB_[:, 0:n], initial=0.0, op0=ALU.mult, op1=ALU.add),
                      reads=[TB0("smask"), TBd("tB")], writes=[TBd("tC")])
                cum = tC
                bcum = TBd("tC")
                if ed == 1:
                    tC3 = tC[:, 0:n].rearrange("p (c t) -> p c t", c=nch, t=64)
                    S.dve(I("tensor_tensor", out=tD[:, 0:n].rearrange("p (c t) -> p c t", c=nch, t=64),
                                                                         in0=tC3[:, :, 63:64].broadcast_to([128, nch, 64]), in1=tC3, op=ALU.subtract),
                          reads=[TBd("tC")], writes=[B("gltmp", "tC", 0)])
                    S.dve(I("tensor_tensor", out=tD[:, 0:n], in0=tD[:, 0:n], in1=tB_[:, 0:n], op=ALU.add), reads=[B("gltmp", "tC", 0), TBd("tB")], writes=[B("gltmp", "tC", 0)])
                    cum = tD
                    bcum = B("gltmp", "tC", 0)
                S.act(I("activation", out=tA[:, 0:n], in_=cum[:, 0:n], func=AF.Exp, scale=-1.0 / 16.0), reads=[bcum], writes=[TBd("tA")])
                S.act(I("activation", out=tB_[:, 0:n], in_=cum[:, 0:n], func=AF.Exp, scale=1.0 / 16.0), reads=[bcum, TBd("tA")], writes=[TBd("tB")])
                dcol = 63 if ed == 0 else 0
                S.dve(I("tensor_copy",
                    out=Dall[:, ed, c0:c0 + nch], in_=tA[:, 0:n].rearrange("p (c t) -> p c t", c=nch, t=64)[:, :, dcol]),
                    reads=[TBd("tA")], writes=[B("Dall")])
                wr = [B("glqk", ed, tt) for tt in tiles_of(t0, n)]
                S.dve(I("tensor_tensor", out=qt[ed][:, t0:t0 + n], in0=qf[:, 0:n], in1=tA[:, 0:n], op=ALU.mult),
                      reads=[TB0("qf"), TBd("tA")], writes=wr)
                S.dve(I("tensor_tensor", out=kt[ed][:, t0:t0 + n], in0=kf[:, 0:n], in1=tB_[:, 0:n], op=ALU.mult),
                       reads=[TB0("kf"), TBd("tB")], writes=wr)
        if _gs <= 1:
            return
        ords = [list(range(36)), [3, 2, 1, 0] + list(range(35, 3, -1))]
        poss = []
        for ed in range(2):
            pos = [0] * 36
            for j, ci in enumerate(ords[ed]):
                pos[ci] = j
            poss.append(pos)
        ensure("ffn0", s, l)
        S.barrier()
        cv2 = Carver()
        cv2.off = off_gate_tmp
        Drep = cv2(F32, 64, 36)
        Sall = cv2(F32, 64 * 36)
        Dord = cv2(F32, 64)
        assert cv2.off <= off_gate_tmp + 20480
        Wd3 = S32.rearrange("p a b -> p (a b)")[:, 0:64 * 36].rearrange("p (v j) -> p v j", v=64, j=36)
        for ed in range(2):
            pos = poss[ed]
            for tt in range(NTT):
                S.pe(I("transpose", pb[:, 128 * (tt % 4):128 * (tt % 4) + 128], kt[ed][:, 128 * tt:128 * tt + 128], identb[:]),
                     reads=[B("glqk", ed, tt), B("identb")], writes=[PBB])
                if tt % 4 == 3 or tt == NTT - 1:
                    n4 = tt % 4 + 1
                    ta = tt - n4 + 1
                    S.act(I("copy", out=ktok[:, ta:ta + n4, :], in_=pb[:, 0:128 * n4].rearrange("p (a b) -> p a b", a=n4, b=128)),
                          reads=[PBB], writes=[B("ktok")])
            if ed == 0:
                S.dve(I("tensor_copy", out=Dord[:, 0:36], in_=Dall[:, 0, 0:36]), reads=[B("Dall")], writes=[B("Dord")])
            else:
                S.dve(I("tensor_copy", out=Dord[:, 0:4], in_=Dall[:, 1, 3::-1]), reads=[B("Dall")], writes=[B("Dord")])
                S.dve(I("tensor_copy", out=Dord[:, 4:36], in_=Dall[:, 1, 35:3:-1]), reads=[B("Dall")], writes=[B("Dord")])
            S.dve(I("memset", Dord[:, 0:1], 0.0), writes=[B("Dord")])
            S.dve(I("tensor_copy", out=Drep, in_=Dord[:, 0:36].unsqueeze(1).broadcast_to([128, 64, 36])), reads=[B("Dord")], writes=[B("Drep")])
            for half in range(2):
              for grp in range(0, 18, 8):
                pw, bpw = bank("glw", [0, 1, 2, 3])
                cis = [2 * t_ + half for t_ in range(grp, min(grp + 8, 18))]
                for gi, ci in enumerate(cis):
                    tt = ci // 2
                    for h in range(4):
                        S.pe(I("matmul",
                            pw[32 * h:32 * h + 32, 64 * gi:64 * gi + 64], ktok[64 * half:64 * half + 64, tt, 32 * h:32 * h + 32],
                            glv[64 * half:64 * half + 64, tt, 64 * h:64 * h + 64], start=True, stop=True, tile_position=(64 * half, 32 * h)),
                            reads=[B("ktok"), B("glv", tt)], writes=[bpw])
                for gi, ci in enumerate(cis):
                    S.act(I("activation",
                        out=Wd3[:, :, pos[ci]], in_=pw[:, 64 * gi:64 * gi + 64], func=AF.Identity, scale=Dall[:, ed, ci:ci + 1]),
                        reads=[bpw, B("Dall")], writes=[B("S32")])
            S.dve(I("tensor_tensor_scan", out=Sall, data0=Drep.rearrange("p v j -> p (v j)"), data1=Wd3.rearrange("p v j -> p (v j)"),
                    initial=0.0, op0=ALU.mult, op1=ALU.add),
                  reads=[B("Drep"), B("S32")], writes=[B("Sall")])
            S.dve(I("memset", Sbf[ed][:, 0, :], 0.0), writes=[B("Sbf", ed)])
            S.dve(I("tensor_copy", out=Sbf[ed][:, 1:36, :], in_=Sall.rearrange("p (v j) -> p j v", v=64, j=36)[:, 0:35, :]),
                  reads=[B("Sall")], writes=[B("Sbf", ed)])
        if _gs <= 2:
            return
        tiles_out = range(NTT) if l < DEPTH - 1 else range(2, NTT)
        tiles_out = list(tiles_out)

        def g_st1(oi):
            tt = tiles_out[oi]
            pa, bpa = bank("glpa", [0, 1, 2, 3])
            A = Abf[oi % 2]
            bA = B("Abf", oi % 2)
            qm = qmb[oi % 2]
            bqm = B("qm", oi % 2)
            for ed in range(2):
                for h in range(4):
                    eng = S.dve
                    eng(I("tensor_scalar_mul", out=qm[:, ed, h, :], in0=qt[ed][:, 128 * tt:128 * tt + 128], scalar1=hmask[:, h:h + 1]),
                        reads=[B("glqk", ed, tt), B("hmask")], writes=[B("qm", oi % 2, ed, h)])
            for half in range(2):
                cols = slice(128 * tt + 64 * half, 128 * tt + 64 * half + 64)
                for ed in range(2):
                    for h in range(4):
                        S.pe(I("matmul",
                            pa[64 * half:64 * half + 64, 256 * ed + 64 * h:256 * ed + 64 * h + 64],
                            kt[ed][:, cols], qm[:, ed, h, 64 * half:64 * half + 64], start=True, stop=True,
                            tile_position=(0, 64 * half)),
                            reads=[B("glqk", ed, tt), B("qm", oi % 2, ed, h)], writes=[bpa])
            S.dve(I("tensor_tensor", out=A, in0=pa[:, 0:512].rearrange("p (a b) -> p a b", a=2, b=256), in1=trim[:], op=ALU.mult),
                  reads=[bpa, B("trim")], writes=[bA])

        def g_st2(oi):
            tt = tiles_out[oi]
            A = Abf[oi % 2]
            bA = B("Abf", oi % 2)
            qm = qmb[oi % 2]
            bqm = B("qm", oi % 2)
            po, bpo = bank("glpo", [4, 5])
            for half in range(2):
                ci = 2 * tt + half
                for h in range(4):
                    for ed in range(2):
                        S.pe(I("matmul",
                            po[64 * half:64 * half + 64, 64 * h:64 * h + 64], A[64 * half:64 * half + 64, ed, 64 * h:64 * h + 64],
                            glv[64 * half:64 * half + 64, tt, 64 * h:64 * h + 64], start=(ed == 0), stop=False,
                            tile_position=(64 * half, 64 * half)),
                            reads=[bA, B("glv", tt)], writes=[bpo])
                    for ed in range(2):
                        pj = poss[ed][ci]
                        S.pe(I("matmul",
                            po[64 * half:64 * half + 64, 64 * h:64 * h + 64], qm[:, ed, h, 64 * half:64 * half + 64],
                            Sbf[ed][:, pj, :], start=False, stop=(ed == 1),
                            tile_position=(0, 64 * half)),
                            reads=[B("qm", oi % 2, ed, h), B("Sbf", ed)], writes=[bpo])
            o2 = oi % 2
            S.act(I("activation", out=osq, in_=po[:, 0:256], func=AF.Square), reads=[bpo], writes=[TB0("osq")])
            S.dve(I("reduce_sum", out=st4[o2][:, 0:4], in_=osq.rearrange("p (h c) -> p h c", h=4, c=64), axis=AX.X),
                  reads=[TB0("osq")], writes=[B("st4", o2)])
            S.dve(I("tensor_scalar", out=st4[o2][:, 0:4], in0=st4[o2][:, 0:4], scalar1=1.0 / 64.0, scalar2=RMS_EPS, op0=ALU.mult, op1=ALU.add),
                  reads=[B("st4", o2)], writes=[B("st4", o2)])
            S.act(I("activation", out=st4[o2][:, 4:8], in_=st4[o2][:, 0:4], func=AF.Sqrt), reads=[B("st4", o2)], writes=[B("st4", o2)])
            S.dve(I("reciprocal", out=st4[o2][:, 0:4], in_=st4[o2][:, 4:8]), reads=[B("st4", o2)], writes=[B("st4", o2)])
            S.dve(I("tensor_tensor", out=on.rearrange("p (h c) -> p h c", h=4, c=64), in0=po[:, 0:256].rearrange("p (h c) -> p h c", h=4, c=64),
                                                        in1=st4[o2][:, 0:4].unsqueeze(2).broadcast_to([128, 4, 64]), op=ALU.mult),
                  reads=[bpo, B("st4", o2)], writes=[TB0("on")])
            S.pool(I("tensor_tensor", out=otok[o2], in0=on, in1=glgw[:, tt, :], op=ALU.mult),
                   reads=[TB0("on"), B("glgw", tt)], writes=[B("glotok", o2)])

        def g_st3(oi):
            tt = tiles_out[oi]
            o2 = oi % 2
            for c2 in range(2):
                S.pe(I("transpose", pb[:, 512 + 128 * c2:512 + 128 * c2 + 128], otok[o2][:, 128 * c2:128 * c2 + 128], identb[:]),
                     reads=[B("glotok", o2), B("identb")], writes=[PBB])
            S.act(I("copy", out=obuf[:, 3:5, 128 * tt:128 * tt + 128], in_=pb[:, 512:768].rearrange("p (a b) -> p a b", a=2, b=128)),
                  reads=[PBB], writes=[B("obuf", tt)])

        for oi in range(len(tiles_out) + 2):
            if oi < len(tiles_out):
                g_st1(oi)
            if 1 <= oi <= len(tiles_out):
                g_st2(oi - 1)
            if oi >= 2:
                g_st3(oi - 2)

    def layer_norm_block(tbuf, bk, n, outs):
        pmean, bpmean = bank("lnm", [4, 5])
        pmsq, bpmsq = bank("lnm", [4, 5])
        for oc in range(8):
            S.pe(I("matmul", pmean[:, 0:n], onesD[:], tbuf[:, oc, 0:n], start=(oc == 0), stop=(oc == 7)),
                 reads=[bk(oc), B("onesD")], writes=[bpmean])
        for oc in range(8):
            sqb = lnsq[oc % 2]
            S.act(I("activation", out=sqb[:, 0:n], in_=tbuf[:, oc, 0:n], func=AF.Square), reads=[bk(oc)], writes=[B("lnsq", oc % 2)])
            S.pe(I("matmul", pmsq[:, 0:n], onesD[:], sqb[:, 0:n], start=(oc == 0), stop=(oc == 7)),
                 reads=[B("lnsq", oc % 2), B("onesD")], writes=[bpmsq])
        S.act(I("copy", out=lnmean[:, 0:n], in_=pmean[:, 0:n]), reads=[bpmean], writes=[B("lnmean")])
        S.dve(I("tensor_tensor", out=lnm2[:, 0:n], in0=lnmean[:, 0:n], in1=lnmean[:, 0:n], op=ALU.mult), reads=[B("lnmean")], writes=[B("lnm2")])
        S.dve(I("tensor_tensor", out=lnm2[:, 0:n], in0=pmsq[:, 0:n], in1=lnm2[:, 0:n], op=ALU.subtract), reads=[bpmsq, B("lnm2")], writes=[B("lnm2")])
        S.act(I("activation", out=lnm2[:, 0:n], in_=lnm2[:, 0:n], func=AF.Ln, bias=epsln[:, 0:1], scale=1.0), reads=[B("lnm2"), B("eps")], writes=[B("lnm2")])
        S.act(I("activation", out=lnrstd[:, 0:n], in_=lnm2[:, 0:n], func=AF.Exp, scale=-0.5), reads=[B("lnm2")], writes=[B("lnrstd")])
        for oc in range(8):
            S.dve(I("tensor_tensor", out=tbuf[:, oc, 0:n], in0=tbuf[:, oc, 0:n], in1=lnmean[:, 0:n], op=ALU.subtract),
                  reads=[bk(oc), B("lnmean")], writes=[bk(oc)])
            S.dve(I("tensor_tensor", out=tbuf[:, oc, 0:n], in0=tbuf[:, oc, 0:n], in1=lnrstd[:, 0:n], op=ALU.mult),
                  reads=[bk(oc), B("lnrstd")], writes=[bk(oc)])
            for (eng, dstf, scf, bif, wrf) in outs:
                if eng == "act":
                    S.act(I("activation", out=dstf(oc), in_=tbuf[:, oc, 0:n], func=AF.Identity, scale=scf(oc), bias=bif(oc)),
                          reads=[bk(oc)] + MD, writes=wrf(oc))
                else:
                    S.dve(I("tensor_scalar", out=dstf(oc), in0=tbuf[:, oc, 0:n], scalar1=scf(oc), scalar2=bif(oc),
                            op0=ALU.mult, op1=ALU.add),
                          reads=[bk(oc)] + MD, writes=wrf(oc))

    self_cv = [None]
    lnsq = [None, None]
    lnmean = lnm2 = lnrstd = None

    def outproj_phase(s, l):
        S.phase = "outproj_phase" + str(l)
        nonlocal lnsq, lnmean, lnm2, lnrstd
        S.barrier()
        cv = Carver()
        self_cv[0] = cv
        tbs = [cv(F32, 8, 512) for _ in range(2)]
        xss = [cv(F32, 8, 512) for _ in range(2)]
        lnsq = [cv(F32, 512) for _ in range(2)]
        lnmean = cv(F32, 512)
        lnm2 = cv(F32, 512)
        lnrstd = cv(F32, 512)
        wv = wview(1, BF16, 8, 1024)
        wb = [B("wbuf", 1)]
        ensure("out", s, l)
        bi = 0
        for (t0, n) in BLOCKS:
            if t0 < LC and l == DEPTH - 1:
                continue
            tb = tbs[bi % 2]
            xs = xss[bi % 2]
            par = bi % 2
            btb = lambda oc, par=par: B("tb", par, oc)
            bxs = lambda oc, par=par: B("xs", par, oc)
            bxs_all = [bxs(oc) for oc in range(8)]
            bi += 1
            mi = msel(s, t0)
            xrb = [B("xres", s, tt) for tt in tiles_of(t0, n)]
            S.dma(I("dma_start", out=xs[:, :, 0:n], in_=xres[s][:, :, t0:t0 + n]), reads=xrb, writes=bxs_all)
            for oc in range(8):
                py, bpy = bank("proj", [0, 1, 2, 3])
                for k in range(8):
                    S.pe(I("matmul", py[:, 0:n], wv[:, k, 128 * oc:128 * oc + 128], obuf[:, k, t0:t0 + n],
                                                                       start=(k == 0), stop=(k == 7)),
                         reads=wb + [B("obuf", tt) for tt in tiles_of(t0, n)], writes=[bpy])
                S.dve(I("scalar_tensor_tensor", out=tb[:, oc, 0:n], in0=py[:, 0:n], scalar=mder[:, l, 1, oc, mi:mi + 1], in1=xs[:, oc, 0:n],
                                                                             op0=ALU.mult, op1=ALU.add),
                      reads=[bpy, bxs(oc)] + MD, writes=[btb(oc)])
            outs = [
                ("act", lambda oc, n=n, xs=xs: xs[:, oc, 0:n], lambda oc: lnT[:, l, 0, oc:oc + 1], lambda oc: lnT[:, l, 1, oc:oc + 1], lambda oc, bxs=bxs: [bxs(oc)]),
                ("dve", lambda oc, t0=t0, n=n: ubuf[:, oc, t0:t0 + n], lambda oc, mi=mi: mder[:, l, 4, oc, mi:mi + 1], lambda oc, mi=mi: mder[:, l, 5, oc, mi:mi + 1],
                 lambda oc, t0=t0, n=n: ub(t0, n)),
            ]
            layer_norm_block(tb, btb, n, outs)
            S.dma(I("dma_start", out=xres[s][:, :, t0:t0 + n], in_=xs[:, :, 0:n]), reads=bxs_all, writes=xrb)

    def ffn_phase(s, l):
        S.phase = "ffn_phase" + str(l)
        nonlocal lnsq, lnmean, lnm2, lnrstd
        S.barrier()
        last = (l == DEPTH - 1)
        cv = Carver()
        self_cv[0] = cv
        PTOK = 1024 if last else 1152
        hT = cv(BF16, NHC, PTOK)
        w2 = [cv(BF16, NHC, 256) for _ in range(2)]
        lnsq = [cv(F32, 512) for _ in range(2)]
        lnmean = cv(F32, 512)
        lnm2 = cv(F32, 512)
        lnrstd = cv(F32, 512)
        sa = [cv(F32, 512) for _ in range(2)]
        ostage = [cv(F32, 1024) for _ in range(2)] if last else None
        tfull = obuf[:].rearrange("p k t -> p (k t)").bitcast(F32)[:, 0:8 * PTOK].rearrange("p (k t) -> p k t", k=8, t=PTOK)
        if last:
            parts = [[BLOCKS[1], BLOCKS[2]], [BLOCKS[3], BLOCKS[4]]]
        else:
            parts = [[(0, 256), (256, 512), (768, 384)], [(1152, 512), (1664, 512), (2176, 128)]]
        wf1 = wf1_d[l].rearrange("(k p) c -> p k c", p=128)
        wf2 = wf2_d[l].rearrange("(j p) c -> p j c", p=128)
        for part in parts:
            pt0 = part[0][0]
            ptn = sum(n for (_, n) in part)
            mi = msel(s, pt0)
            nblk = len(part)
            btf = lambda bi_, oc: B("tfull", bi_, oc)
            btf_all = [btf(bi_, oc) for bi_ in range(nblk) for oc in range(8)]
            for js in range(11):
                wi = js % 2
                wv = wview(wi, BF16, 8, 512)
                wb = [B("wbuf", wi)]
                if js == 0 and part is parts[0]:
                    ensure("ffn0", s, l)
                else:
                    S.dma(I("dma_start", out=wv[:, :, 0:256], in_=wf1[:, :, 256 * js:256 * js + 256]), writes=wb, q="pool")
                    S.dma(I("dma_start", out=wv[:, :, 256:512], in_=wf1[:, :, FH + 256 * js:FH + 256 * js + 256]), writes=wb, q="pool")
                for (t0, n) in part:
                    lo = t0 - pt0
                    for jj in range(2):
                        j = 2 * js + jj
                        pa, bpa = bank("ffa", [0, 1, 2])
                        pg, bpg = bank("ffb", [3, 4, 5])
                        for k in range(8):
                            S.pe(I("matmul", pa[:, 0:n], wv[:, k, 128 * jj:128 * jj + 128], ubuf[:, k, t0:t0 + n],
                                                                                      start=(k == 0), stop=(k == 7)),
                                 reads=wb + ub(t0, n), writes=[bpa])
                        for k in range(8):
                            S.pe(I("matmul", pg[:, 0:n], wv[:, k, 256 + 128 * jj:256 + 128 * jj + 128], ubuf[:, k, t0:t0 + n],
                                                                                      start=(k == 0), stop=(k == 7)),
                                 reads=wb + ub(t0, n), writes=[bpg])
                        si = j % 2
                        S.act(I("activation", out=sa[si][:, 0:n], in_=pa[:, 0:n], func=AF.Silu), reads=[bpa], writes=[B("sa", si)])
                        S.dve(I("tensor_tensor", out=hT[:, j, lo:lo + n], in0=pg[:, 0:n], in1=sa[si][:, 0:n], op=ALU.mult),
                              reads=[bpg, B("sa", si)], writes=[B("hT")])
            if part is parts[-1]:
                ensure("na", *next_sl(s, l))
            S.dma(I("dma_start", out=tfull[:, :, 0:ptn], in_=xres[s][:, :, pt0:pt0 + ptn]),
                  reads=[B("xres", s, tt) for tt in tiles_of(pt0, ptn)], writes=btf_all)
            for oc2 in range(4):
                w2v = w2[oc2 % 2]
                bw2 = [B("w2", oc2 % 2)]
                S.dma(I("dma_start", out=w2v, in_=wf2[:, :, 256 * oc2:256 * oc2 + 256]), writes=bw2, q="pool")
                for bi_, (t0, n) in enumerate(part):
                    lo = t0 - pt0
                    mi = msel(s, t0)
                    for oo in range(2):
                        oc = 2 * oc2 + oo
                        py, bpy = bank("ffy", [0, 1, 2, 3])
                        for j in range(NHC):
                            S.pe(I("matmul", py[:, 0:n], w2v[:, j, 128 * oo:128 * oo + 128], hT[:, j, lo:lo + n],
                                                                                        start=(j == 0), stop=(j == NHC - 1)),
                                 reads=bw2 + [B("hT")], writes=[bpy])
                        S.dve(I("scalar_tensor_tensor", out=tfull[:, oc, lo:lo + n], in0=py[:, 0:n], scalar=mder[:, l, 3, oc, mi:mi + 1],
                                                                                            in1=tfull[:, oc, lo:lo + n], op0=ALU.mult, op1=ALU.add),
                              reads=[bpy, btf(bi_, oc)] + MD, writes=[btf(bi_, oc)])
            for bi_, (t0, n) in enumerate(part):
                lo = t0 - pt0
                mi = msel(s, t0)
                tbv = tfull[:, :, lo:lo + n]
                bkf = lambda oc, bi_=bi_: btf(bi_, oc)
                bkf_all = [bkf(oc) for oc in range(8)]
                xrb = [B("xres", s, tt) for tt in tiles_of(t0, n)]
                if not last:
                    outs = [
                        ("dve", lambda oc, t0=t0, n=n: ubuf[:, oc, t0:t0 + n], lambda oc, mi=mi: mder2[:, l, 0, oc, mi:mi + 1], lambda oc, mi=mi: mder2[:, l, 1, oc, mi:mi + 1],
                         lambda oc, t0=t0, n=n: ub(t0, n)),
                        ("act", lambda oc, tbv=tbv: tbv[:, oc, :], lambda oc: lnT[:, l, 2, oc:oc + 1], lambda oc: lnT[:, l, 3, oc:oc + 1], lambda oc, bkf=bkf: [bkf(oc)]),
                    ]
                    layer_norm_block(tbv, bkf, n, outs)
                    S.dma(I("dma_start", out=xres[s][:, :, t0:t0 + n], in_=tbv), reads=bkf_all, writes=xrb)
                else:
                    outs = [("act", lambda oc, tbv=tbv: tbv[:, oc, :], lambda oc: lnT[:, l, 2, oc:oc + 1], lambda oc: lnT[:, l, 3, oc:oc + 1], lambda oc, bkf=bkf: [bkf(oc)])]
                    layer_norm_block(tbv, bkf, n, outs)
                    for qs in range(n // 128):
                        tq = (t0 + 128 * qs) // 128
                        og = ostage[tq % 2]
                        for half in range(2):
                            pt, bpt = bank("ffo", [0, 1, 2, 3])
                            for kk in range(4):
                                k = 4 * half + kk
                                S.pe(I("transpose", pt[:, 128 * kk:128 * kk + 128], tbv[:, k, 128 * qs:128 * qs + 128], identf[:]),
                                     reads=[bkf(k), B("identf")], writes=[bpt])
                            S.act(I("copy", out=og[:, 512 * half:512 * half + 512], in_=pt[:, 0:512]),
                                  reads=[bpt], writes=[B("ostage", tq % 2)])
                        S.dma(I("dma_start", out=out_d[s, 128 * (tq - 2):128 * (tq - 2) + 128, :], in_=og),
                              reads=[B("ostage", tq % 2)], writes=[B("out")])

    for s in range(nseq):
        load_x(s)
        if stage == 1 and s == 0:
            dump("u0", ubuf[:], [128, 8, T], BF16, B("ubuf", NTT - 1))
        for l in range(DEPTH):
            if stage >= 2:
                na_phase(s, l)
            if stage >= 3:
                gla_phase(s, l)
            if stage >= 4:
                gqa_phase(s, l)
            if stage in (2, 3, 4) and s == 0 and l == 0:
                S.barrier()
                S.pool(I("memset", epsln[:], LN_EPS_S), reads=[B("obuf", tt) for tt in range(NTT)], writes=[B("eps")])
                dump("o0", obuf[:], [128, 8, T], BF16, B("eps"))
                break
            if stage >= 5:
                outproj_phase(s, l)
            if stage == 5 and s == 0 and l == 0:
                S.barrier()
                S.pool(I("memset", epsln[:], LN_EPS_S), reads=[B("ubuf", tt) for tt in range(NTT)], writes=[B("eps")])
                dump("u2", ubuf[:], [128, 8, T], BF16, B("eps"))
                dump("x1", xres[0], [128, 8, T], F32, B("eps"))
                break
            if stage >= 6:
                ffn_phase(s, l)
            if stage == 6 and s == 0 and l == 0:
                S.barrier()
                S.pool(I("memset", epsln[:], LN_EPS_S), reads=[B("ubuf", tt) for tt in range(NTT)], writes=[B("eps")])
                dump("u1n", ubuf[:], [128, 8, T], BF16, B("eps"))
                dump("x2", xres[0], [128, 8, T], F32, B("eps"))
                break


    S.emit(final_bufs=dump_list + [B("out")])
    S.close()
    global LAST_SCHED
    LAST_SCHED = S
    return nc


def host_consts():
    eye = np.eye(128, dtype=np.float32)
    onesd = np.full((128, 128), 1.0 / 1024.0, np.float32)
    bon = np.zeros((128, 128), np.float32)
    bon[:64, :64] = 1.0 / 64
    bon[64:, 64:] = 1.0 / 64
    R = np.zeros((64, 64), np.float32)
    for d in range(16):
        R[d, d + 16] = -1.0
        R[d + 16, d] = 1.0
        R[d + 32, d + 48] = -1.0
        R[d + 48, d + 32] = 1.0
    rl = np.zeros((128, 128), np.float32)
    rl[:64, :64] = R.T
    rl[64:, 64:] = R.T
    hm = np.zeros_like(eye)
    for h in range(4):
        hm[32 * h:32 * h + 32, h] = 1.0
    cm = np.stack([eye, onesd, bon, rl, hm], axis=1)
    sidx = np.arange(64)
    mf = (sidx[None, :] >= sidx[:, None]).astype(np.float32)
    mb = (sidx[None, :] <= sidx[:, None]).astype(np.float32)
    tri = np.zeros((128, 2, 256), np.float32)
    for half in range(2):
        tri[64 * half:64 * half + 64, 0, :] = np.tile(mf, (1, 4))
        tri[64 * half:64 * half + 64, 1, :] = np.tile(mb, (1, 4))
    t = np.arange(NL)
    row = (t // 64).astype(np.float32)
    col = (t % 64).astype(np.float32)
    inv_freq = (10000.0 ** (-np.arange(16, dtype=np.float32) / 16)).astype(np.float32)
    ang_r = row[:, None] * inv_freq
    ang_c = col[:, None] * inv_freq
    ang = np.concatenate([ang_r, ang_r, ang_c, ang_c], axis=-1).astype(np.float32)
    cosT = np.tile(np.cos(ang).T, (2, 1)).astype(np.float32)
    sinT = np.tile(np.sin(ang).T, (2, 1)).astype(np.float32)
    return cm, tri, np.ascontiguousarray(cosT), np.ascontiguousarray(sinT)


def prep_shared(inp):
    f = lambda a: np.ascontiguousarray(np.asarray(a, dtype=np.float32))
    cm, tri, cosT, sinT = host_consts()
    sh = {}
    sh["w_ada"] = f(inp["w_ada"])
    sh["b_adaT"] = f(np.asarray(inp["b_ada"]).reshape(DEPTH, 48, 128).transpose(0, 2, 1))
    sh["w_in"] = f(inp["w_in"])
    sh["w_out"] = f(inp["w_out"])
    sh["w_ffn_in"] = f(inp["w_ffn_in"])
    sh["w_ffn_out"] = f(inp["w_ffn_out"])
    ln = np.stack([np.asarray(inp[k]) for k in ("ln1_g", "ln1_b", "ln2_g", "ln2_b")], axis=1)
    sh["lnT"] = f(ln.reshape(DEPTH, 4, 8, 128).transpose(3, 0, 1, 2))
    valid, dri, dci = na_index_tables()
    rpb = np.asarray(inp["na_rpb"], dtype=np.float32)
    g = rpb[:, :, dri, dci]
    g = np.where(valid[None, None], g, np.float32(-30000.0))
    sh["nab"] = f(g.transpose(0, 3, 1, 2, 4).reshape(DEPTH, 128, 6, NCOMBO * 128))
    wa2 = np.zeros((DEPTH, 2, 32, 128), np.float32)
    gw = np.asarray(inp["gla_wa2"], dtype=np.float32)
    wa2[:, 0, 0:16, :] = gw[:, 0]
    wa2[:, 1, 16:32, :] = gw[:, 1]
    sh["wa2"] = wa2
    sh["nbaT"] = f(np.asarray(inp["gla_ba"]).transpose(2, 0, 1))
    sh["glnw"] = f(np.tile(np.asarray(inp["gla_norm_w"]), (1, 4)).reshape(DEPTH, 1, 256))
    qk = np.stack([np.tile(np.asarray(inp["gqa_qnorm_w"]), (1, 2)), np.tile(np.asarray(inp["gqa_knorm_w"]), (1, 2))], axis=2)
    sh["qkwT"] = f(qk.transpose(1, 0, 2))
    sh["cosT"] = cosT
    sh["sinT"] = sinT
    sh["cmats"] = f(cm)
    sh["trim"] = f(tri)
    return sh


def prep_core(inp, core, sh):
    f = lambda a: np.ascontiguousarray(np.asarray(a, dtype=np.float32))
    b0 = core * NSEQ
    m = dict(sh)
    m["x"] = f(inp["x"][b0:b0 + NSEQ])
    m["ctx"] = f(inp["ctx"][b0:b0 + NSEQ])
    cc = np.stack([np.asarray(inp["c"][b0]), np.asarray(inp["c"][b0 + 1]), np.asarray(inp["c_ctx"])], axis=1)
    m["cT"] = f(cc.reshape(8, 128, 3).transpose(1, 0, 2))
    return m


def kernel(**inputs):
    sh = prep_shared(inputs)
    nc = build_nc()
    in_maps = [prep_core(inputs, c, sh) for c in range(8)]
    res = run_bass_kernel_spmd(nc, in_maps, core_ids=list(range(8)))
    out = np.concatenate([np.asarray(r["out"]) for r in res.results], axis=0)
    return out.astype(np.float32)
```

```python
import contextlib
import numpy as np
import concourse.bass as bass
import concourse.mybir as mybir
from concourse.bass_utils import run_bass_kernel_spmd

F32 = mybir.dt.float32
BF16 = mybir.dt.bfloat16
AF = mybir.ActivationFunctionType
ALU = mybir.AluOpType
AX = mybir.AxisListType

COMPUTE = ("pe", "act", "dve", "pool")
NDMA_SEM = 6


class Buf:
    __slots__ = ("last_w", "readers", "excl")

    def __init__(self, excl=False):
        self.last_w = None
        self.readers = []
        self.excl = excl


class Sched:
    def __init__(self, nc):
        self.nc = nc
        self.ops = {e: [] for e in ("pe", "act", "dve", "pool", "sp")}
        self.stack = contextlib.ExitStack()
        self.nalloc = 0
        self.bufs = {}
        self.pending = {e: set() for e in self.ops}
        self.phase = "pro"

    def B(self, *key):
        b = self.bufs.get(key)
        if b is None:
            b = self.bufs[key] = Buf(excl=(key[0] in ("pf", "pb")))
        return b

    def sbuf(self, shape, dtype, name=None):
        self.nalloc += 1
        return self.stack.enter_context(self.nc.sbuf_tensor("sb_" + (name or f"t{self.nalloc}"), list(shape), dtype))

    def psum(self, shape, dtype, name=None):
        self.nalloc += 1
        return self.stack.enter_context(self.nc.psum_tensor("ps_" + (name or f"t{self.nalloc}"), list(shape), dtype))

    def add(self, eng, fn, reads=(), writes=(), dma=False):
        idx = len(self.ops[eng])
        deps = set(self.pending[eng])
        self.pending[eng] = set()
        for b in reads:
            if b.last_w is not None:
                deps.add(b.last_w)
            if b.excl:
                deps.update(r for r in b.readers if r[0] != eng)
        for b in writes:
            if b.last_w is not None:
                deps.add(b.last_w)
            deps.update(b.readers)
        if eng == "pe":
            deps = {d for d in deps if d[0] != "pe"}
        self.ops[eng].append({"fn": fn, "deps": deps, "dma": dma, "marked": False, "ph": self.phase})
        me = (eng, idx)
        for b in reads:
            b.readers.append(me)
        for b in writes:
            b.last_w = me
            b.readers = []
        return me

    def pe(self, fn, reads=(), writes=()):
        return self.add("pe", fn, reads, writes)

    def act(self, fn, reads=(), writes=()):
        return self.add("act", fn, reads, writes)

    def dve(self, fn, reads=(), writes=()):
        return self.add("dve", fn, reads, writes)

    def pool(self, fn, reads=(), writes=()):
        return self.add("pool", fn, reads, writes)

    def dma(self, fn, reads=(), writes=(), q="sp"):
        return self.add(q, fn, reads, writes, dma=True)

    def barrier(self):
        deps = set()
        for e in ("pe", "act", "dve"):
            if self.ops[e]:
                deps.add((e, len(self.ops[e]) - 1))
        for q in ("sp", "pool"):
            nd = 0
            gotc = False
            for i in range(len(self.ops[q]) - 1, -1, -1):
                if self.ops[q][i]["dma"]:
                    if nd < NDMA_SEM:
                        deps.add((q, i))
                        nd += 1
                elif not gotc:
                    deps.add((q, i))
                    gotc = True
                if nd >= NDMA_SEM and (gotc or q == "sp"):
                    break
        for e in self.pending:
            self.pending[e] |= deps

    def emit(self, final_bufs=()):
        nc = self.nc
        ops = self.ops
        dma_info = {}
        for q in ("sp", "pool"):
            n = 0
            for i, op in enumerate(ops[q]):
                if op["dma"]:
                    slot = n % NDMA_SEM
                    val = 16 * (n // NDMA_SEM + 1)
                    dma_info[(q, i)] = (q, slot, val)
                    op["dmainfo"] = (slot, val)
                    n += 1
        for e in ops:
            for op in ops[e]:
                op["deps"] = {d for d in op["deps"]}
                for d in op["deps"]:
                    if d not in dma_info:
                        ops[d[0]][d[1]]["marked"] = True
        final_deps = set()
        for b in final_bufs:
            if b.last_w is not None:
                final_deps.add(b.last_w)
                if b.last_w not in dma_info:
                    ops[b.last_w[0]][b.last_w[1]]["marked"] = True
        for e in COMPUTE:
            c = 0
            for op in ops[e]:
                if op["marked"] and not op["dma"]:
                    c += 1
                    op["semval"] = c
        st = self.stack
        csem = {e: st.enter_context(nc.semaphore(f"s_{e}")) for e in COMPUTE}
        dsem = {q: [st.enter_context(nc.semaphore(f"d_{q}{j}")) for j in range(NDMA_SEM)] for q in ("sp", "pool")}
        block = st.enter_context(nc.Block())
        engobj = {"pe": block.tensor, "act": block.scalar, "dve": block.vector, "pool": block.gpsimd, "sp": block.sync}

        def resolve(dep):
            if dep in dma_info:
                q, slot, val = dma_info[dep]
                return dsem[q][slot], ("d", q, slot), val
            f, k = dep
            return csem[f], ("c", f), ops[f][k]["semval"]

        def make_body(e):
            def body(eng):
                known = {}
                for op in ops[e]:
                    need = {}
                    for dep in op["deps"]:
                        if dep[0] == e and dep not in dma_info and e == "pe":
                            continue
                        sem, key, val = resolve(dep)
                        if need.get(key, (None, 0))[1] < val:
                            need[key] = (sem, val)
                    if op["dma"]:
                        slot, val = op["dmainfo"]
                        if val > 16:
                            key = ("d", e, slot)
                            if need.get(key, (None, 0))[1] < val - 16:
                                need[key] = (dsem[e][slot], val - 16)
                    for key, (sem, val) in need.items():
                        if known.get(key, 0) >= val:
                            continue
                        known[key] = val
                        eng.wait_ge(sem, val)
                    ins = op["fn"](eng)
                    if op["dma"]:
                        ins.then_inc(dsem[e][op["dmainfo"][0]], 16)
                    elif op["marked"]:
                        ins.then_inc(csem[e], 1)
                if e == "sp":
                    need = {}
                    for dep in final_deps:
                        sem, key, val = resolve(dep)
                        if need.get(key, (None, 0))[1] < val:
                            need[key] = (sem, val)
                    for key, (sem, val) in need.items():
                        eng.wait_ge(sem, val)
            return body

        for e in ("sp", "pool", "act", "dve", "pe"):
            engobj[e](make_body(e))

    def close(self):
        self.stack.close()


def I(name, *args, **kw):
    f = lambda eng: getattr(eng, name)(*args, **kw)
    try:
        f.f32 = name in ("matmul", "transpose") and args[1].dtype == F32
    except Exception:
        f.f32 = False
    return f


D = 1024
LC = 256
NL = 2048
T = LC + NL
NTT = T // 128
FH = 2816
NHC = FH // 128
DEPTH = 2
NSEQ = 2
BLOCKS = [(0, 256)] + [(256 + 512 * i, 512) for i in range(4)]
ALPHA2 = 2.0
LN_EPS_S = 1e-5 / ALPHA2
RMS_EPS = 1e-6
NCOMBO = 21
WBYTES = 91136

C_NAQ, C_NAK, C_NAV = 0, 384, 768
C_GLQ, C_GLK, C_GLV, C_GLG, C_GLR = 1152, 1280, 1408, 1664, 1920
C_GAQ, C_GAK, C_GAV = 1952, 2336, 2464


def na_combos():
    combos = []
    idx = {}

    def get(kind, r, d):
        key = (kind, d)
        if key not in idx:
            idx[key] = len(combos)
            combos.append((r, d))
        return idx[key]

    slots = []
    for p in range(16):
        r = 2 * p
        if p == 0:
            s = [(j, get("e0", r, 2 * j - r)) for j in range(0, 4)]
        elif p == 1:
            s = [(j, get("e1", r, 2 * j - r)) for j in range(0, 4)]
        elif p == 14:
            s = [(j, get("e2", r, 2 * j - r)) for j in range(12, 16)]
        elif p == 15:
            s = [(j, get("e3", r, 2 * j - r)) for j in range(12, 16)]
        else:
            s = [(j, get("g", 8, 2 * j - r)) for j in range(p - 2, p + 3)]
        slots.append(s)
    assert len(combos) == NCOMBO
    return combos, slots


def na_index_tables():
    combos, _ = na_combos()
    cols = np.arange(64)
    col_start = np.clip(cols - 8, 0, 48)
    col_in = (cols[None, :] >= col_start[:, None]) & (cols[None, :] < col_start[:, None] + 16)
    valid = np.zeros((NCOMBO, 128, 128), bool)
    dri = np.zeros((NCOMBO, 128, 128), np.int64)
    dci = np.zeros((NCOMBO, 128, 128), np.int64)
    for ci, (r, d) in enumerate(combos):
        for ko in range(2):
            for qo in range(2):
                kr = r + d + ko
                qr = r + qo
                rs = min(max(qr - 4, 0), 24)
                ok_row = (rs <= kr < rs + 8)
                dr = kr - qr
                blk_valid = col_in.T & ok_row
                dc = np.clip(cols[:, None] - cols[None, :], -15, 15) + 15
                valid[ci, 64 * ko:64 * ko + 64, 64 * qo:64 * qo + 64] = blk_valid
                dri[ci, 64 * ko:64 * ko + 64, 64 * qo:64 * qo + 64] = min(max(dr + 7, 0), 14)
                dci[ci, 64 * ko:64 * ko + 64, 64 * qo:64 * qo + 64] = dc
    return valid, dri, dci


def build_nc(stage=99, nseq=NSEQ, dumps=None):
    nc = bass.Bass("TRN2", target_bir_lowering=False)
    S = Sched(nc)
    B = S.B
    dumps = dumps if dumps is not None else {}

    def din(name, shape, dt=F32):
        return nc.dram_tensor(name, list(shape), dt, kind="ExternalInput").ap()

    x_d = din("x", [NSEQ, NL, D])
    ctx_d = din("ctx", [NSEQ, LC, D])
    cT_d = din("cT", [128, 8, 3])
    wada_d = din("w_ada", [DEPTH, D, 6 * D])
    bada_d = din("b_adaT", [DEPTH, 128, 48])
    win_d = din("w_in", [DEPTH, D, 2592])
    wout_d = din("w_out", [DEPTH, D, D])
    wf1_d = din("w_ffn_in", [DEPTH, D, 2 * FH])
    wf2_d = din("w_ffn_out", [DEPTH, FH, D])
    lnT_d = din("lnT", [128, DEPTH, 4, 8])
    nab_d = din("nab", [DEPTH, 128, 6, NCOMBO * 128])
    wa2_d = din("wa2", [DEPTH, 2, 32, 128])
    nba_d = din("nbaT", [128, DEPTH, 2])
    glnw_d = din("glnw", [DEPTH, 1, 256])
    qkw_d = din("qkwT", [128, DEPTH, 2])
    cos_d = din("cosT", [128, NL])
    sin_d = din("sinT", [128, NL])
    cm_d = din("cmats", [128, 5, 128])
    tri_d = din("trim", [128, 2, 256])
    out_d = nc.dram_tensor("out", [NSEQ, NL, D], F32, kind="ExternalOutput").ap()
    xres = [nc.dram_tensor(f"xres{s}", [128, 8, T], F32).ap() for s in range(NSEQ)]
    dump_list = []

    ubuf = S.sbuf([128, 8, T], BF16, "ubuf")
    obuf = S.sbuf([128, 8, T], BF16, "obuf")
    wbuf = [S.sbuf([128, 8 * 1152], BF16, f"wbuf{i}") for i in range(2)]

    def wview(i, dt, k, c):
        if dt == F32:
            return wbuf[i][:].bitcast(F32)[:, 0:k * c].rearrange("p (k c) -> p k c", k=k, c=c)
        return wbuf[i][:, 0:k * c].rearrange("p (k c) -> p k c", k=k, c=c)

    Wr = S.sbuf([128, WBYTES // 4], F32, "Wr")
    identf = S.sbuf([128, 128], F32, "identf")
    onesD = S.sbuf([128, 128], F32, "onesD")
    identb = S.sbuf([128, 128], BF16, "identb")
    bones = S.sbuf([128, 128], BF16, "bones")
    rotm = S.sbuf([128, 128], BF16, "rotm")
    trim = S.sbuf([128, 2, 256], BF16, "trim")
    scT = S.sbuf([128, 8, 3], F32, "scT")
    modT = S.sbuf([128, DEPTH, 48, 3], F32, "modT")
    mder = S.sbuf([128, DEPTH, 6, 8, 3], F32, "mder")
    mder2 = S.sbuf([128, DEPTH, 2, 8, 3], F32, "mder2")
    lnT = S.sbuf([128, DEPTH, 4, 8], F32, "lnT")
    badaT = S.sbuf([128, DEPTH, 48], F32, "badaT")
    nbaT = S.sbuf([128, DEPTH, 2], F32, "nbaT")
    qkwT = S.sbuf([128, DEPTH, 2], F32, "qkwT")
    glnw = S.sbuf([128, DEPTH, 256], F32, "glnw")
    wa2 = S.sbuf([32, DEPTH, 2, 128], F32, "wa2")
    hmask = S.sbuf([128, 4], F32, "hmask")
    epsln = S.sbuf([128, 1], F32, "epsln")
    epsrms = S.sbuf([128, 1], F32, "epsrms")

    pf = [S.psum([128, 512], F32, f"pf{i}") for i in range(7)]
    pb = S.psum([128, 1024], BF16, "pb")
    pctr = [0]

    def nextp():
        i = pctr[0] % 7
        pctr[0] += 1
        return pf[i], B("pf", i)

    class Carver:
        def __init__(self):
            self.off = 0

        def __call__(self, dt, *dims):
            n = int(np.prod(dims))
            nbytes = n * (4 if dt == F32 else 2)
            nbytes = (nbytes + 63) // 64 * 64
            a = self.off // 4
            self.off += nbytes
            assert self.off <= WBYTES, ("scratch overflow", self.off)
            ap = Wr[:, a:a + nbytes // 4]
            if dt == BF16:
                ap = ap.bitcast(BF16)[:, 0:n]
            else:
                ap = ap[:, 0:n]
            if len(dims) == 2:
                ap = ap.rearrange("p (a b) -> p a b", a=dims[0], b=dims[1])
            elif len(dims) == 3:
                ap = ap.rearrange("p (a b c) -> p a b c", a=dims[0], b=dims[1], c=dims[2])
            return ap

    def dump(name, ap, shape, dt, rbuf):
        d = nc.dram_tensor("dbg_" + name, list(shape), dt, kind="ExternalOutput").ap()
        S.dma(I("dma_start", out=d, in_=ap), reads=[rbuf], writes=[B("dump", name)])
        dump_list.append(B("dump", name))
        dumps[name] = (shape, dt)

    def ld(dst, src, bname, q="sp"):
        S.dma(I("dma_start", out=dst, in_=src), writes=[B(bname)], q=q)

    ld(identf[:], cm_d[:, 0, :], "identf")
    ld(onesD[:], cm_d[:, 1, :], "onesD")
    ld(identb[:], cm_d[:, 0, :], "identb", q="pool")
    ld(bones[:], cm_d[:, 2, :], "bones", q="pool")
    ld(rotm[:], cm_d[:, 3, :], "rotm", q="pool")
    ld(trim[:], tri_d, "trim", q="pool")
    ld(hmask[:], cm_d[:, 4, 0:4], "hmask")
    ld(scT[:], cT_d, "scT")
    ld(lnT[:], lnT_d, "lnT")
    ld(badaT[:], bada_d.rearrange("l p j -> p l j"), "badaT")
    ld(nbaT[:], nba_d, "nbaT")
    ld(qkwT[:], qkw_d, "qkwT")
    for l in range(DEPTH):
        ld(glnw[:, l, :], glnw_d[l].broadcast_to([128, 256]), "glnw")
        ld(wa2[:, l, :, :], wa2_d[l].rearrange("e r c -> r e c"), "wa2")
    S.dve(I("tensor_scalar_mul", out=nbaT[:], in0=nbaT[:], scalar1=-1.0), reads=[B("nbaT")], writes=[B("nbaT")])
    S.pool(I("memset", epsln[:], LN_EPS_S), writes=[B("eps")])
    S.pool(I("memset", epsrms[:], RMS_EPS), writes=[B("eps")])
    S.act(I("activation", out=scT[:], in_=scT[:], func=AF.Silu), reads=[B("scT")], writes=[B("scT")])

    for l in range(DEPTH):
        pm, bpm = nextp()
        for js in range(12):
            wv = wview(js % 2, F32, 8, 512)
            S.dma(I("dma_start",
                out=wv, in_=wada_d[l].rearrange("(k p) c -> p k c", p=128)[:, :, 512 * js:512 * js + 512]),
                writes=[B("wbuf", js % 2)])
            for jj in range(4):
                j = 4 * js + jj
                for k in range(8):
                    S.pe(I("matmul",
                        pm[:, 3 * j:3 * j + 3], wv[:, k, 128 * jj:128 * jj + 128], scT[:, k, :],
                        start=(k == 0), stop=(k == 7)),
                        reads=[B("wbuf", js % 2), B("scT")], writes=[bpm])
        S.dve(I("tensor_tensor",
            out=modT[:, l, :, :], in0=pm[:, 0:144].rearrange("p (j b) -> p j b", j=48, b=3),
            in1=badaT[:, l, :].unsqueeze(2).broadcast_to([128, 48, 3]), op=ALU.add),
            reads=[bpm, B("badaT")], writes=[B("modT")])
    inv_a = float(1.0 / np.sqrt(ALPHA2))
    for l in range(DEPTH):
        mT = modT[:, l, :, :]
        rd = [B("modT"), B("lnT")]
        wr = [B("mder")]
        S.dve(I("tensor_scalar_add", out=mder[:, l, 0, :, :], in0=mT[:, 8:16, :], scalar1=1.0), reads=rd, writes=wr)
        S.dve(I("tensor_scalar_mul", out=mder[:, l, 1, :, :], in0=mT[:, 16:24, :], scalar1=inv_a), reads=rd, writes=wr)
        S.dve(I("tensor_scalar_add", out=mder[:, l, 2, :, :], in0=mT[:, 32:40, :], scalar1=1.0), reads=rd, writes=wr)
        S.dve(I("tensor_scalar_mul", out=mder[:, l, 3, :, :], in0=mT[:, 40:48, :], scalar1=inv_a), reads=rd, writes=wr)
        S.dve(I("tensor_tensor", out=mder[:, l, 4, :, :], in0=mder[:, l, 2, :, :],
                                             in1=lnT[:, l, 0, :].unsqueeze(2).broadcast_to([128, 8, 3]), op=ALU.mult), reads=rd + wr, writes=wr)
        S.dve(I("tensor_tensor", out=mder[:, l, 5, :, :], in0=mder[:, l, 2, :, :],
                                             in1=lnT[:, l, 1, :].unsqueeze(2).broadcast_to([128, 8, 3]), op=ALU.mult), reads=rd + wr, writes=wr)
        S.dve(I("tensor_tensor", out=mder[:, l, 5, :, :], in0=mder[:, l, 5, :, :], in1=mT[:, 24:32, :], op=ALU.add), reads=rd + wr, writes=wr)
    for l in range(DEPTH - 1):
        rd = [B("modT"), B("lnT"), B("mder")]
        wr = [B("mder2")]
        S.dve(I("tensor_tensor", out=mder2[:, l, 0, :, :], in0=mder[:, l + 1, 0, :, :],
                                             in1=lnT[:, l, 2, :].unsqueeze(2).broadcast_to([128, 8, 3]), op=ALU.mult), reads=rd, writes=wr)
        S.dve(I("tensor_tensor", out=mder2[:, l, 1, :, :], in0=mder[:, l + 1, 0, :, :],
                                             in1=lnT[:, l, 3, :].unsqueeze(2).broadcast_to([128, 8, 3]), op=ALU.mult), reads=rd + wr, writes=wr)
        S.dve(I("tensor_tensor", out=mder2[:, l, 1, :, :], in0=mder2[:, l, 1, :, :], in1=modT[:, l + 1, 0:8, :], op=ALU.add), reads=rd + wr, writes=wr)
    MD = [B("mder"), B("mder2"), B("modT"), B("lnT")]
    if stage < 99:
        S.pool(I("memset", obuf[:], 0.0), writes=[B("obuf", tt) for tt in range(NTT)])
    if stage == 0:
        dump("modT", modT[:], [128, DEPTH, 48, 3], F32, B("modT"))

    def msel(s, t0):
        return 2 if t0 < LC else s

    gctr = {}

    def bank(group, banks):
        c = gctr.get(group, 0)
        gctr[group] = c + 1
        i = banks[c % len(banks)]
        return pf[i], B("pf", i)

    def tiles_of(t0, n):
        return range(t0 // 128, (t0 + n) // 128)

    def ub(t0, n):
        return [B("ubuf", tt) for tt in tiles_of(t0, n)]

    win_v = [win_d[l].rearrange("(k p) c -> p k c", p=128) for l in range(DEPTH)]
    issued = set()

    def ensure(kind, s_, l_):
        key = (kind, s_, l_)
        if key in issued or s_ >= nseq or l_ >= DEPTH:
            return
        issued.add(key)
        if kind == "na":
            S.dma(I("dma_start", out=wview(0, BF16, 8, 1152), in_=win_v[l_][:, :, 0:1152]), writes=[B("wbuf", 0)], q="pool")
        elif kind == "gla":
            S.dma(I("dma_start", out=wview(0, BF16, 8, 800), in_=win_v[l_][:, :, C_GLQ:C_GLQ + 800]), writes=[B("wbuf", 0)], q="pool")
        elif kind == "gqa":
            wv_ = wview(1, BF16, 8, 768)
            wb_ = [B("wbuf", 1)]
            S.dma(I("dma_start", out=wv_[:, :, 0:384], in_=win_v[l_][:, :, C_GAQ:C_GAQ + 384]), writes=wb_, q="pool")
            for g in range(2):
                for d in range(2):
                    S.dma(I("dma_start", out=wv_[:, :, 384 + 128 * g + 64 * d:384 + 128 * g + 64 * d + 64],
                            in_=win_v[l_][:, :, C_GAK + 64 * g:C_GAK + 64 * g + 64]), writes=wb_, q="pool")
            S.dma(I("dma_start", out=wv_[:, :, 640:768], in_=win_v[l_][:, :, C_GAV:C_GAV + 128]), writes=wb_, q="pool")
        elif kind == "out":
            S.dma(I("dma_start", out=wview(1, BF16, 8, 1024), in_=wout_d[l_].rearrange("(k p) c -> p k c", p=128)), writes=[B("wbuf", 1)], q="pool")
        elif kind == "ffn0":
            wf1_ = wf1_d[l_].rearrange("(k p) c -> p k c", p=128)
            wv_ = wview(0, BF16, 8, 512)
            S.dma(I("dma_start", out=wv_[:, :, 0:256], in_=wf1_[:, :, 0:256]), writes=[B("wbuf", 0)], q="pool")
            S.dma(I("dma_start", out=wv_[:, :, 256:512], in_=wf1_[:, :, FH:FH + 256]), writes=[B("wbuf", 0)], q="pool")

    def next_sl(s_, l_):
        return (s_, l_ + 1) if l_ + 1 < DEPTH else (s_ + 1, 0)
    combos, na_slots = na_combos()
    PBB = B("pb", 0)

    def load_x(s):
        S.phase = "load_x"
        S.barrier()
        cv = Carver()
        xin = [cv(F32, 1024) for _ in range(2)]
        xst = [cv(F32, 8, 128) for _ in range(2)]
        for tt in range(NTT):
            i2 = tt % 2
            src = ctx_d[s, 128 * tt:128 * tt + 128, :] if tt < 2 else x_d[s, 128 * (tt - 2):128 * (tt - 2) + 128, :]
            S.dma(I("dma_start", out=xin[i2], in_=src), writes=[B("xin", i2)])
            mi = msel(s, 128 * tt)
            for half in range(2):
                pt, bpt = bank("ld", [0, 1, 2, 3])
                for kk in range(4):
                    k = 4 * half + kk
                    S.pe(I("transpose", pt[:, 128 * kk:128 * kk + 128], xin[i2][:, 128 * k:128 * k + 128], identf[:]),
                         reads=[B("xin", i2), B("identf")], writes=[bpt])
                S.dve(I("tensor_copy",
                    out=xst[i2][:, 4 * half:4 * half + 4, :], in_=pt[:].rearrange("p (k t) -> p k t", k=4, t=128)),
                    reads=[bpt], writes=[B("xst", i2)])
                for kk in range(4):
                    k = 4 * half + kk
                    S.act(I("activation",
                        out=ubuf[:, k, 128 * tt:128 * tt + 128], in_=pt[:, 128 * kk:128 * kk + 128], func=AF.Identity,
                        scale=mder[:, 0, 0, k, mi:mi + 1], bias=modT[:, 0, k, mi:mi + 1]),
                        reads=[bpt] + MD, writes=[B("ubuf", tt)])
            S.dma(I("dma_start", out=xres[s][:, :, 128 * tt:128 * tt + 128], in_=xst[i2]),
                  reads=[B("xst", i2)], writes=[B("xres", s, tt)])

    def na_phase(s, l):
        S.phase = "na_phase" + str(l)
        S.barrier()
        cv = Carver()
        naq = cv(BF16, 3, T)
        nak = cv(BF16, 3, T)
        nav = cv(BF16, NTT, 6 * 65)
        natab = cv(BF16, 6, NCOMBO * 128)
        naP = [cv(BF16, 7, 128) for _ in range(3)]
        otok = [cv(BF16, 384) for _ in range(2)]
        rden = [cv(F32, 8) for _ in range(2)]
        nav4 = nav.rearrange("p t (h c) -> p t h c", h=6, c=65)
        wv = wview(0, BF16, 8, 1152)
        ensure("na", s, l)
        S.dma(I("dma_start", out=natab, in_=nab_d[l]), writes=[B("natab")], q="pool")
        ensure("gqa", s, l)
        S.pool(I("memset", nav4[:, :, :, 64:65], 1.0), writes=[B("navones")])
        for (t0, n) in BLOCKS:
            for j in range(6):
                if j < 3 and t0 < LC and l == DEPTH - 1:
                    continue
                ps, bps = bank("proj", [0, 1, 2, 3, 4, 5, 6])
                for k in range(8):
                    S.pe(I("matmul", ps[:, 0:n], wv[:, k, 128 * j:128 * j + 128], ubuf[:, k, t0:t0 + n],
                                                                     start=(k == 0), stop=(k == 7)),
                         reads=[B("wbuf", 0)] + ub(t0, n), writes=[bps])
                wr = [B("naqk", tt) for tt in tiles_of(t0, n)]
                if j < 3:
                    S.act(I("activation", out=naq[:, j, t0:t0 + n], in_=ps[:, 0:n], func=AF.Identity, scale=0.125),
                          reads=[bps], writes=wr)
                else:
                    S.dve(I("tensor_copy", out=nak[:, j - 3, t0:t0 + n], in_=ps[:, 0:n]), reads=[bps], writes=wr)
        for tt in range(NTT):
            ps, bps = bank("proj", [0, 1, 2, 3, 4, 5, 6])
            for k in range(8):
                S.pe(I("matmul", ps[:, 0:384], ubuf[:, k, 128 * tt:128 * tt + 128], wv[:, k, 768:1152],
                                                         start=(k == 0), stop=(k == 7)),
                     reads=[B("wbuf", 0), B("ubuf", tt)], writes=[bps])
            S.dve(I("tensor_copy", out=nav4[:, tt, :, 0:64], in_=ps[:, 0:384].rearrange("p (h c) -> p h c", h=6, c=64)),
                  reads=[bps], writes=[B("nav", tt)])
        ensure("gla", s, l)
        qtiles = []
        if l < DEPTH - 1:
            qtiles += [(0, [(0, None), (1, None)]), (1, [(0, None), (1, None)])]
        for p in range(16):
            qtiles.append((2 + p, [(2 + j, ci) for (j, ci) in na_slots[p]] + [(0, None), (1, None)]))
        naP3 = naP
        units = [(qi, h) for qi in range(len(qtiles)) for h in range(6)]
        pos_ = {}
        ust = {}

        def na_st(u):
            qi, h = units[u]
            tq, slots = qtiles[qi]
            if h == 0:
                pos_[qi] = bank("napo", [4, 5])
            ns = len(slots)
            c = h // 2
            p0 = 64 * (h % 2)
            pa, bpa = bank("napa", [0, 1, 2, 3])
            pa2, bpa2 = bank("napa", [0, 1, 2, 3])
            P = naP3[u % 3]
            bP = B("naP", u % 3)
            ust[u] = (P, bP)
            nlat = sum(1 for (_, ci_) in slots if ci_ is not None)
            ci0 = slots[0][1]
            has_b = [False, False]
            if nlat >= 4:
                S.pe(I("matmul", pa[:, 0:512], identb[:], natab[:, h, 128 * ci0:128 * ci0 + 512], start=True, stop=False),
                     reads=[B("natab"), B("identb")], writes=[bpa])
                has_b[0] = True
            if nlat == 5:
                S.pe(I("matmul", pa2[:, 0:128], identb[:], natab[:, h, 128 * (ci0 + 4):128 * (ci0 + 4) + 128], start=True, stop=False),
                     reads=[B("natab"), B("identb")], writes=[bpa2])
                has_b[1] = True
            for si, (tk, ci) in enumerate(slots):
                bi_ = 0 if si < 4 else 1
                bk, bbk = (pa, bpa) if si < 4 else (pa2, bpa2)
                col = 128 * (si % 4)
                last_in_bank = (si == min(ns, 4) - 1) if bi_ == 0 else (si == ns - 1)
                if has_b[bi_]:
                    st_, sp_ = False, last_in_bank
                else:
                    st_, sp_ = True, True
                S.pe(I("matmul", bk[:, col:col + 128], nak[p0:p0 + 64, c, 128 * tk:128 * tk + 128], naq[p0:p0 + 64, c, 128 * tq:128 * tq + 128],
                       start=st_, stop=sp_),
                     reads=[B("naqk", tk), B("naqk", tq)], writes=[bbk])
            n1 = min(ns, 4)
            S.act(I("activation", out=P[:, 0:n1, :], in_=pa[:, 0:128 * n1].rearrange("p (a b) -> p a b", a=n1, b=128), func=AF.Exp),
                  reads=[bpa], writes=[bP])
            if ns > 4:
                n2 = ns - 4
                S.act(I("activation", out=P[:, 4:4 + n2, :], in_=pa2[:, 0:128 * n2].rearrange("p (a b) -> p a b", a=n2, b=128), func=AF.Exp),
                      reads=[bpa2], writes=[bP])

        def na_pv(u):
            qi, h = units[u]
            tq, slots = qtiles[qi]
            ns = len(slots)
            po, bpo = pos_[qi]
            P, bP = ust.pop(u)
            for si, (tk, ci) in enumerate(slots):
                S.pe(I("matmul", po[:, 65 * h:65 * h + 65], P[:, si, :], nav[:, tk, 65 * h:65 * h + 65], start=(si == 0), stop=(si == ns - 1)),
                     reads=[bP, B("nav", tk), B("navones")], writes=[bpo])
            if h < 5:
                return
            o2 = qi % 2
            po3 = po[:, 0:390].rearrange("p (h c) -> p h c", h=6, c=65)
            S.dve(I("reciprocal", out=rden[o2][:, 0:6], in_=po3[:, :, 64]), reads=[bpo], writes=[B("rden", o2)])
            S.dve(I("tensor_tensor", out=otok[o2].rearrange("p (h c) -> p h c", h=6, c=64), in0=po3[:, :, 0:64],
                    in1=rden[o2][:, 0:6].unsqueeze(2).broadcast_to([128, 6, 64]), op=ALU.mult),
                  reads=[bpo, B("rden", o2)], writes=[B("otok", o2)])
            for c3 in range(3):
                S.pe(I("transpose", pb[:, 128 * c3:128 * c3 + 128], otok[o2][:, 128 * c3:128 * c3 + 128], identb[:]),
                     reads=[B("otok", o2), B("identb")], writes=[PBB])
            S.act(I("copy", out=obuf[:, 0:3, 128 * tq:128 * tq + 128], in_=pb[:, 0:384].rearrange("p (a b) -> p a b", a=3, b=128)),
                  reads=[PBB], writes=[B("obuf", tq)])

        LA = 1
        for u in range(len(units) + LA):
            if u < len(units):
                na_st(u)
            if u >= LA:
                na_pv(u - LA)

    def gqa_phase(s, l):
        S.phase = "gqa_phase" + str(l)
        S.barrier()
        cv = Carver()
        gaq = cv(BF16, 3, T)
        gak = cv(BF16, 2, T)
        gav = cv(BF16, NTT, 130)
        cosv = cv(F32, NL)
        sinv = cv(F32, NL)
        sqs = [cv(BF16, 512) for _ in range(2)]
        rstds = [cv(F32, 512) for _ in range(2)]
        qns = [cv(F32, 512) for _ in range(2)]
        qnbs = [cv(BF16, 512) for _ in range(2)]
        t1s = [cv(F32, 512) for _ in range(2)]
        t2s = [cv(F32, 512) for _ in range(2)]
        gaP = [cv(BF16, 512) for _ in range(6)]
        otok = cv(BF16, 4, 384)
        rden = [cv(F32, 8) for _ in range(2)]
        gav4 = gav.rearrange("p t (g c) -> p t g c", g=2, c=65)
        wv = wview(1, BF16, 8, 768)
        wb = [B("wbuf", 1)]
        ensure("gqa", s, l)
        S.dma(I("dma_start", out=cosv, in_=cos_d), writes=[B("cos")])
        S.dma(I("dma_start", out=sinv, in_=sin_d), writes=[B("cos")])
        S.pool(I("memset", gav4[:, :, :, 64:65], 1.0), writes=[B("gavones")])
        cb = 0
        for (t0, n) in BLOCKS:
            for j in range(5):
                if j < 3 and t0 < LC and l == DEPTH - 1:
                    continue
                p2 = cb % 2
                cb += 1
                sq, rstd, qn, qnb, t1, t2 = sqs[p2], rstds[p2], qns[p2], qnbs[p2], t1s[p2], t2s[p2]
                TB = lambda nm, p2=p2: B("gatmp", nm, p2)
                ps, bps = bank("proj", [0, 1, 2, 3])
                for k in range(8):
                    S.pe(I("matmul", ps[:, 0:n], wv[:, k, 128 * j:128 * j + 128], ubuf[:, k, t0:t0 + n],
                                                                     start=(k == 0), stop=(k == 7)),
                         reads=wb + ub(t0, n), writes=[bps])
                S.act(I("activation", out=sq[:, 0:n], in_=ps[:, 0:n], func=AF.Square), reads=[bps], writes=[TB("sq")])
                pm, bpm = bank("gams", [4, 5])
                S.pe(I("matmul", pm[:, 0:n], bones[:], sq[:, 0:n], start=True, stop=True), reads=[TB("sq"), B("bones")], writes=[bpm])
                S.act(I("activation", out=rstd[:, 0:n], in_=pm[:, 0:n], func=AF.Ln, bias=epsrms[:, 0:1], scale=1.0),
                      reads=[bpm, B("eps")], writes=[TB("rstd")])
                S.act(I("activation", out=rstd[:, 0:n], in_=rstd[:, 0:n], func=AF.Exp, scale=-0.5), reads=[TB("rstd")], writes=[TB("rstd")])
                wi = 0 if j < 3 else 1
                S.dve(I("scalar_tensor_tensor", out=qn[:, 0:n], in0=ps[:, 0:n], scalar=qkwT[:, l, wi:wi + 1], in1=rstd[:, 0:n],
                                                                      op0=ALU.mult, op1=ALU.mult),
                      reads=[bps, TB("rstd"), B("qkwT")], writes=[TB("qn")])
                dst = gaq[:, j, t0:t0 + n] if j < 3 else gak[:, j - 3, t0:t0 + n]
                wr = [B("gaqk", tt) for tt in tiles_of(t0, n)]
                if t0 >= LC:
                    S.act(I("copy", out=qnb[:, 0:n], in_=qn[:, 0:n]), reads=[TB("qn")], writes=[TB("qnb")])
                    pr, bpr = bank("gams", [4, 5])
                    S.pe(I("matmul", pr[:, 0:n], rotm[:], qnb[:, 0:n], start=True, stop=True), reads=[TB("qnb"), B("rotm")], writes=[bpr])
                    S.dve(I("tensor_tensor", out=t1[:, 0:n], in0=qn[:, 0:n], in1=cosv[:, t0 - LC:t0 - LC + n], op=ALU.mult),
                          reads=[TB("qn"), B("cos")], writes=[TB("t1")])
                    S.dve(I("tensor_tensor", out=t2[:, 0:n], in0=pr[:, 0:n], in1=sinv[:, t0 - LC:t0 - LC + n], op=ALU.mult),
                          reads=[bpr, B("cos")], writes=[TB("t2")])
                    S.dve(I("tensor_tensor", out=dst, in0=t1[:, 0:n], in1=t2[:, 0:n], op=ALU.add),
                          reads=[TB("t1"), TB("t2")], writes=wr)
                else:
                    S.act(I("copy", out=dst, in_=qn[:, 0:n]), reads=[TB("qn")], writes=wr)
        for tt in range(NTT):
            ps, bps = bank("proj", [0, 1, 2, 3])
            for k in range(8):
                S.pe(I("matmul", ps[:, 0:128], ubuf[:, k, 128 * tt:128 * tt + 128], wv[:, k, 640:768],
                                                         start=(k == 0), stop=(k == 7)),
                     reads=wb + [B("ubuf", tt)], writes=[bps])
            S.dve(I("tensor_copy", out=gav4[:, tt, :, 0:64], in_=ps[:, 0:128].rearrange("p (g c) -> p g c", g=2, c=64)),
                  reads=[bps], writes=[B("gav", tt)])
        ensure("out", s, l)
        qblocks = []
        if l < DEPTH - 1:
            qblocks.append((0, 256, [0, 1]))
        for i in range(4):
            qblocks.append((256 + 512 * i, 512, list(range(NTT))))
        pi = 0
        for (t0, n, ktiles) in qblocks:
            nsub = n // 128
            nk = len(ktiles)
            for hp in range(3):
                heads = (2 * hp, 2 * hp + 1)
                pos2 = []
                for h in heads:
                    po, bpo = bank("gapo", [4, 5, 6])
                    S.dve(I("memset", po[:, 0:65 * nsub], 0.0), writes=[bpo])
                    pos2.append((po, bpo))
                Ps = {}

                def st(ki):
                    nonlocal pi
                    tk = ktiles[ki]
                    lst = []
                    for h in heads:
                        pa, bpa = bank("gapa", [0, 1, 2, 3])
                        P = gaP[pi % 6]
                        bP = B("gaP", pi % 6)
                        pi += 1
                        lst.append((pa, bpa, P, bP))
                    Ps[ki] = lst
                    for h, (pa, bpa, P, bP) in zip(heads, lst):
                        g = h // 3
                        p0 = 64 * (h % 2)
                        S.pe(I("matmul", pa[:, 0:n], gak[p0:p0 + 64, g, 128 * tk:128 * tk + 128], gaq[p0:p0 + 64, hp, t0:t0 + n],
                               start=True, stop=True),
                             reads=[B("gaqk", tk)] + [B("gaqk", tt) for tt in tiles_of(t0, n)], writes=[bpa])
                    for h, (pa, bpa, P, bP) in zip(heads, lst):
                        S.act(I("activation", out=P[:, 0:n], in_=pa[:, 0:n], func=AF.Exp, scale=0.125), reads=[bpa], writes=[bP])

                def pv(ki):
                    tk = ktiles[ki]
                    lst = Ps.pop(ki)
                    for h, (pa, bpa, P, bP), (po, bpo) in zip(heads, lst, pos2):
                        g = h // 3
                        for qs in range(nsub):
                            S.pe(I("matmul", po[:, 65 * qs:65 * qs + 65], P[:, 128 * qs:128 * qs + 128], gav[:, tk, 65 * g:65 * g + 65],
                                   start=False, stop=(ki == nk - 1), skip_group_check=True),
                                 reads=[bP, B("gav", tk), B("gavones")], writes=[bpo])

                GLA_ = 1
                for step in range(nk + GLA_):
                    if step < nk:
                        st(step)
                    if step >= GLA_:
                        pv(step - GLA_)
                for h, (po, bpo) in zip(heads, pos2):
                    o2 = h % 2
                    po3 = po[:, 0:65 * nsub].rearrange("p (q c) -> p q c", q=nsub, c=65)
                    S.dve(I("reciprocal", out=rden[o2][:, 0:nsub], in_=po3[:, :, 64]), reads=[bpo], writes=[B("rden", o2)])
                    S.dve(I("tensor_tensor", out=otok[:, 0:nsub, 64 * h:64 * h + 64], in0=po3[:, :, 0:64],
                            in1=rden[o2][:, 0:nsub].unsqueeze(2).broadcast_to([128, nsub, 64]), op=ALU.mult),
                          reads=[bpo, B("rden", o2)], writes=[B("gaotok")])
            for qs in range(nsub):
                tq = t0 // 128 + qs
                for c3 in range(3):
                    S.pe(I("transpose", pb[:, 128 * c3:128 * c3 + 128], otok[:, qs, 128 * c3:128 * c3 + 128], identb[:]),
                         reads=[B("gaotok"), B("identb")], writes=[PBB])
                S.act(I("copy", out=obuf[:, 5:8, 128 * tq:128 * tq + 128], in_=pb[:, 0:384].rearrange("p (a b) -> p a b", a=3, b=128)),
                      reads=[PBB], writes=[B("obuf", tq)])

    def gla_phase(s, l):
        S.phase = "gla_phase" + str(l)
        S.barrier()
        cv = Carver()
        glv = cv(BF16, NTT, 256)
        glgw = cv(BF16, NTT, 256)
        qt = [cv(BF16, T) for _ in range(2)]
        kt = [cv(BF16, T) for _ in range(2)]
        ktok = cv(BF16, NTT, 128)
        Dall = cv(F32, 2, 36)
        S32 = cv(F32, 37, 64)
        Sbf = [cv(BF16, 36, 64) for _ in range(2)]
        off_gate_tmp = cv.off
        qf = cv(F32, 512)
        kf = cv(F32, 512)
        lrf = cv(F32, 512)
        smask = cv(F32, 512)
        tAs = [cv(F32, 512) for _ in range(2)]
        tBs = [cv(F32, 512) for _ in range(2)]
        tCs = [cv(F32, 512) for _ in range(2)]
        tD = tCs[0]
        Abf = [cv(BF16, 2, 256) for _ in range(2)]
        qmb = [cv(BF16, 2, 4, 128) for _ in range(2)]
        osq = cv(F32, 256)
        sg = osq
        on = cv(F32, 256)
        otok = [cv(BF16, 256) for _ in range(2)]
        st4 = [cv(F32, 8) for _ in range(2)]
        TB = lambda n: B("gltmp", n)
        TB0 = TB
        wv = wview(0, BF16, 8, 800)
        wb = [B("wbuf", 0)]
        ensure("gla", s, l)
        S.pool(I("memset", smask, 1.0), writes=[TB0("smask")])
        S.pool(I("memset", smask.rearrange("p (c t) -> p c t", c=8, t=64)[:, :, 0:1], 0.0), writes=[TB0("smask")])
        for tt in range(NTT):
            ps, bps = bank("proj", [0, 1, 2, 3])
            for k in range(8):
                S.pe(I("matmul", ps[:, 0:512], ubuf[:, k, 128 * tt:128 * tt + 128], wv[:, k, 256:768],
                                                         start=(k == 0), stop=(k == 7)),
                     reads=wb + [B("ubuf", tt)], writes=[bps])
            S.dve(I("tensor_copy", out=glv[:, tt, :], in_=ps[:, 0:256]), reads=[bps], writes=[B("glv", tt)])
            S.act(I("activation", out=sg, in_=ps[:, 256:512], func=AF.Silu), reads=[bps], writes=[TB0("osq")])
            S.pool(I("tensor_tensor", out=glgw[:, tt, :], in0=sg, in1=glnw[:, l, :], op=ALU.mult),
                   reads=[TB0("osq"), B("glnw")], writes=[B("glgw", tt)])
        import os as _os
        _gs = int(_os.environ.get('GLASTOP', '9'))
        _sub = int(_os.environ.get('GLASUB', '9'))
        if _gs <= 0:
            return
        for (t0, n) in BLOCKS:
            nch = n // 64
            c0 = t0 // 64
            pq, bpq = bank("proj", [0, 1, 2, 3])
            pk, bpk = bank("proj", [0, 1, 2, 3])
            pl, bpl = bank("proj", [0, 1, 2, 3])
            for (pp, col, M) in ((pq, 0, 128), (pk, 128, 128), (pl, 768, 32)):
                bb = {id(pq): bpq, id(pk): bpk, id(pl): bpl}[id(pp)]
                for k in range(8):
                    S.pe(I("matmul", pp[0:M, 0:n], wv[:, k, col:col + M], ubuf[:, k, t0:t0 + n],
                                                                              start=(k == 0), stop=(k == 7)),
                         reads=wb + ub(t0, n), writes=[bb])
            S.act(I("activation", out=qf[:, 0:n], in_=pq[:, 0:n], func=AF.Identity, scale=float(32 ** -0.5)), reads=[bpq], writes=[TB0("qf")])
            S.dve(I("tensor_copy", out=kf[:, 0:n], in_=pk[:, 0:n]), reads=[bpk], writes=[TB0("kf")])
            S.dve(I("tensor_copy", out=lrf[0:32, 0:n], in_=pl[0:32, 0:n]), reads=[bpl], writes=[TB0("lrf")])
            for ed in range(2):
                tA, tB_, tC = tAs[ed], tBs[ed], tCs[ed]
                TBd = lambda n_, ed=ed: B("gltmp", n_, ed)
                pz, bpz = bank("glz", [4, 5])
                S.pe(I("matmul", pz[:, 0:n], wa2[:, l, ed, :], lrf[0:32, 0:n], start=True, stop=True),
                     reads=[TB0("lrf"), B("wa2")], writes=[bpz])
                S.act(I("activation", out=tA[:, 0:n], in_=pz[:, 0:n], func=AF.Exp, scale=-1.0, bias=nbaT[:, l, ed:ed + 1]),
                      reads=[bpz, B("nbaT")], writes=[TBd("tA")])
                S.act(I("activation", out=tB_[:, 0:n], in_=tA[:, 0:n], func=AF.Ln, bias=1.0, scale=1.0), reads=[TBd("tA")], writes=[TBd("tB")])
                S.dve(I("tensor_tensor_scan", out=tC[:, 0:n], data0=smask[:, 0:n], data1=tB_[:, 0:n], initial=0.0, op0=ALU.mult, op1=ALU.add),
                      reads=[TB0("smask"), TBd("tB")], writes=[TBd("tC")])
                cum = tC
                bcum = TBd("tC")
                if ed == 1:
                    tC3 = tC[:, 0:n].rearrange("p (c t) -> p c t", c=nch, t=64)
                    S.dve(I("tensor_tensor", out=tD[:, 0:n].rearrange("p (c t) -> p c t", c=nch, t=64),
                                                                         in0=tC3[:, :, 63:64].broadcast_to([128, nch, 64]), in1=tC3, op=ALU.subtract),
                          reads=[TBd("tC")], writes=[B("gltmp", "tC", 0)])
                    S.dve(I("tensor_tensor", out=tD[:, 0:n], in0=tD[:, 0:n], in1=tB_[:, 0:n], op=ALU.add), reads=[B("gltmp", "tC", 0), TBd("tB")], writes=[B("gltmp", "tC", 0)])
                    cum = tD
                    bcum = B("gltmp", "tC", 0)
                S.act(I("activation", out=tA[:, 0:n], in_=cum[:, 0:n], func=AF.Exp, scale=-1.0 / 16.0), reads=[bcum], writes=[TBd("tA")])
                S.act(I("activation", out=tB_[:, 0:n], in_=cum[:, 0:n], func=AF.Exp, scale=1.0 / 16.0), reads=[bcum, TBd("tA")], writes=[TBd("tB")])
                dcol = 63 if ed == 0 else 0
                S.dve(I("tensor_copy",
                    out=Dall[:, ed, c0:c0 + nch], in_=tA[:, 0:n].rearrange("p (c t) -> p c t", c=nch, t=64)[:, :, dcol]),
                    reads=[TBd("tA")], writes=[B("Dall")])
                wr = [B("glqk", ed, tt) for tt in tiles_of(t0, n)]
                S.dve(I("tensor_tensor", out=qt[ed][:, t0:t0 + n], in0=qf[:, 0:n], in1=tA[:, 0:n], op=ALU.mult),
                      reads=[TB0("qf"), TBd("tA")], writes=wr)
                S.dve(I("tensor_tensor", out=kt[ed][:, t0:t0 + n], in0=kf[:, 0:n], in1=tB_[:, 0:n], op=ALU.mult),
                       reads=[TB0("kf"), TBd("tB")], writes=wr)
        if _gs <= 1:
            return
        ords = [list(range(36)), [3, 2, 1, 0] + list(range(35, 3, -1))]
        poss = []
        for ed in range(2):
            pos = [0] * 36
            for j, ci in enumerate(ords[ed]):
                pos[ci] = j
            poss.append(pos)
        ensure("ffn0", s, l)
        S.barrier()
        cv2 = Carver()
        cv2.off = off_gate_tmp
        Drep = cv2(F32, 64, 36)
        Sall = cv2(F32, 64 * 36)
        Dord = cv2(F32, 64)
        assert cv2.off <= off_gate_tmp + 20480
        Wd3 = S32.rearrange("p a b -> p (a b)")[:, 0:64 * 36].rearrange("p (v j) -> p v j", v=64, j=36)
        for ed in range(2):
            pos = poss[ed]
            for tt in range(NTT):
                S.pe(I("transpose", pb[:, 128 * (tt % 4):128 * (tt % 4) + 128], kt[ed][:, 128 * tt:128 * tt + 128], identb[:]),
                     reads=[B("glqk", ed, tt), B("identb")], writes=[PBB])
                if tt % 4 == 3 or tt == NTT - 1:
                    n4 = tt % 4 + 1
                    ta = tt - n4 + 1
                    S.act(I("copy", out=ktok[:, ta:ta + n4, :], in_=pb[:, 0:128 * n4].rearrange("p (a b) -> p a b", a=n4, b=128)),
                          reads=[PBB], writes=[B("ktok")])
            if ed == 0:
                S.dve(I("tensor_copy", out=Dord[:, 0:36], in_=Dall[:, 0, 0:36]), reads=[B("Dall")], writes=[B("Dord")])
            else:
                S.dve(I("tensor_copy", out=Dord[:, 0:4], in_=Dall[:, 1, 3::-1]), reads=[B("Dall")], writes=[B("Dord")])
                S.dve(I("tensor_copy", out=Dord[:, 4:36], in_=Dall[:, 1, 35:3:-1]), reads=[B("Dall")], writes=[B("Dord")])
            S.dve(I("memset", Dord[:, 0:1], 0.0), writes=[B("Dord")])
            S.dve(I("tensor_copy", out=Drep, in_=Dord[:, 0:36].unsqueeze(1).broadcast_to([128, 64, 36])), reads=[B("Dord")], writes=[B("Drep")])
            for half in range(2):
              for grp in range(0, 18, 8):
                pw, bpw = bank("glw", [0, 1, 2, 3])
                cis = [2 * t_ + half for t_ in range(grp, min(grp + 8, 18))]
                for gi, ci in enumerate(cis):
                    tt = ci // 2
                    for h in range(4):
                        S.pe(I("matmul",
                            pw[32 * h:32 * h + 32, 64 * gi:64 * gi + 64], ktok[64 * half:64 * half + 64, tt, 32 * h:32 * h + 32],
                            glv[64 * half:64 * half + 64, tt, 64 * h:64 * h + 64], start=True, stop=True, tile_position=(64 * half, 32 * h)),
                            reads=[B("ktok"), B("glv", tt)], writes=[bpw])
                for gi, ci in enumerate(cis):
                    S.act(I("activation",
                        out=Wd3[:, :, pos[ci]], in_=pw[:, 64 * gi:64 * gi + 64], func=AF.Identity, scale=Dall[:, ed, ci:ci + 1]),
                        reads=[bpw, B("Dall")], writes=[B("S32")])
            S.dve(I("tensor_tensor_scan", out=Sall, data0=Drep.rearrange("p v j -> p (v j)"), data1=Wd3.rearrange("p v j -> p (v j)"),
                    initial=0.0, op0=ALU.mult, op1=ALU.add),
                  reads=[B("Drep"), B("S32")], writes=[B("Sall")])
            S.dve(I("memset", Sbf[ed][:, 0, :], 0.0), writes=[B("Sbf", ed)])
            S.dve(I("tensor_copy", out=Sbf[ed][:, 1:36, :], in_=Sall.rearrange("p (v j) -> p j v", v=64, j=36)[:, 0:35, :]),
                  reads=[B("Sall")], writes=[B("Sbf", ed)])
        if _gs <= 2:
            return
        tiles_out = range(NTT) if l < DEPTH - 1 else range(2, NTT)
        tiles_out = list(tiles_out)

        def g_st1(oi):
            tt = tiles_out[oi]
            pa, bpa = bank("glpa", [0, 1, 2, 3])
            A = Abf[oi % 2]
            bA = B("Abf", oi % 2)
            qm = qmb[oi % 2]
            bqm = B("qm", oi % 2)
            for ed in range(2):
                for h in range(4):
                    eng = S.dve
                    eng(I("tensor_scalar_mul", out=qm[:, ed, h, :], in0=qt[ed][:, 128 * tt:128 * tt + 128], scalar1=hmask[:, h:h + 1]),
                        reads=[B("glqk", ed, tt), B("hmask")], writes=[B("qm", oi % 2, ed, h)])
            for half in range(2):
                cols = slice(128 * tt + 64 * half, 128 * tt + 64 * half + 64)
                for ed in range(2):
                    for h in range(4):
                        S.pe(I("matmul",
                            pa[64 * half:64 * half + 64, 256 * ed + 64 * h:256 * ed + 64 * h + 64],
                            kt[ed][:, cols], qm[:, ed, h, 64 * half:64 * half + 64], start=True, stop=True,
                            tile_position=(0, 64 * half)),
                            reads=[B("glqk", ed, tt), B("qm", oi % 2, ed, h)], writes=[bpa])
            S.dve(I("tensor_tensor", out=A, in0=pa[:, 0:512].rearrange("p (a b) -> p a b", a=2, b=256), in1=trim[:], op=ALU.mult),
                  reads=[bpa, B("trim")], writes=[bA])

        def g_st2(oi):
            tt = tiles_out[oi]
            A = Abf[oi % 2]
            bA = B("Abf", oi % 2)
            qm = qmb[oi % 2]
            bqm = B("qm", oi % 2)
            po, bpo = bank("glpo", [4, 5])
            for half in range(2):
                ci = 2 * tt + half
                for h in range(4):
                    for ed in range(2):
                        S.pe(I("matmul",
                            po[64 * half:64 * half + 64, 64 * h:64 * h + 64], A[64 * half:64 * half + 64, ed, 64 * h:64 * h + 64],
                            glv[64 * half:64 * half + 64, tt, 64 * h:64 * h + 64], start=(ed == 0), stop=False,
                            tile_position=(64 * half, 64 * half)),
                            reads=[bA, B("glv", tt)], writes=[bpo])
                    for ed in range(2):
                        pj = poss[ed][ci]
                        S.pe(I("matmul",
                            po[64 * half:64 * half + 64, 64 * h:64 * h + 64], qm[:, ed, h, 64 * half:64 * half + 64],
                            Sbf[ed][:, pj, :], start=False, stop=(ed == 1),
                            tile_position=(0, 64 * half)),
                            reads=[B("qm", oi % 2, ed, h), B("Sbf", ed)], writes=[bpo])
            o2 = oi % 2
            S.act(I("activation", out=osq, in_=po[:, 0:256], func=AF.Square), reads=[bpo], writes=[TB0("osq")])
            S.dve(I("reduce_sum", out=st4[o2][:, 0:4], in_=osq.rearrange("p (h c) -> p h c", h=4, c=64), axis=AX.X),
                  reads=[TB0("osq")], writes=[B("st4", o2)])
            S.dve(I("tensor_scalar", out=st4[o2][:, 0:4], in0=st4[o2][:, 0:4], scalar1=1.0 / 64.0, scalar2=RMS_EPS, op0=ALU.mult, op1=ALU.add),
                  reads=[B("st4", o2)], writes=[B("st4", o2)])
            S.act(I("activation", out=st4[o2][:, 4:8], in_=st4[o2][:, 0:4], func=AF.Sqrt), reads=[B("st4", o2)], writes=[B("st4", o2)])
            S.dve(I("reciprocal", out=st4[o2][:, 0:4], in_=st4[o2][:, 4:8]), reads=[B("st4", o2)], writes=[B("st4", o2)])
            S.dve(I("tensor_tensor", out=on.rearrange("p (h c) -> p h c", h=4, c=64), in0=po[:, 0:256].rearrange("p (h c) -> p h c", h=4, c=64),
                                                        in1=st4[o2][:, 0:4].unsqueeze(2).broadcast_to([128, 4, 64]), op=ALU.mult),
                  reads=[bpo, B("st4", o2)], writes=[TB0("on")])
            S.pool(I("tensor_tensor", out=otok[o2], in0=on, in1=glgw[:, tt, :], op=ALU.mult),
                   reads=[TB0("on"), B("glgw", tt)], writes=[B("glotok", o2)])

        def g_st3(oi):
            tt = tiles_out[oi]
            o2 = oi % 2
            for c2 in range(2):
                S.pe(I("transpose", pb[:, 512 + 128 * c2:512 + 128 * c2 + 128], otok[o2][:, 128 * c2:128 * c2 + 128], identb[:]),
                     reads=[B("glotok", o2), B("identb")], writes=[PBB])
            S.act(I("copy", out=obuf[:, 3:5, 128 * tt:128 * tt + 128], in_=pb[:, 512:768].rearrange("p (a b) -> p a b", a=2, b=128)),
                  reads=[PBB], writes=[B("obuf", tt)])

        for oi in range(len(tiles_out) + 2):
            if oi < len(tiles_out):
                g_st1(oi)
            if 1 <= oi <= len(tiles_out):
                g_st2(oi - 1)
            if oi >= 2:
                g_st3(oi - 2)

    def layer_norm_block(tbuf, bk, n, outs):
        pmean, bpmean = bank("lnm", [4, 5])
        pmsq, bpmsq = bank("lnm", [4, 5])
        for oc in range(8):
            S.pe(I("matmul", pmean[:, 0:n], onesD[:], tbuf[:, oc, 0:n], start=(oc == 0), stop=(oc == 7)),
                 reads=[bk(oc), B("onesD")], writes=[bpmean])
        for oc in range(8):
            sqb = lnsq[oc % 2]
            S.act(I("activation", out=sqb[:, 0:n], in_=tbuf[:, oc, 0:n], func=AF.Square), reads=[bk(oc)], writes=[B("lnsq", oc % 2)])
            S.pe(I("matmul", pmsq[:, 0:n], onesD[:], sqb[:, 0:n], start=(oc == 0), stop=(oc == 7)),
                 reads=[B("lnsq", oc % 2), B("onesD")], writes=[bpmsq])
        S.act(I("copy", out=lnmean[:, 0:n], in_=pmean[:, 0:n]), reads=[bpmean], writes=[B("lnmean")])
        S.dve(I("tensor_tensor", out=lnm2[:, 0:n], in0=lnmean[:, 0:n], in1=lnmean[:, 0:n], op=ALU.mult), reads=[B("lnmean")], writes=[B("lnm2")])
        S.dve(I("tensor_tensor", out=lnm2[:, 0:n], in0=pmsq[:, 0:n], in1=lnm2[:, 0:n], op=ALU.subtract), reads=[bpmsq, B("lnm2")], writes=[B("lnm2")])
        S.act(I("activation", out=lnm2[:, 0:n], in_=lnm2[:, 0:n], func=AF.Ln, bias=epsln[:, 0:1], scale=1.0), reads=[B("lnm2"), B("eps")], writes=[B("lnm2")])
        S.act(I("activation", out=lnrstd[:, 0:n], in_=lnm2[:, 0:n], func=AF.Exp, scale=-0.5), reads=[B("lnm2")], writes=[B("lnrstd")])
        for oc in range(8):
            S.dve(I("tensor_tensor", out=tbuf[:, oc, 0:n], in0=tbuf[:, oc, 0:n], in1=lnmean[:, 0:n], op=ALU.subtract),
                  reads=[bk(oc), B("lnmean")], writes=[bk(oc)])
            S.dve(I("tensor_tensor", out=tbuf[:, oc, 0:n], in0=tbuf[:, oc, 0:n], in1=lnrstd[:, 0:n], op=ALU.mult),
                  reads=[bk(oc), B("lnrstd")], writes=[bk(oc)])
            for (eng, dstf, scf, bif, wrf) in outs:
                if eng == "act":
                    S.act(I("activation", out=dstf(oc), in_=tbuf[:, oc, 0:n], func=AF.Identity, scale=scf(oc), bias=bif(oc)),
                          reads=[bk(oc)] + MD, writes=wrf(oc))
                else:
                    S.dve(I("tensor_scalar", out=dstf(oc), in0=tbuf[:, oc, 0:n], scalar1=scf(oc), scalar2=bif(oc),
                            op0=ALU.mult, op1=ALU.add),
                          reads=[bk(oc)] + MD, writes=wrf(oc))

    self_cv = [None]
    lnsq = [None, None]
    lnmean = lnm2 = lnrstd = None

    def outproj_phase(s, l):
        S.phase = "outproj_phase" + str(l)
        nonlocal lnsq, lnmean, lnm2, lnrstd
        S.barrier()
        cv = Carver()
        self_cv[0] = cv
        tbs = [cv(F32, 8, 512) for _ in range(2)]
        xss = [cv(F32, 8, 512) for _ in range(2)]
        lnsq = [cv(F32, 512) for _ in range(2)]
        lnmean = cv(F32, 512)
        lnm2 = cv(F32, 512)
        lnrstd = cv(F32, 512)
        wv = wview(1, BF16, 8, 1024)
        wb = [B("wbuf", 1)]
        ensure("out", s, l)
        bi = 0
        for (t0, n) in BLOCKS:
            if t0 < LC and l == DEPTH - 1:
                continue
            tb = tbs[bi % 2]
            xs = xss[bi % 2]
            par = bi % 2
            btb = lambda oc, par=par: B("tb", par, oc)
            bxs = lambda oc, par=par: B("xs", par, oc)
            bxs_all = [bxs(oc) for oc in range(8)]
            bi += 1
            mi = msel(s, t0)
            xrb = [B("xres", s, tt) for tt in tiles_of(t0, n)]
            S.dma(I("dma_start", out=xs[:, :, 0:n], in_=xres[s][:, :, t0:t0 + n]), reads=xrb, writes=bxs_all)
            for oc in range(8):
                py, bpy = bank("proj", [0, 1, 2, 3])
                for k in range(8):
                    S.pe(I("matmul", py[:, 0:n], wv[:, k, 128 * oc:128 * oc + 128], obuf[:, k, t0:t0 + n],
                                                                       start=(k == 0), stop=(k == 7)),
                         reads=wb + [B("obuf", tt) for tt in tiles_of(t0, n)], writes=[bpy])
                S.dve(I("scalar_tensor_tensor", out=tb[:, oc, 0:n], in0=py[:, 0:n], scalar=mder[:, l, 1, oc, mi:mi + 1], in1=xs[:, oc, 0:n],
                                                                             op0=ALU.mult, op1=ALU.add),
                      reads=[bpy, bxs(oc)] + MD, writes=[btb(oc)])
            outs = [
                ("act", lambda oc, n=n, xs=xs: xs[:, oc, 0:n], lambda oc: lnT[:, l, 0, oc:oc + 1], lambda oc: lnT[:, l, 1, oc:oc + 1], lambda oc, bxs=bxs: [bxs(oc)]),
                ("dve", lambda oc, t0=t0, n=n: ubuf[:, oc, t0:t0 + n], lambda oc, mi=mi: mder[:, l, 4, oc, mi:mi + 1], lambda oc, mi=mi: mder[:, l, 5, oc, mi:mi + 1],
                 lambda oc, t0=t0, n=n: ub(t0, n)),
            ]
            layer_norm_block(tb, btb, n, outs)
            S.dma(I("dma_start", out=xres[s][:, :, t0:t0 + n], in_=xs[:, :, 0:n]), reads=bxs_all, writes=xrb)

    def ffn_phase(s, l):
        S.phase = "ffn_phase" + str(l)
        nonlocal lnsq, lnmean, lnm2, lnrstd
        S.barrier()
        last = (l == DEPTH - 1)
        cv = Carver()
        self_cv[0] = cv
        PTOK = 1024 if last else 1152
        hT = cv(BF16, NHC, PTOK)
        w2 = [cv(BF16, NHC, 256) for _ in range(2)]
        lnsq = [cv(F32, 512) for _ in range(2)]
        lnmean = cv(F32, 512)
        lnm2 = cv(F32, 512)
        lnrstd = cv(F32, 512)
        sa = [cv(F32, 512) for _ in range(2)]
        ostage = [cv(F32, 1024) for _ in range(2)] if last else None
        tfull = obuf[:].rearrange("p k t -> p (k t)").bitcast(F32)[:, 0:8 * PTOK].rearrange("p (k t) -> p k t", k=8, t=PTOK)
        if last:
            parts = [[BLOCKS[1], BLOCKS[2]], [BLOCKS[3], BLOCKS[4]]]
        else:
            parts = [[(0, 256), (256, 512), (768, 384)], [(1152, 512), (1664, 512), (2176, 128)]]
        wf1 = wf1_d[l].rearrange("(k p) c -> p k c", p=128)
        wf2 = wf2_d[l].rearrange("(j p) c -> p j c", p=128)
        for part in parts:
            pt0 = part[0][0]
            ptn = sum(n for (_, n) in part)
            mi = msel(s, pt0)
            nblk = len(part)
            btf = lambda bi_, oc: B("tfull", bi_, oc)
            btf_all = [btf(bi_, oc) for bi_ in range(nblk) for oc in range(8)]
            for js in range(11):
                wi = js % 2
                wv = wview(wi, BF16, 8, 512)
                wb = [B("wbuf", wi)]
                if js == 0 and part is parts[0]:
                    ensure("ffn0", s, l)
                else:
                    S.dma(I("dma_start", out=wv[:, :, 0:256], in_=wf1[:, :, 256 * js:256 * js + 256]), writes=wb, q="pool")
                    S.dma(I("dma_start", out=wv[:, :, 256:512], in_=wf1[:, :, FH + 256 * js:FH + 256 * js + 256]), writes=wb, q="pool")
                for (t0, n) in part:
                    lo = t0 - pt0
                    for jj in range(2):
                        j = 2 * js + jj
                        pa, bpa = bank("ffa", [0, 1, 2])
                        pg, bpg = bank("ffb", [3, 4, 5])
                        for k in range(8):
                            S.pe(I("matmul", pa[:, 0:n], wv[:, k, 128 * jj:128 * jj + 128], ubuf[:, k, t0:t0 + n],
                                                                                      start=(k == 0), stop=(k == 7)),
                                 reads=wb + ub(t0, n), writes=[bpa])
                        for k in range(8):
                            S.pe(I("matmul", pg[:, 0:n], wv[:, k, 256 + 128 * jj:256 + 128 * jj + 128], ubuf[:, k, t0:t0 + n],
                                                                                      start=(k == 0), stop=(k == 7)),
                                 reads=wb + ub(t0, n), writes=[bpg])
                        si = j % 2
                        S.act(I("activation", out=sa[si][:, 0:n], in_=pa[:, 0:n], func=AF.Silu), reads=[bpa], writes=[B("sa", si)])
                        S.dve(I("tensor_tensor", out=hT[:, j, lo:lo + n], in0=pg[:, 0:n], in1=sa[si][:, 0:n], op=ALU.mult),
                              reads=[bpg, B("sa", si)], writes=[B("hT")])
            if part is parts[-1]:
                ensure("na", *next_sl(s, l))
            S.dma(I("dma_start", out=tfull[:, :, 0:ptn], in_=xres[s][:, :, pt0:pt0 + ptn]),
                  reads=[B("xres", s, tt) for tt in tiles_of(pt0, ptn)], writes=btf_all)
            for oc2 in range(4):
                w2v = w2[oc2 % 2]
                bw2 = [B("w2", oc2 % 2)]
                S.dma(I("dma_start", out=w2v, in_=wf2[:, :, 256 * oc2:256 * oc2 + 256]), writes=bw2, q="pool")
                for bi_, (t0, n) in enumerate(part):
                    lo = t0 - pt0
                    mi = msel(s, t0)
                    for oo in range(2):
                        oc = 2 * oc2 + oo
                        py, bpy = bank("ffy", [0, 1, 2, 3])
                        for j in range(NHC):
                            S.pe(I("matmul", py[:, 0:n], w2v[:, j, 128 * oo:128 * oo + 128], hT[:, j, lo:lo + n],
                                                                                        start=(j == 0), stop=(j == NHC - 1)),
                                 reads=bw2 + [B("hT")], writes=[bpy])
                        S.dve(I("scalar_tensor_tensor", out=tfull[:, oc, lo:lo + n], in0=py[:, 0:n], scalar=mder[:, l, 3, oc, mi:mi + 1],
                                                                                            in1=tfull[:, oc, lo:lo + n], op0=ALU.mult, op1=ALU.add),
                              reads=[bpy, btf(bi_, oc)] + MD, writes=[btf(bi_, oc)])
            for bi_, (t0, n) in enumerate(part):
                lo = t0 - pt0
                mi = msel(s, t0)
                tbv = tfull[:, :, lo:lo + n]
                bkf = lambda oc, bi_=bi_: btf(bi_, oc)
                bkf_all = [bkf(oc) for oc in range(8)]
                xrb = [B("xres", s, tt) for tt in tiles_of(t0, n)]
                if not last:
                    outs = [
                        ("dve", lambda oc, t0=t0, n=n: ubuf[:, oc, t0:t0 + n], lambda oc, mi=mi: mder2[:, l, 0, oc, mi:mi + 1], lambda oc, mi=mi: mder2[:, l, 1, oc, mi:mi + 1],
                         lambda oc, t0=t0, n=n: ub(t0, n)),
                        ("act", lambda oc, tbv=tbv: tbv[:, oc, :], lambda oc: lnT[:, l, 2, oc:oc + 1], lambda oc: lnT[:, l, 3, oc:oc + 1], lambda oc, bkf=bkf: [bkf(oc)]),
                    ]
                    layer_norm_block(tbv, bkf, n, outs)
                    S.dma(I("dma_start", out=xres[s][:, :, t0:t0 + n], in_=tbv), reads=bkf_all, writes=xrb)
                else:
                    outs = [("act", lambda oc, tbv=tbv: tbv[:, oc, :], lambda oc: lnT[:, l, 2, oc:oc + 1], lambda oc: lnT[:, l, 3, oc:oc + 1], lambda oc, bkf=bkf: [bkf(oc)])]
                    layer_norm_block(tbv, bkf, n, outs)
                    for qs in range(n // 128):
                        tq = (t0 + 128 * qs) // 128
                        og = ostage[tq % 2]
                        for half in range(2):
                            pt, bpt = bank("ffo", [0, 1, 2, 3])
                            for kk in range(4):
                                k = 4 * half + kk
                                S.pe(I("transpose", pt[:, 128 * kk:128 * kk + 128], tbv[:, k, 128 * qs:128 * qs + 128], identf[:]),
                                     reads=[bkf(k), B("identf")], writes=[bpt])
                            S.act(I("copy", out=og[:, 512 * half:512 * half + 512], in_=pt[:, 0:512]),
                                  reads=[bpt], writes=[B("ostage", tq % 2)])
                        S.dma(I("dma_start", out=out_d[s, 128 * (tq - 2):128 * (tq - 2) + 128, :], in_=og),
                              reads=[B("ostage", tq % 2)], writes=[B("out")])

    for s in range(nseq):
        load_x(s)
        if stage == 1 and s == 0:
            dump("u0", ubuf[:], [128, 8, T], BF16, B("ubuf", NTT - 1))
        for l in range(DEPTH):
            if stage >= 2:
                na_phase(s, l)
            if stage >= 3:
                gla_phase(s, l)
            if stage >= 4:
                gqa_phase(s, l)
            if stage in (2, 3, 4) and s == 0 and l == 0:
                S.barrier()
                S.pool(I("memset", epsln[:], LN_EPS_S), reads=[B("obuf", tt) for tt in range(NTT)], writes=[B("eps")])
                dump("o0", obuf[:], [128, 8, T], BF16, B("eps"))
                break
            if stage >= 5:
                outproj_phase(s, l)
            if stage == 5 and s == 0 and l == 0:
                S.barrier()
                S.pool(I("memset", epsln[:], LN_EPS_S), reads=[B("ubuf", tt) for tt in range(NTT)], writes=[B("eps")])
                dump("u2", ubuf[:], [128, 8, T], BF16, B("eps"))
                dump("x1", xres[0], [128, 8, T], F32, B("eps"))
                break
            if stage >= 6:
                ffn_phase(s, l)
            if stage == 6 and s == 0 and l == 0:
                S.barrier()
                S.pool(I("memset", epsln[:], LN_EPS_S), reads=[B("ubuf", tt) for tt in range(NTT)], writes=[B("eps")])
                dump("u1n", ubuf[:], [128, 8, T], BF16, B("eps"))
                dump("x2", xres[0], [128, 8, T], F32, B("eps"))
                break


    S.emit(final_bufs=dump_list + [B("out")])
    S.close()
    global LAST_SCHED
    LAST_SCHED = S
    return nc


def host_consts():
    eye = np.eye(128, dtype=np.float32)
    onesd = np.full((128, 128), 1.0 / 1024.0, np.float32)
    bon = np.zeros((128, 128), np.float32)
    bon[:64, :64] = 1.0 / 64
    bon[64:, 64:] = 1.0 / 64
    R = np.zeros((64, 64), np.float32)
    for d in range(16):
        R[d, d + 16] = -1.0
        R[d + 16, d] = 1.0
        R[d + 32, d + 48] = -1.0
        R[d + 48, d + 32] = 1.0
    rl = np.zeros((128, 128), np.float32)
    rl[:64, :64] = R.T
    rl[64:, 64:] = R.T
    hm = np.zeros_like(eye)
    for h in range(4):
        hm[32 * h:32 * h + 32, h] = 1.0
    cm = np.stack([eye, onesd, bon, rl, hm], axis=1)
    sidx = np.arange(64)
    mf = (sidx[None, :] >= sidx[:, None]).astype(np.float32)
    mb = (sidx[None, :] <= sidx[:, None]).astype(np.float32)
    tri = np.zeros((128, 2, 256), np.float32)
    for half in range(2):
        tri[64 * half:64 * half + 64, 0, :] = np.tile(mf, (1, 4))
        tri[64 * half:64 * half + 64, 1, :] = np.tile(mb, (1, 4))
    t = np.arange(NL)
    row = (t // 64).astype(np.float32)
    col = (t % 64).astype(np.float32)
    inv_freq = (10000.0 ** (-np.arange(16, dtype=np.float32) / 16)).astype(np.float32)
    ang_r = row[:, None] * inv_freq
    ang_c = col[:, None] * inv_freq
    ang = np.concatenate([ang_r, ang_r, ang_c, ang_c], axis=-1).astype(np.float32)
    cosT = np.tile(np.cos(ang).T, (2, 1)).astype(np.float32)
    sinT = np.tile(np.sin(ang).T, (2, 1)).astype(np.float32)
    return cm, tri, np.ascontiguousarray(cosT), np.ascontiguousarray(sinT)


def prep_shared(inp):
    f = lambda a: np.ascontiguousarray(np.asarray(a, dtype=np.float32))
    cm, tri, cosT, sinT = host_consts()
    sh = {}
    sh["w_ada"] = f(inp["w_ada"])
    sh["b_adaT"] = f(np.asarray(inp["b_ada"]).reshape(DEPTH, 48, 128).transpose(0, 2, 1))
    sh["w_in"] = f(inp["w_in"])
    sh["w_out"] = f(inp["w_out"])
    sh["w_ffn_in"] = f(inp["w_ffn_in"])
    sh["w_ffn_out"] = f(inp["w_ffn_out"])
    ln = np.stack([np.asarray(inp[k]) for k in ("ln1_g", "ln1_b", "ln2_g", "ln2_b")], axis=1)
    sh["lnT"] = f(ln.reshape(DEPTH, 4, 8, 128).transpose(3, 0, 1, 2))
    valid, dri, dci = na_index_tables()
    rpb = np.asarray(inp["na_rpb"], dtype=np.float32)
    g = rpb[:, :, dri, dci]
    g = np.where(valid[None, None], g, np.float32(-30000.0))
    sh["nab"] = f(g.transpose(0, 3, 1, 2, 4).reshape(DEPTH, 128, 6, NCOMBO * 128))
    wa2 = np.zeros((DEPTH, 2, 32, 128), np.float32)
    gw = np.asarray(inp["gla_wa2"], dtype=np.float32)
    wa2[:, 0, 0:16, :] = gw[:, 0]
    wa2[:, 1, 16:32, :] = gw[:, 1]
    sh["wa2"] = wa2
    sh["nbaT"] = f(np.asarray(inp["gla_ba"]).transpose(2, 0, 1))
    sh["glnw"] = f(np.tile(np.asarray(inp["gla_norm_w"]), (1, 4)).reshape(DEPTH, 1, 256))
    qk = np.stack([np.tile(np.asarray(inp["gqa_qnorm_w"]), (1, 2)), np.tile(np.asarray(inp["gqa_knorm_w"]), (1, 2))], axis=2)
    sh["qkwT"] = f(qk.transpose(1, 0, 2))
    sh["cosT"] = cosT
    sh["sinT"] = sinT
    sh["cmats"] = f(cm)
    sh["trim"] = f(tri)
    return sh


def prep_core(inp, core, sh):
    f = lambda a: np.ascontiguousarray(np.asarray(a, dtype=np.float32))
    b0 = core * NSEQ
    m = dict(sh)
    m["x"] = f(inp["x"][b0:b0 + NSEQ])
    m["ctx"] = f(inp["ctx"][b0:b0 + NSEQ])
    cc = np.stack([np.asarray(inp["c"][b0]), np.asarray(inp["c"][b0 + 1]), np.asarray(inp["c_ctx"])], axis=1)
    m["cT"] = f(cc.reshape(8, 128, 3).transpose(1, 0, 2))
    return m


def kernel(**inputs):
    sh = prep_shared(inputs)
    nc = build_nc()
    in_maps = [prep_core(inputs, c, sh) for c in range(8)]
    res = run_bass_kernel_spmd(nc, in_maps, core_ids=list(range(8)))
    out = np.concatenate([np.asarray(r["out"]) for r in res.results], axis=0)
    return out.astype(np.float32)
```

```python
import contextlib
import numpy as np
import concourse.bass as bass
import concourse.mybir as mybir
from concourse.bass_utils import run_bass_kernel_spmd

F32 = mybir.dt.float32
BF16 = mybir.dt.bfloat16
AF = mybir.ActivationFunctionType
ALU = mybir.AluOpType
AX = mybir.AxisListType

COMPUTE = ("pe", "act", "dve", "pool")
NDMA_SEM = 6


class Buf:
    __slots__ = ("last_w", "readers", "excl")

    def __init__(self, excl=False):
        self.last_w = None
        self.readers = []
        self.excl = excl


class Sched:
    def __init__(self, nc):
        self.nc = nc
        self.ops = {e: [] for e in ("pe", "act", "dve", "pool", "sp")}
        self.stack = contextlib.ExitStack()
        self.nalloc = 0
        self.bufs = {}
        self.pending = {e: set() for e in self.ops}
        self.phase = "pro"

    def B(self, *key):
        b = self.bufs.get(key)
        if b is None:
            b = self.bufs[key] = Buf(excl=(key[0] in ("pf", "pb")))
        return b

    def sbuf(self, shape, dtype, name=None):
        self.nalloc += 1
        return self.stack.enter_context(self.nc.sbuf_tensor("sb_" + (name or f"t{self.nalloc}"), list(shape), dtype))

    def psum(self, shape, dtype, name=None):
        self.nalloc += 1
        return self.stack.enter_context(self.nc.psum_tensor("ps_" + (name or f"t{self.nalloc}"), list(shape), dtype))

    def add(self, eng, fn, reads=(), writes=(), dma=False):
        idx = len(self.ops[eng])
        deps = set(self.pending[eng])
        self.pending[eng] = set()
        for b in reads:
            if b.last_w is not None:
                deps.add(b.last_w)
            if b.excl:
                deps.update(r for r in b.readers if r[0] != eng)
        for b in writes:
            if b.last_w is not None:
                deps.add(b.last_w)
            deps.update(b.readers)
        if eng == "pe":
            deps = {d for d in deps if d[0] != "pe"}
        self.ops[eng].append({"fn": fn, "deps": deps, "dma": dma, "marked": False, "ph": self.phase})
        me = (eng, idx)
        for b in reads:
            b.readers.append(me)
        for b in writes:
            b.last_w = me
            b.readers = []
        return me

    def pe(self, fn, reads=(), writes=()):
        return self.add("pe", fn, reads, writes)

    def act(self, fn, reads=(), writes=()):
        return self.add("act", fn, reads, writes)

    def dve(self, fn, reads=(), writes=()):
        return self.add("dve", fn, reads, writes)

    def pool(self, fn, reads=(), writes=()):
        return self.add("pool", fn, reads, writes)

    def dma(self, fn, reads=(), writes=(), q="sp"):
        return self.add(q, fn, reads, writes, dma=True)

    def barrier(self):
        deps = set()
        for e in ("pe", "act", "dve"):
            if self.ops[e]:
                deps.add((e, len(self.ops[e]) - 1))
        for q in ("sp", "pool"):
            nd = 0
            gotc = False
            for i in range(len(self.ops[q]) - 1, -1, -1):
                if self.ops[q][i]["dma"]:
                    if nd < NDMA_SEM:
                        deps.add((q, i))
                        nd += 1
                elif not gotc:
                    deps.add((q, i))
                    gotc = True
                if nd >= NDMA_SEM and (gotc or q == "sp"):
                    break
        for e in self.pending:
            self.pending[e] |= deps

    def emit(self, final_bufs=()):
        nc = self.nc
        ops = self.ops
        dma_info = {}
        for q in ("sp", "pool"):
            n = 0
            for i, op in enumerate(ops[q]):
                if op["dma"]:
                    slot = n % NDMA_SEM
                    val = 16 * (n // NDMA_SEM + 1)
                    dma_info[(q, i)] = (q, slot, val)
                    op["dmainfo"] = (slot, val)
                    n += 1
        for e in ops:
            for op in ops[e]:
                op["deps"] = {d for d in op["deps"]}
                for d in op["deps"]:
                    if d not in dma_info:
                        ops[d[0]][d[1]]["marked"] = True
        final_deps = set()
        for b in final_bufs:
            if b.last_w is not None:
                final_deps.add(b.last_w)
                if b.last_w not in dma_info:
                    ops[b.last_w[0]][b.last_w[1]]["marked"] = True
        for e in COMPUTE:
            c = 0
            for op in ops[e]:
                if op["marked"] and not op["dma"]:
                    c += 1
                    op["semval"] = c
        st = self.stack
        csem = {e: st.enter_context(nc.semaphore(f"s_{e}")) for e in COMPUTE}
        dsem = {q: [st.enter_context(nc.semaphore(f"d_{q}{j}")) for j in range(NDMA_SEM)] for q in ("sp", "pool")}
        block = st.enter_context(nc.Block())
        engobj = {"pe": block.tensor, "act": block.scalar, "dve": block.vector, "pool": block.gpsimd, "sp": block.sync}

        def resolve(dep):
            if dep in dma_info:
                q, slot, val = dma_info[dep]
                return dsem[q][slot], ("d", q, slot), val
            f, k = dep
            return csem[f], ("c", f), ops[f][k]["semval"]

        def make_body(e):
            def body(eng):
                known = {}
                for op in ops[e]:
                    need = {}
                    for dep in op["deps"]:
                        if dep[0] == e and dep not in dma_info and e == "pe":
                            continue
                        sem, key, val = resolve(dep)
                        if need.get(key, (None, 0))[1] < val:
                            need[key] = (sem, val)
                    if op["dma"]:
                        slot, val = op["dmainfo"]
                        if val > 16:
                            key = ("d", e, slot)
                            if need.get(key, (None, 0))[1] < val - 16:
                                need[key] = (dsem[e][slot], val - 16)
                    for key, (sem, val) in need.items():
                        if known.get(key, 0) >= val:
                            continue
                        known[key] = val
                        eng.wait_ge(sem, val)
                    ins = op["fn"](eng)
                    if op["dma"]:
                        ins.then_inc(dsem[e][op["dmainfo"][0]], 16)
                    elif op["marked"]:
                        ins.then_inc(csem[e], 1)
                if e == "sp":
                    need = {}
                    for dep in final_deps:
                        sem, key, val = resolve(dep)
                        if need.get(key, (None, 0))[1] < val:
                            need[key] = (sem, val)
                    for key, (sem, val) in need.items():
                        eng.wait_ge(sem, val)
            return body

        for e in ("sp", "pool", "act", "dve", "pe"):
            engobj[e](make_body(e))

    def close(self):
        self.stack.close()


def I(name, *args, **kw):
    f = lambda eng: getattr(eng, name)(*args, **kw)
    try:
        f.f32 = name in ("matmul", "transpose") and args[1].dtype == F32
    except Exception:
        f.f32 = False
    return f


D = 1024
LC = 256
NL = 2048
T = LC + NL
NTT = T // 128
FH = 2816
NHC = FH // 128
DEPTH = 2
NSEQ = 2
BLOCKS = [(0, 256)] + [(256 + 512 * i, 512) for i in range(4)]
ALPHA2 = 2.0
LN_EPS_S = 1e-5 / ALPHA2
RMS_EPS = 1e-6
NCOMBO = 21
WBYTES = 91136

C_NAQ, C_NAK, C_NAV = 0, 384, 768
C_GLQ, C_GLK, C_GLV, C_GLG, C_GLR = 1152, 1280, 1408, 1664, 1920
C_GAQ, C_GAK, C_GAV = 1952, 2336, 2464


def na_combos():
    combos = []
    idx = {}

    def get(kind, r, d):
        key = (kind, d)
        if key not in idx:
            idx[key] = len(combos)
            combos.append((r, d))
        return idx[key]

    slots = []
    for p in range(16):
        r = 2 * p
        if p == 0:
            s = [(j, get("e0", r, 2 * j - r)) for j in range(0, 4)]
        elif p == 1:
            s = [(j, get("e1", r, 2 * j - r)) for j in range(0, 4)]
        elif p == 14:
            s = [(j, get("e2", r, 2 * j - r)) for j in range(12, 16)]
        elif p == 15:
            s = [(j, get("e3", r, 2 * j - r)) for j in range(12, 16)]
        else:
            s = [(j, get("g", 8, 2 * j - r)) for j in range(p - 2, p + 3)]
        slots.append(s)
    assert len(combos) == NCOMBO
    return combos, slots


def na_index_tables():
    combos, _ = na_combos()
    cols = np.arange(64)
    col_start = np.clip(cols - 8, 0, 48)
    col_in = (cols[None, :] >= col_start[:, None]) & (cols[None, :] < col_start[:, None] + 16)
    valid = np.zeros((NCOMBO, 128, 128), bool)
    dri = np.zeros((NCOMBO, 128, 128), np.int64)
    dci = np.zeros((NCOMBO, 128, 128), np.int64)
    for ci, (r, d) in enumerate(combos):
        for ko in range(2):
            for qo in range(2):
                kr = r + d + ko
                qr = r + qo
                rs = min(max(qr - 4, 0), 24)
                ok_row = (rs <= kr < rs + 8)
                dr = kr - qr
                blk_valid = col_in.T & ok_row
                dc = np.clip(cols[:, None] - cols[None, :], -15, 15) + 15
                valid[ci, 64 * ko:64 * ko + 64, 64 * qo:64 * qo + 64] = blk_valid
                dri[ci, 64 * ko:64 * ko + 64, 64 * qo:64 * qo + 64] = min(max(dr + 7, 0), 14)
                dci[ci, 64 * ko:64 * ko + 64, 64 * qo:64 * qo + 64] = dc
    return valid, dri, dci


def build_nc(stage=99, nseq=NSEQ, dumps=None):
    nc = bass.Bass("TRN2", target_bir_lowering=False)
    S = Sched(nc)
    B = S.B
    dumps = dumps if dumps is not None else {}

    def din(name, shape, dt=F32):
        return nc.dram_tensor(name, list(shape), dt, kind="ExternalInput").ap()

    x_d = din("x", [NSEQ, NL, D])
    ctx_d = din("ctx", [NSEQ, LC, D])
    cT_d = din("cT", [128, 8, 3])
    wada_d = din("w_ada", [DEPTH, D, 6 * D])
    bada_d = din("b_adaT", [DEPTH, 128, 48])
    win_d = din("w_in", [DEPTH, D, 2592])
    wout_d = din("w_out", [DEPTH, D, D])
    wf1_d = din("w_ffn_in", [DEPTH, D, 2 * FH])
    wf2_d = din("w_ffn_out", [DEPTH, FH, D])
    lnT_d = din("lnT", [128, DEPTH, 4, 8])
    nab_d = din("nab", [DEPTH, 128, 6, NCOMBO * 128])
    wa2_d = din("wa2", [DEPTH, 2, 32, 128])
    nba_d = din("nbaT", [128, DEPTH, 2])
    glnw_d = din("glnw", [DEPTH, 1, 256])
    qkw_d = din("qkwT", [128, DEPTH, 2])
    cos_d = din("cosT", [128, NL])
    sin_d = din("sinT", [128, NL])
    cm_d = din("cmats", [128, 5, 128])
    tri_d = din("trim", [128, 2, 256])
    out_d = nc.dram_tensor("out", [NSEQ, NL, D], F32, kind="ExternalOutput").ap()
    xres = [nc.dram_tensor(f"xres{s}", [128, 8, T], F32).ap() for s in range(NSEQ)]
    dump_list = []

    ubuf = S.sbuf([128, 8, T], BF16, "ubuf")
    obuf = S.sbuf([128, 8, T], BF16, "obuf")
    wbuf = [S.sbuf([128, 8 * 1152], BF16, f"wbuf{i}") for i in range(2)]

    def wview(i, dt, k, c):
        if dt == F32:
            return wbuf[i][:].bitcast(F32)[:, 0:k * c].rearrange("p (k c) -> p k c", k=k, c=c)
        return wbuf[i][:, 0:k * c].rearrange("p (k c) -> p k c", k=k, c=c)

    Wr = S.sbuf([128, WBYTES // 4], F32, "Wr")
    identf = S.sbuf([128, 128], F32, "identf")
    onesD = S.sbuf([128, 128], F32, "onesD")
    identb = S.sbuf([128, 128], BF16, "identb")
    onesDb = S.sbuf([128, 128], BF16, "onesDb")
    bones = S.sbuf([128, 128], BF16, "bones")
    rotm = S.sbuf([128, 128], BF16, "rotm")
    trim = S.sbuf([128, 2, 256], BF16, "trim")
    scT = S.sbuf([128, 8, 3], F32, "scT")
    modT = S.sbuf([128, DEPTH, 48, 3], F32, "modT")
    mder = S.sbuf([128, DEPTH, 6, 8, 3], F32, "mder")
    mder2 = S.sbuf([128, DEPTH, 2, 8, 3], F32, "mder2")
    lnT = S.sbuf([128, DEPTH, 4, 8], F32, "lnT")
    badaT = S.sbuf([128, DEPTH, 48], F32, "badaT")
    nbaT = S.sbuf([128, DEPTH, 2], F32, "nbaT")
    qkwT = S.sbuf([128, DEPTH, 2], F32, "qkwT")
    glnw = S.sbuf([128, DEPTH, 256], F32, "glnw")
    wa2 = S.sbuf([32, DEPTH, 2, 128], F32, "wa2")
    hmask = S.sbuf([128, 4], F32, "hmask")
    epsln = S.sbuf([128, 1], F32, "epsln")
    epsrms = S.sbuf([128, 1], F32, "epsrms")

    pf = [S.psum([128, 512], F32, f"pf{i}") for i in range(7)]
    pb = S.psum([128, 1024], BF16, "pb")
    pctr = [0]

    def nextp():
        i = pctr[0] % 7
        pctr[0] += 1
        return pf[i], B("pf", i)

    class Carver:
        def __init__(self):
            self.off = 0

        def __call__(self, dt, *dims):
            n = int(np.prod(dims))
            nbytes = n * (4 if dt == F32 else 2)
            nbytes = (nbytes + 63) // 64 * 64
            a = self.off // 4
            self.off += nbytes
            assert self.off <= WBYTES, ("scratch overflow", self.off)
            ap = Wr[:, a:a + nbytes // 4]
            if dt == BF16:
                ap = ap.bitcast(BF16)[:, 0:n]
            else:
                ap = ap[:, 0:n]
            if len(dims) == 2:
                ap = ap.rearrange("p (a b) -> p a b", a=dims[0], b=dims[1])
            elif len(dims) == 3:
                ap = ap.rearrange("p (a b c) -> p a b c", a=dims[0], b=dims[1], c=dims[2])
            return ap

    def dump(name, ap, shape, dt, rbuf):
        d = nc.dram_tensor("dbg_" + name, list(shape), dt, kind="ExternalOutput").ap()
        S.dma(I("dma_start", out=d, in_=ap), reads=[rbuf], writes=[B("dump", name)])
        dump_list.append(B("dump", name))
        dumps[name] = (shape, dt)

    def ld(dst, src, bname, q="sp"):
        S.dma(I("dma_start", out=dst, in_=src), writes=[B(bname)], q=q)

    ld(identf[:], cm_d[:, 0, :], "identf")
    ld(onesD[:], cm_d[:, 1, :], "onesD")
    ld(identb[:], cm_d[:, 0, :], "identb", q="pool")
    ld(onesDb[:], cm_d[:, 1, :], "onesDb", q="pool")
    ld(bones[:], cm_d[:, 2, :], "bones", q="pool")
    ld(rotm[:], cm_d[:, 3, :], "rotm", q="pool")
    ld(trim[:], tri_d, "trim", q="pool")
    ld(hmask[:], cm_d[:, 4, 0:4], "hmask")
    ld(scT[:], cT_d, "scT")
    ld(lnT[:], lnT_d, "lnT")
    ld(badaT[:], bada_d.rearrange("l p j -> p l j"), "badaT")
    ld(nbaT[:], nba_d, "nbaT")
    ld(qkwT[:], qkw_d, "qkwT")
    for l in range(DEPTH):
        ld(glnw[:, l, :], glnw_d[l].broadcast_to([128, 256]), "glnw")
        ld(wa2[:, l, :, :], wa2_d[l].rearrange("e r c -> r e c"), "wa2")
    S.dve(I("tensor_scalar_mul", out=nbaT[:], in0=nbaT[:], scalar1=-1.0), reads=[B("nbaT")], writes=[B("nbaT")])
    S.pool(I("memset", epsln[:], LN_EPS_S), writes=[B("eps")])
    S.pool(I("memset", epsrms[:], RMS_EPS), writes=[B("eps")])
    S.act(I("activation", out=scT[:], in_=scT[:], func=AF.Silu), reads=[B("scT")], writes=[B("scT")])

    for l in range(DEPTH):
        pm, bpm = nextp()
        for js in range(12):
            wv = wview(js % 2, F32, 8, 512)
            S.dma(I("dma_start",
                out=wv, in_=wada_d[l].rearrange("(k p) c -> p k c", p=128)[:, :, 512 * js:512 * js + 512]),
                writes=[B("wbuf", js % 2)])
            for jj in range(4):
                j = 4 * js + jj
                for k in range(8):
                    S.pe(I("matmul",
                        pm[:, 3 * j:3 * j + 3], wv[:, k, 128 * jj:128 * jj + 128], scT[:, k, :],
                        start=(k == 0), stop=(k == 7)),
                        reads=[B("wbuf", js % 2), B("scT")], writes=[bpm])
        S.dve(I("tensor_tensor",
            out=modT[:, l, :, :], in0=pm[:, 0:144].rearrange("p (j b) -> p j b", j=48, b=3),
            in1=badaT[:, l, :].unsqueeze(2).broadcast_to([128, 48, 3]), op=ALU.add),
            reads=[bpm, B("badaT")], writes=[B("modT")])
    inv_a = float(1.0 / np.sqrt(ALPHA2))
    for l in range(DEPTH):
        mT = modT[:, l, :, :]
        rd = [B("modT"), B("lnT")]
        wr = [B("mder")]
        S.dve(I("tensor_scalar_add", out=mder[:, l, 0, :, :], in0=mT[:, 8:16, :], scalar1=1.0), reads=rd, writes=wr)
        S.dve(I("tensor_scalar_mul", out=mder[:, l, 1, :, :], in0=mT[:, 16:24, :], scalar1=inv_a), reads=rd, writes=wr)
        S.dve(I("tensor_scalar_add", out=mder[:, l, 2, :, :], in0=mT[:, 32:40, :], scalar1=1.0), reads=rd, writes=wr)
        S.dve(I("tensor_scalar_mul", out=mder[:, l, 3, :, :], in0=mT[:, 40:48, :], scalar1=inv_a), reads=rd, writes=wr)
        S.dve(I("tensor_tensor", out=mder[:, l, 4, :, :], in0=mder[:, l, 2, :, :],
                                             in1=lnT[:, l, 0, :].unsqueeze(2).broadcast_to([128, 8, 3]), op=ALU.mult), reads=rd + wr, writes=wr)
        S.dve(I("tensor_tensor", out=mder[:, l, 5, :, :], in0=mder[:, l, 2, :, :],
                                             in1=lnT[:, l, 1, :].unsqueeze(2).broadcast_to([128, 8, 3]), op=ALU.mult), reads=rd + wr, writes=wr)
        S.dve(I("tensor_tensor", out=mder[:, l, 5, :, :], in0=mder[:, l, 5, :, :], in1=mT[:, 24:32, :], op=ALU.add), reads=rd + wr, writes=wr)
    for l in range(DEPTH - 1):
        rd = [B("modT"), B("lnT"), B("mder")]
        wr = [B("mder2")]
        S.dve(I("tensor_tensor", out=mder2[:, l, 0, :, :], in0=mder[:, l + 1, 0, :, :],
                                             in1=lnT[:, l, 2, :].unsqueeze(2).broadcast_to([128, 8, 3]), op=ALU.mult), reads=rd, writes=wr)
        S.dve(I("tensor_tensor", out=mder2[:, l, 1, :, :], in0=mder[:, l + 1, 0, :, :],
                                             in1=lnT[:, l, 3, :].unsqueeze(2).broadcast_to([128, 8, 3]), op=ALU.mult), reads=rd + wr, writes=wr)
        S.dve(I("tensor_tensor", out=mder2[:, l, 1, :, :], in0=mder2[:, l, 1, :, :], in1=modT[:, l + 1, 0:8, :], op=ALU.add), reads=rd + wr, writes=wr)
    MD = [B("mder"), B("mder2"), B("modT"), B("lnT")]
    if stage < 99:
        S.pool(I("memset", obuf[:], 0.0), writes=[B("obuf", tt) for tt in range(NTT)])
    if stage == 0:
        dump("modT", modT[:], [128, DEPTH, 48, 3], F32, B("modT"))

    def msel(s, t0):
        return 2 if t0 < LC else s

    gctr = {}

    def bank(group, banks):
        c = gctr.get(group, 0)
        gctr[group] = c + 1
        i = banks[c % len(banks)]
        return pf[i], B("pf", i)

    def tiles_of(t0, n):
        return range(t0 // 128, (t0 + n) // 128)

    def ub(t0, n):
        return [B("ubuf", tt) for tt in tiles_of(t0, n)]

    win_v = [win_d[l].rearrange("(k p) c -> p k c", p=128) for l in range(DEPTH)]
    issued = set()

    def ensure(kind, s_, l_):
        key = (kind, s_, l_)
        if key in issued or s_ >= nseq or l_ >= DEPTH:
            return
        issued.add(key)
        if kind == "na":
            S.dma(I("dma_start", out=wview(0, BF16, 8, 1152), in_=win_v[l_][:, :, 0:1152]), writes=[B("wbuf", 0)], q="pool")
        elif kind == "gla":
            S.dma(I("dma_start", out=wview(0, BF16, 8, 800), in_=win_v[l_][:, :, C_GLQ:C_GLQ + 800]), writes=[B("wbuf", 0)], q="pool")
        elif kind == "gqa":
            wv_ = wview(1, BF16, 8, 768)
            wb_ = [B("wbuf", 1)]
            S.dma(I("dma_start", out=wv_[:, :, 0:384], in_=win_v[l_][:, :, C_GAQ:C_GAQ + 384]), writes=wb_, q="pool")
            for g in range(2):
                for d in range(2):
                    S.dma(I("dma_start", out=wv_[:, :, 384 + 128 * g + 64 * d:384 + 128 * g + 64 * d + 64],
                            in_=win_v[l_][:, :, C_GAK + 64 * g:C_GAK + 64 * g + 64]), writes=wb_, q="pool")
            S.dma(I("dma_start", out=wv_[:, :, 640:768], in_=win_v[l_][:, :, C_GAV:C_GAV + 128]), writes=wb_, q="pool")
        elif kind == "out":
            S.dma(I("dma_start", out=wview(1, BF16, 8, 1024), in_=wout_d[l_].rearrange("(k p) c -> p k c", p=128)), writes=[B("wbuf", 1)], q="pool")
        elif kind == "ffn0":
            wf1_ = wf1_d[l_].rearrange("(k p) c -> p k c", p=128)
            wv_ = wview(0, BF16, 8, 512)
            S.dma(I("dma_start", out=wv_[:, :, 0:256], in_=wf1_[:, :, 0:256]), writes=[B("wbuf", 0)], q="pool")
            S.dma(I("dma_start", out=wv_[:, :, 256:512], in_=wf1_[:, :, FH:FH + 256]), writes=[B("wbuf", 0)], q="pool")

    def next_sl(s_, l_):
        return (s_, l_ + 1) if l_ + 1 < DEPTH else (s_ + 1, 0)
    combos, na_slots = na_combos()
    PBB = B("pb", 0)

    def load_x(s):
        S.phase = "load_x"
        S.barrier()
        cv = Carver()
        xin = [cv(F32, 1024) for _ in range(2)]
        xst = [cv(F32, 8, 128) for _ in range(2)]
        for tt in range(NTT):
            i2 = tt % 2
            src = ctx_d[s, 128 * tt:128 * tt + 128, :] if tt < 2 else x_d[s, 128 * (tt - 2):128 * (tt - 2) + 128, :]
            S.dma(I("dma_start", out=xin[i2], in_=src), writes=[B("xin", i2)])
            mi = msel(s, 128 * tt)
            for half in range(2):
                pt, bpt = bank("ld", [0, 1, 2, 3])
                for kk in range(4):
                    k = 4 * half + kk
                    S.pe(I("transpose", pt[:, 128 * kk:128 * kk + 128], xin[i2][:, 128 * k:128 * k + 128], identf[:]),
                         reads=[B("xin", i2), B("identf")], writes=[bpt])
                S.dve(I("tensor_copy",
                    out=xst[i2][:, 4 * half:4 * half + 4, :], in_=pt[:].rearrange("p (k t) -> p k t", k=4, t=128)),
                    reads=[bpt], writes=[B("xst", i2)])
                for kk in range(4):
                    k = 4 * half + kk
                    S.act(I("activation",
                        out=ubuf[:, k, 128 * tt:128 * tt + 128], in_=pt[:, 128 * kk:128 * kk + 128], func=AF.Identity,
                        scale=mder[:, 0, 0, k, mi:mi + 1], bias=modT[:, 0, k, mi:mi + 1]),
                        reads=[bpt] + MD, writes=[B("ubuf", tt)])
            S.dma(I("dma_start", out=xres[s][:, :, 128 * tt:128 * tt + 128], in_=xst[i2]),
                  reads=[B("xst", i2)], writes=[B("xres", s, tt)])

    def na_phase(s, l):
        S.phase = "na_phase" + str(l)
        S.barrier()
        cv = Carver()
        naq = cv(BF16, 3, T)
        nak = cv(BF16, 3, T)
        nav = cv(BF16, NTT, 6 * 65)
        natab = cv(BF16, 6, NCOMBO * 128)
        naP = [cv(BF16, 7, 128) for _ in range(3)]
        otok = [cv(BF16, 384) for _ in range(2)]
        rden = [cv(F32, 8) for _ in range(2)]
        nav4 = nav.rearrange("p t (h c) -> p t h c", h=6, c=65)
        wv = wview(0, BF16, 8, 1152)
        ensure("na", s, l)
        S.dma(I("dma_start", out=natab, in_=nab_d[l]), writes=[B("natab")], q="pool")
        ensure("gqa", s, l)
        S.pool(I("memset", nav4[:, :, :, 64:65], 1.0), writes=[B("navones")])
        for (t0, n) in BLOCKS:
            for j in range(6):
                if j < 3 and t0 < LC and l == DEPTH - 1:
                    continue
                ps, bps = bank("proj", [0, 1, 2, 3, 4, 5, 6])
                for k in range(8):
                    S.pe(I("matmul", ps[:, 0:n], wv[:, k, 128 * j:128 * j + 128], ubuf[:, k, t0:t0 + n],
                                                                     start=(k == 0), stop=(k == 7)),
                         reads=[B("wbuf", 0)] + ub(t0, n), writes=[bps])
                wr = [B("naqk", tt) for tt in tiles_of(t0, n)]
                if j < 3:
                    S.act(I("activation", out=naq[:, j, t0:t0 + n], in_=ps[:, 0:n], func=AF.Identity, scale=0.125),
                          reads=[bps], writes=wr)
                else:
                    S.dve(I("tensor_copy", out=nak[:, j - 3, t0:t0 + n], in_=ps[:, 0:n]), reads=[bps], writes=wr)
        for tt in range(NTT):
            ps, bps = bank("proj", [0, 1, 2, 3, 4, 5, 6])
            for k in range(8):
                S.pe(I("matmul", ps[:, 0:384], ubuf[:, k, 128 * tt:128 * tt + 128], wv[:, k, 768:1152],
                                                         start=(k == 0), stop=(k == 7)),
                     reads=[B("wbuf", 0), B("ubuf", tt)], writes=[bps])
            S.dve(I("tensor_copy", out=nav4[:, tt, :, 0:64], in_=ps[:, 0:384].rearrange("p (h c) -> p h c", h=6, c=64)),
                  reads=[bps], writes=[B("nav", tt)])
        ensure("gla", s, l)
        qtiles = []
        if l < DEPTH - 1:
            qtiles += [(0, [(0, None), (1, None)]), (1, [(0, None), (1, None)])]
        for p in range(16):
            qtiles.append((2 + p, [(2 + j, ci) for (j, ci) in na_slots[p]] + [(0, None), (1, None)]))
        naP3 = naP
        units = [(qi, h) for qi in range(len(qtiles)) for h in range(6)]
        pos_ = {}
        ust = {}

        def na_st(u):
            qi, h = units[u]
            tq, slots = qtiles[qi]
            if h == 0:
                pos_[qi] = bank("napo", [4, 5])
            ns = len(slots)
            c = h // 2
            p0 = 64 * (h % 2)
            pa, bpa = bank("napa", [0, 1, 2, 3])
            pa2, bpa2 = bank("napa", [0, 1, 2, 3])
            P = naP3[u % 3]
            bP = B("naP", u % 3)
            ust[u] = (P, bP)
            nlat = sum(1 for (_, ci_) in slots if ci_ is not None)
            ci0 = slots[0][1]
            has_b = [False, False]
            if nlat >= 4:
                S.pe(I("matmul", pa[:, 0:512], identb[:], natab[:, h, 128 * ci0:128 * ci0 + 512], start=True, stop=False),
                     reads=[B("natab"), B("identb")], writes=[bpa])
                has_b[0] = True
            if nlat == 5:
                S.pe(I("matmul", pa2[:, 0:128], identb[:], natab[:, h, 128 * (ci0 + 4):128 * (ci0 + 4) + 128], start=True, stop=False),
                     reads=[B("natab"), B("identb")], writes=[bpa2])
                has_b[1] = True
            for si, (tk, ci) in enumerate(slots):
                bi_ = 0 if si < 4 else 1
                bk, bbk = (pa, bpa) if si < 4 else (pa2, bpa2)
                col = 128 * (si % 4)
                last_in_bank = (si == min(ns, 4) - 1) if bi_ == 0 else (si == ns - 1)
                if has_b[bi_]:
                    st_, sp_ = False, last_in_bank
                else:
                    st_, sp_ = True, True
                S.pe(I("matmul", bk[:, col:col + 128], nak[p0:p0 + 64, c, 128 * tk:128 * tk + 128], naq[p0:p0 + 64, c, 128 * tq:128 * tq + 128],
                       start=st_, stop=sp_),
                     reads=[B("naqk", tk), B("naqk", tq)], writes=[bbk])
            n1 = min(ns, 4)
            S.act(I("activation", out=P[:, 0:n1, :], in_=pa[:, 0:128 * n1].rearrange("p (a b) -> p a b", a=n1, b=128), func=AF.Exp),
                  reads=[bpa], writes=[bP])
            if ns > 4:
                n2 = ns - 4
                S.act(I("activation", out=P[:, 4:4 + n2, :], in_=pa2[:, 0:128 * n2].rearrange("p (a b) -> p a b", a=n2, b=128), func=AF.Exp),
                      reads=[bpa2], writes=[bP])

        def na_pv(u):
            qi, h = units[u]
            tq, slots = qtiles[qi]
            ns = len(slots)
            po, bpo = pos_[qi]
            P, bP = ust.pop(u)
            for si, (tk, ci) in enumerate(slots):
                S.pe(I("matmul", po[:, 65 * h:65 * h + 65], P[:, si, :], nav[:, tk, 65 * h:65 * h + 65], start=(si == 0), stop=(si == ns - 1)),
                     reads=[bP, B("nav", tk), B("navones")], writes=[bpo])
            if h < 5:
                return
            o2 = qi % 2
            po3 = po[:, 0:390].rearrange("p (h c) -> p h c", h=6, c=65)
            S.dve(I("reciprocal", out=rden[o2][:, 0:6], in_=po3[:, :, 64]), reads=[bpo], writes=[B("rden", o2)])
            S.dve(I("tensor_tensor", out=otok[o2].rearrange("p (h c) -> p h c", h=6, c=64), in0=po3[:, :, 0:64],
                    in1=rden[o2][:, 0:6].unsqueeze(2).broadcast_to([128, 6, 64]), op=ALU.mult),
                  reads=[bpo, B("rden", o2)], writes=[B("otok", o2)])
            for c3 in range(3):
                S.pe(I("transpose", pb[:, 128 * c3:128 * c3 + 128], otok[o2][:, 128 * c3:128 * c3 + 128], identb[:]),
                     reads=[B("otok", o2), B("identb")], writes=[PBB])
            S.act(I("copy", out=obuf[:, 0:3, 128 * tq:128 * tq + 128], in_=pb[:, 0:384].rearrange("p (a b) -> p a b", a=3, b=128)),
                  reads=[PBB], writes=[B("obuf", tq)])

        LA = 1
        for u in range(len(units) + LA):
            if u < len(units):
                na_st(u)
            if u >= LA:
                na_pv(u - LA)

    def gqa_phase(s, l):
        S.phase = "gqa_phase" + str(l)
        S.barrier()
        cv = Carver()
        gaq = cv(BF16, 3, T)
        gak = cv(BF16, 2, T)
        gav = cv(BF16, NTT, 130)
        cosv = cv(F32, NL)
        sinv = cv(F32, NL)
        sqs = [cv(BF16, 512) for _ in range(2)]
        rstds = [cv(F32, 512) for _ in range(2)]
        qns = [cv(F32, 512) for _ in range(2)]
        qnbs = [cv(BF16, 512) for _ in range(2)]
        t1s = [cv(F32, 512) for _ in range(2)]
        t2s = [cv(F32, 512) for _ in range(2)]
        gaP = [cv(BF16, 512) for _ in range(6)]
        otok = cv(BF16, 4, 384)
        rden = [cv(F32, 8) for _ in range(2)]
        gav4 = gav.rearrange("p t (g c) -> p t g c", g=2, c=65)
        wv = wview(1, BF16, 8, 768)
        wb = [B("wbuf", 1)]
        ensure("gqa", s, l)
        S.dma(I("dma_start", out=cosv, in_=cos_d), writes=[B("cos")])
        S.dma(I("dma_start", out=sinv, in_=sin_d), writes=[B("cos")])
        S.pool(I("memset", gav4[:, :, :, 64:65], 1.0), writes=[B("gavones")])
        cb = 0
        for (t0, n) in BLOCKS:
            for j in range(5):
                if j < 3 and t0 < LC and l == DEPTH - 1:
                    continue
                p2 = cb % 2
                cb += 1
                sq, rstd, qn, qnb, t1, t2 = sqs[p2], rstds[p2], qns[p2], qnbs[p2], t1s[p2], t2s[p2]
                TB = lambda nm, p2=p2: B("gatmp", nm, p2)
                ps, bps = bank("proj", [0, 1, 2, 3])
                for k in range(8):
                    S.pe(I("matmul", ps[:, 0:n], wv[:, k, 128 * j:128 * j + 128], ubuf[:, k, t0:t0 + n],
                                                                     start=(k == 0), stop=(k == 7)),
                         reads=wb + ub(t0, n), writes=[bps])
                S.act(I("activation", out=sq[:, 0:n], in_=ps[:, 0:n], func=AF.Square), reads=[bps], writes=[TB("sq")])
                pm, bpm = bank("gams", [4, 5])
                S.pe(I("matmul", pm[:, 0:n], bones[:], sq[:, 0:n], start=True, stop=True), reads=[TB("sq"), B("bones")], writes=[bpm])
                S.act(I("activation", out=rstd[:, 0:n], in_=pm[:, 0:n], func=AF.Ln, bias=epsrms[:, 0:1], scale=1.0),
                      reads=[bpm, B("eps")], writes=[TB("rstd")])
                S.act(I("activation", out=rstd[:, 0:n], in_=rstd[:, 0:n], func=AF.Exp, scale=-0.5), reads=[TB("rstd")], writes=[TB("rstd")])
                wi = 0 if j < 3 else 1
                S.dve(I("scalar_tensor_tensor", out=qn[:, 0:n], in0=ps[:, 0:n], scalar=qkwT[:, l, wi:wi + 1], in1=rstd[:, 0:n],
                                                                      op0=ALU.mult, op1=ALU.mult),
                      reads=[bps, TB("rstd"), B("qkwT")], writes=[TB("qn")])
                dst = gaq[:, j, t0:t0 + n] if j < 3 else gak[:, j - 3, t0:t0 + n]
                wr = [B("gaqk", tt) for tt in tiles_of(t0, n)]
                if t0 >= LC:
                    S.act(I("copy", out=qnb[:, 0:n], in_=qn[:, 0:n]), reads=[TB("qn")], writes=[TB("qnb")])
                    pr, bpr = bank("gams", [4, 5])
                    S.pe(I("matmul", pr[:, 0:n], rotm[:], qnb[:, 0:n], start=True, stop=True), reads=[TB("qnb"), B("rotm")], writes=[bpr])
                    S.dve(I("tensor_tensor", out=t1[:, 0:n], in0=qn[:, 0:n], in1=cosv[:, t0 - LC:t0 - LC + n], op=ALU.mult),
                          reads=[TB("qn"), B("cos")], writes=[TB("t1")])
                    S.dve(I("tensor_tensor", out=t2[:, 0:n], in0=pr[:, 0:n], in1=sinv[:, t0 - LC:t0 - LC + n], op=ALU.mult),
                          reads=[bpr, B("cos")], writes=[TB("t2")])
                    S.dve(I("tensor_tensor", out=dst, in0=t1[:, 0:n], in1=t2[:, 0:n], op=ALU.add),
                          reads=[TB("t1"), TB("t2")], writes=wr)
                else:
                    S.act(I("copy", out=dst, in_=qn[:, 0:n]), reads=[TB("qn")], writes=wr)
        for tt in range(NTT):
            ps, bps = bank("proj", [0, 1, 2, 3])
            for k in range(8):
                S.pe(I("matmul", ps[:, 0:128], ubuf[:, k, 128 * tt:128 * tt + 128], wv[:, k, 640:768],
                                                         start=(k == 0), stop=(k == 7)),
                     reads=wb + [B("ubuf", tt)], writes=[bps])
            S.dve(I("tensor_copy", out=gav4[:, tt, :, 0:64], in_=ps[:, 0:128].rearrange("p (g c) -> p g c", g=2, c=64)),
                  reads=[bps], writes=[B("gav", tt)])
        ensure("out", s, l)
        qblocks = []
        if l < DEPTH - 1:
            qblocks.append((0, 256, [0, 1]))
        for i in range(4):
            qblocks.append((256 + 512 * i, 512, list(range(NTT))))
        pi = 0
        for (t0, n, ktiles) in qblocks:
            nsub = n // 128
            nk = len(ktiles)
            for hp in range(3):
                heads = (2 * hp, 2 * hp + 1)
                pos2 = []
                for h in heads:
                    po, bpo = bank("gapo", [4, 5, 6])
                    S.dve(I("memset", po[:, 0:65 * nsub], 0.0), writes=[bpo])
                    pos2.append((po, bpo))
                Ps = {}

                def st(ki):
                    nonlocal pi
                    tk = ktiles[ki]
                    lst = []
                    for h in heads:
                        pa, bpa = bank("gapa", [0, 1, 2, 3])
                        P = gaP[pi % 6]
                        bP = B("gaP", pi % 6)
                        pi += 1
                        lst.append((pa, bpa, P, bP))
                    Ps[ki] = lst
                    for h, (pa, bpa, P, bP) in zip(heads, lst):
                        g = h // 3
                        p0 = 64 * (h % 2)
                        S.pe(I("matmul", pa[:, 0:n], gak[p0:p0 + 64, g, 128 * tk:128 * tk + 128], gaq[p0:p0 + 64, hp, t0:t0 + n],
                               start=True, stop=True),
                             reads=[B("gaqk", tk)] + [B("gaqk", tt) for tt in tiles_of(t0, n)], writes=[bpa])
                    for h, (pa, bpa, P, bP) in zip(heads, lst):
                        S.act(I("activation", out=P[:, 0:n], in_=pa[:, 0:n], func=AF.Exp, scale=0.125), reads=[bpa], writes=[bP])

                def pv(ki):
                    tk = ktiles[ki]
                    lst = Ps.pop(ki)
                    for h, (pa, bpa, P, bP), (po, bpo) in zip(heads, lst, pos2):
                        g = h // 3
                        for qs in range(nsub):
                            S.pe(I("matmul", po[:, 65 * qs:65 * qs + 65], P[:, 128 * qs:128 * qs + 128], gav[:, tk, 65 * g:65 * g + 65],
                                   start=False, stop=(ki == nk - 1), skip_group_check=True),
                                 reads=[bP, B("gav", tk), B("gavones")], writes=[bpo])

                GLA_ = 1
                for step in range(nk + GLA_):
                    if step < nk:
                        st(step)
                    if step >= GLA_:
                        pv(step - GLA_)
                for h, (po, bpo) in zip(heads, pos2):
                    o2 = h % 2
                    po3 = po[:, 0:65 * nsub].rearrange("p (q c) -> p q c", q=nsub, c=65)
                    S.dve(I("reciprocal", out=rden[o2][:, 0:nsub], in_=po3[:, :, 64]), reads=[bpo], writes=[B("rden", o2)])
                    S.dve(I("tensor_tensor", out=otok[:, 0:nsub, 64 * h:64 * h + 64], in0=po3[:, :, 0:64],
                            in1=rden[o2][:, 0:nsub].unsqueeze(2).broadcast_to([128, nsub, 64]), op=ALU.mult),
                          reads=[bpo, B("rden", o2)], writes=[B("gaotok")])
            for qs in range(nsub):
                tq = t0 // 128 + qs
                for c3 in range(3):
                    S.pe(I("transpose", pb[:, 128 * c3:128 * c3 + 128], otok[:, qs, 128 * c3:128 * c3 + 128], identb[:]),
                         reads=[B("gaotok"), B("identb")], writes=[PBB])
                S.act(I("copy", out=obuf[:, 5:8, 128 * tq:128 * tq + 128], in_=pb[:, 0:384].rearrange("p (a b) -> p a b", a=3, b=128)),
                      reads=[PBB], writes=[B("obuf", tq)])

    def gla_phase(s, l):
        S.phase = "gla_phase" + str(l)
        S.barrier()
        cv = Carver()
        glv = cv(BF16, NTT, 256)
        glgw = cv(BF16, NTT, 256)
        qt = [cv(BF16, T) for _ in range(2)]
        kt = [cv(BF16, T) for _ in range(2)]
        ktok = cv(BF16, NTT, 128)
        Dall = cv(F32, 2, 36)
        S32 = cv(F32, 37, 64)
        Sbf = [cv(BF16, 36, 64) for _ in range(2)]
        off_gate_tmp = cv.off
        qf = cv(F32, 512)
        kf = cv(F32, 512)
        lrf = cv(F32, 512)
        smask = cv(F32, 512)
        tAs = [cv(F32, 512) for _ in range(2)]
        tBs = [cv(F32, 512) for _ in range(2)]
        tCs = [cv(F32, 512) for _ in range(2)]
        tD = tCs[0]
        Abf = [cv(BF16, 2, 256) for _ in range(2)]
        qmb = [cv(BF16, 2, 4, 128) for _ in range(2)]
        osq = cv(F32, 256)
        sg = osq
        on = cv(F32, 256)
        otok = [cv(BF16, 256) for _ in range(2)]
        st4 = [cv(F32, 8) for _ in range(2)]
        TB = lambda n: B("gltmp", n)
        TB0 = TB
        wv = wview(0, BF16, 8, 800)
        wb = [B("wbuf", 0)]
        ensure("gla", s, l)
        S.pool(I("memset", smask, 1.0), writes=[TB0("smask")])
        S.pool(I("memset", smask.rearrange("p (c t) -> p c t", c=8, t=64)[:, :, 0:1], 0.0), writes=[TB0("smask")])
        for tt in range(NTT):
            ps, bps = bank("proj", [0, 1, 2, 3])
            for k in range(8):
                S.pe(I("matmul", ps[:, 0:512], ubuf[:, k, 128 * tt:128 * tt + 128], wv[:, k, 256:768],
                                                         start=(k == 0), stop=(k == 7)),
                     reads=wb + [B("ubuf", tt)], writes=[bps])
            S.dve(I("tensor_copy", out=glv[:, tt, :], in_=ps[:, 0:256]), reads=[bps], writes=[B("glv", tt)])
            S.act(I("activation", out=sg, in_=ps[:, 256:512], func=AF.Silu), reads=[bps], writes=[TB0("osq")])
            S.pool(I("tensor_tensor", out=glgw[:, tt, :], in0=sg, in1=glnw[:, l, :], op=ALU.mult),
                   reads=[TB0("osq"), B("glnw")], writes=[B("glgw", tt)])
        import os as _os
        _gs = int(_os.environ.get('GLASTOP', '9'))
        _sub = int(_os.environ.get('GLASUB', '9'))
        if _gs <= 0:
            return
        for (t0, n) in BLOCKS:
            nch = n // 64
            c0 = t0 // 64
            pq, bpq = bank("proj", [0, 1, 2, 3])
            pk, bpk = bank("proj", [0, 1, 2, 3])
            pl, bpl = bank("proj", [0, 1, 2, 3])
            for (pp, col, M) in ((pq, 0, 128), (pk, 128, 128), (pl, 768, 32)):
                bb = {id(pq): bpq, id(pk): bpk, id(pl): bpl}[id(pp)]
                for k in range(8):
                    S.pe(I("matmul", pp[0:M, 0:n], wv[:, k, col:col + M], ubuf[:, k, t0:t0 + n],
                                                                              start=(k == 0), stop=(k == 7)),
                         reads=wb + ub(t0, n), writes=[bb])
            S.act(I("activation", out=qf[:, 0:n], in_=pq[:, 0:n], func=AF.Identity, scale=float(32 ** -0.5)), reads=[bpq], writes=[TB0("qf")])
            S.dve(I("tensor_copy", out=kf[:, 0:n], in_=pk[:, 0:n]), reads=[bpk], writes=[TB0("kf")])
            S.dve(I("tensor_copy", out=lrf[0:32, 0:n], in_=pl[0:32, 0:n]), reads=[bpl], writes=[TB0("lrf")])
            for ed in range(2):
                tA, tB_, tC = tAs[ed], tBs[ed], tCs[ed]
                TBd = lambda n_, ed=ed: B("gltmp", n_, ed)
                pz, bpz = bank("glz", [4, 5])
                S.pe(I("matmul", pz[:, 0:n], wa2[:, l, ed, :], lrf[0:32, 0:n], start=True, stop=True),
                     reads=[TB0("lrf"), B("wa2")], writes=[bpz])
                S.act(I("activation", out=tA[:, 0:n], in_=pz[:, 0:n], func=AF.Exp, scale=-1.0, bias=nbaT[:, l, ed:ed + 1]),
                      reads=[bpz, B("nbaT")], writes=[TBd("tA")])
                S.act(I("activation", out=tB_[:, 0:n], in_=tA[:, 0:n], func=AF.Ln, bias=1.0, scale=1.0), reads=[TBd("tA")], writes=[TBd("tB")])
                S.dve(I("tensor_tensor_scan", out=tC[:, 0:n], data0=smask[:, 0:n], data1=tB_[:, 0:n], initial=0.0, op0=ALU.mult, op1=ALU.add),
                      reads=[TB0("smask"), TBd("tB")], writes=[TBd("tC")])
                cum = tC
                bcum = TBd("tC")
                if ed == 1:
                    tC3 = tC[:, 0:n].rearrange("p (c t) -> p c t", c=nch, t=64)
                    S.dve(I("tensor_tensor", out=tD[:, 0:n].rearrange("p (c t) -> p c t", c=nch, t=64),
                                                                         in0=tC3[:, :, 63:64].broadcast_to([128, nch, 64]), in1=tC3, op=ALU.subtract),
                          reads=[TBd("tC")], writes=[B("gltmp", "tC", 0)])
                    S.dve(I("tensor_tensor", out=tD[:, 0:n], in0=tD[:, 0:n], in1=tB_[:, 0:n], op=ALU.add), reads=[B("gltmp", "tC", 0), TBd("tB")], writes=[B("gltmp", "tC", 0)])
                    cum = tD
                    bcum = B("gltmp", "tC", 0)
                S.act(I("activation", out=tA[:, 0:n], in_=cum[:, 0:n], func=AF.Exp, scale=-1.0 / 16.0), reads=[bcum], writes=[TBd("tA")])
                S.act(I("activation", out=tB_[:, 0:n], in_=cum[:, 0:n], func=AF.Exp, scale=1.0 / 16.0), reads=[bcum, TBd("tA")], writes=[TBd("tB")])
                dcol = 63 if ed == 0 else 0
                S.dve(I("tensor_copy",
                    out=Dall[:, ed, c0:c0 + nch], in_=tA[:, 0:n].rearrange("p (c t) -> p c t", c=nch, t=64)[:, :, dcol]),
                    reads=[TBd("tA")], writes=[B("Dall")])
                wr = [B("glqk", ed, tt) for tt in tiles_of(t0, n)]
                S.dve(I("tensor_tensor", out=qt[ed][:, t0:t0 + n], in0=qf[:, 0:n], in1=tA[:, 0:n], op=ALU.mult),
                      reads=[TB0("qf"), TBd("tA")], writes=wr)
                S.dve(I("tensor_tensor", out=kt[ed][:, t0:t0 + n], in0=kf[:, 0:n], in1=tB_[:, 0:n], op=ALU.mult),
                       reads=[TB0("kf"), TBd("tB")], writes=wr)
        if _gs <= 1:
            return
        ords = [list(range(36)), [3, 2, 1, 0] + list(range(35, 3, -1))]
        poss = []
        for ed in range(2):
            pos = [0] * 36
            for j, ci in enumerate(ords[ed]):
                pos[ci] = j
            poss.append(pos)
        ensure("ffn0", s, l)
        S.barrier()
        cv2 = Carver()
        cv2.off = off_gate_tmp
        Drep = cv2(F32, 64, 36)
        Sall = cv2(F32, 64 * 36)
        Dord = cv2(F32, 64)
        assert cv2.off <= off_gate_tmp + 20480
        Wd3 = S32.rearrange("p a b -> p (a b)")[:, 0:64 * 36].rearrange("p (v j) -> p v j", v=64, j=36)
        for ed in range(2):
            pos = poss[ed]
            for tt in range(NTT):
                S.pe(I("transpose", pb[:, 128 * (tt % 4):128 * (tt % 4) + 128], kt[ed][:, 128 * tt:128 * tt + 128], identb[:]),
                     reads=[B("glqk", ed, tt), B("identb")], writes=[PBB])
                if tt % 4 == 3 or tt == NTT - 1:
                    n4 = tt % 4 + 1
                    ta = tt - n4 + 1
                    S.act(I("copy", out=ktok[:, ta:ta + n4, :], in_=pb[:, 0:128 * n4].rearrange("p (a b) -> p a b", a=n4, b=128)),
                          reads=[PBB], writes=[B("ktok")])
            if ed == 0:
                S.dve(I("tensor_copy", out=Dord[:, 0:36], in_=Dall[:, 0, 0:36]), reads=[B("Dall")], writes=[B("Dord")])
            else:
                S.dve(I("tensor_copy", out=Dord[:, 0:4], in_=Dall[:, 1, 3::-1]), reads=[B("Dall")], writes=[B("Dord")])
                S.dve(I("tensor_copy", out=Dord[:, 4:36], in_=Dall[:, 1, 35:3:-1]), reads=[B("Dall")], writes=[B("Dord")])
            S.dve(I("memset", Dord[:, 0:1], 0.0), writes=[B("Dord")])
            S.dve(I("tensor_copy", out=Drep, in_=Dord[:, 0:36].unsqueeze(1).broadcast_to([128, 64, 36])), reads=[B("Dord")], writes=[B("Drep")])
            for half in range(2):
              for grp in range(0, 18, 8):
                pw, bpw = bank("glw", [0, 1, 2, 3])
                cis = [2 * t_ + half for t_ in range(grp, min(grp + 8, 18))]
                for gi, ci in enumerate(cis):
                    tt = ci // 2
                    for h in range(4):
                        S.pe(I("matmul",
                            pw[32 * h:32 * h + 32, 64 * gi:64 * gi + 64], ktok[64 * half:64 * half + 64, tt, 32 * h:32 * h + 32],
                            glv[64 * half:64 * half + 64, tt, 64 * h:64 * h + 64], start=True, stop=True, tile_position=(64 * half, 32 * h)),
                            reads=[B("ktok"), B("glv", tt)], writes=[bpw])
                for gi, ci in enumerate(cis):
                    S.act(I("activation",
                        out=Wd3[:, :, pos[ci]], in_=pw[:, 64 * gi:64 * gi + 64], func=AF.Identity, scale=Dall[:, ed, ci:ci + 1]),
                        reads=[bpw, B("Dall")], writes=[B("S32")])
            S.dve(I("tensor_tensor_scan", out=Sall, data0=Drep.rearrange("p v j -> p (v j)"), data1=Wd3.rearrange("p v j -> p (v j)"),
                    initial=0.0, op0=ALU.mult, op1=ALU.add),
                  reads=[B("Drep"), B("S32")], writes=[B("Sall")])
            S.dve(I("memset", Sbf[ed][:, 0, :], 0.0), writes=[B("Sbf", ed)])
            S.dve(I("tensor_copy", out=Sbf[ed][:, 1:36, :], in_=Sall.rearrange("p (v j) -> p j v", v=64, j=36)[:, 0:35, :]),
                  reads=[B("Sall")], writes=[B("Sbf", ed)])
        if _gs <= 2:
            return
        tiles_out = range(NTT) if l < DEPTH - 1 else range(2, NTT)
        tiles_out = list(tiles_out)

        def g_st1(oi):
            tt = tiles_out[oi]
            pa, bpa = bank("glpa", [0, 1, 2, 3])
            A = Abf[oi % 2]
            bA = B("Abf", oi % 2)
            qm = qmb[oi % 2]
            bqm = B("qm", oi % 2)
            for ed in range(2):
                for h in range(4):
                    S.act(I("activation", out=qm[:, ed, h, :], in_=qt[ed][:, 128 * tt:128 * tt + 128], func=AF.Identity, scale=hmask[:, h:h + 1]),
                          reads=[B("glqk", ed, tt), B("hmask")], writes=[B("qm", oi % 2, ed, h)])
            for half in range(2):
                cols = slice(128 * tt + 64 * half, 128 * tt + 64 * half + 64)
                for ed in range(2):
                    for h in range(4):
                        S.pe(I("matmul",
                            pa[64 * half:64 * half + 64, 256 * ed + 64 * h:256 * ed + 64 * h + 64],
                            kt[ed][:, cols], qm[:, ed, h, 64 * half:64 * half + 64], start=True, stop=True,
                            tile_position=(0, 64 * half)),
                            reads=[B("glqk", ed, tt), B("qm", oi % 2, ed, h)], writes=[bpa])
            S.dve(I("tensor_tensor", out=A, in0=pa[:, 0:512].rearrange("p (a b) -> p a b", a=2, b=256), in1=trim[:], op=ALU.mult),
                  reads=[bpa, B("trim")], writes=[bA])

        def g_st2(oi):
            tt = tiles_out[oi]
            A = Abf[oi % 2]
            bA = B("Abf", oi % 2)
            qm = qmb[oi % 2]
            bqm = B("qm", oi % 2)
            po, bpo = bank("glpo", [4, 5])
            for half in range(2):
                ci = 2 * tt + half
                for h in range(4):
                    for ed in range(2):
                        S.pe(I("matmul",
                            po[64 * half:64 * half + 64, 64 * h:64 * h + 64], A[64 * half:64 * half + 64, ed, 64 * h:64 * h + 64],
                            glv[64 * half:64 * half + 64, tt, 64 * h:64 * h + 64], start=(ed == 0), stop=False,
                            tile_position=(64 * half, 64 * half)),
                            reads=[bA, B("glv", tt)], writes=[bpo])
                    for ed in range(2):
                        pj = poss[ed][ci]
                        S.pe(I("matmul",
                            po[64 * half:64 * half + 64, 64 * h:64 * h + 64], qm[:, ed, h, 64 * half:64 * half + 64],
                            Sbf[ed][:, pj, :], start=False, stop=(ed == 1),
                            tile_position=(0, 64 * half)),
                            reads=[B("qm", oi % 2, ed, h), B("Sbf", ed)], writes=[bpo])
            o2 = oi % 2
            S.act(I("activation", out=osq, in_=po[:, 0:256], func=AF.Square), reads=[bpo], writes=[TB0("osq")])
            S.dve(I("reduce_sum", out=st4[o2][:, 0:4], in_=osq.rearrange("p (h c) -> p h c", h=4, c=64), axis=AX.X),
                  reads=[TB0("osq")], writes=[B("st4", o2)])
            S.dve(I("tensor_scalar", out=st4[o2][:, 0:4], in0=st4[o2][:, 0:4], scalar1=1.0 / 64.0, scalar2=RMS_EPS, op0=ALU.mult, op1=ALU.add),
                  reads=[B("st4", o2)], writes=[B("st4", o2)])
            S.act(I("activation", out=st4[o2][:, 4:8], in_=st4[o2][:, 0:4], func=AF.Sqrt), reads=[B("st4", o2)], writes=[B("st4", o2)])
            S.dve(I("reciprocal", out=st4[o2][:, 0:4], in_=st4[o2][:, 4:8]), reads=[B("st4", o2)], writes=[B("st4", o2)])
            S.dve(I("tensor_tensor", out=on.rearrange("p (h c) -> p h c", h=4, c=64), in0=po[:, 0:256].rearrange("p (h c) -> p h c", h=4, c=64),
                                                        in1=st4[o2][:, 0:4].unsqueeze(2).broadcast_to([128, 4, 64]), op=ALU.mult),
                  reads=[bpo, B("st4", o2)], writes=[TB0("on")])
            S.pool(I("tensor_tensor", out=otok[o2], in0=on, in1=glgw[:, tt, :], op=ALU.mult),
                   reads=[TB0("on"), B("glgw", tt)], writes=[B("glotok", o2)])

        def g_st3(oi):
            tt = tiles_out[oi]
            o2 = oi % 2
            for c2 in range(2):
                S.pe(I("transpose", pb[:, 512 + 128 * c2:512 + 128 * c2 + 128], otok[o2][:, 128 * c2:128 * c2 + 128], identb[:]),
                     reads=[B("glotok", o2), B("identb")], writes=[PBB])
            S.act(I("copy", out=obuf[:, 3:5, 128 * tt:128 * tt + 128], in_=pb[:, 512:768].rearrange("p (a b) -> p a b", a=2, b=128)),
                  reads=[PBB], writes=[B("obuf", tt)])

        for oi in range(len(tiles_out) + 2):
            if oi < len(tiles_out):
                g_st1(oi)
            if 1 <= oi <= len(tiles_out):
                g_st2(oi - 1)
            if oi >= 2:
                g_st3(oi - 2)

    def layer_norm_block(tbuf, bk, n, outs):
        pmean, bpmean = bank("lnm", [4, 5])
        pmsq, bpmsq = bank("lnm", [4, 5])
        for oc in range(8):
            tb16 = lntb[oc % 2]
            S.act(I("copy", out=tb16[:, 0:n], in_=tbuf[:, oc, 0:n]), reads=[bk(oc)], writes=[B("lntb", oc % 2)])
            S.pe(I("matmul", pmean[:, 0:n], onesDb[:], tb16[:, 0:n], start=(oc == 0), stop=(oc == 7)),
                 reads=[B("lntb", oc % 2), B("onesDb")], writes=[bpmean])
            sqb = lnsq[oc % 2]
            S.act(I("activation", out=sqb[:, 0:n], in_=tbuf[:, oc, 0:n], func=AF.Square), reads=[bk(oc)], writes=[B("lnsq", oc % 2)])
            S.pe(I("matmul", pmsq[:, 0:n], onesDb[:], sqb[:, 0:n], start=(oc == 0), stop=(oc == 7)),
                 reads=[B("lnsq", oc % 2), B("onesDb")], writes=[bpmsq])
        S.act(I("copy", out=lnmean[:, 0:n], in_=pmean[:, 0:n]), reads=[bpmean], writes=[B("lnmean")])
        S.dve(I("tensor_tensor", out=lnm2[:, 0:n], in0=lnmean[:, 0:n], in1=lnmean[:, 0:n], op=ALU.mult), reads=[B("lnmean")], writes=[B("lnm2")])
        S.dve(I("tensor_tensor", out=lnm2[:, 0:n], in0=pmsq[:, 0:n], in1=lnm2[:, 0:n], op=ALU.subtract), reads=[bpmsq, B("lnm2")], writes=[B("lnm2")])
        S.act(I("activation", out=lnm2[:, 0:n], in_=lnm2[:, 0:n], func=AF.Ln, bias=epsln[:, 0:1], scale=1.0), reads=[B("lnm2"), B("eps")], writes=[B("lnm2")])
        S.act(I("activation", out=lnrstd[:, 0:n], in_=lnm2[:, 0:n], func=AF.Exp, scale=-0.5), reads=[B("lnm2")], writes=[B("lnrstd")])
        for oc in range(8):
            S.dve(I("tensor_tensor", out=tbuf[:, oc, 0:n], in0=tbuf[:, oc, 0:n], in1=lnmean[:, 0:n], op=ALU.subtract),
                  reads=[bk(oc), B("lnmean")], writes=[bk(oc)])
            S.dve(I("tensor_tensor", out=tbuf[:, oc, 0:n], in0=tbuf[:, oc, 0:n], in1=lnrstd[:, 0:n], op=ALU.mult),
                  reads=[bk(oc), B("lnrstd")], writes=[bk(oc)])
            for (eng, dstf, scf, bif, wrf) in outs:
                if eng == "act":
                    S.act(I("activation", out=dstf(oc), in_=tbuf[:, oc, 0:n], func=AF.Identity, scale=scf(oc), bias=bif(oc)),
                          reads=[bk(oc)] + MD, writes=wrf(oc))
                else:
                    S.dve(I("tensor_scalar", out=dstf(oc), in0=tbuf[:, oc, 0:n], scalar1=scf(oc), scalar2=bif(oc),
                            op0=ALU.mult, op1=ALU.add),
                          reads=[bk(oc)] + MD, writes=wrf(oc))

    self_cv = [None]
    lnsq = [None, None]
    lntb = [None, None]
    lnmean = lnm2 = lnrstd = None

    def outproj_phase(s, l):
        S.phase = "outproj_phase" + str(l)
        nonlocal lnsq, lntb, lnmean, lnm2, lnrstd
        S.barrier()
        cv = Carver()
        self_cv[0] = cv
        tbs = [cv(F32, 8, 512) for _ in range(2)]
        xss = [cv(F32, 8, 512) for _ in range(2)]
        lnsq = [cv(BF16, 512) for _ in range(2)]
        lntb = [cv(BF16, 512) for _ in range(2)]
        lnmean = cv(F32, 512)
        lnm2 = cv(F32, 512)
        lnrstd = cv(F32, 512)
        wv = wview(1, BF16, 8, 1024)
        wb = [B("wbuf", 1)]
        ensure("out", s, l)
        bi = 0
        for (t0, n) in BLOCKS:
            if t0 < LC and l == DEPTH - 1:
                continue
            tb = tbs[bi % 2]
            xs = xss[bi % 2]
            par = bi % 2
            btb = lambda oc, par=par: B("tb", par, oc)
            bxs = lambda oc, par=par: B("xs", par, oc)
            bxs_all = [bxs(oc) for oc in range(8)]
            bi += 1
            mi = msel(s, t0)
            xrb = [B("xres", s, tt) for tt in tiles_of(t0, n)]
            S.dma(I("dma_start", out=xs[:, :, 0:n], in_=xres[s][:, :, t0:t0 + n]), reads=xrb, writes=bxs_all)
            for oc in range(8):
                py, bpy = bank("proj", [0, 1, 2, 3])
                for k in range(8):
                    S.pe(I("matmul", py[:, 0:n], wv[:, k, 128 * oc:128 * oc + 128], obuf[:, k, t0:t0 + n],
                                                                       start=(k == 0), stop=(k == 7)),
                         reads=wb + [B("obuf", tt) for tt in tiles_of(t0, n)], writes=[bpy])
                S.dve(I("scalar_tensor_tensor", out=tb[:, oc, 0:n], in0=py[:, 0:n], scalar=mder[:, l, 1, oc, mi:mi + 1], in1=xs[:, oc, 0:n],
                                                                             op0=ALU.mult, op1=ALU.add),
                      reads=[bpy, bxs(oc)] + MD, writes=[btb(oc)])
            outs = [
                ("act", lambda oc, n=n, xs=xs: xs[:, oc, 0:n], lambda oc: lnT[:, l, 0, oc:oc + 1], lambda oc: lnT[:, l, 1, oc:oc + 1], lambda oc, bxs=bxs: [bxs(oc)]),
                ("dve", lambda oc, t0=t0, n=n: ubuf[:, oc, t0:t0 + n], lambda oc, mi=mi: mder[:, l, 4, oc, mi:mi + 1], lambda oc, mi=mi: mder[:, l, 5, oc, mi:mi + 1],
                 lambda oc, t0=t0, n=n: ub(t0, n)),
            ]
            layer_norm_block(tb, btb, n, outs)
            S.dma(I("dma_start", out=xres[s][:, :, t0:t0 + n], in_=xs[:, :, 0:n]), reads=bxs_all, writes=xrb)

    def ffn_phase(s, l):
        S.phase = "ffn_phase" + str(l)
        nonlocal lnsq, lntb, lnmean, lnm2, lnrstd
        S.barrier()
        last = (l == DEPTH - 1)
        cv = Carver()
        self_cv[0] = cv
        PTOK = 1024 if last else 1152
        hT = cv(BF16, NHC, PTOK)
        w2 = [cv(BF16, NHC, 256) for _ in range(2)]
        lnsq = [cv(BF16, 512) for _ in range(2)]
        lntb = [cv(BF16, 512) for _ in range(2)]
        lnmean = cv(F32, 512)
        lnm2 = cv(F32, 512)
        lnrstd = cv(F32, 512)
        sa = [cv(F32, 512) for _ in range(2)]
        ostage = [cv(F32, 1024) for _ in range(2)] if last else None
        tfull = obuf[:].rearrange("p k t -> p (k t)").bitcast(F32)[:, 0:8 * PTOK].rearrange("p (k t) -> p k t", k=8, t=PTOK)
        if last:
            parts = [[BLOCKS[1], BLOCKS[2]], [BLOCKS[3], BLOCKS[4]]]
        else:
            parts = [[(0, 256), (256, 512), (768, 384)], [(1152, 512), (1664, 512), (2176, 128)]]
        wf1 = wf1_d[l].rearrange("(k p) c -> p k c", p=128)
        wf2 = wf2_d[l].rearrange("(j p) c -> p j c", p=128)
        for part in parts:
            pt0 = part[0][0]
            ptn = sum(n for (_, n) in part)
            mi = msel(s, pt0)
            nblk = len(part)
            btf = lambda bi_, oc: B("tfull", bi_, oc)
            btf_all = [btf(bi_, oc) for bi_ in range(nblk) for oc in range(8)]
            for js in range(11):
                wi = js % 2
                wv = wview(wi, BF16, 8, 512)
                wb = [B("wbuf", wi)]
                if js == 0 and part is parts[0]:
                    ensure("ffn0", s, l)
                else:
                    S.dma(I("dma_start", out=wv[:, :, 0:256], in_=wf1[:, :, 256 * js:256 * js + 256]), writes=wb, q="pool")
                    S.dma(I("dma_start", out=wv[:, :, 256:512], in_=wf1[:, :, FH + 256 * js:FH + 256 * js + 256]), writes=wb, q="pool")
                for (t0, n) in part:
                    lo = t0 - pt0
                    for jj in range(2):
                        j = 2 * js + jj
                        pa, bpa = bank("ffa", [0, 1, 2])
                        pg, bpg = bank("ffb", [3, 4, 5])
                        for k in range(8):
                            S.pe(I("matmul", pa[:, 0:n], wv[:, k, 128 * jj:128 * jj + 128], ubuf[:, k, t0:t0 + n],
                                                                                      start=(k == 0), stop=(k == 7)),
                                 reads=wb + ub(t0, n), writes=[bpa])
                        for k in range(8):
                            S.pe(I("matmul", pg[:, 0:n], wv[:, k, 256 + 128 * jj:256 + 128 * jj + 128], ubuf[:, k, t0:t0 + n],
                                                                                      start=(k == 0), stop=(k == 7)),
                                 reads=wb + ub(t0, n), writes=[bpg])
                        si = j % 2
                        S.act(I("activation", out=sa[si][:, 0:n], in_=pa[:, 0:n], func=AF.Silu), reads=[bpa], writes=[B("sa", si)])
                        S.dve(I("tensor_tensor", out=hT[:, j, lo:lo + n], in0=pg[:, 0:n], in1=sa[si][:, 0:n], op=ALU.mult),
                              reads=[bpg, B("sa", si)], writes=[B("hT")])
            if part is parts[-1]:
                ensure("na", *next_sl(s, l))
            S.dma(I("dma_start", out=tfull[:, :, 0:ptn], in_=xres[s][:, :, pt0:pt0 + ptn]),
                  reads=[B("xres", s, tt) for tt in tiles_of(pt0, ptn)], writes=btf_all)
            for oc2 in range(4):
                w2v = w2[oc2 % 2]
                bw2 = [B("w2", oc2 % 2)]
                S.dma(I("dma_start", out=w2v, in_=wf2[:, :, 256 * oc2:256 * oc2 + 256]), writes=bw2, q="pool")
                for bi_, (t0, n) in enumerate(part):
                    lo = t0 - pt0
                    mi = msel(s, t0)
                    for oo in range(2):
                        oc = 2 * oc2 + oo
                        py, bpy = bank("ffy", [0, 1, 2, 3])
                        for j in range(NHC):
                            S.pe(I("matmul", py[:, 0:n], w2v[:, j, 128 * oo:128 * oo + 128], hT[:, j, lo:lo + n],
                                                                                        start=(j == 0), stop=(j == NHC - 1)),
                                 reads=bw2 + [B("hT")], writes=[bpy])
                        S.dve(I("scalar_tensor_tensor", out=tfull[:, oc, lo:lo + n], in0=py[:, 0:n], scalar=mder[:, l, 3, oc, mi:mi + 1],
                                                                                            in1=tfull[:, oc, lo:lo + n], op0=ALU.mult, op1=ALU.add),
                              reads=[bpy, btf(bi_, oc)] + MD, writes=[btf(bi_, oc)])
            for bi_, (t0, n) in enumerate(part):
                lo = t0 - pt0
                mi = msel(s, t0)
                tbv = tfull[:, :, lo:lo + n]
                bkf = lambda oc, bi_=bi_: btf(bi_, oc)
                bkf_all = [bkf(oc) for oc in range(8)]
                xrb = [B("xres", s, tt) for tt in tiles_of(t0, n)]
                if not last:
                    outs = [
                        ("dve", lambda oc, t0=t0, n=n: ubuf[:, oc, t0:t0 + n], lambda oc, mi=mi: mder2[:, l, 0, oc, mi:mi + 1], lambda oc, mi=mi: mder2[:, l, 1, oc, mi:mi + 1],
                         lambda oc, t0=t0, n=n: ub(t0, n)),
                        ("act", lambda oc, tbv=tbv: tbv[:, oc, :], lambda oc: lnT[:, l, 2, oc:oc + 1], lambda oc: lnT[:, l, 3, oc:oc + 1], lambda oc, bkf=bkf: [bkf(oc)]),
                    ]
                    layer_norm_block(tbv, bkf, n, outs)
                    S.dma(I("dma_start", out=xres[s][:, :, t0:t0 + n], in_=tbv), reads=bkf_all, writes=xrb)
                else:
                    outs = [("act", lambda oc, tbv=tbv: tbv[:, oc, :], lambda oc: lnT[:, l, 2, oc:oc + 1], lambda oc: lnT[:, l, 3, oc:oc + 1], lambda oc, bkf=bkf: [bkf(oc)])]
                    layer_norm_block(tbv, bkf, n, outs)
                    for qs in range(n // 128):
                        tq = (t0 + 128 * qs) // 128
                        og = ostage[tq % 2]
                        for half in range(2):
                            pt, bpt = bank("ffo", [0, 1, 2, 3])
                            for kk in range(4):
                                k = 4 * half + kk
                                S.pe(I("transpose", pt[:, 128 * kk:128 * kk + 128], tbv[:, k, 128 * qs:128 * qs + 128], identf[:]),
                                     reads=[bkf(k), B("identf")], writes=[bpt])
                            S.act(I("copy", out=og[:, 512 * half:512 * half + 512], in_=pt[:, 0:512]),
                                  reads=[bpt], writes=[B("ostage", tq % 2)])
                        S.dma(I("dma_start", out=out_d[s, 128 * (tq - 2):128 * (tq - 2) + 128, :], in_=og),
                              reads=[B("ostage", tq % 2)], writes=[B("out")])

    for s in range(nseq):
        load_x(s)
        if stage == 1 and s == 0:
            dump("u0", ubuf[:], [128, 8, T], BF16, B("ubuf", NTT - 1))
        for l in range(DEPTH):
            if stage >= 2:
                na_phase(s, l)
            if stage >= 3:
                gla_phase(s, l)
            if stage >= 4:
                gqa_phase(s, l)
            if stage in (2, 3, 4) and s == 0 and l == 0:
                S.barrier()
                S.pool(I("memset", epsln[:], LN_EPS_S), reads=[B("obuf", tt) for tt in range(NTT)], writes=[B("eps")])
                dump("o0", obuf[:], [128, 8, T], BF16, B("eps"))
                break
            if stage >= 5:
                outproj_phase(s, l)
            if stage == 5 and s == 0 and l == 0:
                S.barrier()
                S.pool(I("memset", epsln[:], LN_EPS_S), reads=[B("ubuf", tt) for tt in range(NTT)], writes=[B("eps")])
                dump("u2", ubuf[:], [128, 8, T], BF16, B("eps"))
                dump("x1", xres[0], [128, 8, T], F32, B("eps"))
                break
            if stage >= 6:
                ffn_phase(s, l)
            if stage == 6 and s == 0 and l == 0:
                S.barrier()
                S.pool(I("memset", epsln[:], LN_EPS_S), reads=[B("ubuf", tt) for tt in range(NTT)], writes=[B("eps")])
                dump("u1n", ubuf[:], [128, 8, T], BF16, B("eps"))
                dump("x2", xres[0], [128, 8, T], F32, B("eps"))
                break


    S.emit(final_bufs=dump_list + [B("out")])
    S.close()
    global LAST_SCHED
    LAST_SCHED = S
    return nc


def host_consts():
    eye = np.eye(128, dtype=np.float32)
    onesd = np.full((128, 128), 1.0 / 1024.0, np.float32)
    bon = np.zeros((128, 128), np.float32)
    bon[:64, :64] = 1.0 / 64
    bon[64:, 64:] = 1.0 / 64
    R = np.zeros((64, 64), np.float32)
    for d in range(16):
        R[d, d + 16] = -1.0
        R[d + 16, d] = 1.0
        R[d + 32, d + 48] = -1.0
        R[d + 48, d + 32] = 1.0
    rl = np.zeros((128, 128), np.float32)
    rl[:64, :64] = R.T
    rl[64:, 64:] = R.T
    hm = np.zeros_like(eye)
    for h in range(4):
        hm[32 * h:32 * h + 32, h] = 1.0
    cm = np.stack([eye, onesd, bon, rl, hm], axis=1)
    sidx = np.arange(64)
    mf = (sidx[None, :] >= sidx[:, None]).astype(np.float32)
    mb = (sidx[None, :] <= sidx[:, None]).astype(np.float32)
    tri = np.zeros((128, 2, 256), np.float32)
    for half in range(2):
        tri[64 * half:64 * half + 64, 0, :] = np.tile(mf, (1, 4))
        tri[64 * half:64 * half + 64, 1, :] = np.tile(mb, (1, 4))
    t = np.arange(NL)
    row = (t // 64).astype(np.float32)
    col = (t % 64).astype(np.float32)
    inv_freq = (10000.0 ** (-np.arange(16, dtype=np.float32) / 16)).astype(np.float32)
    ang_r = row[:, None] * inv_freq
    ang_c = col[:, None] * inv_freq
    ang = np.concatenate([ang_r, ang_r, ang_c, ang_c], axis=-1).astype(np.float32)
    cosT = np.tile(np.cos(ang).T, (2, 1)).astype(np.float32)
    sinT = np.tile(np.sin(ang).T, (2, 1)).astype(np.float32)
    return cm, tri, np.ascontiguousarray(cosT), np.ascontiguousarray(sinT)


def prep_shared(inp):
    f = lambda a: np.ascontiguousarray(np.asarray(a, dtype=np.float32))
    cm, tri, cosT, sinT = host_consts()
    sh = {}
    sh["w_ada"] = f(inp["w_ada"])
    sh["b_adaT"] = f(np.asarray(inp["b_ada"]).reshape(DEPTH, 48, 128).transpose(0, 2, 1))
    sh["w_in"] = f(inp["w_in"])
    sh["w_out"] = f(inp["w_out"])
    sh["w_ffn_in"] = f(inp["w_ffn_in"])
    sh["w_ffn_out"] = f(inp["w_ffn_out"])
    ln = np.stack([np.asarray(inp[k]) for k in ("ln1_g", "ln1_b", "ln2_g", "ln2_b")], axis=1)
    sh["lnT"] = f(ln.reshape(DEPTH, 4, 8, 128).transpose(3, 0, 1, 2))
    valid, dri, dci = na_index_tables()
    rpb = np.asarray(inp["na_rpb"], dtype=np.float32)
    g = rpb[:, :, dri, dci]
    g = np.where(valid[None, None], g, np.float32(-30000.0))
    sh["nab"] = f(g.transpose(0, 3, 1, 2, 4).reshape(DEPTH, 128, 6, NCOMBO * 128))
    wa2 = np.zeros((DEPTH, 2, 32, 128), np.float32)
    gw = np.asarray(inp["gla_wa2"], dtype=np.float32)
    wa2[:, 0, 0:16, :] = gw[:, 0]
    wa2[:, 1, 16:32, :] = gw[:, 1]
    sh["wa2"] = wa2
    sh["nbaT"] = f(np.asarray(inp["gla_ba"]).transpose(2, 0, 1))
    sh["glnw"] = f(np.tile(np.asarray(inp["gla_norm_w"]), (1, 4)).reshape(DEPTH, 1, 256))
    qk = np.stack([np.tile(np.asarray(inp["gqa_qnorm_w"]), (1, 2)), np.tile(np.asarray(inp["gqa_knorm_w"]), (1, 2))], axis=2)
    sh["qkwT"] = f(qk.transpose(1, 0, 2))
    sh["cosT"] = cosT
    sh["sinT"] = sinT
    sh["cmats"] = f(cm)
    sh["trim"] = f(tri)
    return sh


def prep_core(inp, core, sh):
    f = lambda a: np.ascontiguousarray(np.asarray(a, dtype=np.float32))
    b0 = core * NSEQ
    m = dict(sh)
    m["x"] = f(inp["x"][b0:b0 + NSEQ])
    m["ctx"] = f(inp["ctx"][b0:b0 + NSEQ])
    cc = np.stack([np.asarray(inp["c"][b0]), np.asarray(inp["c"][b0 + 1]), np.asarray(inp["c_ctx"])], axis=1)
    m["cT"] = f(cc.reshape(8, 128, 3).transpose(1, 0, 2))
    return m


def kernel(**inputs):
    sh = prep_shared(inputs)
    nc = build_nc()
    in_maps = [prep_core(inputs, c, sh) for c in range(8)]
    res = run_bass_kernel_spmd(nc, in_maps, core_ids=list(range(8)))
    out = np.concatenate([np.asarray(r["out"]) for r in res.results], axis=0)
    return out.astype(np.float32)
```
